# Optimizing a Trainium2 kernel written in Bass

```python
import math
import jax, jax.numpy as jnp
from jax import lax
import numpy as np

D_MODEL = 1024
BATCH = 4
SEQ = 4096
DEPTH = 2
DEC_BATCH = 128
DEC_SEQ = 4
PAST_LEN = 2048
PAGE_SIZE = 128

N_EVEN = (DEPTH + 1) // 2
N_ODD = DEPTH // 2
H_A = 4
DH_A = 64
A_QK = H_A * 2 * DH_A
A_V = H_A * 2 * DH_A
Q_BLK = 128
S5_CH = 512
S5_GROUP = 16
S5_G = S5_CH // S5_GROUP
S5_P = 64
H_C = 8
DH_C = 64
C_W = H_C * DH_C
C_BRANCHES = ((128, 1), (512, 4), (2048, 16))
C_MAX_WIN = 2048
C_BLK = 128
H_D = 8
N_D = 64
D_W = H_D * N_D
D_LORA_W = 64
D_LORA_A = 64
D_LORA_G = 128
D_COLS = 3 * D_W + D_LORA_W + D_LORA_A + D_LORA_G
GN_EPS = 64e-5
EVEN_IN = 2 * A_QK + A_V + S5_CH
ODD_IN = 3 * C_W + D_COLS
MIX_W = 1024
D_FF = 2816
ALPHA = (2.0 * DEPTH) ** 0.25
BETA = (8.0 * DEPTH) ** -0.25
LN_EPS = 1e-5
NEG = -1e30

kernel_name = 'hybrid_diffattn_s5_dilated_rwkv7_step'


def _layer_norm(x, g, b):
    xf = x.astype(jnp.float32)
    mu = jnp.mean(xf, -1, keepdims=True)
    var = jnp.mean(jnp.square(xf - mu), -1, keepdims=True)
    return ((xf - mu) * lax.rsqrt(var + LN_EPS) * g + b).astype(x.dtype)


def _rms_norm(x, g):
    xf = x.astype(jnp.float32)
    return (xf * lax.rsqrt(jnp.mean(jnp.square(xf), -1, keepdims=True) + LN_EPS) * g).astype(x.dtype)


def _modulate(x, m):
    return x * (1.0 + m[:, 1]) + m[:, 0]


def _swiglu(h, wg, wu, wd):
    return (jax.nn.silu(h @ wg) * (h @ wu)) @ wd


def _diff_core(q, k, v, lam, qpos, kpos):
    s = jnp.einsum('bqhmd,bkhmd->bmhqk', q, k).astype(jnp.float32) * (DH_A ** -0.5)
    s = jnp.where(kpos[None, :] <= qpos[:, None], s, NEG)
    p = jax.nn.softmax(s, axis=-1)
    w = p[:, 0] - lam * p[:, 1]
    return jnp.einsum('bhqk,bkhe->bqhe', w.astype(v.dtype), v)


def _diff_attn_prompt(q, k, v, lam):
    bq, t = q.shape[0], q.shape[1]
    kpos = jnp.arange(t)

    def block(i):
        start = i * Q_BLK
        qb = lax.dynamic_slice_in_dim(q, start, Q_BLK, axis=1)
        return _diff_core(qb, k, v, lam, start + jnp.arange(Q_BLK), kpos)

    out = lax.map(block, jnp.arange(t // Q_BLK))
    return jnp.moveaxis(out, 0, 1).reshape(bq, t, H_A, 2 * DH_A)


def _complex_affine_combine(e1, e2):
    ar1, ai1, br1, bi1 = e1
    ar2, ai2, br2, bi2 = e2
    return (ar2 * ar1 - ai2 * ai1, ar2 * ai1 + ai2 * ar1,
            ar2 * br1 - ai2 * bi1 + br2, ar2 * bi1 + ai2 * br1 + bi2)


def _s5(u, h0_re, h0_im, a_re, a_im, log_dt, b_re, b_im, c_re, c_im, d, glu_w, glu_b):
    f32 = jnp.float32
    bu, t, _ = u.shape
    a_re, a_im = a_re.astype(f32), a_im.astype(f32)
    dt = jnp.exp(log_dt.astype(f32))[:, None]
    mag = jnp.exp(a_re * dt)
    lam_re, lam_im = mag * jnp.cos(a_im * dt), mag * jnp.sin(a_im * dt)
    den = a_re * a_re + a_im * a_im
    nr = lam_re - 1.0
    f_re = (nr * a_re + lam_im * a_im) / den
    f_im = (lam_im * a_re - nr * a_im) / den
    b_re, b_im = b_re.astype(f32), b_im.astype(f32)
    bb_re = f_re[..., None] * b_re - f_im[..., None] * b_im
    bb_im = f_re[..., None] * b_im + f_im[..., None] * b_re
    ug = u.astype(f32).reshape(bu, t, S5_G, S5_GROUP)
    x_re = jnp.einsum('btgc,gpc->btgp', ug, bb_re)
    x_im = jnp.einsum('btgc,gpc->btgp', ug, bb_im)
    h0_re, h0_im = h0_re.astype(f32), h0_im.astype(f32)
    x_re = x_re.at[:, 0].add(lam_re * h0_re - lam_im * h0_im)
    x_im = x_im.at[:, 0].add(lam_re * h0_im + lam_im * h0_re)
    ar = jnp.broadcast_to(lam_re, x_re.shape)
    ai = jnp.broadcast_to(lam_im, x_im.shape)
    _, _, h_re, h_im = lax.associative_scan(_complex_affine_combine, (ar, ai, x_re, x_im), axis=1)
    y = (jnp.einsum('btgp,gcp->btgc', h_re, c_re.astype(f32))
         - jnp.einsum('btgp,gcp->btgc', h_im, c_im.astype(f32)))
    y = y.reshape(bu, t, S5_CH) + d * u.astype(f32)
    z = jax.nn.gelu(y)
    out = z * jax.nn.sigmoid(z @ glu_w.astype(f32) + glu_b)
    return out, h_re[:, -1], h_im[:, -1]


def _dilated_branch_prompt(q, k, v, dil):
    bq, t, h, e = q.shape
    span = dil * C_BLK
    t_pad = -(-t // span) * span
    n_sub = t_pad // dil
    nb = n_sub // C_BLK

    def blocks(a):
        a = jnp.pad(a, ((0, 0), (0, t_pad - t), (0, 0), (0, 0)))
        a = a.reshape(bq, n_sub, dil, h, e).transpose(0, 2, 1, 3, 4)
        return a.reshape(bq, dil, nb, C_BLK, h, e)

    def with_prev(a):
        prev = jnp.pad(a, ((0, 0), (0, 0), (1, 0), (0, 0), (0, 0), (0, 0)))[:, :, :-1]
        return jnp.concatenate([prev, a], axis=3)

    qb = blocks(q)
    kk, vv = with_prev(blocks(k)), with_prev(blocks(v))
    s = jnp.einsum('brnqhe,brnkhe->brnhqk', qb, kk).astype(jnp.float32) * (DH_C ** -0.5)
    qi = jnp.arange(C_BLK)[:, None] + C_BLK
    ki = jnp.arange(2 * C_BLK)[None, :]
    dist = qi - ki
    band = (dist >= 0) & (dist <= C_BLK)
    exists = (jnp.arange(nb)[:, None, None] > 0) | (ki[None] >= C_BLK)
    mask = band[None] & exists
    s = jnp.where(mask[None, None, :, None], s, NEG)
    m = jnp.max(s, -1, keepdims=True)
    pr = jnp.exp(s - m)
    den = jnp.sum(pr, -1, keepdims=True)
    o = jnp.einsum('brnhqk,brnkhe->brnqhe', (pr / den).astype(v.dtype), vv)
    lse = (m + jnp.log(den))[..., 0]
    o = o.reshape(bq, dil, n_sub, h, e).transpose(0, 2, 1, 3, 4).reshape(bq, t_pad, h, e)[:, :t]
    lse = lse.transpose(0, 1, 2, 4, 3).reshape(bq, dil, n_sub, h)
    lse = lse.transpose(0, 2, 1, 3).reshape(bq, t_pad, h)[:, :t]
    return o, lse


def _dilated_branch_sample(q, k_all, v_all, buf_len, dil):
    s_len = q.shape[1]
    j = jnp.arange(C_BLK + 1)
    idx = buf_len + jnp.arange(s_len)[:, None] - dil * j[None, :]
    valid = idx >= 0
    idx = jnp.maximum(idx, 0)
    kg, vg = k_all[:, idx], v_all[:, idx]
    s = jnp.einsum('bshe,bsjhe->bhsj', q, kg).astype(jnp.float32) * (DH_C ** -0.5)
    s = jnp.where(valid[None, None], s, NEG)
    m = jnp.max(s, -1, keepdims=True)
    pr = jnp.exp(s - m)
    den = jnp.sum(pr, -1, keepdims=True)
    o = jnp.einsum('bhsj,bsjhe->bshe', (pr / den).astype(v_all.dtype), vg)
    lse = jnp.transpose((m + jnp.log(den))[..., 0], (0, 2, 1))
    return o, lse


def _combine_by_denominator(outs, lses):
    wts = jax.nn.softmax(jnp.stack(lses, 0), axis=0)
    return jnp.einsum('nbth,nbthe->bthe', wts, jnp.stack(outs, 0).astype(jnp.float32))


def _rwkv7(pd, shift0, s0, mu, w0, w2, a0, a2, g2, k_k, k_a, r_k, gn_w, gn_b):
    f32 = jnp.float32
    bh, t, _ = pd.shape
    prev = jnp.concatenate([shift0[:, None].astype(pd.dtype), pd[:, :-1]], axis=1)
    xm = (pd + (prev - pd) * mu).astype(f32)
    o1, o2, o3 = D_W, 2 * D_W, 3 * D_W
    o4, o5 = o3 + D_LORA_W, o3 + D_LORA_W + D_LORA_A
    r, k, v = xm[..., :o1], xm[..., o1:o2], xm[..., o2:o3]
    wl, al, gl = xm[..., o3:o4], xm[..., o4:o5], xm[..., o5:]
    w_log = -jax.nn.softplus(-(w0 + jnp.tanh(wl) @ w2)) - 0.5
    decay = jnp.exp(-jnp.exp(w_log.astype(f32)))
    a = jax.nn.sigmoid(a0 + al @ a2).astype(f32)
    g = (jax.nn.sigmoid(gl) @ g2).astype(f32)

    def heads(z):
        return z.reshape(bh, t, H_D, N_D)

    rh, kh, vh, dh, ah = heads(r), heads(k), heads(v), heads(decay), heads(a)
    kk = kh * k_k.reshape(H_D, N_D)
    kk = kk / jnp.maximum(jnp.sqrt(jnp.sum(kk * kk, -1, keepdims=True)), 1e-12)
    kh = kh * (1.0 + (ah - 1.0) * k_a.reshape(H_D, N_D))

    def step(s, inp):
        r_t, w_t, k_t, v_t, kk_t, a_t = inp
        sk = jnp.einsum('bhvk,bhk->bhv', s, kk_t)
        s = (s * w_t[:, :, None, :] - sk[..., None] * (kk_t * a_t)[:, :, None, :]
             + v_t[..., None] * k_t[:, :, None, :])
        return s, jnp.einsum('bhvk,bhk->bhv', s, r_t)

    seq = tuple(jnp.moveaxis(z.astype(f32), 1, 0) for z in (rh, dh, kh, vh, kk, ah))
    s_final, ys = lax.scan(step, s0.astype(f32), seq)
    y = jnp.moveaxis(ys, 0, 1)
    my = jnp.mean(y, -1, keepdims=True)
    vy = jnp.mean(jnp.square(y - my), -1, keepdims=True)
    y = ((y - my) * lax.rsqrt(vy + GN_EPS)).reshape(bh, t, D_W) * gn_w + gn_b
    bonus = (jnp.sum(rh * kh * r_k, -1, keepdims=True) * vh).reshape(bh, t, D_W)
    return (y + bonus) * g, pd[:, -1], s_final


def setup_inputs(seed: int = 0) -> dict:
    key = jax.random.key(seed)
    ks = iter(jax.random.split(key, 64))

    def nrm(shape, scale=1.0):
        return jax.random.normal(next(ks), shape, jnp.float32) * scale

    def unif(shape, lo, hi):
        return jax.random.uniform(next(ks), shape, jnp.float32, lo, hi)

    n_pages = PAST_LEN // PAGE_SIZE
    used = DEC_BATCH * n_pages
    n_pool = used + (used + 3) // 4
    win_buf = min(C_MAX_WIN, PAST_LEN)
    page_table = jax.random.permutation(next(ks), n_pool)[:used].reshape(DEC_BATCH, n_pages).astype(jnp.int32)

    even_w_in = nrm((N_EVEN, D_MODEL, EVEN_IN), D_MODEL ** -0.5)
    even_w_in = even_w_in.at[..., 2 * A_QK:2 * A_QK + A_V].multiply(BETA)
    odd_w_in = nrm((N_ODD, D_MODEL, ODD_IN), D_MODEL ** -0.5)
    odd_w_in = odd_w_in.at[..., 2 * C_W:3 * C_W].multiply(BETA)
    odd_w_in = odd_w_in.at[..., 3 * C_W + 2 * D_W:3 * C_W + 3 * D_W].multiply(BETA)
    n_idx = jnp.arange(S5_P, dtype=jnp.float32)

    return {
        'x_prompt': nrm((BATCH, SEQ, D_MODEL)),
        'x_sample': nrm((DEC_BATCH, DEC_SEQ, D_MODEL)),
        'cache_a_k': nrm((N_EVEN, n_pool, PAGE_SIZE, H_A, 2, DH_A)),
        'cache_a_v': nrm((N_EVEN, n_pool, PAGE_SIZE, H_A, 2 * DH_A), BETA),
        'state_s5_re': nrm((N_EVEN, DEC_BATCH, S5_G, S5_P), 0.1),
        'state_s5_im': nrm((N_EVEN, DEC_BATCH, S5_G, S5_P), 0.1),
        'cache_c_k': nrm((N_ODD, DEC_BATCH, win_buf, H_C, DH_C)),
        'cache_c_v': nrm((N_ODD, DEC_BATCH, win_buf, H_C, DH_C), BETA),
        'state_d_wkv': nrm((N_ODD, DEC_BATCH, H_D, N_D, N_D), 0.1),
        'state_d_shift': nrm((N_ODD, DEC_BATCH, D_COLS)),
        'page_table': page_table,
        'c_prompt': nrm((BATCH, D_MODEL)),
        'c_sample': nrm((DEC_BATCH, D_MODEL)),
        'ada_w': nrm((DEPTH, D_MODEL, 9 * D_MODEL), 0.2 * D_MODEL ** -0.5),
        'ada_b': nrm((DEPTH, 9 * D_MODEL), 0.02),
        'ln_g': 1.0 + nrm((DEPTH, 3, D_MODEL), 0.02),
        'ln_b': nrm((DEPTH, 3, D_MODEL), 0.02),
        'ffn_w_gate': nrm((DEPTH, 2, D_MODEL, D_FF), D_MODEL ** -0.5),
        'ffn_w_up': nrm((DEPTH, 2, D_MODEL, D_FF), D_MODEL ** -0.5),
        'ffn_w_down': nrm((DEPTH, 2, D_FF, D_MODEL), BETA * D_FF ** -0.5),
        'even_w_in': even_w_in,
        'even_w_out': nrm((N_EVEN, MIX_W, D_MODEL), BETA * MIX_W ** -0.5),
        'diff_lambda': nrm((N_EVEN, 4, DH_A), 0.1),
        'diff_subln': 1.0 + nrm((N_EVEN, 2 * DH_A), 0.02),
        's5_a_re': -0.5 + nrm((N_EVEN, S5_G, S5_P), 0.02),
        's5_a_im': math.pi * n_idx + nrm((N_EVEN, S5_G, S5_P), 0.02),
        's5_log_dt': unif((N_EVEN, S5_G), math.log(1e-3), math.log(1e-1)),
        's5_b_re': nrm((N_EVEN, S5_G, S5_P, S5_GROUP), (2 * S5_GROUP) ** -0.5),
        's5_b_im': nrm((N_EVEN, S5_G, S5_P, S5_GROUP), (2 * S5_GROUP) ** -0.5),
        's5_c_re': nrm((N_EVEN, S5_G, S5_GROUP, S5_P), (2 * S5_P) ** -0.5),
        's5_c_im': nrm((N_EVEN, S5_G, S5_GROUP, S5_P), (2 * S5_P) ** -0.5),
        's5_d': nrm((N_EVEN, S5_CH)),
        's5_glu_w': nrm((N_EVEN, S5_CH, S5_CH), S5_CH ** -0.5),
        's5_glu_b': nrm((N_EVEN, S5_CH), 0.02),
        'odd_w_in': odd_w_in,
        'odd_w_out': nrm((N_ODD, MIX_W, D_MODEL), BETA * MIX_W ** -0.5),
        'rwkv_mu': unif((N_ODD, D_COLS), 0.0, 1.0),
        'rwkv_w0': unif((N_ODD, D_W), -6.0, -1.0),
        'rwkv_w2': nrm((N_ODD, D_LORA_W, D_W), 0.5 * D_LORA_W ** -0.5),
        'rwkv_a0': nrm((N_ODD, D_W), 0.1),
        'rwkv_a2': nrm((N_ODD, D_LORA_A, D_W), 0.5 * D_LORA_A ** -0.5),
        'rwkv_g2': nrm((N_ODD, D_LORA_G, D_W), D_LORA_G ** -0.5),
        'rwkv_k_k': 0.85 + nrm((N_ODD, D_W), 0.02),
        'rwkv_k_a': 1.0 + nrm((N_ODD, D_W), 0.02),
        'rwkv_r_k': nrm((N_ODD, H_D, N_D), 0.1),
        'rwkv_gn_w': 1.0 + nrm((N_ODD, D_W), 0.02),
        'rwkv_gn_b': nrm((N_ODD, D_W), 0.02),
    }


def reference(x_prompt, x_sample, cache_a_k, cache_a_v, state_s5_re, state_s5_im, cache_c_k, cache_c_v,
              state_d_wkv, state_d_shift, page_table, c_prompt, c_sample, ada_w, ada_b, ln_g, ln_b,
              ffn_w_gate, ffn_w_up, ffn_w_down, even_w_in, even_w_out, diff_lambda, diff_subln,
              s5_a_re, s5_a_im, s5_log_dt, s5_b_re, s5_b_im, s5_c_re, s5_c_im, s5_d, s5_glu_w, s5_glu_b,
              odd_w_in, odd_w_out, rwkv_mu, rwkv_w0, rwkv_w2, rwkv_a0, rwkv_a2, rwkv_g2, rwkv_k_k,
              rwkv_k_a, rwkv_r_k, rwkv_gn_w, rwkv_gn_b):
    past_len = page_table.shape[1] * cache_a_k.shape[2]
    win_buf = cache_c_k.shape[2]

    def even_mix(h, l, sample):
        e = l // 2
        bh, t, _ = h.shape
        p = h @ even_w_in[e]
        q = p[..., :A_QK].reshape(bh, t, H_A, 2, DH_A)
        k = p[..., A_QK:2 * A_QK].reshape(bh, t, H_A, 2, DH_A)
        v = p[..., 2 * A_QK:2 * A_QK + A_V].reshape(bh, t, H_A, 2 * DH_A)
        u = p[..., 2 * A_QK + A_V:]
        lam_init = 0.8 - 0.6 * math.exp(-0.3 * l)
        lp = diff_lambda[e].astype(jnp.float32)
        lam = jnp.exp(jnp.sum(lp[0] * lp[1])) - jnp.exp(jnp.sum(lp[2] * lp[3])) + lam_init
        if sample:
            def gather(pool):
                return pool[page_table].reshape((bh, past_len) + pool.shape[2:])
            k_all = jnp.concatenate([gather(cache_a_k[e]), k], axis=1)
            v_all = jnp.concatenate([gather(cache_a_v[e]), v], axis=1)
            att = _diff_core(q, k_all, v_all, lam, past_len + jnp.arange(t), jnp.arange(past_len + t))
            h0r, h0i = state_s5_re[e], state_s5_im[e]
        else:
            att = _diff_attn_prompt(q, k, v, lam)
            h0r = jnp.zeros((bh, S5_G, S5_P), jnp.float32)
            h0i = jnp.zeros((bh, S5_G, S5_P), jnp.float32)
        att = _rms_norm(att, diff_subln[e]) * (1.0 - lam_init)
        y5, hr, hi = _s5(u, h0r, h0i, s5_a_re[e], s5_a_im[e], s5_log_dt[e], s5_b_re[e], s5_b_im[e],
                         s5_c_re[e], s5_c_im[e], s5_d[e], s5_glu_w[e], s5_glu_b[e])
        mix = jnp.concatenate([att.reshape(bh, t, A_V).astype(h.dtype), y5.astype(h.dtype)], -1) @ even_w_out[e]
        return mix, (k, v, hr, hi)

    def odd_mix(h, l, sample):
        o = l // 2
        bh, t, _ = h.shape
        p = h @ odd_w_in[o]
        q = p[..., :C_W].reshape(bh, t, H_C, DH_C)
        k = p[..., C_W:2 * C_W].reshape(bh, t, H_C, DH_C)
        v = p[..., 2 * C_W:3 * C_W].reshape(bh, t, H_C, DH_C)
        pd = p[..., 3 * C_W:]
        outs, lses = [], []
        if sample:
            k_all = jnp.concatenate([cache_c_k[o], k], axis=1)
            v_all = jnp.concatenate([cache_c_v[o], v], axis=1)
            for _, dil in C_BRANCHES:
                ob, lb = _dilated_branch_sample(q, k_all, v_all, win_buf, dil)
                outs.append(ob)
                lses.append(lb)
            shift0, s0 = state_d_shift[o], state_d_wkv[o]
            k_keep, v_keep = k, v
        else:
            for _, dil in C_BRANCHES:
                ob, lb = _dilated_branch_prompt(q, k, v, dil)
                outs.append(ob)
                lses.append(lb)
            shift0 = jnp.zeros((bh, D_COLS), pd.dtype)
            s0 = jnp.zeros((bh, H_D, N_D, N_D), jnp.float32)
            keep = min(C_MAX_WIN, t)
            k_keep, v_keep = k[:, t - keep:], v[:, t - keep:]
        att = _combine_by_denominator(outs, lses)
        yd, shift_new, s_new = _rwkv7(pd, shift0, s0, rwkv_mu[o], rwkv_w0[o], rwkv_w2[o], rwkv_a0[o],
                                      rwkv_a2[o], rwkv_g2[o], rwkv_k_k[o], rwkv_k_a[o], rwkv_r_k[o],
                                      rwkv_gn_w[o], rwkv_gn_b[o])
        mix = jnp.concatenate([att.reshape(bh, t, C_W).astype(h.dtype), yd.astype(h.dtype)], -1) @ odd_w_out[o]
        return mix, (k_keep, v_keep, s_new, shift_new)

    def trunk(x, c, sample):
        bx = x.shape[0]
        ak, av, s5r, s5i, ck, cv, dw, ds = [], [], [], [], [], [], [], []
        for l in range(DEPTH):
            mod = (jax.nn.silu(c) @ ada_w[l] + ada_b[l]).reshape(bx, 3, 3, 1, D_MODEL)
            f = _swiglu(_modulate(x, mod[:, 0]), ffn_w_gate[l, 0], ffn_w_up[l, 0], ffn_w_down[l, 0])
            x = _layer_norm(ALPHA * x + 0.5 * (1.0 + mod[:, 0, 2]) * f, ln_g[l, 0], ln_b[l, 0])
            h = _modulate(x, mod[:, 1])
            if l % 2 == 0:
                m, (k_r, v_r, hr, hi) = even_mix(h, l, sample)
                ak.append(k_r)
                av.append(v_r)
                s5r.append(hr)
                s5i.append(hi)
            else:
                m, (k_r, v_r, s_new, sh_new) = odd_mix(h, l, sample)
                ck.append(k_r)
                cv.append(v_r)
                dw.append(s_new)
                ds.append(sh_new)
            x = _layer_norm(ALPHA * x + (1.0 + mod[:, 1, 2]) * m, ln_g[l, 1], ln_b[l, 1])
            f = _swiglu(_modulate(x, mod[:, 2]), ffn_w_gate[l, 1], ffn_w_up[l, 1], ffn_w_down[l, 1])
            x = _layer_norm(ALPHA * x + 0.5 * (1.0 + mod[:, 2, 2]) * f, ln_g[l, 2], ln_b[l, 2])
        return x, [jnp.stack(z, 0) for z in (ak, av, s5r, s5i, ck, cv, dw, ds)]

    y_prompt, st_p = trunk(x_prompt, c_prompt, False)
    y_sample, st_s = trunk(x_sample, c_sample, True)
    a_k_p, a_v_p, s5_re_p, s5_im_p, c_k_p, c_v_p, d_wkv_p, d_shift_p = st_p
    a_k_s, a_v_s, s5_re_s, s5_im_s, c_k_s, c_v_s, d_wkv_s, d_shift_s = st_s
    return (y_prompt, y_sample, a_k_p, a_k_s, a_v_p, a_v_s, s5_re_p, s5_re_s, s5_im_p, s5_im_s,
            c_k_p, c_k_s, c_v_p, c_v_s, d_wkv_p, d_wkv_s, d_shift_p, d_shift_s)
```

```python
from contextlib import ExitStack
import numpy as np
import concourse.bass as bass
import concourse.mybir as mybir
from concourse.bass_utils import run_bass_kernel_spmd

F32 = mybir.dt.float32
BF16 = mybir.dt.bfloat16
I32 = mybir.dt.int32
ALU = mybir.AluOpType
AF = mybir.ActivationFunctionType
AX = mybir.AxisListType

D = 1024
DFF = 2816
NKC = 8
NFC = 22
SEQ = 4096
NS = 64
NSB = 16
NTILE = 33
NROW = NTILE * 128
DEPTH = 2
ALPHA = (2.0 * DEPTH) ** 0.25
LN_EPS = 1e-5
N_CORES = 8


class Res:
    __slots__ = ("w", "r")

    def __init__(self):
        self.w = None
        self.r = []


class EngQ:
    def __init__(self, name, eng, sem):
        self.name = name
        self.eng = eng
        self.sem = sem
        self.count = 0
        self.ops = []
        self.waited = {}
        self.dma_slots = []
        self.dma_i = 0


class Sched:
    def __init__(self, nc, ndma_slots=8):
        self.nc = nc
        self.q = {}
        for name, eng in (("pe", nc.tensor), ("act", nc.scalar), ("dve", nc.vector),
                          ("pool", nc.gpsimd), ("sp", nc.sync)):
            q = EngQ(name, eng, nc.alloc_semaphore(name="s_" + name))
            for i in range(ndma_slots):
                q.dma_slots.append([nc.alloc_semaphore(name=f"d_{name}{i}"), 0])
            self.q[name] = q

    def _wait(self, q, tok):
        sem, val = tok
        key = id(sem)
        if q.waited.get(key, 0) >= val:
            return
        q.waited[key] = val
        q.ops.append(("wait", sem, val))

    def _deps(self, q, reads, writes, skip_same):
        deps = []
        for r in reads:
            if r.w is not None:
                deps.append(r.w)
        for w in writes:
            if w.w is not None:
                deps.append(w.w)
            deps.extend(w.r)
        for tok in deps:
            if skip_same and tok[0] is q.sem:
                continue
            self._wait(q, tok)

    @staticmethod
    def _commit(tok, reads, writes):
        for r in reads:
            r.r.append(tok)
            if len(r.r) > 64:
                best = {}
                for t in r.r:
                    k = id(t[0])
                    if k not in best or best[k][1] < t[1]:
                        best[k] = t
                r.r = list(best.values())
        for w in writes:
            w.w = tok
            w.r = []

    def op(self, qn, fn, reads=(), writes=()):
        q = self.q[qn]
        self._deps(q, reads, writes, skip_same=(qn == "pe"))
        q.count += 1
        tok = (q.sem, q.count)
        q.ops.append(("op", fn, q.sem, 1))
        self._commit(tok, reads, writes)
        return tok

    def dma(self, qn, fn, reads=(), writes=()):
        q = self.q[qn]
        self._deps(q, reads, writes, skip_same=False)
        slot = q.dma_slots[q.dma_i % len(q.dma_slots)]
        q.dma_i += 1
        if slot[1] > 0:
            self._wait(q, (slot[0], slot[1]))
        slot[1] += 16
        tok = (slot[0], slot[1])
        q.ops.append(("op", fn, slot[0], 16))
        self._commit(tok, reads, writes)
        return tok

    def barrier(self):
        toks = []
        for q in self.q.values():
            if q.count:
                toks.append((q.sem, q.count))
            for sl in q.dma_slots:
                if sl[1]:
                    toks.append((sl[0], sl[1]))
        for q in self.q.values():
            for t in toks:
                if t[0] is q.sem:
                    continue
                self._wait(q, t)

    def emit(self):
        nc = self.nc
        with nc.Block() as block:
            def mk(q):
                def body(eng):
                    for o in q.ops:
                        if o[0] == "wait":
                            eng.wait_ge(o[1], o[2])
                        else:
                            o[1](eng).then_inc(o[2], o[3])
                return body
            block.tensor(mk(self.q["pe"]))
            block.scalar(mk(self.q["act"]))
            block.vector(mk(self.q["dve"]))
            block.gpsimd(mk(self.q["pool"]))
            block.sync(mk(self.q["sp"]))


class Builder:
    def __init__(self, cfg):
        self.cfg = cfg
        nc = self.nc = bass.Bass("TRN2", target_bir_lowering=False)
        self.s = Sched(nc)
        self.uid = 0
        di = self.din = {}
        do = self.dout = {}

        def inp(name, shape, dt=F32):
            di[name] = nc.dram_tensor(name, list(shape), dt, kind="ExternalInput").ap()

        def outp(name, shape, dt=F32):
            do[name] = nc.dram_tensor(name, list(shape), dt, kind="ExternalOutput").ap()

        inp("xp", [SEQ, D]); inp("xs", [NS, D]); inp("c17", [17, D])
        inp("ada_w", [DEPTH, D, 9 * D]); inp("ada_b", [DEPTH, 9 * D])
        inp("ln_g", [DEPTH, 3, D]); inp("ln_b", [DEPTH, 3, D])
        inp("ffn_w_gate", [4 * D, DFF]); inp("ffn_w_up", [4 * D, DFF]); inp("ffn_w_down", [4 * DFF, D])
        inp("even_w_in", [D, 2048]); inp("even_w_out", [D, D])
        inp("odd_w_in", [D, 3328]); inp("odd_w_out", [D, D])
        inp("diff_lambda", [1, 256]); inp("diff_subln", [1, 128])
        inp("s5_a_re", [32, 64]); inp("s5_a_im", [32, 64]); inp("s5_log_dt", [1, 32])
        inp("s5_b_re", [32, 64, 16]); inp("s5_b_im", [32, 64, 16]); inp("s5_c_re", [32, 16, 64]); inp("s5_c_im", [32, 16, 64])
        inp("s5_d", [1, 512]); inp("s5_glu_w", [512, 512]); inp("s5_glu_b", [1, 512])
        inp("state_s5_re", [NSB, 2048]); inp("state_s5_im", [NSB, 2048])
        inp("page_table", [1, NSB * 16], I32)
        inp("cache_c_k", [NSB * 2048, 512]); inp("cache_c_v", [NSB * 2048, 512])
        inp("state_d_wkv", [NSB * 512, 64]); inp("state_d_shift", [NSB, 1792])
        inp("rwkv_mu", [1, 1792]); inp("rwkv_w0", [1, 512]); inp("rwkv_a0", [1, 512]); inp("rwkv_k_k", [1, 512]); inp("rwkv_k_a", [1, 512])
        inp("rwkv_r_k", [1, 512]); inp("rwkv_gn_w", [1, 512]); inp("rwkv_gn_b", [1, 512])
        inp("rwkv_w2", [64, 512]); inp("rwkv_a2", [64, 512]); inp("rwkv_g2", [128, 512])
        npool = 2560 * 128 if "S" in cfg.get("parts", "ASB") else 128
        inp("cache_a_k", [npool, 512]); inp("cache_a_v", [npool, 512])
        outp("yp", [SEQ, D]); outp("ys", [NS, D])
        outp("akp", [SEQ, 512]); outp("aks", [NS, 512]); outp("avp", [SEQ, 512]); outp("avs", [NS, 512])
        outp("ckp", [2048, 512]); outp("cks", [NS, 512]); outp("cvp", [2048, 512]); outp("cvs", [NS, 512])
        outp("dsp", [1, 1792]); outp("dss", [NSB, 1792])
        outp("dwp", [512, 64]); outp("dws", [NSB * 512, 64])
        outp("s5rp", [1, 2048]); outp("s5ip", [1, 2048]); outp("s5rs", [NSB, 2048]); outp("s5is", [NSB, 2048])
        if cfg.get("dbg"):
            outp("dbg", [NROW, D])
        self.wg_b = nc.dram_tensor("wg_b", [4 * D, DFF], BF16).ap()
        self.wu_b = nc.dram_tensor("wu_b", [4 * D, DFF], BF16).ap()
        self.wd_b = nc.dram_tensor("wd_b", [4 * DFF, D], BF16).ap()
        self.ewin_b = nc.dram_tensor("ewin_b", [D, 2048], BF16).ap()
        self.owin_b = nc.dram_tensor("owin_b", [D, 3328], BF16).ap()
        self.ewout_b = nc.dram_tensor("ewout_b", [D, D], BF16).ap()
        self.owout_b = nc.dram_tensor("owout_b", [D, D], BF16).ap()
        self.mod17 = nc.dram_tensor("mod17", [17, 9 * D], F32).ap()
        self.XS = [nc.dram_tensor(f"xscr{i}", [NROW, D], F32).ap() for i in range(2)]
        self.xres = [[Res() for _ in range(NTILE)] for _ in range(2)]
        self.mixbuf = nc.dram_tensor("mixbuf", [NROW, D], BF16).ap()
        self.mixres = [Res() for _ in range(NTILE)]
        self.ubuf = nc.dram_tensor("ubuf", [NROW, 512], BF16).ap()
        self.ures = [Res() for _ in range(NTILE)]
        self.qsbuf = nc.dram_tensor("qsbuf", [NS, 512], BF16).ap()
        self.r_aks = Res()
        self.r_cks = Res()
        self.rw = nc.dram_tensor("rwscr", [8, NROW, 512], F32).ap()
        self.rwres = [Res() for _ in range(NTILE)]
        self.ybuf = nc.dram_tensor("ybuf", [NROW, 512], F32).ap()
        self.yres = [Res() for _ in range(NTILE)]
        self.pdbuf = nc.dram_tensor("pdbuf", [NROW, 1792], F32).ap()
        self.pdres = [Res() for _ in range(NTILE)]

    def name(self, p):
        self.uid += 1
        return f"{p}{self.uid}"

    def setup_consts(self):
        nc, s = self.nc, self.s
        A = nc.alloc_sbuf_tensor
        self.ident_b = A("ident_b", [128, 128], BF16)
        self.ident_f = A("ident_f", [128, 128], F32)
        self.selP = A("selP", [17, 128], F32)
        self.selS = A("selS", [17, 128], F32)
        self.ones1 = A("ones1", [1, 128], F32)
        self.eps_t = A("eps_t", [128, 1], F32)
        self.rconst = Res()
        rc = [self.rconst]
        for idt in (self.ident_b, self.ident_f):
            s.op("pool", lambda e, t=idt: e.memset(t[:], 0.0), writes=rc)
            s.op("pool", lambda e, t=idt: e.affine_select(out=t[:], in_=t[:], pattern=[[-1, 128]],
                                                            compare_op=ALU.not_equal, fill=1.0, base=0,
                                                            channel_multiplier=1), reads=rc, writes=rc)
        s.op("pool", lambda e: e.memset(self.selP[:], 0.0), writes=rc)
        s.op("pool", lambda e: e.memset(self.selP[0:1, :], 1.0), writes=rc)
        s.op("pool", lambda e: e.memset(self.selS[:], 1.0), writes=rc)
        s.op("pool", lambda e: e.affine_select(out=self.selS[:], in_=self.selS[:], pattern=[[1, 128]],
                                                compare_op=ALU.is_ge, fill=0.0, base=4, channel_multiplier=-4),
             reads=rc, writes=rc)
        s.op("pool", lambda e: e.affine_select(out=self.selS[:], in_=self.selS[:], pattern=[[-1, 128]],
                                                compare_op=ALU.is_ge, fill=0.0, base=-1, channel_multiplier=4),
             reads=rc, writes=rc)
        s.op("pool", lambda e: e.memset(self.ones1[:], 1.0), writes=rc)
        s.op("pool", lambda e: e.memset(self.eps_t[:], LN_EPS), writes=rc)

    def cast_weights(self):
        nc, s, di = self.nc, self.s, self.din
        jobs = [(di["ffn_w_gate"], self.wg_b, 4 * D, DFF), (di["ffn_w_up"], self.wu_b, 4 * D, DFF),
                (di["ffn_w_down"], self.wd_b, 4 * DFF, D), (di["even_w_in"], self.ewin_b, D, 2048),
                (di["even_w_out"], self.ewout_b, D, D), (di["odd_w_in"], self.owin_b, D, 3328),
                (di["odd_w_out"], self.owout_b, D, D)]
        if self.cfg.get("nocast"):
            jobs = []
        with ExitStack() as es:
            NB = 3
            stg = [es.enter_context(nc.sbuf_tensor(f"cs{i}", [128, 2, 3328], F32)) for i in range(NB)]
            stb = [es.enter_context(nc.sbuf_tensor(f"cb{i}", [128, 2, 3328], BF16)) for i in range(NB)]
            rs = [Res() for _ in range(NB)]
            rb = [Res() for _ in range(NB)]
            i = 0
            for src, dst, R, C in jobs:
                nrt = R // 128
                G = 2
                for r0 in range(0, nrt, G):
                    g = min(G, nrt - r0)
                    k = i % NB
                    sv = src[r0 * 128:(r0 + g) * 128, :].rearrange("(g p) c -> p g c", p=128)
                    dv = dst[r0 * 128:(r0 + g) * 128, :].rearrange("(g p) c -> p g c", p=128)
                    s.dma("sp", lambda e, k=k, g=g, C=C, sv=sv: e.dma_start(out=stg[k][:, :g, :C], in_=sv),
                          writes=[rs[k]])
                    eng = ("act", "dve", "pool")[i % 3]
                    if eng == "act":
                        fn = lambda e, k=k, g=g, C=C: e.copy(out=stb[k][:, :g, :C], in_=stg[k][:, :g, :C])
                    else:
                        fn = lambda e, k=k, g=g, C=C: e.tensor_copy(out=stb[k][:, :g, :C], in_=stg[k][:, :g, :C])
                    s.op(eng, fn, reads=[rs[k]], writes=[rb[k]])
                    s.dma("pool", lambda e, k=k, g=g, C=C, dv=dv: e.dma_start(out=dv, in_=stb[k][:, :g, :C]),
                          reads=[rb[k]])
                    i += 1
            s.barrier()

    def adaln(self, l):
        nc, s, di = self.nc, self.s, self.din
        with ExitStack() as es:
            T = lambda n, sh, dt: es.enter_context(nc.sbuf_tensor(self.name(n), sh, dt))
            P = lambda n, sh, dt: es.enter_context(nc.psum_tensor(self.name(n), sh, dt))
            c_sb = T("c_sb", [17, D], F32)
            sc = T("sc", [17, D], F32)
            scT = T("scT", [128, NKC, 17], F32)
            w_sb = [T("adw", [128, NKC, 512], F32) for _ in range(2)]
            b_sb = [T("adb", [1, 512], F32) for _ in range(2)]
            o_sb = [T("ado", [17, 512], F32) for _ in range(2)]
            pt = P("adpt", [128, NKC, 32], F32)
            po = [P("adpo", [17, 512], F32) for _ in range(2)]
            r_c, r_sc, r_scT, r_pt = Res(), Res(), Res(), Res()
            r_w = [Res(), Res()]; r_b = [Res(), Res()]; r_o = [Res(), Res()]; r_po = [Res(), Res()]
            s.dma("sp", lambda e: e.dma_start(out=c_sb[:], in_=di["c17"][:, :]), writes=[r_c])
            s.op("act", lambda e: e.activation(out=sc[:], in_=c_sb[:], func=AF.Silu), reads=[r_c], writes=[r_sc])
            for kc in range(NKC):
                s.op("pe", lambda e, kc=kc: e.transpose(out=pt[:, kc, 0:17], in_=sc[:, kc * 128:(kc + 1) * 128],
                                                        identity=self.ident_f[0:17, 0:17]),
                     reads=[r_sc, self.rconst], writes=[r_pt])
            s.op("dve", lambda e: e.tensor_copy(out=scT[:], in_=pt[:, :, 0:17]), reads=[r_pt], writes=[r_scT])
            for cc in range(18):
                k = cc % 2
                wv = di["ada_w"][l, :, cc * 512:(cc + 1) * 512].rearrange("(kc p) n -> p kc n", p=128)
                s.dma("sp", lambda e, k=k, wv=wv: e.dma_start(out=w_sb[k][:], in_=wv), writes=[r_w[k]])
                bv = di["ada_b"][l:l + 1, cc * 512:(cc + 1) * 512]
                s.dma("sp", lambda e, k=k, bv=bv: e.dma_start(out=b_sb[k][:], in_=bv), writes=[r_b[k]])
                for kc in range(NKC):
                    s.op("pe", lambda e, k=k, kc=kc: e.matmul(po[k][:], lhsT=scT[:, kc, :], rhs=w_sb[k][:, kc, :],
                                                              start=(kc == 0), stop=False),
                         reads=[r_scT, r_w[k]], writes=[r_po[k]])
                s.op("pe", lambda e, k=k: e.matmul(po[k][:], lhsT=self.ones1[0:1, 0:17], rhs=b_sb[k][:],
                                                   start=False, stop=True),
                     reads=[r_b[k], self.rconst], writes=[r_po[k]])
                isub, kind = divmod(cc // 2, 3)
                if kind == 0:
                    fn = lambda e, k=k: e.tensor_copy(out=o_sb[k][:], in_=po[k][:])
                elif kind == 1:
                    fn = lambda e, k=k: e.tensor_scalar(out=o_sb[k][:], in0=po[k][:], scalar1=1.0, scalar2=None,
                                                        op0=ALU.add)
                else:
                    r = 1.0 if isub == 1 else 0.5
                    fn = lambda e, k=k, r=r: e.tensor_scalar(out=o_sb[k][:], in0=po[k][:], scalar1=1.0, scalar2=r,
                                                             op0=ALU.add, op1=ALU.mult)
                s.op("dve", fn, reads=[r_po[k]], writes=[r_o[k]])
                mv = self.mod17[:, cc * 512:(cc + 1) * 512]
                s.dma("pool", lambda e, k=k, mv=mv: e.dma_start(out=mv, in_=o_sb[k][:]), reads=[r_o[k]])
            s.barrier()

    def alloc_mod(self, es):
        nc = self.nc
        T = lambda n, sh, dt: es.enter_context(nc.sbuf_tensor(self.name(n), sh, dt))
        m = {}
        m["m3"] = T("m3", [17, 3, D], F32)
        m["SH"] = T("SH", [128, D], F32)
        m["SC"] = T("SC", [128, D], F32)
        m["G"] = T("G", [128, D], F32)
        m["LG"] = T("LG", [128, D], F32)
        m["LB"] = T("LB", [128, D], F32)
        m["r_m3"] = Res(); m["r_mod"] = Res(); m["r_ln"] = Res()
        return m

    def load_mod(self, m, l, isub, pmod):
        s, di = self.s, self.din
        mv = self.mod17[:, isub * 3 * D:(isub + 1) * 3 * D].rearrange("r (k d) -> r k d", k=3)
        s.dma("sp", lambda e: e.dma_start(out=m["m3"][:], in_=mv), writes=[m["r_m3"]])
        s.dma("sp", lambda e: e.dma_start(out=m["LG"][:], in_=di["ln_g"][l, isub:isub + 1, :].to_broadcast([128, D])),
              writes=[m["r_ln"]])
        s.dma("sp", lambda e: e.dma_start(out=m["LB"][:], in_=di["ln_b"][l, isub:isub + 1, :].to_broadcast([128, D])),
              writes=[m["r_ln"]])

    def bcast_mod(self, m, sample, pmod, r_pmod):
        s = self.s
        sel = self.selS if sample else self.selP
        for kind, key in enumerate(("SH", "SC", "G")):
            for h in range(2):
                s.op("pe", lambda e, kind=kind, h=h: e.matmul(pmod[:], lhsT=sel[:], rhs=m["m3"][:, kind, h * 512:(h + 1) * 512],
                                                              start=True, stop=True),
                     reads=[m["r_m3"], self.rconst], writes=[r_pmod])
                s.op("act", lambda e, key=key, h=h: e.copy(out=m[key][:, h * 512:(h + 1) * 512], in_=pmod[:]),
                     reads=[r_pmod], writes=[m["r_mod"]])

    def epilogue(self, m, po_halves, r_po, x_ap, r_x, t1, r_t1, st, r_st, dst_ap, dst_res, extra_dst=None):
        s = self.s
        if po_halves is not None:
            for h in range(2):
                s.op("dve", lambda e, h=h: e.tensor_tensor(out=t1[:, h * 512:(h + 1) * 512], in0=po_halves[h],
                                                           in1=m["G"][:, h * 512:(h + 1) * 512], op=ALU.mult),
                     reads=[r_po[h], m["r_mod"]], writes=[r_t1])
            s.op("dve", lambda e: e.scalar_tensor_tensor(out=t1[:], in0=x_ap, scalar=ALPHA, in1=t1[:],
                                                          op0=ALU.mult, op1=ALU.add),
                 reads=[r_x, r_t1], writes=[r_t1])
        else:
            s.op("pool", lambda e: e.tensor_scalar(out=t1[:], in0=x_ap, scalar1=ALPHA, scalar2=None, op0=ALU.mult),
                 reads=[r_x], writes=[r_t1])
        for h in range(2):
            s.op("dve", lambda e, h=h: e.bn_stats(out=st[:, h * 6:(h + 1) * 6], in_=t1[:, h * 512:(h + 1) * 512]),
                 reads=[r_t1], writes=[r_st])
        s.op("dve", lambda e: e.bn_aggr(out=st[:, 12:14], in_=st[:, 0:12]), reads=[r_st], writes=[r_st])
        s.op("act", lambda e: e.activation(out=st[:, 14:15], in_=st[:, 13:14], func=AF.Sqrt, bias=self.eps_t[:, 0:1],
                                           scale=1.0), reads=[r_st, self.rconst], writes=[r_st])
        s.op("dve", lambda e: e.reciprocal(out=st[:, 15:16], in_=st[:, 14:15]), reads=[r_st], writes=[r_st])
        s.op("dve", lambda e: e.tensor_scalar(out=t1[:], in0=t1[:], scalar1=st[:, 12:13], scalar2=st[:, 15:16],
                                              op0=ALU.subtract, op1=ALU.mult), reads=[r_st, r_t1], writes=[r_t1])
        s.op("pool", lambda e: e.tensor_tensor(out=t1[:], in0=t1[:], in1=m["LG"][:], op=ALU.mult),
             reads=[r_t1, m["r_ln"]], writes=[r_t1])
        s.op("dve", lambda e: e.tensor_tensor(out=t1[:], in0=t1[:], in1=m["LB"][:], op=ALU.add),
             reads=[r_t1, m["r_ln"]], writes=[r_t1])
        toks = []
        for ap, res, nrows in dst_ap:
            toks.append(s.dma("pool", lambda e, ap=ap, nrows=nrows: e.dma_start(out=ap, in_=t1[0:nrows, :]),
                              reads=[r_t1], writes=[res] if res is not None else []))
        return toks

    def x_src(self, first, cur, tile):
        if first:
            if tile < 32:
                return self.din["xp"][tile * 128:(tile + 1) * 128, :], 128, None
            return self.din["xs"][:, :], NS, None
        return self.XS[cur][tile * 128:(tile + 1) * 128, :], 128, self.xres[cur][tile]

    def x_dst(self, last, nxt, tile):
        if last:
            if tile < 32:
                return [(self.dout["yp"][tile * 128:(tile + 1) * 128, :], None, 128)]
            return [(self.dout["ys"][:, :], None, NS)]
        return [(self.XS[nxt][tile * 128:(tile + 1) * 128, :], self.xres[nxt][tile], 128)]

    def ffn_sublayer(self, l, f, isub, first, last, cur, nxt):
        nc, s = self.nc, self.s
        wrow = (l * 2 + f)
        nblk = self.cfg.get("nblk", 9)
        with ExitStack() as es:
            T = lambda n, sh, dt: es.enter_context(nc.sbuf_tensor(self.name(n), sh, dt))
            P = lambda n, sh, dt: es.enter_context(nc.psum_tensor(self.name(n), sh, dt))
            m = self.alloc_mod(es)
            xblk = [T("xblk", [128, 4, D], F32) for _ in range(2)]
            hb = [T("hb", [128, D], BF16) for _ in range(2)]
            htmp = [T("htmp", [128, D], F32) for _ in range(2)]
            hT = [T("hT", [128, NKC, 512], BF16) for _ in range(2)]
            NBW = 3
            wgu = [T("wgu", [128, 2, NKC, 256], BF16) for _ in range(NBW)]
            sg = [T("sg", [128, 512], BF16) for _ in range(2)]
            act = T("actb", [128, NFC, 512], BF16)
            wd = [T("wd", [128, NFC, 512], BF16) for _ in range(2)]
            t1 = [T("t1", [128, D], F32) for _ in range(2)]
            st = [T("st", [128, 16], F32) for _ in range(2)]
            tp = [P("tp", [128, NKC, 128], BF16) for _ in range(2)]
            pg = [P("pg", [128, 512], F32) for _ in range(2)]
            pu = [P("pu", [128, 512], F32) for _ in range(2)]
            po = [P("po", [128, 512], F32) for _ in range(2)]
            r_x = [Res(), Res()]; r_hb = [Res(), Res()]; r_htmp = [Res(), Res()]; r_hT = [Res(), Res()]
            r_wgu = [Res() for _ in range(NBW)]; r_sg = [Res(), Res()]; r_act = Res(); r_wd = [Res(), Res()]
            r_t1 = [Res(), Res()]; r_st = [Res(), Res()]; r_tp = [Res(), Res()]
            r_pg = [Res(), Res()]; r_pu = [Res(), Res()]; r_po = [Res(), Res()]
            for b in xblk:
                s.op("pool", lambda e, b=b: e.memset(b[:], 0.0), writes=[r_x[0], r_x[1]])
            self.load_mod(m, l, isub, None)
            self.bcast_mod(m, False, po[0], r_po[0])
            iw = 0
            iwd = 0
            ihb = 0
            igu = 0
            ipo = 0
            it1 = 0
            for blk in range(nblk):
                sample = (blk == 8)
                nt = 1 if sample else 4
                TB = nt * 128
                xb = blk % 2
                if sample:
                    self.bcast_mod(m, True, po[0], r_po[0])
                for t in range(nt):
                    ap, nrows, res = self.x_src(first, cur, blk * 4 + t)
                    s.dma("sp", lambda e, xb=xb, t=t, ap=ap, nrows=nrows: e.dma_start(out=xblk[xb][0:nrows, t, :], in_=ap),
                          reads=[res] if res is not None else [], writes=[r_x[xb]])
                for t in range(nt):
                    k = ihb % 2
                    ihb += 1
                    s.op("pool", lambda e, k=k, xb=xb, t=t: e.tensor_tensor(out=htmp[k][:], in0=xblk[xb][:, t, :], in1=m["SC"][:],
                                                                            op=ALU.mult),
                         reads=[r_x[xb], m["r_mod"]], writes=[r_htmp[k]])
                    s.op("dve", lambda e, k=k: e.tensor_tensor(out=hb[k][:], in0=htmp[k][:], in1=m["SH"][:], op=ALU.add),
                         reads=[r_htmp[k], m["r_mod"]], writes=[r_hb[k]])
                    for kc in range(NKC):
                        s.op("pe", lambda e, k=k, kc=kc: e.transpose(out=tp[k][:, kc, :], in_=hb[k][:, kc * 128:(kc + 1) * 128],
                                                                     identity=self.ident_b[:]),
                             reads=[r_hb[k], self.rconst], writes=[r_tp[k]])
                    s.op("act", lambda e, k=k, xb=xb, t=t: e.copy(out=hT[xb][:, :, t * 128:(t + 1) * 128], in_=tp[k][:]),
                         reads=[r_tp[k]], writes=[r_hT[xb]])
                for fp in range(NFC // 2):
                    kw = iw % NBW
                    iw += 1
                    gv = self.wg_b[wrow * D:(wrow + 1) * D, fp * 256:(fp + 1) * 256].rearrange("(kc p) n -> p kc n", p=128)
                    uv = self.wu_b[wrow * D:(wrow + 1) * D, fp * 256:(fp + 1) * 256].rearrange("(kc p) n -> p kc n", p=128)
                    s.dma("sp", lambda e, kw=kw, gv=gv: e.dma_start(out=wgu[kw][:, 0], in_=gv), writes=[r_wgu[kw]])
                    s.dma("sp", lambda e, kw=kw, uv=uv: e.dma_start(out=wgu[kw][:, 1], in_=uv), writes=[r_wgu[kw]])
                    for j in range(2):
                        fc = fp * 2 + j
                        kg = igu % 2
                        igu += 1
                        for which, pp, rr in ((0, pg, r_pg), (1, pu, r_pu)):
                            for kc in range(NKC):
                                s.op("pe", lambda e, kw=kw, which=which, kc=kc, j=j, pp=pp, kg=kg, xb=xb, TB=TB:
                                     e.matmul(pp[kg][:, 0:TB], lhsT=wgu[kw][:, which, kc, j * 128:(j + 1) * 128],
                                              rhs=hT[xb][:, kc, 0:TB], start=(kc == 0), stop=(kc == NKC - 1)),
                                     reads=[r_wgu[kw], r_hT[xb]], writes=[rr[kg]])
                        s.op("act", lambda e, kg=kg, TB=TB: e.activation(out=sg[kg][:, 0:TB], in_=pg[kg][:, 0:TB], func=AF.Silu),
                             reads=[r_pg[kg]], writes=[r_sg[kg]])
                        s.op("dve", lambda e, kg=kg, fc=fc, TB=TB: e.tensor_tensor(out=act[:, fc, 0:TB], in0=sg[kg][:, 0:TB],
                                                                                   in1=pu[kg][:, 0:TB], op=ALU.mult),
                             reads=[r_sg[kg], r_pu[kg]], writes=[r_act])
                for h in range(2):
                    dv = self.wd_b[wrow * DFF:(wrow + 1) * DFF, h * 512:(h + 1) * 512].rearrange("(fc p) n -> p fc n", p=128)
                    s.dma("sp", lambda e, h=h, dv=dv: e.dma_start(out=wd[h][:], in_=dv), writes=[r_wd[h]])
                for t in range(nt):
                    tile = blk * 4 + t
                    for h in range(2):
                        for fc in range(NFC):
                            s.op("pe", lambda e, h=h, fc=fc, t=t: e.matmul(po[h][:], lhsT=act[:, fc, t * 128:(t + 1) * 128],
                                                                           rhs=wd[h][:, fc, :], start=(fc == 0), stop=(fc == NFC - 1)),
                                 reads=[r_act, r_wd[h]], writes=[r_po[h]])
                    k1 = it1 % 2
                    it1 += 1
                    self.epilogue(m, [po[0][:], po[1][:]], r_po, xblk[xb][:, t, :], r_x[xb], t1[k1], r_t1[k1], st[k1], r_st[k1],
                                  self.x_dst(last, nxt, tile), None)
            s.barrier()

    def stub_mixer(self, l, cur, nxt):
        nc, s = self.nc, self.s
        with ExitStack() as es:
            T = lambda n, sh, dt: es.enter_context(nc.sbuf_tensor(self.name(n), sh, dt))
            m = self.alloc_mod(es)
            xt = [T("xt", [128, D], F32) for _ in range(2)]
            t1 = [T("t1", [128, D], F32) for _ in range(2)]
            st = [T("st", [128, 16], F32) for _ in range(2)]
            r_x = [Res(), Res()]; r_t1 = [Res(), Res()]; r_st = [Res(), Res()]
            self.load_mod(m, l, 1, None)
            for tile in range(NTILE):
                k = tile % 2
                ap, nrows, res = self.x_src(False, cur, tile)
                s.dma("sp", lambda e, k=k, ap=ap: e.dma_start(out=xt[k][:], in_=ap), reads=[res], writes=[r_x[k]])
                self.epilogue(m, None, None, xt[k][:], r_x[k], t1[k], r_t1[k], st[k], r_st[k],
                              self.x_dst(False, nxt, tile), None)
            s.barrier()

    def mixer0_A(self, cur):
        nc, s, di, do = self.nc, self.s, self.din, self.dout
        LAM_INIT = 0.8 - 0.6 * float(np.exp(-0.3 * 0))
        with ExitStack() as es:
            T = lambda n, sh, dt: es.enter_context(nc.sbuf_tensor(self.name(n), sh, dt))
            P = lambda n, sh, dt: es.enter_context(nc.psum_tensor(self.name(n), sh, dt))
            m = self.alloc_mod(es)
            ewin = T("ewin", [128, NKC, 2048], BF16)
            xt = [T("xt", [128, D], F32) for _ in range(2)]
            tmpf = T("tmpf", [128, D], F32)
            hb = [T("hb", [128, D], BF16) for _ in range(2)]
            hT = [T("hT", [128, NKC, 128], BF16) for _ in range(2)]
            q_b = [T("q_b", [128, 512], BF16) for _ in range(2)]
            k_b = [T("k_b", [128, 512], BF16) for _ in range(2)]
            u_b = [T("u_b", [128, 512], BF16) for _ in range(2)]
            kv_f = [T("kv_f", [128, 1024], F32) for _ in range(2)]
            qT = [T("qT", [128, 4, 128], BF16) for _ in range(2)]
            KT = T("KT", [128, 4, SEQ], BF16)
            VP = T("VP", [128, 32, 4, 132], BF16)
            E = [T("E", [128, 4, 128], BF16) for _ in range(3)]
            tri = T("tri", [128, 128], BF16)
            dl = T("dl", [128, 4, 64], F32)
            dlp = T("dlp", [128, 2, 64], F32)
            lamt = T("lamt", [128, 8], F32)
            subl = T("subl", [128, 128], F32)
            a1 = [T("a1", [128, 128], F32) for _ in range(2)]
            junk = T("junk", [128, 128], F32)
            rr = [T("rr", [128, 8], F32) for _ in range(2)]
            mixf = [T("mixf", [128, 512], BF16) for _ in range(2)]
            pp = [P("pp", [128, 512], F32) for _ in range(4)]
            tpb = P("tpb", [128, NKC, 128], BF16)
            st = [P("st", [128, 512], F32) for _ in range(2)]
            oaccs = [P("oacc", [128, 512], F32)]
            r_ewin = Res(); r_x = [Res(), Res()]; r_tmpf = Res(); r_hb = [Res(), Res()]; r_hT = [Res(), Res()]
            r_q = [Res(), Res()]; r_k = [Res(), Res()]; r_u = [Res(), Res()]; r_kv = [Res(), Res()]; r_qT = [Res(), Res()]
            r_KT = [Res() for _ in range(32)]; r_VP = [Res() for _ in range(32)]; r_E = [Res() for _ in range(3)]
            r_c = Res(); r_a1 = [Res(), Res()]; r_rr = [Res(), Res()]; r_mixf = [Res(), Res()]; r_junk = Res()
            r_pp = [Res() for _ in range(4)]; r_tp = Res(); r_st = [Res(), Res()]; r_oacc = [Res()]
            s.dma("sp", lambda e: e.dma_start(out=ewin[:], in_=self.ewin_b[:, :].rearrange("(kc p) n -> p kc n", p=128)),
                  writes=[r_ewin])
            s.op("pool", lambda e: e.memset(VP[:, :, :, 128:132], 1.0), writes=r_VP)
            s.op("pool", lambda e: e.memset(tri[:], 1.0), writes=[r_c])
            s.op("pool", lambda e: e.affine_select(out=tri[:], in_=tri[:], pattern=[[1, 128]], compare_op=ALU.is_ge,
                                                    fill=0.0, base=0, channel_multiplier=-1), reads=[r_c], writes=[r_c])
            self.diff_consts(dl, dlp, lamt, subl, junk, r_c, LAM_INIT)
            self.load_mod(m, 0, 1, None)
            self.bcast_mod(m, False, pp[0], r_pp[0])
            ist = 0
            ie = 0
            ihd = 0
            for i in (range(NTILE) if not self.cfg.get('tilesA') else self.cfg['tilesA']):
                sample = (i == 32)
                k = i % 2
                if sample:
                    self.bcast_mod(m, True, pp[0], r_pp[0])
                ap, nrows, res = self.x_src(False, cur, i)
                s.dma("sp", lambda e, k=k, ap=ap: e.dma_start(out=xt[k][:], in_=ap), reads=[res], writes=[r_x[k]])
                s.op("pool", lambda e, k=k: e.tensor_tensor(out=tmpf[:], in0=xt[k][:], in1=m["SC"][:], op=ALU.mult),
                     reads=[r_x[k], m["r_mod"]], writes=[r_tmpf])
                s.op("dve", lambda e, k=k: e.tensor_tensor(out=hb[k][:], in0=tmpf[:], in1=m["SH"][:], op=ALU.add),
                     reads=[r_tmpf, m["r_mod"]], writes=[r_hb[k]])
                for kc in range(NKC):
                    s.op("pe", lambda e, k=k, kc=kc: e.transpose(out=tpb[:, kc, :], in_=hb[k][:, kc * 128:(kc + 1) * 128],
                                                                 identity=self.ident_b[:]),
                         reads=[r_hb[k], self.rconst], writes=[r_tp])
                s.op("act", lambda e, k=k: e.copy(out=hT[k][:], in_=tpb[:]), reads=[r_tp], writes=[r_hT[k]])
                for c in range(4):
                    for kc in range(NKC):
                        s.op("pe", lambda e, k=k, kc=kc, c=c: e.matmul(pp[c][:], lhsT=hT[k][:, kc, :],
                                                                       rhs=ewin[:, kc, c * 512:(c + 1) * 512],
                                                                       start=(kc == 0), stop=(kc == NKC - 1)),
                             reads=[r_hT[k], r_ewin], writes=[r_pp[c]])
                if self.cfg.get("stopA") == 1:
                    continue
                s.op("act", lambda e, k=k: e.copy(out=q_b[k][:], in_=pp[0][:]), reads=[r_pp[0]], writes=[r_q[k]])
                s.op("dve", lambda e, k=k: e.tensor_copy(out=kv_f[k][:, 0:512], in_=pp[1][:]), reads=[r_pp[1]], writes=[r_kv[k]])
                s.op("dve", lambda e, k=k: e.tensor_copy(out=kv_f[k][:, 512:1024], in_=pp[2][:]), reads=[r_pp[2]], writes=[r_kv[k]])
                s.op("act", lambda e, k=k: e.copy(out=u_b[k][:], in_=pp[3][:]), reads=[r_pp[3]], writes=[r_u[k]])
                s.op("act", lambda e, k=k: e.copy(out=k_b[k][:], in_=kv_f[k][:, 0:512]), reads=[r_kv[k]], writes=[r_k[k]])
                if not sample:
                    s.op("act", lambda e, i=i, k=k: e.copy(out=VP[:, i, :, 0:128], in_=kv_f[k][:, 512:1024].rearrange("p (h e) -> p h e", h=4)),
                         reads=[r_kv[k]], writes=[r_VP[i]])
                rows = slice(i * 128, (i + 1) * 128)
                if self.cfg.get("stopA") == 2:
                    continue
                s.dma("pool", lambda e, k=k, rows=rows: e.dma_start(out=self.ubuf[rows, :], in_=u_b[k][:]),
                      reads=[r_u[k]], writes=[self.ures[i]])
                if not sample:
                    s.dma("pool", lambda e, k=k, rows=rows: e.dma_start(out=do["akp"][rows, :], in_=kv_f[k][:, 0:512]), reads=[r_kv[k]])
                    s.dma("pool", lambda e, k=k, rows=rows: e.dma_start(out=do["avp"][rows, :], in_=kv_f[k][:, 512:1024]), reads=[r_kv[k]])
                else:
                    s.dma("pool", lambda e, k=k: e.dma_start(out=do["aks"][:, :], in_=kv_f[k][0:NS, 0:512]), reads=[r_kv[k]],
                          writes=[self.r_aks])
                    s.dma("pool", lambda e, k=k: e.dma_start(out=do["avs"][:, :], in_=kv_f[k][0:NS, 512:1024]), reads=[r_kv[k]],
                          writes=[self.r_aks])
                    s.dma("pool", lambda e, k=k: e.dma_start(out=self.qsbuf[:, :], in_=q_b[k][0:NS, :]), reads=[r_q[k]],
                          writes=[self.r_aks])
                    continue
                if self.cfg.get("stopA") == 3:
                    continue
                for h in range(4):
                    s.op("pe", lambda e, k=k, h=h: e.transpose(out=tpb[:, h, :], in_=q_b[k][:, h * 128:(h + 1) * 128],
                                                               identity=self.ident_b[:]),
                         reads=[r_q[k], self.rconst], writes=[r_tp])
                    s.op("pe", lambda e, k=k, h=h: e.transpose(out=tpb[:, 4 + h, :], in_=k_b[k][:, h * 128:(h + 1) * 128],
                                                               identity=self.ident_b[:]),
                         reads=[r_k[k], self.rconst], writes=[r_tp])
                s.op("act", lambda e, k=k: e.copy(out=qT[k][:], in_=tpb[:, 0:4, :]), reads=[r_tp], writes=[r_qT[k]])
                s.op("act", lambda e, i=i: e.copy(out=KT[:, :, i * 128:(i + 1) * 128], in_=tpb[:, 4:8, :]), reads=[r_tp],
                     writes=[r_KT[i]])
                for h in range(4 if not self.cfg.get("noattn") else 0):
                    ob = 0
                    oacc = oaccs[ob]
                    for mm in range(2):
                        pr = slice(mm * 64, (mm + 1) * 64)
                        oc = mm * 132
                        for j0 in range(0, i + 1, 4):
                            js = list(range(j0, min(j0 + 4, i + 1)))
                            sb = ist % 2
                            ist += 1
                            for jj, j in enumerate(js):
                                s.op("pe", lambda e, sb=sb, jj=jj, j=j, h=h, pr=pr, k=k:
                                     e.matmul(st[sb][:, jj * 128:(jj + 1) * 128], lhsT=KT[pr, h, j * 128:(j + 1) * 128],
                                              rhs=qT[k][pr, h, :], start=True, stop=True),
                                     reads=[r_KT[j], r_qT[k]], writes=[r_st[sb]])
                            eb = ie % 3
                            ie += 1
                            w = len(js)
                            s.op("act", lambda e, eb=eb, sb=sb, w=w: e.activation(out=E[eb][:, 0:w, :],
                                                                                 in_=st[sb][:, 0:w * 128].rearrange("p (a b) -> p a b", a=w),
                                                                                 func=AF.Exp, scale=0.125),
                                 reads=[r_st[sb]], writes=[r_E[eb]])
                            if js[-1] == i:
                                jj = i - j0
                                s.op("pool", lambda e, eb=eb, jj=jj: e.tensor_tensor(out=E[eb][:, jj, :], in0=E[eb][:, jj, :],
                                                                                      in1=tri[:], op=ALU.mult),
                                     reads=[r_E[eb], r_c], writes=[r_E[eb]])
                            for jj, j in enumerate(js):
                                s.op("pe", lambda e, eb=eb, jj=jj, j=j, h=h, oc=oc, oacc=oacc, i=i:
                                     e.matmul(oacc[:, oc:oc + 129], lhsT=E[eb][:, jj, :], rhs=VP[:, j, h, 0:129],
                                              start=(j == 0), stop=(j == i)),
                                     reads=[r_E[eb], r_VP[j]], writes=[r_oacc[ob]])
                    kk = ihd % 2
                    ihd += 1
                    self.diff_finalize(oacc, r_oacc[ob], rr[kk], r_rr[kk], a1[kk], r_a1[kk], junk, r_junk, lamt, subl, r_c,
                                       mixf[k][:, h * 128:(h + 1) * 128], r_mixf[k], 128)
                s.dma("pool", lambda e, k=k, rows=rows: e.dma_start(out=self.mixbuf[rows, 0:512], in_=mixf[k][:]),
                      reads=[r_mixf[k]], writes=[self.mixres[i]])
            s.barrier()

    def diff_consts(self, dl, dlp, lamt, subl, junk, r_c, lam_init):
        s, di = self.s, self.din
        s.dma("sp", lambda e: e.dma_start(out=dl[:].rearrange("p a b -> p (a b)"),
                                          in_=di["diff_lambda"][0:1, :].to_broadcast([128, 256])), writes=[r_c])
        s.dma("sp", lambda e: e.dma_start(out=subl[:], in_=di["diff_subln"][0:1, :].to_broadcast([128, 128])), writes=[r_c])
        s.op("dve", lambda e: e.tensor_tensor(out=dlp[:, 0, :], in0=dl[:, 0, :], in1=dl[:, 1, :], op=ALU.mult), reads=[r_c], writes=[r_c])
        s.op("dve", lambda e: e.tensor_tensor(out=dlp[:, 1, :], in0=dl[:, 2, :], in1=dl[:, 3, :], op=ALU.mult), reads=[r_c], writes=[r_c])
        s.op("dve", lambda e: e.tensor_reduce(out=lamt[:, 0:2], in_=dlp[:], axis=AX.X, op=ALU.add), reads=[r_c], writes=[r_c])
        s.op("act", lambda e: e.activation(out=lamt[:, 4:6], in_=lamt[:, 0:2], func=AF.Exp), reads=[r_c], writes=[r_c])
        s.op("dve", lambda e: e.tensor_tensor(out=lamt[:, 2:3], in0=lamt[:, 4:5], in1=lamt[:, 5:6], op=ALU.subtract), reads=[r_c], writes=[r_c])
        s.op("dve", lambda e: e.tensor_scalar(out=lamt[:, 2:3], in0=lamt[:, 2:3], scalar1=lam_init, scalar2=None, op0=ALU.add),
             reads=[r_c], writes=[r_c])
        s.op("dve", lambda e: e.tensor_scalar(out=lamt[:, 3:4], in0=lamt[:, 2:3], scalar1=-1.0, scalar2=None, op0=ALU.mult),
             reads=[r_c], writes=[r_c])
        s.op("dve", lambda e: e.tensor_scalar(out=subl[:], in0=subl[:], scalar1=1.0 - lam_init, scalar2=None, op0=ALU.mult),
             reads=[r_c], writes=[r_c])

    def diff_finalize(self, oacc, r_oacc, rr, r_rr, a1, r_a1, junk, r_junk, lamt, subl, r_c, out_ap, r_out, np_):
        s = self.s
        P_ = slice(0, np_)
        s.op("dve", lambda e: e.reciprocal(out=rr[P_, 0:1], in_=oacc[P_, 128:129]), reads=[r_oacc], writes=[r_rr])
        s.op("dve", lambda e: e.reciprocal(out=rr[P_, 1:2], in_=oacc[P_, 260:261]), reads=[r_oacc], writes=[r_rr])
        s.op("dve", lambda e: e.tensor_tensor(out=rr[P_, 2:3], in0=rr[P_, 1:2], in1=lamt[P_, 3:4], op=ALU.mult),
             reads=[r_rr, r_c], writes=[r_rr])
        s.op("dve", lambda e: e.tensor_scalar(out=a1[P_, :], in0=oacc[P_, 0:128], scalar1=rr[P_, 0:1], scalar2=None, op0=ALU.mult),
             reads=[r_oacc, r_rr], writes=[r_a1])
        s.op("dve", lambda e: e.scalar_tensor_tensor(out=a1[P_, :], in0=oacc[P_, 132:260], scalar=rr[P_, 2:3], in1=a1[P_, :],
                                                     op0=ALU.mult, op1=ALU.add), reads=[r_oacc, r_rr, r_a1], writes=[r_a1])
        s.op("act", lambda e: e.activation(out=junk[P_, :], in_=a1[P_, :], func=AF.Square, accum_out=rr[P_, 3:4]),
             reads=[r_a1], writes=[r_junk, r_rr])
        s.op("act", lambda e: e.activation(out=rr[P_, 4:5], in_=rr[P_, 3:4], func=AF.Sqrt, bias=self.eps_t[P_, 0:1], scale=1.0 / 128.0),
             reads=[r_rr, self.rconst], writes=[r_rr])
        s.op("dve", lambda e: e.reciprocal(out=rr[P_, 5:6], in_=rr[P_, 4:5]), reads=[r_rr], writes=[r_rr])
        s.op("dve", lambda e: e.scalar_tensor_tensor(out=out_ap, in0=a1[P_, :], scalar=rr[P_, 5:6], in1=subl[P_, :],
                                                     op0=ALU.mult, op1=ALU.mult), reads=[r_a1, r_rr, r_c], writes=[r_out])

    def mixer0_S(self):
        nc, s, di, do = self.nc, self.s, self.din, self.dout
        LAM_INIT = 0.8 - 0.6 * float(np.exp(-0.3 * 0))
        NPG = 16
        with ExitStack() as es:
            T = lambda n, sh, dt: es.enter_context(nc.sbuf_tensor(self.name(n), sh, dt))
            P = lambda n, sh, dt: es.enter_context(nc.psum_tensor(self.name(n), sh, dt))
            pt_i = T("pt_i", [128, NSB * NPG], I32)
            io_i = T("io_i", [128, NSB * NPG], I32)
            idx = T("idx", [128, NSB * NPG], I32)
            pt_f = T("pt_f", [128, NSB * NPG], F32)
            io_f = T("io_f", [128, NSB * NPG], F32)
            selrows = T("selrows", [64, 64, 128], BF16)
            q_s = T("q_s", [64, 512], BF16)
            tri = T("tri", [128, 128], F32)
            dl = T("dl", [128, 4, 64], F32)
            dlp = T("dlp", [128, 2, 64], F32)
            lamt = T("lamt", [128, 8], F32)
            subl = T("subl", [128, 128], F32)
            junk = T("junk", [128, 128], F32)
            Qbc = [T("Qbc", [128, 4, 512], BF16) for _ in range(2)]
            Kg = [T("Kg", [128, 512], F32) for _ in range(4)]
            Vg = [T("Vg", [128, 512], F32) for _ in range(3)]
            VPb = [T("VPb", [128, NPG, 4, 132], BF16) for _ in range(2)]
            Kn = [T("Kn", [4, 512], F32) for _ in range(2)]
            Vn = [T("Vn", [4, 512], F32) for _ in range(2)]
            VPn = [T("VPn", [4, 4, 132], BF16) for _ in range(2)]
            prod = [T("prod", [128, 512], F32) for _ in range(2)]
            Sc = [T("Sc", [128, NPG + 1, 4, 2, 4], F32) for _ in range(2)]
            Eb = [T("Eb", [128, NPG + 1, 4, 2, 4], BF16) for _ in range(2)]
            rr = [T("rr", [128, 8], F32) for _ in range(2)]
            a1 = [T("a1", [128, 128], F32) for _ in range(2)]
            mixs = [T("mixs", [4, 512], BF16) for _ in range(2)]
            qps = [P("qps", [128, 512], F32) for _ in range(2)]
            ops_ = [P("ops", [128, 512], F32) for _ in range(4)]
            r_c = Res(); r_Qbc = [Res(), Res()]; r_Kg = [Res() for _ in range(4)]; r_Vg = [Res() for _ in range(3)]
            r_VPb = [Res(), Res()]; r_Kn = [Res(), Res()]; r_Vn = [Res(), Res()]; r_VPn = [Res(), Res()]
            r_prod = [Res(), Res()]; r_Sc = [Res(), Res()]; r_Eb = [Res(), Res()]; r_rr = [Res(), Res()]; r_a1 = [Res(), Res()]
            r_mixs = [Res(), Res()]; r_qps = [Res(), Res()]; r_ops = [Res() for _ in range(4)]; r_junk = Res()
            s.dma("sp", lambda e: e.dma_start(out=pt_i[:], in_=di["page_table"].rearrange("b g -> (b g)").rearrange("(o n) -> o n", o=1).to_broadcast([128, NSB * NPG])),
                  writes=[r_c])
            s.op("pool", lambda e: e.iota(io_i[:], pattern=[[0, NSB * NPG]], base=0, channel_multiplier=1), writes=[r_c])
            s.op("dve", lambda e: e.tensor_copy(out=pt_f[:], in_=pt_i[:]), reads=[r_c], writes=[r_c])
            s.op("dve", lambda e: e.tensor_copy(out=io_f[:], in_=io_i[:]), reads=[r_c], writes=[r_c])
            s.op("dve", lambda e: e.scalar_tensor_tensor(out=pt_f[:], in0=pt_f[:], scalar=128.0, in1=io_f[:], op0=ALU.mult, op1=ALU.add),
                 reads=[r_c], writes=[r_c])
            s.op("dve", lambda e: e.tensor_copy(out=idx[:], in_=pt_f[:]), reads=[r_c], writes=[r_c])
            s.op("pool", lambda e: e.memset(selrows[:], 1.0), writes=[r_c])
            s.op("pool", lambda e: e.affine_select(out=selrows[:], in_=selrows[:], pattern=[[-1, 64], [0, 128]],
                                                    compare_op=ALU.is_equal, fill=0.0, base=0, channel_multiplier=1),
                 reads=[r_c], writes=[r_c])
            s.op("pool", lambda e: e.memset(tri[:], 1.0), writes=[r_c])
            s.op("pool", lambda e: e.affine_select(out=tri[:], in_=tri[:], pattern=[[1, 128]], compare_op=ALU.is_ge,
                                                    fill=0.0, base=0, channel_multiplier=-1), reads=[r_c], writes=[r_c])
            s.dma("sp", lambda e: e.dma_start(out=q_s[:], in_=self.qsbuf[:, :]), reads=[self.r_aks], writes=[r_c])
            self.diff_consts(dl, dlp, lamt, subl, junk, r_c, LAM_INIT)
            for v in VPb:
                s.op("pool", lambda e, v=v: e.memset(v[:, :, :, 128:132], 1.0), writes=r_VPb)
            for v in VPn:
                s.op("pool", lambda e, v=v: e.memset(v[:, :, 128:132], 1.0), writes=r_VPn)
            for sc_ in Sc:
                s.op("pool", lambda e, sc_=sc_: e.memset(sc_[:], 0.0), writes=r_Sc)
            ikg = 0
            ivg = 0
            ipr = 0
            cak = di["cache_a_k"]
            cav = di["cache_a_v"]
            for b in range(NSB):
                kb = b % 2
                for qi in range(4):
                    kq = (b * 4 + qi) % 2
                    s.op("pe", lambda e, kq=kq, b=b, qi=qi: e.matmul(qps[kq][:], lhsT=selrows[:, 4 * b + qi, :], rhs=q_s[:, :],
                                                                     start=True, stop=True),
                         reads=[r_c], writes=[r_qps[kq]])
                    s.op("act", lambda e, kq=kq, kb=kb, qi=qi: e.copy(out=Qbc[kb][:, qi, :], in_=qps[kq][:]),
                         reads=[r_qps[kq]], writes=[r_Qbc[kb]])
                s.dma("sp", lambda e, kb=kb, b=b: e.dma_start(out=Kn[kb][:], in_=do["aks"][4 * b:4 * b + 4, :]),
                      reads=[self.r_aks], writes=[r_Kn[kb]])
                s.dma("sp", lambda e, kb=kb, b=b: e.dma_start(out=Vn[kb][:], in_=do["avs"][4 * b:4 * b + 4, :]),
                      reads=[self.r_aks], writes=[r_Vn[kb]])
                s.op("act", lambda e, kb=kb: e.copy(out=VPn[kb][:, :, 0:128], in_=Vn[kb][:].rearrange("p (h e) -> p h e", h=4)),
                     reads=[r_Vn[kb]], writes=[r_VPn[kb]])
                for pg in range(NPG):
                    kv = ivg % 3
                    ivg += 1
                    col = b * NPG + pg
                    s.dma("pool", lambda e, kv=kv, col=col: e.indirect_dma_start(
                        out=Vg[kv][:], out_offset=None, in_=cav[:, :],
                        in_offset=bass.IndirectOffsetOnAxis(ap=idx[:, col:col + 1], axis=0)),
                        reads=[r_c], writes=[r_Vg[kv]])
                    s.op("act", lambda e, kv=kv, kb=kb, pg=pg: e.copy(out=VPb[kb][:, pg, :, 0:128],
                                                                      in_=Vg[kv][:].rearrange("p (h e) -> p h e", h=4)),
                         reads=[r_Vg[kv]], writes=[r_VPb[kb]])
                for pg in range(NPG + 1):
                    if pg < NPG:
                        kk = ikg % 4
                        ikg += 1
                        col = b * NPG + pg
                        s.dma("pool", lambda e, kk=kk, col=col: e.indirect_dma_start(
                            out=Kg[kk][:], out_offset=None, in_=cak[:, :],
                            in_offset=bass.IndirectOffsetOnAxis(ap=idx[:, col:col + 1], axis=0)),
                            reads=[r_c], writes=[r_Kg[kk]])
                        ksrc, r_ks, npart = Kg[kk], r_Kg[kk], 128
                    else:
                        ksrc, r_ks, npart = Kn[kb], r_Kn[kb], 4
                    for qi in range(4):
                        kp = ipr % 2
                        ipr += 1
                        s.op("dve", lambda e, kp=kp, ksrc=ksrc, npart=npart, kb=kb, qi=qi:
                             e.tensor_tensor(out=prod[kp][0:npart, :], in0=ksrc[0:npart, :], in1=Qbc[kb][0:npart, qi, :], op=ALU.mult),
                             reads=[r_ks, r_Qbc[kb]], writes=[r_prod[kp]])
                        s.op("dve", lambda e, kp=kp, npart=npart, kb=kb, pg=pg, qi=qi:
                             e.tensor_reduce(out=Sc[kb][0:npart, pg, :, :, qi].rearrange("p h m -> p (h m)"),
                                             in_=prod[kp][0:npart, :].rearrange("p (g d) -> p g d", d=64), axis=AX.X, op=ALU.add),
                             reads=[r_prod[kp]], writes=[r_Sc[kb]])
                s.op("act", lambda e, kb=kb: e.activation(out=Eb[kb][:].rearrange("p a h m q -> p (a h m q)"),
                                                          in_=Sc[kb][:].rearrange("p a h m q -> p (a h m q)"),
                                                          func=AF.Exp, scale=0.125), reads=[r_Sc[kb]], writes=[r_Eb[kb]])
                s.op("pool", lambda e, kb=kb: e.tensor_tensor(out=Eb[kb][0:4, NPG].rearrange("p h m q -> p (h m) q"),
                                                               in0=Eb[kb][0:4, NPG].rearrange("p h m q -> p (h m) q"),
                                                               in1=tri[0:4, 0:4].unsqueeze(1).to_broadcast([4, 8, 4]), op=ALU.mult),
                     reads=[r_Eb[kb], r_c], writes=[r_Eb[kb]])
                for h in range(4):
                    for mm in range(2):
                        oc = mm * 132
                        for pg in range(NPG + 1):
                            if pg < NPG:
                                s.op("pe", lambda e, h=h, mm=mm, oc=oc, pg=pg, kb=kb:
                                     e.matmul(ops_[h][0:4, oc:oc + 129], lhsT=Eb[kb][:, pg, h, mm, :], rhs=VPb[kb][:, pg, h, 0:129],
                                              start=(pg == 0), stop=False),
                                     reads=[r_Eb[kb], r_VPb[kb]], writes=[r_ops[h]])
                            else:
                                s.op("pe", lambda e, h=h, mm=mm, oc=oc, pg=pg, kb=kb:
                                     e.matmul(ops_[h][0:4, oc:oc + 129], lhsT=Eb[kb][0:4, pg, h, mm, :], rhs=VPn[kb][0:4, h, 0:129],
                                              start=False, stop=True),
                                     reads=[r_Eb[kb], r_VPn[kb]], writes=[r_ops[h]])
                    kk2 = (b * 4 + h) % 2
                    self.diff_finalize(ops_[h], r_ops[h], rr[kk2], r_rr[kk2], a1[kk2], r_a1[kk2], junk, r_junk, lamt, subl, r_c,
                                       mixs[kb][0:4, h * 128:(h + 1) * 128], r_mixs[kb], 4)
                s.dma("sp", lambda e, kb=kb, b=b: e.dma_start(out=self.mixbuf[SEQ + 4 * b:SEQ + 4 * b + 4, 0:512], in_=mixs[kb][:]),
                      reads=[r_mixs[kb]], writes=[self.mixres[32]])
            s.barrier()

    def s5_setup(self, es, S):
        nc, s, di = self.nc, self.s, self.din
        T = lambda n, sh, dt: es.enter_context(nc.sbuf_tensor(self.name(n), sh, dt))
        PI = float(np.pi)
        r = S["r_c"] = Res()
        rc = [r]
        par = T("s5par", [128, 24, 16], F32)
        pari = T("s5pari", [128, 128], I32)
        S["par"] = par
        (A_RE, A_IM, LDT, DT, MAG, ANG, SN, CS, LRE, LIM, DEN, NR, FRE, FIM, T0, T1, NEGPI, TWOPI) = range(18)
        S["idx"] = dict(RHO=MAG, SN=SN, CS=CS, LRE=LRE, LIM=LIM)
        pv = lambda i: par[:, i, :]
        flat = lambda a: a.rearrange("g p -> (g p)").rearrange("(j q) -> q j", q=128)
        s.dma("sp", lambda e: e.dma_start(out=pv(A_RE), in_=flat(di["s5_a_re"]), allow_slow_non_contiguous=True), writes=rc)
        s.dma("sp", lambda e: e.dma_start(out=pv(A_IM), in_=flat(di["s5_a_im"]), allow_slow_non_contiguous=True), writes=rc)
        ld2 = di["s5_log_dt"].rearrange("o (j two) -> o two j", two=2)
        s.dma("sp", lambda e: e.dma_start(out=par[0:64, LDT, :], in_=ld2[0:1, 0, :].to_broadcast([64, 16]), allow_slow_non_contiguous=True), writes=rc)
        s.dma("sp", lambda e: e.dma_start(out=par[64:128, LDT, :], in_=ld2[0:1, 1, :].to_broadcast([64, 16]), allow_slow_non_contiguous=True), writes=rc)
        s.op("pool", lambda e: e.memset(pv(NEGPI), -PI), writes=rc)
        s.op("pool", lambda e: e.memset(pv(TWOPI), 2 * PI), writes=rc)
        V = lambda fn: s.op("dve", fn, reads=rc, writes=rc)
        Aop = lambda fn: s.op("act", fn, reads=rc, writes=rc)
        Aop(lambda e: e.activation(out=pv(DT), in_=pv(LDT), func=AF.Exp))
        V(lambda e: e.tensor_tensor(out=pv(T0), in0=pv(A_RE), in1=pv(DT), op=ALU.mult))
        Aop(lambda e: e.activation(out=pv(MAG), in_=pv(T0), func=AF.Exp))
        V(lambda e: e.tensor_tensor(out=pv(ANG), in0=pv(A_IM), in1=pv(DT), op=ALU.mult))

        def sincos(out_sin, out_cos, ang_ap, tmp_ap, tmpi_ap):
            TWO_PI_S = 2 * PI - 2e-6
            for dst, off in ((out_sin, 0.0), (out_cos, 0.25)):
                V(lambda e, off=off: e.tensor_scalar(out=tmp_ap, in0=ang_ap, scalar1=1.0 / (2 * PI), scalar2=off, op0=ALU.mult, op1=ALU.add))
                V(lambda e: e.tensor_copy(out=tmpi_ap, in_=tmp_ap))
                V(lambda e: e.tensor_tensor(out=tmp_ap, in0=tmp_ap, in1=tmpi_ap, op=ALU.subtract))
                Aop(lambda e, dst=dst: e.activation(out=dst, in_=tmp_ap, func=AF.Sin, scale=TWO_PI_S))
        S["sincos"] = sincos
        sincos(pv(SN), pv(CS), pv(ANG), pv(T0), pari[:, 0:16])
        V(lambda e: e.tensor_tensor(out=pv(LRE), in0=pv(MAG), in1=pv(CS), op=ALU.mult))
        V(lambda e: e.tensor_tensor(out=pv(LIM), in0=pv(MAG), in1=pv(SN), op=ALU.mult))
        V(lambda e: e.tensor_tensor(out=pv(DEN), in0=pv(A_RE), in1=pv(A_RE), op=ALU.mult))
        V(lambda e: e.tensor_tensor(out=pv(T0), in0=pv(A_IM), in1=pv(A_IM), op=ALU.mult))
        V(lambda e: e.tensor_tensor(out=pv(DEN), in0=pv(DEN), in1=pv(T0), op=ALU.add))
        V(lambda e: e.reciprocal(out=pv(DEN), in_=pv(DEN)))
        V(lambda e: e.tensor_scalar(out=pv(NR), in0=pv(LRE), scalar1=-1.0, scalar2=None, op0=ALU.add))
        V(lambda e: e.tensor_tensor(out=pv(T0), in0=pv(NR), in1=pv(A_RE), op=ALU.mult))
        V(lambda e: e.tensor_tensor(out=pv(T1), in0=pv(LIM), in1=pv(A_IM), op=ALU.mult))
        V(lambda e: e.tensor_tensor(out=pv(T0), in0=pv(T0), in1=pv(T1), op=ALU.add))
        V(lambda e: e.tensor_tensor(out=pv(FRE), in0=pv(T0), in1=pv(DEN), op=ALU.mult))
        V(lambda e: e.tensor_tensor(out=pv(T0), in0=pv(LIM), in1=pv(A_RE), op=ALU.mult))
        V(lambda e: e.tensor_tensor(out=pv(T1), in0=pv(NR), in1=pv(A_IM), op=ALU.mult))
        V(lambda e: e.tensor_tensor(out=pv(T0), in0=pv(T0), in1=pv(T1), op=ALU.subtract))
        V(lambda e: e.tensor_tensor(out=pv(FIM), in0=pv(T0), in1=pv(DEN), op=ALU.mult))
        M4 = T("M4", [128, 4, 8, 16], F32)
        s.op("pool", lambda e: e.memset(M4[:], 0.0), writes=rc)
        for v in range(4):
            s.op("pool", lambda e, v=v: e.memset(M4[0:64, v, 2 * v, :], 1.0), writes=rc)
            s.op("pool", lambda e, v=v: e.memset(M4[64:128, v, 2 * v + 1, :], 1.0), writes=rc)
        BBT = S["BBT"] = T("BBT", [128, 2, 16, 128], BF16)
        CX = S["CX"] = T("CX", [128, 2, 16, 128], BF16)
        with ExitStack() as es2:
            T2 = lambda n, sh, dt: es2.enter_context(nc.sbuf_tensor(self.name(n), sh, dt))
            P2 = lambda n, sh, dt: es2.enter_context(nc.psum_tensor(self.name(n), sh, dt))
            Bre = T2("Bre", [128, 16, 16], F32); Bim = T2("Bim", [128, 16, 16], F32)
            bbr = T2("bbr", [128, 16, 16], F32); bbi = T2("bbi", [128, 16, 16], F32); tt = T2("tt", [128, 16, 16], F32)
            BX = T2("BX", [128, 128], F32)
            Cnat = T2("Cnat", [128, 4, 128], F32)
            CT = [T2("CTd", [128, 4, 128], F32) for _ in range(2)]
            ptr = [P2("ptr", [128, 128], F32) for _ in range(2)]
            r_ptr = [Res(), Res()]; r_BX = Res()
            bflat = lambda a: a.rearrange("g p c -> (g p) c").rearrange("(j q) c -> q j c", q=128)
            s.dma("sp", lambda e: e.dma_start(out=Bre[:], in_=bflat(di["s5_b_re"])), writes=rc)
            s.dma("sp", lambda e: e.dma_start(out=Bim[:], in_=bflat(di["s5_b_im"])), writes=rc)
            fre_b = par[:, FRE, :].unsqueeze(2).to_broadcast([128, 16, 16])
            fim_b = par[:, FIM, :].unsqueeze(2).to_broadcast([128, 16, 16])
            V(lambda e: e.tensor_tensor(out=bbr[:], in0=Bre[:], in1=fre_b, op=ALU.mult))
            V(lambda e: e.tensor_tensor(out=tt[:], in0=Bim[:], in1=fim_b, op=ALU.mult))
            V(lambda e: e.tensor_tensor(out=bbr[:], in0=bbr[:], in1=tt[:], op=ALU.subtract))
            V(lambda e: e.tensor_tensor(out=bbi[:], in0=Bim[:], in1=fre_b, op=ALU.mult))
            V(lambda e: e.tensor_tensor(out=tt[:], in0=Bre[:], in1=fim_b, op=ALU.mult))
            V(lambda e: e.tensor_tensor(out=bbi[:], in0=bbi[:], in1=tt[:], op=ALU.add))
            it = 0
            for ri, bb in enumerate((bbr, bbi)):
                for j in range(16):
                    k = it % 2
                    it += 1
                    s.op("dve", lambda e, bb=bb, j=j: e.tensor_tensor(out=BX[:].rearrange("p (g c) -> p g c", g=8),
                                                                      in0=bb[:, j, :].unsqueeze(1).to_broadcast([128, 8, 16]),
                                                                      in1=M4[:, j % 4, :, :], op=ALU.mult),
                         reads=rc + [r_BX], writes=[r_BX])
                    s.op("pe", lambda e, k=k: e.transpose(out=ptr[k][:], in_=BX[:], identity=self.ident_f[:]),
                         reads=[r_BX, self.rconst], writes=[r_ptr[k]])
                    s.op("act", lambda e, k=k, ri=ri, j=j: e.copy(out=BBT[:, ri, j, :], in_=ptr[k][:]), reads=[r_ptr[k]], writes=rc)
            for ri, nm in enumerate(("s5_c_re", "s5_c_im")):
                cnatv = di[nm].rearrange("g c p -> (g c) p").rearrange("(m q) p -> q m p", q=128)
                s.dma("sp", lambda e, cnatv=cnatv: e.dma_start(out=Cnat[:, :, 0:64], in_=cnatv), writes=rc)
                s.dma("sp", lambda e, cnatv=cnatv: e.dma_start(out=Cnat[:, :, 64:128], in_=cnatv), writes=rc)
                for mch in range(4):
                    k = it % 2
                    it += 1
                    s.op("pe", lambda e, k=k, mch=mch: e.transpose(out=ptr[k][:], in_=Cnat[:, mch, :], identity=self.ident_f[:]),
                         reads=rc + [self.rconst], writes=[r_ptr[k]])
                    s.op("act", lambda e, k=k, ri=ri, mch=mch: e.copy(out=CT[ri][:, mch, :], in_=ptr[k][:]), reads=[r_ptr[k]], writes=rc)
                sgn = 1.0 if ri == 0 else -1.0
                for j in range(16):
                    V(lambda e, ri=ri, j=j, sgn=sgn: e.scalar_tensor_tensor(out=CX[:, ri, j, :], in0=CT[ri][:, j // 4, :], scalar=sgn,
                                                                            in1=M4[:, j % 4, :, :].rearrange("p g c -> p (g c)"),
                                                                            op0=ALU.mult, op1=ALU.mult))
            s.barrier()
        COS = S["COS"] = T("COS", [128, 16, 128], F32)
        SIN = S["SIN"] = T("SIN", [128, 16, 128], F32)
        jl = T("jl", [128, 128], F32)
        tb = T("tb", [128, 128], F32)
        tb2 = T("tb2", [128, 128], F32)
        jli = T("jli", [128, 128], I32)
        s.op("pool", lambda e: e.iota(jli[:], pattern=[[1, 128]], base=0, channel_multiplier=0), writes=rc)
        V(lambda e: e.tensor_copy(out=jl[:], in_=jli[:]))
        for j in range(16):
            V(lambda e, j=j: e.tensor_scalar(out=tb[:], in0=jl[:], scalar1=par[:, ANG, j:j + 1], scalar2=None, op0=ALU.mult))
            sincos(SIN[:, j, :], COS[:, j, :], tb[:], tb2[:], pari[:, :])
        dg = S["dg"] = T("dg", [128, 2, 4], F32)
        s.dma("sp", lambda e: e.dma_start(out=dg[:, 0, :], in_=di["s5_d"].rearrange("o (m q) -> q (o m)", q=128), allow_slow_non_contiguous=True), writes=rc)
        s.dma("sp", lambda e: e.dma_start(out=dg[:, 1, :], in_=di["s5_glu_b"].rearrange("o (m q) -> q (o m)", q=128), allow_slow_non_contiguous=True), writes=rc)
        gwf = T("gwf", [128, 4, 512], F32)
        gw = S["gw"] = T("gw", [128, 4, 512], BF16)
        s.dma("sp", lambda e: e.dma_start(out=gwf[:], in_=di["s5_glu_w"].rearrange("(m q) n -> q m n", q=128)), writes=rc)
        V(lambda e: e.tensor_copy(out=gw[:], in_=gwf[:]))

    def mixer0_B(self, cur, nxt):
        nc, s, di, do = self.nc, self.s, self.din, self.dout
        with ExitStack() as es:
            T = lambda n, sh, dt: es.enter_context(nc.sbuf_tensor(self.name(n), sh, dt))
            P = lambda n, sh, dt: es.enter_context(nc.psum_tensor(self.name(n), sh, dt))
            S = {}
            self.s5_setup(es, S)
            par, COS, SIN, BBT, CX, dg, gw = S["par"], S["COS"], S["SIN"], S["BBT"], S["CX"], S["dg"], S["gw"]
            ix = S["idx"]
            r_c = S["r_c"]
            m = self.alloc_mod(es)
            ewout = T("ewout", [128, NKC, D], BF16)
            xt = [T("xt", [128, D], F32) for _ in range(2)]
            ua = [T("ua", [128, 2, 512], BF16) for _ in range(2)]
            fT = [T("fT", [128, 8, 128], BF16) for _ in range(2)]
            y5T = [T("y5T", [128, 4, 128], BF16) for _ in range(2)]
            z = [T("z", [128, 4, 128], BF16) for _ in range(2)]
            NW = 6
            w = [T("w", [128, 4, 128], F32) for _ in range(NW)]
            gr = T("gr", [128, 4, 128], F32); gi = T("gi", [128, 4, 128], F32)
            hr = T("hr", [128, 4, 128], F32); hi = T("hi", [128, 4, 128], F32)
            hrb = [T("hrb", [128, 4, 128], BF16) for _ in range(2)]; hib = [T("hib", [128, 4, 128], BF16) for _ in range(2)]
            ysb = T("ysb", [128, 128], F32); ysq = T("ysq", [128, 128], F32); ysg = T("ysg", [128, 128], F32)
            carry = T("carry", [128, 4, 16], F32)
            ctmp = T("ctmp", [128, 2, 16], F32)
            t1 = [T("t1", [128, D], F32) for _ in range(2)]
            stt = [T("st", [128, 16], F32) for _ in range(2)]
            COSs = T("COSs", [128, 16, 64], F32); SINs = T("SINs", [128, 16, 64], F32); RHOm = T("RHOm", [128, 16, 64], F32)
            bmask = T("bmask", [128, 16, 4], F32)
            h0 = T("h0", [16, 2, 2048], F32); h0n = T("h0n", [128, 2, 16, 16], F32); inj = T("inj", [128, 2, 16, 16], F32)
            hout = T("hout", [128, 2, 16, 16], F32); houtT = T("houtT", [16, 2, 2048], F32)
            xre = P("xre", [128, 4, 128], F32); xim = P("xim", [128, 4, 128], F32)
            py = P("py", [128, 512], F32); pgl = P("pgl", [128, 512], F32)
            tpb = P("tpb", [128, 8, 128], BF16)
            po = [P("po", [128, 512], F32) for _ in range(2)]
            pmisc = P("pmisc", [128, 512], F32)
            r_ew = Res(); r_x = [Res(), Res()]; r_ua = [Res(), Res()]; r_fT = [Res(), Res()]; r_y5 = [Res(), Res()]; r_z = [Res(), Res()]
            r_w = [Res() for _ in range(NW)]; r_gr = Res(); r_gi = Res(); r_hr = Res(); r_hi = Res(); r_hb = [Res(), Res()]
            r_ys = Res(); r_carry = Res(); r_t1 = [Res(), Res()]; r_st = [Res(), Res()]
            r_xre = Res(); r_xim = Res(); r_py = Res(); r_pgl = Res(); r_tp = Res(); r_po = [Res(), Res()]; r_pm = Res(); r_s = Res()
            s.dma("sp", lambda e: e.dma_start(out=ewout[:], in_=self.ewout_b[:, :].rearrange("(kc p) n -> p kc n", p=128)), writes=[r_ew])
            self.load_mod(m, 0, 1, None)
            self.bcast_mod(m, False, pmisc, r_pm)
            s.op("pool", lambda e: e.memset(carry[:], 0.0), writes=[r_carry])
            Vs = lambda fn: s.op("dve", fn, reads=[r_c, r_s], writes=[r_s])
            Vs(lambda e: e.tensor_copy(out=COSs[:].rearrange("p j (b t) -> p j b t", t=4), in_=COS[:, :, 0:4].unsqueeze(2).to_broadcast([128, 16, 16, 4])))
            Vs(lambda e: e.tensor_copy(out=SINs[:].rearrange("p j (b t) -> p j b t", t=4), in_=SIN[:, :, 0:4].unsqueeze(2).to_broadcast([128, 16, 16, 4])))
            s.op("pool", lambda e: e.memset(bmask[:], 1.0), reads=[r_s], writes=[r_s])
            s.op("pool", lambda e: e.memset(bmask[:, :, 0:1], 0.0), reads=[r_s], writes=[r_s])
            Vs(lambda e: e.tensor_tensor(out=RHOm[:], in0=par[:, ix["RHO"], :].unsqueeze(2).to_broadcast([128, 16, 64]),
                                         in1=bmask[:].rearrange("p b t -> p (b t)").unsqueeze(1).to_broadcast([128, 16, 64]), op=ALU.mult))
            s.dma("sp", lambda e: e.dma_start(out=h0[:, 0, :], in_=di["state_s5_re"][:, :]), writes=[r_s])
            s.dma("sp", lambda e: e.dma_start(out=h0[:, 1, :], in_=di["state_s5_im"][:, :]), writes=[r_s])
            for ri in range(2):
                for j in range(16):
                    s.op("pe", lambda e, ri=ri, j=j: e.transpose(out=pmisc[:, (ri * 16 + j) * 16:(ri * 16 + j + 1) * 16],
                                                                 in_=h0[:, ri, j * 128:(j + 1) * 128], identity=self.ident_f[0:16, 0:16]),
                         reads=[r_s, self.rconst], writes=[r_pm])
            Vs2 = lambda fn: s.op("dve", fn, reads=[r_c, r_s, r_pm], writes=[r_s])
            Vs2(lambda e: e.tensor_copy(out=h0n[:].rearrange("p a j b -> p (a j b)"), in_=pmisc[:, 0:512]))
            lre_b = par[:, ix["LRE"], :].unsqueeze(2).to_broadcast([128, 16, 16])
            lim_b = par[:, ix["LIM"], :].unsqueeze(2).to_broadcast([128, 16, 16])
            Vs(lambda e: e.tensor_tensor(out=inj[:, 0], in0=h0n[:, 0], in1=lre_b, op=ALU.mult))
            Vs(lambda e: e.tensor_tensor(out=hout[:, 0], in0=h0n[:, 1], in1=lim_b, op=ALU.mult))
            Vs(lambda e: e.tensor_tensor(out=inj[:, 0], in0=inj[:, 0], in1=hout[:, 0], op=ALU.subtract))
            Vs(lambda e: e.tensor_tensor(out=inj[:, 1], in0=h0n[:, 1], in1=lre_b, op=ALU.mult))
            Vs(lambda e: e.tensor_tensor(out=hout[:, 0], in0=h0n[:, 0], in1=lim_b, op=ALU.mult))
            Vs(lambda e: e.tensor_tensor(out=inj[:, 1], in0=inj[:, 1], in1=hout[:, 0], op=ALU.add))
            iw = [0]

            def W():
                k = iw[0] % NW
                iw[0] += 1
                return w[k], r_w[k]

            for i in range(NTILE):
                sample = (i == 32)
                k = i % 2
                Wd = 64 if sample else 128
                if sample:
                    self.bcast_mod(m, True, pmisc, r_pm)
                ap, nrows, res = self.x_src(False, cur, i)
                rows = slice(i * 128, (i + 1) * 128)
                s.dma("sp", lambda e, k=k, ap=ap: e.dma_start(out=xt[k][:], in_=ap), reads=[res], writes=[r_x[k]])
                s.dma("sp", lambda e, k=k, rows=rows: e.dma_start(out=ua[k][:, 0, :], in_=self.ubuf[rows, :]), reads=[self.ures[i]], writes=[r_ua[k]])
                s.dma("sp", lambda e, k=k, rows=rows: e.dma_start(out=ua[k][:, 1, :], in_=self.mixbuf[rows, 0:512]), reads=[self.mixres[i]], writes=[r_ua[k]])
                for c in range(8):
                    s.op("pe", lambda e, k=k, c=c: e.transpose(out=tpb[:, c, :], in_=ua[k][:, c // 4, (c % 4) * 128:(c % 4 + 1) * 128],
                                                               identity=self.ident_b[:]), reads=[r_ua[k], self.rconst], writes=[r_tp])
                s.op("act", lambda e, k=k: e.copy(out=fT[k][:], in_=tpb[:]), reads=[r_tp], writes=[r_fT[k]])
                Ct = COSs if sample else COS
                St = SINs if sample else SIN
                for jg in range(4):
                    js = slice(jg * 4, jg * 4 + 4)
                    for jj in range(4):
                        j = jg * 4 + jj
                        s.op("pe", lambda e, k=k, j=j, jj=jj, jg=jg, Wd=Wd: e.matmul(xre[:, jj, 0:Wd], lhsT=BBT[:, 0, j, :], rhs=fT[k][:, jg, 0:Wd],
                                                                                   start=True, stop=True), reads=[r_c, r_fT[k]], writes=[r_xre])
                        s.op("pe", lambda e, k=k, j=j, jj=jj, jg=jg, Wd=Wd: e.matmul(xim[:, jj, 0:Wd], lhsT=BBT[:, 1, j, :], rhs=fT[k][:, jg, 0:Wd],
                                                                                   start=True, stop=True), reads=[r_c, r_fT[k]], writes=[r_xim])
                    wa, r_wa = W(); wb, r_wb = W(); xr, r_xr = W(); xi, r_xi = W()
                    Cv = Ct[:, js, 0:Wd]; Sv = St[:, js, 0:Wd]
                    rcs = [r_c, r_s]
                    s.op("dve", lambda e, wa=wa, Cv=Cv, Wd=Wd: e.tensor_tensor(out=wa[:, :, 0:Wd], in0=xre[:, :, 0:Wd], in1=Cv, op=ALU.mult), reads=[r_xre] + rcs, writes=[r_wa])
                    s.op("dve", lambda e, wb=wb, Sv=Sv, Wd=Wd: e.tensor_tensor(out=wb[:, :, 0:Wd], in0=xim[:, :, 0:Wd], in1=Sv, op=ALU.mult), reads=[r_xim] + rcs, writes=[r_wb])
                    s.op("pool", lambda e, wa=wa, wb=wb, xr=xr, Wd=Wd: e.tensor_tensor(out=xr[:, :, 0:Wd], in0=wa[:, :, 0:Wd], in1=wb[:, :, 0:Wd], op=ALU.add), reads=[r_wa, r_wb], writes=[r_xr])
                    wa2, r_wa2 = W(); wb2, r_wb2 = W()
                    s.op("dve", lambda e, wa2=wa2, Cv=Cv, Wd=Wd: e.tensor_tensor(out=wa2[:, :, 0:Wd], in0=xim[:, :, 0:Wd], in1=Cv, op=ALU.mult), reads=[r_xim] + rcs, writes=[r_wa2])
                    s.op("dve", lambda e, wb2=wb2, Sv=Sv, Wd=Wd: e.tensor_tensor(out=wb2[:, :, 0:Wd], in0=xre[:, :, 0:Wd], in1=Sv, op=ALU.mult), reads=[r_xre] + rcs, writes=[r_wb2])
                    s.op("pool", lambda e, wa2=wa2, wb2=wb2, xi=xi, Wd=Wd: e.tensor_tensor(out=xi[:, :, 0:Wd], in0=wa2[:, :, 0:Wd], in1=wb2[:, :, 0:Wd], op=ALU.subtract), reads=[r_wa2, r_wb2], writes=[r_xi])
                    if sample:
                        s.op("pool", lambda e, xr=xr, js=js: e.tensor_tensor(out=xr[:, :, 0:64].rearrange("p j (b t) -> p j b t", t=4)[:, :, :, 0],
                                                                             in0=xr[:, :, 0:64].rearrange("p j (b t) -> p j b t", t=4)[:, :, :, 0],
                                                                             in1=inj[:, 0, js, :], op=ALU.add), reads=[r_xr, r_s], writes=[r_xr])
                        s.op("pool", lambda e, xi=xi, js=js: e.tensor_tensor(out=xi[:, :, 0:64].rearrange("p j (b t) -> p j b t", t=4)[:, :, :, 0],
                                                                             in0=xi[:, :, 0:64].rearrange("p j (b t) -> p j b t", t=4)[:, :, :, 0],
                                                                             in1=inj[:, 1, js, :], op=ALU.add), reads=[r_xi, r_s], writes=[r_xi])
                    for jj in range(4):
                        j = jg * 4 + jj
                        if sample:
                            d0 = RHOm[:, j, :]
                            ini_r = 0.0
                            ini_i = 0.0
                        else:
                            d0 = par[:, ix["RHO"], j:j + 1].to_broadcast([128, 128])
                            ini_r = carry[:, 2, j:j + 1]
                            ini_i = carry[:, 3, j:j + 1]
                        s.op("dve", lambda e, xr=xr, jj=jj, d0=d0, ini_r=ini_r, Wd=Wd: e.tensor_tensor_scan(out=gr[:, jj, 0:Wd], data0=d0, data1=xr[:, jj, 0:Wd],
                                                                                                       initial=ini_r, op0=ALU.mult, op1=ALU.add),
                             reads=[r_xr, r_c, r_s, r_carry], writes=[r_gr])
                        s.op("dve", lambda e, xi=xi, jj=jj, d0=d0, ini_i=ini_i, Wd=Wd: e.tensor_tensor_scan(out=gi[:, jj, 0:Wd], data0=d0, data1=xi[:, jj, 0:Wd],
                                                                                                       initial=ini_i, op0=ALU.mult, op1=ALU.add),
                             reads=[r_xi, r_c, r_s, r_carry], writes=[r_gi])
                    wa, r_wa = W(); wb, r_wb = W()
                    s.op("dve", lambda e, wa=wa, Cv=Cv, Wd=Wd: e.tensor_tensor(out=wa[:, :, 0:Wd], in0=gr[:, :, 0:Wd], in1=Cv, op=ALU.mult), reads=[r_gr] + rcs, writes=[r_wa])
                    s.op("pool", lambda e, wb=wb, Sv=Sv, Wd=Wd: e.tensor_tensor(out=wb[:, :, 0:Wd], in0=gi[:, :, 0:Wd], in1=Sv, op=ALU.mult), reads=[r_gi] + rcs, writes=[r_wb])
                    s.op("dve", lambda e, wa=wa, wb=wb, Wd=Wd: e.tensor_tensor(out=hr[:, :, 0:Wd], in0=wa[:, :, 0:Wd], in1=wb[:, :, 0:Wd], op=ALU.subtract), reads=[r_wa, r_wb], writes=[r_hr])
                    wa2, r_wa2 = W(); wb2, r_wb2 = W()
                    s.op("pool", lambda e, wa2=wa2, Sv=Sv, Wd=Wd: e.tensor_tensor(out=wa2[:, :, 0:Wd], in0=gr[:, :, 0:Wd], in1=Sv, op=ALU.mult), reads=[r_gr] + rcs, writes=[r_wa2])
                    s.op("dve", lambda e, wb2=wb2, Cv=Cv, Wd=Wd: e.tensor_tensor(out=wb2[:, :, 0:Wd], in0=gi[:, :, 0:Wd], in1=Cv, op=ALU.mult), reads=[r_gi] + rcs, writes=[r_wb2])
                    s.op("dve", lambda e, wa2=wa2, wb2=wb2, Wd=Wd: e.tensor_tensor(out=hi[:, :, 0:Wd], in0=wa2[:, :, 0:Wd], in1=wb2[:, :, 0:Wd], op=ALU.add), reads=[r_wa2, r_wb2], writes=[r_hi])
                    kh = (i * 4 + jg) % 2
                    s.op("act", lambda e, kh=kh, Wd=Wd: e.copy(out=hrb[kh][:, :, 0:Wd], in_=hr[:, :, 0:Wd]), reads=[r_hr], writes=[r_hb[kh]])
                    s.op("act", lambda e, kh=kh, Wd=Wd: e.copy(out=hib[kh][:, :, 0:Wd], in_=hi[:, :, 0:Wd]), reads=[r_hi], writes=[r_hb[kh]])
                    if not sample:
                        s.op("pool", lambda e, js=js: e.tensor_copy(out=carry[:, 0, js], in_=hr[:, :, 127]), reads=[r_hr], writes=[r_carry])
                        s.op("pool", lambda e, js=js: e.tensor_copy(out=carry[:, 1, js], in_=hi[:, :, 127]), reads=[r_hi], writes=[r_carry])
                        cs_ = par[:, ix["CS"], js]; sn_ = par[:, ix["SN"], js]
                        s.op("dve", lambda e, js=js, cs_=cs_: e.tensor_tensor(out=ctmp[:, 0, js], in0=carry[:, 0, js], in1=cs_, op=ALU.mult), reads=[r_carry, r_c], writes=[r_carry])
                        s.op("dve", lambda e, js=js, sn_=sn_: e.tensor_tensor(out=ctmp[:, 1, js], in0=carry[:, 1, js], in1=sn_, op=ALU.mult), reads=[r_carry, r_c], writes=[r_carry])
                        s.op("dve", lambda e, js=js: e.tensor_tensor(out=carry[:, 2, js], in0=ctmp[:, 0, js], in1=ctmp[:, 1, js], op=ALU.subtract), reads=[r_carry], writes=[r_carry])
                        s.op("dve", lambda e, js=js, sn_=sn_: e.tensor_tensor(out=ctmp[:, 0, js], in0=carry[:, 0, js], in1=sn_, op=ALU.mult), reads=[r_carry, r_c], writes=[r_carry])
                        s.op("dve", lambda e, js=js, cs_=cs_: e.tensor_tensor(out=ctmp[:, 1, js], in0=carry[:, 1, js], in1=cs_, op=ALU.mult), reads=[r_carry, r_c], writes=[r_carry])
                        s.op("dve", lambda e, js=js: e.tensor_tensor(out=carry[:, 3, js], in0=ctmp[:, 0, js], in1=ctmp[:, 1, js], op=ALU.add), reads=[r_carry], writes=[r_carry])
                    else:
                        s.op("pool", lambda e, js=js: e.tensor_copy(out=hout[:, 0, js, :], in_=hr[:, :, 0:64].rearrange("p j (b t) -> p j b t", t=4)[:, :, :, 3]),
                             reads=[r_hr, r_s], writes=[r_s])
                        s.op("pool", lambda e, js=js: e.tensor_copy(out=hout[:, 1, js, :], in_=hi[:, :, 0:64].rearrange("p j (b t) -> p j b t", t=4)[:, :, :, 3]),
                             reads=[r_hi, r_s], writes=[r_s])
                    for jj in range(4):
                        j = jg * 4 + jj
                        s.op("pe", lambda e, kh=kh, j=j, jj=jj, Wd=Wd: e.matmul(py[:, 0:Wd], lhsT=CX[:, 0, j, :], rhs=hrb[kh][:, jj, 0:Wd],
                                                                                start=(jj == 0), stop=False), reads=[r_c, r_hb[kh]], writes=[r_py])
                        s.op("pe", lambda e, kh=kh, j=j, jj=jj, Wd=Wd: e.matmul(py[:, 0:Wd], lhsT=CX[:, 1, j, :], rhs=hib[kh][:, jj, 0:Wd],
                                                                                start=False, stop=(jj == 3)), reads=[r_c, r_hb[kh]], writes=[r_py])
                    s.op("dve", lambda e, k=k, jg=jg, Wd=Wd: e.scalar_tensor_tensor(out=ysb[:, 0:Wd], in0=fT[k][:, jg, 0:Wd], scalar=dg[:, 0, jg:jg + 1],
                                                                                     in1=py[:, 0:Wd], op0=ALU.mult, op1=ALU.add),
                         reads=[r_fT[k], r_py, r_c], writes=[r_ys])
                    s.op("pool", lambda e, Wd=Wd: e.tensor_tensor(out=ysq[:, 0:Wd], in0=ysb[:, 0:Wd], in1=ysb[:, 0:Wd], op=ALU.mult), reads=[r_ys], writes=[r_ys])
                    s.op("pool", lambda e, Wd=Wd: e.tensor_scalar(out=ysq[:, 0:Wd], in0=ysq[:, 0:Wd], scalar1=0.044715, scalar2=1.0, op0=ALU.mult, op1=ALU.add), reads=[r_ys], writes=[r_ys])
                    s.op("pool", lambda e, Wd=Wd: e.tensor_tensor(out=ysq[:, 0:Wd], in0=ysq[:, 0:Wd], in1=ysb[:, 0:Wd], op=ALU.mult), reads=[r_ys], writes=[r_ys])
                    s.op("act", lambda e, Wd=Wd: e.activation(out=ysg[:, 0:Wd], in_=ysq[:, 0:Wd], func=AF.Sigmoid, scale=1.5957691216057308), reads=[r_ys], writes=[r_ys])
                    s.op("dve", lambda e, k=k, jg=jg, Wd=Wd: e.tensor_tensor(out=z[k][:, jg, 0:Wd], in0=ysb[:, 0:Wd], in1=ysg[:, 0:Wd], op=ALU.mult), reads=[r_ys], writes=[r_z[k]])
                for mo in range(4):
                    for mi in range(4):
                        s.op("pe", lambda e, k=k, mo=mo, mi=mi, Wd=Wd: e.matmul(pgl[:, 0:Wd], lhsT=gw[:, mi, mo * 128:(mo + 1) * 128], rhs=z[k][:, mi, 0:Wd],
                                                                                start=(mi == 0), stop=(mi == 3)), reads=[r_c, r_z[k]], writes=[r_pgl])
                    s.op("act", lambda e, mo=mo, Wd=Wd: e.activation(out=ysg[:, 0:Wd], in_=pgl[:, 0:Wd], func=AF.Sigmoid, bias=dg[:, 1, mo:mo + 1], scale=1.0),
                         reads=[r_pgl, r_c, r_ys], writes=[r_ys])
                    s.op("dve", lambda e, k=k, mo=mo, Wd=Wd: e.tensor_tensor(out=y5T[k][:, mo, 0:Wd], in0=z[k][:, mo, 0:Wd], in1=ysg[:, 0:Wd], op=ALU.mult),
                         reads=[r_z[k], r_ys], writes=[r_y5[k]])
                if sample:
                    s.op("pool", lambda e, k=k: e.memset(y5T[k][:, :, 64:128], 0.0), reads=[r_y5[k]], writes=[r_y5[k]])
                for h in range(2):
                    for kc in range(8):
                        lhs = fT[k][:, 4 + kc, :] if kc < 4 else y5T[k][:, kc - 4, :]
                        rd = r_fT[k] if kc < 4 else r_y5[k]
                        s.op("pe", lambda e, h=h, kc=kc, lhs=lhs: e.matmul(po[h][:], lhsT=lhs, rhs=ewout[:, kc, h * 512:(h + 1) * 512],
                                                                           start=(kc == 0), stop=(kc == 7)), reads=[rd, r_ew], writes=[r_po[h]])
                self.epilogue(m, [po[0][:], po[1][:]], r_po, xt[k][:], r_x[k], t1[k], r_t1[k], stt[k], r_st[k],
                              self.x_dst(False, nxt, i), None)
            s.dma("pool", lambda e: e.dma_start(out=do["s5rp"].rearrange("o (j q) -> q (o j)", q=128), in_=carry[:, 0, :], allow_slow_non_contiguous=True), reads=[r_carry])
            s.dma("pool", lambda e: e.dma_start(out=do["s5ip"].rearrange("o (j q) -> q (o j)", q=128), in_=carry[:, 1, :], allow_slow_non_contiguous=True), reads=[r_carry])
            for ri in range(2):
                for j in range(16):
                    s.op("pe", lambda e, ri=ri, j=j: e.transpose(out=pmisc[0:16, ((ri * 16 + j) % 4) * 128:((ri * 16 + j) % 4 + 1) * 128],
                                                                 in_=hout[:, ri, j, :], identity=self.ident_f[:]),
                         reads=[r_s, self.rconst], writes=[r_pm])
                    s.op("act", lambda e, ri=ri, j=j: e.copy(out=houtT[:, ri, j * 128:(j + 1) * 128],
                                                             in_=pmisc[0:16, ((ri * 16 + j) % 4) * 128:((ri * 16 + j) % 4 + 1) * 128]),
                         reads=[r_pm], writes=[r_s])
            s.dma("pool", lambda e: e.dma_start(out=do["s5rs"][:, :], in_=houtT[:, 0, :]), reads=[r_s])
            s.dma("pool", lambda e: e.dma_start(out=do["s5is"][:, :], in_=houtT[:, 1, :]), reads=[r_s])
            s.barrier()

    def dil_masks(self, Wm_out, nfree, pattern, base, es2, r_c):
        nc, s = self.nc, self.s
        T2 = lambda n, sh, dt: es2.enter_context(nc.sbuf_tensor(self.name(n), sh, dt))
        Di = T2("Di", [128, nfree], I32); Df = T2("Df", [128, nfree], F32); Ti = T2("Ti", [128, nfree], I32)
        ge0 = T2("ge0", [128, nfree], F32); acc = T2("accm", [128, nfree], F32); tf = T2("tfm", [128, nfree], F32); t2 = T2("t2m", [128, nfree], F32)
        rc = [r_c]
        V = lambda fn: s.op("dve", fn, reads=rc, writes=rc)
        s.op("pool", lambda e: e.iota(Di[:], pattern=pattern, base=base, channel_multiplier=-1), reads=rc, writes=rc)
        V(lambda e: e.tensor_copy(out=Df[:], in_=Di[:]))
        V(lambda e: e.tensor_scalar(out=ge0[:], in0=Df[:], scalar1=0.0, scalar2=None, op0=ALU.is_ge))
        V(lambda e: e.tensor_scalar(out=acc[:], in0=Df[:], scalar1=128.0, scalar2=None, op0=ALU.is_le))
        for msk, lim in ((3, 512.0), (15, 2048.0)):
            V(lambda e, msk=msk: e.tensor_scalar(out=Ti[:], in0=Di[:], scalar1=msk, scalar2=None, op0=ALU.bitwise_and))
            V(lambda e: e.tensor_copy(out=tf[:], in_=Ti[:]))
            V(lambda e: e.tensor_scalar(out=tf[:], in0=tf[:], scalar1=0.0, scalar2=None, op0=ALU.is_equal))
            V(lambda e, lim=lim: e.tensor_scalar(out=t2[:], in0=Df[:], scalar1=lim, scalar2=None, op0=ALU.is_le))
            V(lambda e: e.tensor_tensor(out=tf[:], in0=tf[:], in1=t2[:], op=ALU.mult))
            V(lambda e: e.tensor_tensor(out=acc[:], in0=acc[:], in1=tf[:], op=ALU.add))
        V(lambda e: e.tensor_tensor(out=Wm_out, in0=acc[:], in1=ge0[:], op=ALU.mult))

    def mixer1_A(self, cur):
        nc, s, di, do = self.nc, self.s, self.din, self.dout
        ND = 17
        with ExitStack() as es:
            T = lambda n, sh, dt: es.enter_context(nc.sbuf_tensor(self.name(n), sh, dt))
            P = lambda n, sh, dt: es.enter_context(nc.psum_tensor(self.name(n), sh, dt))
            r_c = Res()
            Wm = T("Wm", [128, ND, 128], BF16)
            with ExitStack() as es2:
                self.dil_masks(Wm[:].rearrange("p a b -> p (a b)"), ND * 128, [[128, ND], [1, 128]], 0, es2, r_c)
                s.barrier()
            m = self.alloc_mod(es)
            owin = T("owin", [128, NKC, 3328], BF16)
            xt = [T("xt", [128, D], F32) for _ in range(2)]
            tmpf = T("tmpf", [128, D], F32)
            hb = [T("hb", [128, D], BF16) for _ in range(2)]
            hT = [T("hT", [128, NKC, 128], BF16) for _ in range(2)]
            kv_f = [T("kv_f", [128, 1024], F32) for _ in range(2)]
            pd_f1 = T("pd_f", [128, 1792], F32)
            pd_f = [pd_f1, pd_f1]
            q_b = [T("q_b", [128, 512], BF16) for _ in range(2)]
            k_b = [T("k_b", [128, 512], BF16) for _ in range(2)]
            qT = [T("qT", [128, 4, 128], BF16) for _ in range(2)]
            KT = T("KT1", [128, 4, SEQ], BF16)
            VP = T("VP1", [128, 32, 8, 66], BF16)
            E = [T("E", [128, 4, 128], BF16) for _ in range(3)]
            rr = [T("rr", [128, 2], F32) for _ in range(2)]
            mixf = [T("mixf", [128, 512], BF16) for _ in range(2)]
            pp = [P("pp", [128, 512], F32) for _ in range(4)]
            tpb = P("tpb", [128, NKC, 128], BF16)
            st = [P("st", [128, 512], F32) for _ in range(2)]
            oacc = P("oacc", [128, 512], F32)
            r_ow = Res(); r_x = [Res(), Res()]; r_tmpf = Res(); r_hb = [Res(), Res()]; r_hT = [Res(), Res()]
            r_kv = [Res(), Res()]; r_pd1 = Res(); r_pd = [r_pd1, r_pd1]; r_pp = [Res() for _ in range(4)]; r_tp = Res()
            r_q = [Res(), Res()]; r_k = [Res(), Res()]; r_qT = [Res(), Res()]
            r_KT = [Res() for _ in range(32)]; r_VP = [Res() for _ in range(32)]; r_E = [Res() for _ in range(3)]
            r_rr = [Res(), Res()]; r_mixf = [Res(), Res()]; r_st = [Res(), Res()]; r_oacc = Res()
            s.dma("sp", lambda e: e.dma_start(out=owin[:], in_=self.owin_b[:, :].rearrange("(kc p) n -> p kc n", p=128)), writes=[r_ow])
            s.op("pool", lambda e: e.memset(VP[:].rearrange("p a h e -> p (a h e)"), 1.0), writes=r_VP)
            self.load_mod(m, 1, 1, None)
            self.bcast_mod(m, False, pp[0], r_pp[0])
            widths = [512] * 6 + [256]
            ist = 0; ie = 0; ihd = 0
            for i in range(NTILE):
                sample = (i == 32)
                k = i % 2
                if sample:
                    self.bcast_mod(m, True, pp[0], r_pp[0])
                ap, nrows, res = self.x_src(False, cur, i)
                s.dma("sp", lambda e, k=k, ap=ap: e.dma_start(out=xt[k][:], in_=ap), reads=[res], writes=[r_x[k]])
                s.op("pool", lambda e, k=k: e.tensor_tensor(out=tmpf[:], in0=xt[k][:], in1=m["SC"][:], op=ALU.mult),
                     reads=[r_x[k], m["r_mod"]], writes=[r_tmpf])
                s.op("dve", lambda e, k=k: e.tensor_tensor(out=hb[k][:], in0=tmpf[:], in1=m["SH"][:], op=ALU.add),
                     reads=[r_tmpf, m["r_mod"]], writes=[r_hb[k]])
                for kc in range(NKC):
                    s.op("pe", lambda e, k=k, kc=kc: e.transpose(out=tpb[:, kc, :], in_=hb[k][:, kc * 128:(kc + 1) * 128],
                                                                 identity=self.ident_b[:]),
                         reads=[r_hb[k], self.rconst], writes=[r_tp])
                s.op("act", lambda e, k=k: e.copy(out=hT[k][:], in_=tpb[:]), reads=[r_tp], writes=[r_hT[k]])

                def proj(c, bank):
                    wdt = widths[c]
                    for kc in range(NKC):
                        s.op("pe", lambda e, kc=kc, k=k, wdt=wdt, c=c, bank=bank: e.matmul(pp[bank][:, 0:wdt], lhsT=hT[k][:, kc, :],
                                                             rhs=owin[:, kc, c * 512:c * 512 + wdt],
                                                             start=(kc == 0), stop=(kc == NKC - 1)),
                             reads=[r_hT[k], r_ow], writes=[r_pp[bank]])
                for c in range(3):
                    proj(c, c)
                s.op("act", lambda e, k=k: e.copy(out=q_b[k][:], in_=pp[0][:]), reads=[r_pp[0]], writes=[r_q[k]])
                s.op("dve", lambda e, k=k: e.tensor_copy(out=kv_f[k][:, 0:512], in_=pp[1][:]), reads=[r_pp[1]], writes=[r_kv[k]])
                s.op("dve", lambda e, k=k: e.tensor_copy(out=kv_f[k][:, 512:1024], in_=pp[2][:]), reads=[r_pp[2]], writes=[r_kv[k]])
                for c in range(3, 7):
                    proj(c, c - 3)
                    wdt = widths[c]
                    s.op("dve", lambda e, c=c, wdt=wdt, k=k: e.tensor_copy(out=pd_f[k][:, (c - 3) * 512:(c - 3) * 512 + wdt], in_=pp[c - 3][:, 0:wdt]),
                         reads=[r_pp[c - 3]], writes=[r_pd[k]])
                rows_all = slice(i * 128, (i + 1) * 128)
                s.dma("pool", lambda e, k=k, rows_all=rows_all: e.dma_start(out=self.pdbuf[rows_all, :], in_=pd_f[k][:]),
                      reads=[r_pd[k]], writes=[self.pdres[i]])
                if not sample:
                    if i >= 16:
                        rows = slice((i - 16) * 128, (i - 15) * 128)
                        s.dma("pool", lambda e, k=k, rows=rows: e.dma_start(out=do["ckp"][rows, :], in_=kv_f[k][:, 0:512]), reads=[r_kv[k]])
                        s.dma("pool", lambda e, k=k, rows=rows: e.dma_start(out=do["cvp"][rows, :], in_=kv_f[k][:, 512:1024]), reads=[r_kv[k]])
                    if i == 31:
                        s.dma("pool", lambda e, k=k: e.dma_start(out=do["dsp"][0:1, :], in_=pd_f[k][127:128, :]), reads=[r_pd[k]])
                else:
                    s.dma("pool", lambda e, k=k: e.dma_start(out=do["cks"][:, :], in_=kv_f[k][0:NS, 0:512]), reads=[r_kv[k]], writes=[self.r_cks])
                    s.dma("pool", lambda e, k=k: e.dma_start(out=do["cvs"][:, :], in_=kv_f[k][0:NS, 512:1024]), reads=[r_kv[k]], writes=[self.r_cks])
                    s.dma("pool", lambda e, k=k: e.dma_start(out=self.qsbuf[:, :], in_=q_b[k][0:NS, :]), reads=[r_q[k]], writes=[self.r_cks])
                    for b in range(NSB):
                        s.dma("pool", lambda e, b=b, k=k: e.dma_start(out=do["dss"][b:b + 1, :], in_=pd_f[k][4 * b + 3:4 * b + 4, :]), reads=[r_pd[k]])
                    continue
                if self.cfg.get("nodil"):
                    continue
                s.op("act", lambda e, k=k: e.copy(out=k_b[k][:], in_=kv_f[k][:, 0:512]), reads=[r_kv[k]], writes=[r_k[k]])
                s.op("act", lambda e, i=i, k=k: e.copy(out=VP[:, i, :, 0:64], in_=kv_f[k][:, 512:1024].rearrange("p (h e) -> p h e", h=8)),
                     reads=[r_kv[k]], writes=[r_VP[i]])
                for h in range(4):
                    s.op("pe", lambda e, k=k, h=h: e.transpose(out=tpb[:, h, :], in_=q_b[k][:, h * 128:(h + 1) * 128],
                                                               identity=self.ident_b[:]), reads=[r_q[k], self.rconst], writes=[r_tp])
                    s.op("pe", lambda e, k=k, h=h: e.transpose(out=tpb[:, 4 + h, :], in_=k_b[k][:, h * 128:(h + 1) * 128],
                                                               identity=self.ident_b[:]), reads=[r_k[k], self.rconst], writes=[r_tp])
                s.op("act", lambda e, k=k: e.copy(out=qT[k][:], in_=tpb[:, 0:4, :]), reads=[r_tp], writes=[r_qT[k]])
                s.op("act", lambda e, i=i: e.copy(out=KT[:, :, i * 128:(i + 1) * 128], in_=tpb[:, 4:8, :]), reads=[r_tp], writes=[r_KT[i]])
                nd = min(ND, i + 1)
                for h in range(8):
                    hp = h // 2
                    pr = slice((h % 2) * 64, (h % 2) * 64 + 64)
                    oc = (h % 2) * 66
                    for d0 in range(0, nd, 4):
                        ds_ = list(range(d0, min(d0 + 4, nd)))
                        sb = ist % 2; ist += 1
                        for jj, dlt in enumerate(ds_):
                            j = i - dlt
                            s.op("pe", lambda e, sb=sb, jj=jj, j=j, hp=hp, pr=pr, k=k:
                                 e.matmul(st[sb][:, jj * 128:(jj + 1) * 128], lhsT=KT[pr, hp, j * 128:(j + 1) * 128],
                                          rhs=qT[k][pr, hp, :], start=True, stop=True),
                                 reads=[r_KT[j], r_qT[k]], writes=[r_st[sb]])
                        eb = ie % 3; ie += 1
                        w = len(ds_)
                        s.op("act", lambda e, eb=eb, sb=sb, w=w: e.activation(out=E[eb][:, 0:w, :],
                                                                             in_=st[sb][:, 0:w * 128].rearrange("p (a b) -> p a b", a=w),
                                                                             func=AF.Exp, scale=0.125), reads=[r_st[sb]], writes=[r_E[eb]])
                        s.op("pool", lambda e, eb=eb, w=w, d0=d0: e.tensor_tensor(out=E[eb][:, 0:w, :], in0=E[eb][:, 0:w, :],
                                                                                   in1=Wm[:, d0:d0 + w, :], op=ALU.mult),
                             reads=[r_E[eb], r_c], writes=[r_E[eb]])
                        for jj, dlt in enumerate(ds_):
                            j = i - dlt
                            s.op("pe", lambda e, eb=eb, jj=jj, j=j, h=h, oc=oc, dlt=dlt, nd=nd:
                                 e.matmul(oacc[:, oc:oc + 65], lhsT=E[eb][:, jj, :], rhs=VP[:, j, h, 0:65],
                                          start=(dlt == 0), stop=(dlt == nd - 1)),
                                 reads=[r_E[eb], r_VP[j]], writes=[r_oacc])
                    kk = ihd % 2; ihd += 1
                    s.op("dve", lambda e, kk=kk, oc=oc: e.reciprocal(out=rr[kk][:, 0:1], in_=oacc[:, oc + 64:oc + 65]), reads=[r_oacc], writes=[r_rr[kk]])
                    s.op("dve", lambda e, kk=kk, oc=oc, h=h, k=k: e.tensor_scalar(out=mixf[k][:, h * 64:(h + 1) * 64], in0=oacc[:, oc:oc + 64],
                                                                                  scalar1=rr[kk][:, 0:1], scalar2=None, op0=ALU.mult),
                         reads=[r_oacc, r_rr[kk]], writes=[r_mixf[k]])
                s.dma("pool", lambda e, k=k, rows_all=rows_all: e.dma_start(out=self.mixbuf[rows_all, 0:512], in_=mixf[k][:]),
                      reads=[r_mixf[k]], writes=[self.mixres[i]])
            s.barrier()

    def mixer1_S(self):
        nc, s, di, do = self.nc, self.s, self.din, self.dout
        NPG = 16
        with ExitStack() as es:
            T = lambda n, sh, dt: es.enter_context(nc.sbuf_tensor(self.name(n), sh, dt))
            P = lambda n, sh, dt: es.enter_context(nc.psum_tensor(self.name(n), sh, dt))
            r_c = Res()
            Ws = T("Ws", [128, NPG + 1, 4], BF16)
            with ExitStack() as es2:
                self.dil_masks(Ws[:].rearrange("p a b -> p (a b)"), (NPG + 1) * 4, [[-128, NPG + 1], [1, 4]], 2048, es2, r_c)
                s.barrier()
            selrows = T("selrows", [64, 64, 128], BF16)
            q_s = T("q_s", [64, 512], BF16)
            Qbc = [T("Qbc", [128, 4, 512], BF16) for _ in range(2)]
            Kg = [T("Kg", [128, 512], F32) for _ in range(4)]
            Vg = [T("Vg", [128, 512], F32) for _ in range(3)]
            VPb = [T("VPb", [128, NPG, 8, 66], BF16) for _ in range(2)]
            Kn = [T("Kn", [4, 512], F32) for _ in range(2)]
            Vn = [T("Vn", [4, 512], F32) for _ in range(2)]
            VPn = [T("VPn", [4, 8, 66], BF16) for _ in range(2)]
            prod = [T("prod", [128, 512], F32) for _ in range(2)]
            Sc = [T("Sc", [128, NPG + 1, 8, 4], F32) for _ in range(2)]
            Eb = [T("Eb", [128, NPG + 1, 8, 4], BF16) for _ in range(2)]
            rr = [T("rr", [128, 2], F32) for _ in range(2)]
            mixs = [T("mixs", [4, 512], BF16) for _ in range(2)]
            qps = [P("qps", [128, 512], F32) for _ in range(2)]
            ops_ = [P("ops", [128, 512], F32) for _ in range(2)]
            r_Qbc = [Res(), Res()]; r_Kg = [Res() for _ in range(4)]; r_Vg = [Res() for _ in range(3)]
            r_VPb = [Res(), Res()]; r_Kn = [Res(), Res()]; r_Vn = [Res(), Res()]; r_VPn = [Res(), Res()]
            r_prod = [Res(), Res()]; r_Sc = [Res(), Res()]; r_Eb = [Res(), Res()]; r_rr = [Res(), Res()]
            r_mixs = [Res(), Res()]; r_qps = [Res(), Res()]; r_ops = [Res(), Res()]
            s.op("pool", lambda e: e.memset(selrows[:], 1.0), writes=[r_c])
            s.op("pool", lambda e: e.affine_select(out=selrows[:], in_=selrows[:], pattern=[[-1, 64], [0, 128]],
                                                    compare_op=ALU.is_equal, fill=0.0, base=0, channel_multiplier=1),
                 reads=[r_c], writes=[r_c])
            s.dma("sp", lambda e: e.dma_start(out=q_s[:], in_=self.qsbuf[:, :]), reads=[self.r_cks], writes=[r_c])
            for v in VPb:
                s.op("pool", lambda e, v=v: e.memset(v[:].rearrange("p a h e -> p (a h e)"), 1.0), writes=r_VPb)
            for v in VPn:
                s.op("pool", lambda e, v=v: e.memset(v[:].rearrange("p h e -> p (h e)"), 1.0), writes=r_VPn)
            for sc_ in Sc:
                s.op("pool", lambda e, sc_=sc_: e.memset(sc_[:].rearrange("p a h q -> p (a h q)"), 0.0), writes=r_Sc)
            ikg = 0; ivg = 0; ipr = 0
            cck = di["cache_c_k"]; ccv = di["cache_c_v"]
            for b in range(NSB):
                kb = b % 2
                for qi in range(4):
                    kq = (b * 4 + qi) % 2
                    s.op("pe", lambda e, kq=kq, b=b, qi=qi: e.matmul(qps[kq][:], lhsT=selrows[:, 4 * b + qi, :], rhs=q_s[:, :],
                                                                     start=True, stop=True), reads=[r_c], writes=[r_qps[kq]])
                    s.op("act", lambda e, kq=kq, kb=kb, qi=qi: e.copy(out=Qbc[kb][:, qi, :], in_=qps[kq][:]),
                         reads=[r_qps[kq]], writes=[r_Qbc[kb]])
                s.dma("sp", lambda e, kb=kb, b=b: e.dma_start(out=Kn[kb][:], in_=do["cks"][4 * b:4 * b + 4, :]),
                      reads=[self.r_cks], writes=[r_Kn[kb]])
                s.dma("sp", lambda e, kb=kb, b=b: e.dma_start(out=Vn[kb][:], in_=do["cvs"][4 * b:4 * b + 4, :]),
                      reads=[self.r_cks], writes=[r_Vn[kb]])
                s.op("act", lambda e, kb=kb: e.copy(out=VPn[kb][:, :, 0:64], in_=Vn[kb][:].rearrange("p (h e) -> p h e", h=8)),
                     reads=[r_Vn[kb]], writes=[r_VPn[kb]])
                for pg in range(NPG):
                    kv = ivg % 3; ivg += 1
                    r0 = b * 2048 + pg * 128
                    s.dma("sp", lambda e, kv=kv, r0=r0: e.dma_start(out=Vg[kv][:], in_=ccv[r0:r0 + 128, :]), writes=[r_Vg[kv]])
                    s.op("act", lambda e, kv=kv, kb=kb, pg=pg: e.copy(out=VPb[kb][:, pg, :, 0:64],
                                                                      in_=Vg[kv][:].rearrange("p (h e) -> p h e", h=8)),
                         reads=[r_Vg[kv]], writes=[r_VPb[kb]])
                for pg in range(NPG + 1):
                    if pg < NPG:
                        kk = ikg % 4; ikg += 1
                        r0 = b * 2048 + pg * 128
                        s.dma("sp", lambda e, kk=kk, r0=r0: e.dma_start(out=Kg[kk][:], in_=cck[r0:r0 + 128, :]), writes=[r_Kg[kk]])
                        ksrc, r_ks, npart = Kg[kk], r_Kg[kk], 128
                    else:
                        ksrc, r_ks, npart = Kn[kb], r_Kn[kb], 4
                    for qi in range(4):
                        kp = ipr % 2; ipr += 1
                        s.op("dve", lambda e, kp=kp, ksrc=ksrc, npart=npart, kb=kb, qi=qi:
                             e.tensor_tensor(out=prod[kp][0:npart, :], in0=ksrc[0:npart, :], in1=Qbc[kb][0:npart, qi, :], op=ALU.mult),
                             reads=[r_ks, r_Qbc[kb]], writes=[r_prod[kp]])
                        s.op("dve", lambda e, kp=kp, npart=npart, kb=kb, pg=pg, qi=qi:
                             e.tensor_reduce(out=Sc[kb][0:npart, pg, :, qi],
                                             in_=prod[kp][0:npart, :].rearrange("p (g d) -> p g d", d=64), axis=AX.X, op=ALU.add),
                             reads=[r_prod[kp]], writes=[r_Sc[kb]])
                s.op("act", lambda e, kb=kb: e.activation(out=Eb[kb][:].rearrange("p a h q -> p (a h q)"),
                                                          in_=Sc[kb][:].rearrange("p a h q -> p (a h q)"),
                                                          func=AF.Exp, scale=0.125), reads=[r_Sc[kb]], writes=[r_Eb[kb]])
                s.op("pool", lambda e, kb=kb: e.tensor_tensor(out=Eb[kb][:], in0=Eb[kb][:],
                                                               in1=Ws[:].unsqueeze(2).to_broadcast([128, NPG + 1, 8, 4]), op=ALU.mult),
                     reads=[r_Eb[kb], r_c], writes=[r_Eb[kb]])
                for h in range(8):
                    ob = h // 4
                    oc = (h % 4) * 66
                    for pg in range(NPG + 1):
                        if pg < NPG:
                            s.op("pe", lambda e, h=h, oc=oc, ob=ob, pg=pg, kb=kb:
                                 e.matmul(ops_[ob][0:4, oc:oc + 65], lhsT=Eb[kb][:, pg, h, :], rhs=VPb[kb][:, pg, h, 0:65],
                                          start=(pg == 0), stop=False), reads=[r_Eb[kb], r_VPb[kb]], writes=[r_ops[ob]])
                        else:
                            s.op("pe", lambda e, h=h, oc=oc, ob=ob, pg=pg, kb=kb:
                                 e.matmul(ops_[ob][0:4, oc:oc + 65], lhsT=Eb[kb][0:4, pg, h, :], rhs=VPn[kb][0:4, h, 0:65],
                                          start=False, stop=True), reads=[r_Eb[kb], r_VPn[kb]], writes=[r_ops[ob]])
                    k2 = (b * 8 + h) % 2
                    s.op("dve", lambda e, k2=k2, ob=ob, oc=oc: e.reciprocal(out=rr[k2][0:4, 0:1], in_=ops_[ob][0:4, oc + 64:oc + 65]),
                         reads=[r_ops[ob]], writes=[r_rr[k2]])
                    s.op("dve", lambda e, k2=k2, ob=ob, oc=oc, h=h, kb=kb: e.tensor_scalar(out=mixs[kb][0:4, h * 64:(h + 1) * 64], in0=ops_[ob][0:4, oc:oc + 64],
                                                                                           scalar1=rr[k2][0:4, 0:1], scalar2=None, op0=ALU.mult),
                         reads=[r_ops[ob], r_rr[k2]], writes=[r_mixs[kb]])
                s.dma("sp", lambda e, kb=kb, b=b: e.dma_start(out=self.mixbuf[SEQ + 4 * b:SEQ + 4 * b + 4, 0:512], in_=mixs[kb][:]),
                      reads=[r_mixs[kb]], writes=[self.mixres[32]])
            s.barrier()

    def mixer1_P(self):
        nc, s, di, do = self.nc, self.s, self.din, self.dout
        with ExitStack() as es:
            T = lambda n, sh, dt: es.enter_context(nc.sbuf_tensor(self.name(n), sh, dt))
            P = lambda n, sh, dt: es.enter_context(nc.psum_tensor(self.name(n), sh, dt))
            MU = T("MU", [128, 1792], F32)
            CB = T("CB", [128, 5, 512], F32)
            W12 = T("W12", [128, 512], F32); G2 = T("G2", [128, 512], F32)
            pd = [T("pdt", [128, 1792], F32) for _ in range(2)]
            pv = [T("pvt", [128, 1792], F32) for _ in range(2)]
            xm = [T("xm", [128, 1792], F32) for _ in range(2)]
            X3 = T("X3", [128, 256], F32); T12 = T("T12", [128, 2, 128], F32)
            o = [T("rwo", [128, 8, 512], F32) for _ in range(2)]
            tA = T("tA", [128, 512], F32); tB = T("tB", [128, 512], F32); av = T("av", [128, 512], F32)
            sm = T("sm", [128, 32], F32)
            ptr = [P("ptr", [128, 128], F32) for _ in range(2)]
            pL = [P("pL", [128, 512], F32) for _ in range(3)]
            r_c = Res(); r_pd = [Res(), Res()]; r_pv = [Res(), Res()]; r_xm = [Res(), Res()]; r_X3 = Res(); r_T12 = Res()
            r_o = [Res(), Res()]; r_t = Res(); r_ptr = [Res(), Res()]; r_pL = [Res() for _ in range(3)]
            bc = lambda ap, n: ap.to_broadcast([128, n])
            s.dma("sp", lambda e: e.dma_start(out=MU[:], in_=bc(di["rwkv_mu"][0:1, :], 1792)), writes=[r_c])
            for j, nm in enumerate(("rwkv_w0", "rwkv_a0", "rwkv_k_k", "rwkv_k_a", "rwkv_r_k")):
                s.dma("sp", lambda e, j=j, nm=nm: e.dma_start(out=CB[:, j, :], in_=bc(di[nm][0:1, :], 512)), writes=[r_c])
            s.dma("sp", lambda e: e.dma_start(out=W12[0:64, :], in_=di["rwkv_w2"][:, :]), writes=[r_c])
            s.dma("sp", lambda e: e.dma_start(out=W12[64:128, :], in_=di["rwkv_a2"][:, :]), writes=[r_c])
            s.dma("sp", lambda e: e.dma_start(out=G2[:], in_=di["rwkv_g2"][:, :]), writes=[r_c])
            for b_ in pv:
                s.op("pool", lambda e, b_=b_: e.memset(b_[:], 0.0), writes=r_pv)
            for i in range(NTILE):
                k = i % 2
                sample = (i == 32)
                rows = slice(i * 128, (i + 1) * 128)
                s.dma("sp", lambda e, k=k, rows=rows: e.dma_start(out=pd[k][:], in_=self.pdbuf[rows, :]), reads=[self.pdres[i]], writes=[r_pd[k]])
                if i == 0:
                    s.dma("sp", lambda e, k=k: e.dma_start(out=pv[k][1:128, :], in_=self.pdbuf[0:127, :]), reads=[self.pdres[0]], writes=[r_pv[k]])
                elif not sample:
                    s.dma("sp", lambda e, k=k, i=i: e.dma_start(out=pv[k][:, :], in_=self.pdbuf[i * 128 - 1:i * 128 + 127, :]),
                          reads=[self.pdres[i], self.pdres[i - 1]], writes=[r_pv[k]])
                else:
                    s.dma("sp", lambda e, k=k: e.dma_start(out=pv[k][1:64, :], in_=self.pdbuf[SEQ:SEQ + 63, :]), reads=[self.pdres[32]], writes=[r_pv[k]])
                    for b in range(NSB):
                        s.dma("sp", lambda e, k=k, b=b: e.dma_start(out=pv[k][4 * b:4 * b + 1, :], in_=di["state_d_shift"][b:b + 1, :]), writes=[r_pv[k]])
                X, PD, PV_ = xm[k], pd[k], pv[k]
                s.op("pool", lambda e, X=X, PD=PD, PV_=PV_: e.tensor_tensor(out=X[:], in0=PV_[:], in1=PD[:], op=ALU.subtract), reads=[r_pd[k], r_pv[k]], writes=[r_xm[k]])
                s.op("dve", lambda e, X=X: e.tensor_tensor(out=X[:], in0=X[:], in1=MU[:], op=ALU.mult), reads=[r_xm[k], r_c], writes=[r_xm[k]])
                s.op("pool", lambda e, X=X, PD=PD: e.tensor_tensor(out=X[:], in0=X[:], in1=PD[:], op=ALU.add), reads=[r_xm[k], r_pd[k]], writes=[r_xm[k]])
                s.op("act", lambda e, X=X: e.activation(out=X3[:, 0:64], in_=X[:, 1536:1600], func=AF.Tanh), reads=[r_xm[k]], writes=[r_X3])
                s.op("act", lambda e, X=X: e.copy(out=X3[:, 64:128], in_=X[:, 1600:1664]), reads=[r_xm[k]], writes=[r_X3])
                s.op("act", lambda e, X=X: e.activation(out=X3[:, 128:256], in_=X[:, 1664:1792], func=AF.Sigmoid), reads=[r_xm[k]], writes=[r_X3])
                for j in range(2):
                    s.op("pe", lambda e, j=j: e.transpose(out=ptr[j][:], in_=X3[:, j * 128:(j + 1) * 128], identity=self.ident_f[:]),
                         reads=[r_X3, self.rconst], writes=[r_ptr[j]])
                    s.op("dve", lambda e, j=j: e.tensor_copy(out=T12[:, j, :], in_=ptr[j][:]), reads=[r_ptr[j]], writes=[r_T12])
                s.op("pe", lambda e: e.matmul(pL[0][:], lhsT=T12[0:64, 0, :], rhs=W12[0:64, :], start=True, stop=True), reads=[r_T12, r_c], writes=[r_pL[0]])
                s.op("pe", lambda e: e.matmul(pL[1][:], lhsT=T12[64:128, 0, :], rhs=W12[64:128, :], start=True, stop=True), reads=[r_T12, r_c], writes=[r_pL[1]])
                s.op("pe", lambda e: e.matmul(pL[2][:], lhsT=T12[:, 1, :], rhs=G2[:, :], start=True, stop=True), reads=[r_T12, r_c], writes=[r_pL[2]])
                O = o[k]
                ro = [r_o[k]]
                rt = [r_t]
                s.op("dve", lambda e: e.tensor_tensor(out=tA[:], in0=pL[0][:], in1=CB[:, 0, :], op=ALU.add), reads=[r_pL[0], r_c], writes=rt)
                s.op("act", lambda e: e.activation(out=tA[:], in_=tA[:], func=AF.Sigmoid), reads=rt, writes=rt)
                s.op("act", lambda e, O=O: e.activation(out=O[:, 1, :], in_=tA[:], func=AF.Exp, scale=-0.6065306597126334), reads=rt, writes=ro)
                s.op("dve", lambda e: e.tensor_tensor(out=av[:], in0=pL[1][:], in1=CB[:, 1, :], op=ALU.add), reads=[r_pL[1], r_c], writes=rt)
                s.op("act", lambda e: e.activation(out=av[:], in_=av[:], func=AF.Sigmoid), reads=rt, writes=rt)
                s.op("act", lambda e, O=O: e.copy(out=O[:, 6, :], in_=pL[2][:]), reads=[r_pL[2]], writes=ro)
                s.op("act", lambda e, O=O, X=X: e.copy(out=O[:, 0, :], in_=X[:, 0:512]), reads=[r_xm[k]], writes=ro)
                s.op("act", lambda e, O=O, X=X: e.copy(out=O[:, 3, :], in_=X[:, 1024:1536]), reads=[r_xm[k]], writes=ro)
                s.op("pool", lambda e, X=X: e.tensor_tensor(out=tA[:], in0=X[:, 512:1024], in1=CB[:, 2, :], op=ALU.mult), reads=[r_xm[k], r_c] + rt, writes=rt)
                s.op("pool", lambda e: e.tensor_tensor(out=tB[:], in0=tA[:], in1=tA[:], op=ALU.mult), reads=rt, writes=rt)
                s.op("dve", lambda e: e.tensor_reduce(out=sm[:, 0:8], in_=tB[:].rearrange("p (h k) -> p h k", h=8), axis=AX.X, op=ALU.add), reads=rt, writes=rt)
                s.op("act", lambda e: e.activation(out=sm[:, 8:16], in_=sm[:, 0:8], func=AF.Sqrt), reads=rt, writes=rt)
                s.op("dve", lambda e: e.tensor_scalar(out=sm[:, 8:16], in0=sm[:, 8:16], scalar1=1e-12, scalar2=None, op0=ALU.max), reads=rt, writes=rt)
                s.op("dve", lambda e: e.reciprocal(out=sm[:, 16:24], in_=sm[:, 8:16]), reads=rt, writes=rt)
                s.op("dve", lambda e, O=O: e.tensor_tensor(out=O[:, 4, :].rearrange("p (h k) -> p h k", h=8), in0=tA[:].rearrange("p (h k) -> p h k", h=8),
                                                           in1=sm[:, 16:24].unsqueeze(2).to_broadcast([128, 8, 64]), op=ALU.mult), reads=rt, writes=ro)
                s.op("dve", lambda e: e.scalar_tensor_tensor(out=tB[:], in0=av[:], scalar=-1.0, in1=CB[:, 3, :], op0=ALU.add, op1=ALU.mult), reads=rt + [r_c], writes=rt)
                s.op("dve", lambda e, O=O, X=X: e.scalar_tensor_tensor(out=O[:, 2, :], in0=tB[:], scalar=1.0, in1=X[:, 512:1024], op0=ALU.add, op1=ALU.mult),
                     reads=rt + [r_xm[k]], writes=ro)
                s.op("dve", lambda e, O=O: e.scalar_tensor_tensor(out=O[:, 5, :], in0=O[:, 4, :], scalar=-1.0, in1=av[:], op0=ALU.mult, op1=ALU.mult), reads=rt + ro, writes=ro)
                s.op("pool", lambda e, O=O: e.tensor_tensor(out=tA[:], in0=O[:, 0, :], in1=O[:, 2, :], op=ALU.mult), reads=ro + rt, writes=rt)
                s.op("pool", lambda e: e.tensor_tensor(out=tA[:], in0=tA[:], in1=CB[:, 4, :], op=ALU.mult), reads=rt + [r_c], writes=rt)
                s.op("dve", lambda e: e.tensor_reduce(out=sm[:, 24:32], in_=tA[:].rearrange("p (h k) -> p h k", h=8), axis=AX.X, op=ALU.add), reads=rt, writes=rt)
                s.op("dve", lambda e, O=O: e.tensor_tensor(out=O[:, 7, :].rearrange("p (h k) -> p h k", h=8), in0=O[:, 3, :].rearrange("p (h k) -> p h k", h=8),
                                                           in1=sm[:, 24:32].unsqueeze(2).to_broadcast([128, 8, 64]), op=ALU.mult), reads=rt + ro, writes=ro)
                s.dma("pool", lambda e, O=O, rows=rows: e.dma_start(out=self.rw[:, rows, :].rearrange("a t c -> t a c"), in_=O[:]),
                      reads=ro, writes=[self.rwres[i]])
            s.barrier()

    def mixer1_R(self):
        nc, s, di, do = self.nc, self.s, self.din, self.dout
        with ExitStack() as es:
            T = lambda n, sh, dt: es.enter_context(nc.sbuf_tensor(self.name(n), sh, dt))
            P = lambda n, sh, dt: es.enter_context(nc.psum_tensor(self.name(n), sh, dt))
            bm = T("bm8", [8, 8, 64], F32)
            A = T("Ast", [64, 8, 64], F32)
            tok = [T("tokm", [128, 3, 512], F32) for _ in range(2)]
            cols = [T("cols", [64, 3, 128, 8], F32) for _ in range(2)]
            hm1 = T("hm", [8, 3, 128, 64], F32)
            hm = [hm1, hm1]
            Um = [T("Um", [8, 512], F32) for _ in range(2)]
            Vm = [T("Vm", [8, 512], F32) for _ in range(2)]
            Ym = [T("Ym", [8, 8, 64], F32) for _ in range(2)]
            yb1 = T("yb", [8, 128, 64], F32)
            yb = [yb1, yb1]
            Sio = T("Sio", [64, 8, 64], F32)
            ptr = [P("ptr", [64, 4, 128], F32) for _ in range(2)]
            pu = [P("pu", [8, 512], F32) for _ in range(2)]
            pdA = [P("pdA", [64, 512], F32) for _ in range(2)]
            py = [P("py", [8, 512], F32) for _ in range(2)]
            r_c = Res(); r_A = Res(); r_tok = [Res(), Res()]; r_cols = [Res(), Res()]; r_hm1 = Res(); r_hm = [r_hm1, r_hm1]
            r_Um = [Res(), Res()]; r_Vm = [Res(), Res()]; r_Ym = [Res(), Res()]; r_yb1 = Res(); r_yb = [r_yb1, r_yb1]; r_S = Res()
            r_ptr = [Res(), Res()]; r_pu = [Res(), Res()]; r_pdA = [Res(), Res()]; r_py = [Res(), Res()]
            s.op("pool", lambda e: e.memset(bm[:], 1.0), writes=[r_c])
            s.op("pool", lambda e: e.affine_select(out=bm[:], in_=bm[:], pattern=[[-1, 8], [0, 64]], compare_op=ALU.is_equal, fill=0.0,
                                                    base=0, channel_multiplier=1), reads=[r_c], writes=[r_c])
            s.op("pool", lambda e: e.memset(A[:], 0.0), writes=[r_A])
            Af = A[:].rearrange("p h v -> p (h v)")
            bmf = bm[:].rearrange("p h v -> p (h v)")
            cnt = [0]

            def load_tile(i, k):
                rows = slice(i * 128, (i + 1) * 128)
                for j, src in enumerate((4, 0, 1)):
                    s.dma("sp", lambda e, j=j, src=src: e.dma_start(out=tok[k][:, j, :], in_=self.rw[src, rows, :]), reads=[self.rwres[i]], writes=[r_tok[k]])
                for j, src in enumerate((5, 2, 3)):
                    s.dma("sp", lambda e, j=j, src=src: e.dma_start(out=hm[k][:, j, :, :], in_=self.rw[src, rows, :].rearrange("t (h c) -> h t c", h=8)),
                          reads=[self.rwres[i]], writes=[r_hm[k]])
                it = 0
                for j in range(3):
                    for h0 in (0, 4):
                        kp = it % 2; it += 1
                        for hh in range(4):
                            h = h0 + hh
                            s.op("pe", lambda e, j=j, h=h, hh=hh, kp=kp: e.transpose(out=ptr[kp][:, hh, :], in_=tok[k][:, j, h * 64:(h + 1) * 64], identity=self.ident_f[:]),
                                 reads=[r_tok[k], self.rconst], writes=[r_ptr[kp]])
                        s.op("act", lambda e, j=j, h0=h0, kp=kp: e.copy(out=cols[k][:, j, :, h0:h0 + 4], in_=ptr[kp][:].rearrange("p h t -> p t h")),
                             reads=[r_ptr[kp]], writes=[r_cols[k]])

            def step(k, t):
                c = cnt[0] % 2
                cnt[0] += 1
                s.op("pe", lambda e: e.matmul(pu[c][:], lhsT=cols[k][:, 0, t, :], rhs=Af, start=True, stop=True), reads=[r_cols[k], r_A], writes=[r_pu[c]])
                s.op("dve", lambda e: e.tensor_tensor(out=Um[c][:], in0=pu[c][:], in1=bmf, op=ALU.mult), reads=[r_pu[c], r_c], writes=[r_Um[c]])
                s.op("pool", lambda e: e.tensor_tensor(out=Vm[c][:].rearrange("p (h v) -> p h v", h=8), in0=hm[k][:, 2, t, :].unsqueeze(1).to_broadcast([8, 8, 64]),
                                                       in1=bm[:], op=ALU.mult), reads=[r_hm[k], r_c], writes=[r_Vm[c]])
                s.op("pe", lambda e: e.matmul(pdA[c][:], lhsT=hm[k][:, 0, t, :], rhs=Um[c][:], start=True, stop=False), reads=[r_hm[k], r_Um[c]], writes=[r_pdA[c]])
                s.op("pe", lambda e: e.matmul(pdA[c][:], lhsT=hm[k][:, 1, t, :], rhs=Vm[c][:], start=False, stop=True), reads=[r_hm[k], r_Vm[c]], writes=[r_pdA[c]])
                s.op("dve", lambda e: e.tensor_tensor(out=A[:], in0=A[:], in1=cols[k][:, 2, t, :].unsqueeze(2).to_broadcast([64, 8, 64]), op=ALU.mult),
                     reads=[r_A, r_cols[k]], writes=[r_A])
                s.op("dve", lambda e: e.tensor_tensor(out=Af, in0=Af, in1=pdA[c][:], op=ALU.add), reads=[r_A, r_pdA[c]], writes=[r_A])
                s.op("pe", lambda e: e.matmul(py[c][:], lhsT=cols[k][:, 1, t, :], rhs=Af, start=True, stop=True), reads=[r_cols[k], r_A], writes=[r_py[c]])
                s.op("dve", lambda e: e.tensor_tensor(out=Ym[c][:].rearrange("p h v -> p (h v)"), in0=py[c][:], in1=bmf, op=ALU.mult), reads=[r_py[c], r_c], writes=[r_Ym[c]])
                s.op("dve", lambda e: e.tensor_reduce(out=yb[k][:, t, :], in_=Ym[c][:].rearrange("p h v -> p v h"), axis=AX.X, op=ALU.add),
                     reads=[r_Ym[c]], writes=[r_yb[k]])

            def store_y(i, k, nt):
                rows = slice(i * 128, i * 128 + nt)
                s.dma("pool", lambda e: e.dma_start(out=self.ybuf[rows, :].rearrange("t (h v) -> h t v", h=8), in_=yb[k][:, 0:nt, :]),
                      reads=[r_yb[k]], writes=[self.yres[i]])

            def state_out(dst_ap):
                for h0 in (0, 4):
                    kp = (h0 // 4)
                    for hh in range(4):
                        s.op("pe", lambda e, hh=hh, h0=h0, kp=kp: e.transpose(out=ptr[kp][:, hh, 0:64], in_=A[:, h0 + hh, :], identity=self.ident_f[0:64, 0:64]),
                             reads=[r_A, self.rconst], writes=[r_ptr[kp]])
                    s.op("act", lambda e, h0=h0, kp=kp: e.copy(out=Sio[:, h0:h0 + 4, :], in_=ptr[kp][:, :, 0:64]), reads=[r_ptr[kp]], writes=[r_S])
                s.dma("pool", lambda e: e.dma_start(out=dst_ap.rearrange("(h v) c -> v h c", h=8), in_=Sio[:]), reads=[r_S])

            ntp = self.cfg.get("rw_tiles", 32)
            for i in range(ntp):
                k = i % 2
                load_tile(i, k)
                for t in range(128):
                    step(k, t)
                store_y(i, k, 128)
            state_out(do["dwp"][:, :])
            k = 0
            load_tile(32, k)
            for b in range(NSB):
                s.dma("sp", lambda e, b=b: e.dma_start(out=Sio[:], in_=di["state_d_wkv"][b * 512:(b + 1) * 512, :].rearrange("(h v) c -> v h c", h=8)),
                      reads=[r_S], writes=[r_S])
                for h0 in (0, 4):
                    kp = (h0 // 4)
                    for hh in range(4):
                        s.op("pe", lambda e, hh=hh, h0=h0, kp=kp: e.transpose(out=ptr[kp][:, hh, 0:64], in_=Sio[:, h0 + hh, :], identity=self.ident_f[0:64, 0:64]),
                             reads=[r_S, self.rconst], writes=[r_ptr[kp]])
                    s.op("act", lambda e, h0=h0, kp=kp: e.copy(out=A[:, h0:h0 + 4, :], in_=ptr[kp][:, :, 0:64]), reads=[r_ptr[kp]], writes=[r_A])
                for j in range(4):
                    step(k, 4 * b + j)
                state_out(do["dws"][b * 512:(b + 1) * 512, :])
            store_y(32, k, 64)
            s.barrier()

    def mixer1_Q(self, cur, nxt):
        nc, s, di, do = self.nc, self.s, self.din, self.dout
        GN_EPS = 64e-5
        with ExitStack() as es:
            T = lambda n, sh, dt: es.enter_context(nc.sbuf_tensor(self.name(n), sh, dt))
            P = lambda n, sh, dt: es.enter_context(nc.psum_tensor(self.name(n), sh, dt))
            m = self.alloc_mod(es)
            owout = T("owout", [128, NKC, D], BF16)
            GN = T("GNc", [128, 2, 512], F32)
            gne = T("gne", [128, 1], F32)
            xt = [T("xt", [128, D], F32) for _ in range(2)]
            yv = [T("yv", [128, 3, 512], F32) for _ in range(2)]
            ft = [T("ft", [128, 2, 512], BF16) for _ in range(2)]
            fT = [T("fT", [128, 8, 128], BF16) for _ in range(2)]
            ta = T("ta", [128, 512], F32); tb = T("tb", [128, 512], F32); sm = T("smq", [128, 32], F32)
            t1 = [T("t1", [128, D], F32) for _ in range(2)]
            stt = [T("st", [128, 16], F32) for _ in range(2)]
            tpb = P("tpb", [128, 8, 128], BF16)
            po = [P("po", [128, 512], F32) for _ in range(2)]
            pmisc = P("pmisc", [128, 512], F32)
            r_c = Res(); r_x = [Res(), Res()]; r_yv = [Res(), Res()]; r_ft = [Res(), Res()]; r_fT = [Res(), Res()]; r_t = Res()
            r_t1 = [Res(), Res()]; r_st = [Res(), Res()]; r_tp = Res(); r_po = [Res(), Res()]; r_pm = Res(); r_ow = Res()
            s.dma("sp", lambda e: e.dma_start(out=owout[:], in_=self.owout_b[:, :].rearrange("(kc p) n -> p kc n", p=128)), writes=[r_ow])
            s.dma("sp", lambda e: e.dma_start(out=GN[:, 0, :], in_=di["rwkv_gn_w"][0:1, :].to_broadcast([128, 512])), writes=[r_c])
            s.dma("sp", lambda e: e.dma_start(out=GN[:, 1, :], in_=di["rwkv_gn_b"][0:1, :].to_broadcast([128, 512])), writes=[r_c])
            s.op("pool", lambda e: e.memset(gne[:], GN_EPS), writes=[r_c])
            self.load_mod(m, 1, 1, None)
            self.bcast_mod(m, False, pmisc, r_pm)
            rt = [r_t]
            v3 = lambda ap: ap.rearrange("p (h k) -> p h k", h=8)
            for i in range(NTILE):
                k = i % 2
                sample = (i == 32)
                if sample:
                    self.bcast_mod(m, True, pmisc, r_pm)
                rows = slice(i * 128, (i + 1) * 128)
                ap, nrows, res = self.x_src(False, cur, i)
                s.dma("sp", lambda e, k=k, ap=ap: e.dma_start(out=xt[k][:], in_=ap), reads=[res], writes=[r_x[k]])
                s.dma("sp", lambda e, k=k, rows=rows: e.dma_start(out=yv[k][:, 0, :], in_=self.ybuf[rows, :]), reads=[self.yres[i]], writes=[r_yv[k]])
                s.dma("sp", lambda e, k=k, rows=rows: e.dma_start(out=yv[k][:, 1, :], in_=self.rw[6, rows, :]), reads=[self.rwres[i]], writes=[r_yv[k]])
                s.dma("sp", lambda e, k=k, rows=rows: e.dma_start(out=yv[k][:, 2, :], in_=self.rw[7, rows, :]), reads=[self.rwres[i]], writes=[r_yv[k]])
                s.dma("sp", lambda e, k=k, rows=rows: e.dma_start(out=ft[k][:, 0, :], in_=self.mixbuf[rows, 0:512]), reads=[self.mixres[i]], writes=[r_ft[k]])
                Y = yv[k]
                ry = [r_yv[k]]
                s.op("dve", lambda e, Y=Y: e.tensor_reduce(out=sm[:, 0:8], in_=v3(Y[:, 0, :]), axis=AX.X, op=ALU.add), reads=ry + rt, writes=rt)
                s.op("dve", lambda e: e.tensor_scalar(out=sm[:, 0:8], in0=sm[:, 0:8], scalar1=1.0 / 64.0, scalar2=None, op0=ALU.mult), reads=rt, writes=rt)
                s.op("dve", lambda e, Y=Y: e.tensor_tensor(out=v3(ta[:]), in0=v3(Y[:, 0, :]), in1=sm[:, 0:8].unsqueeze(2).to_broadcast([128, 8, 64]), op=ALU.subtract),
                     reads=ry + rt, writes=rt)
                s.op("pool", lambda e: e.tensor_tensor(out=tb[:], in0=ta[:], in1=ta[:], op=ALU.mult), reads=rt, writes=rt)
                s.op("dve", lambda e: e.tensor_reduce(out=sm[:, 8:16], in_=v3(tb[:]), axis=AX.X, op=ALU.add), reads=rt, writes=rt)
                s.op("act", lambda e: e.activation(out=sm[:, 16:24], in_=sm[:, 8:16], func=AF.Sqrt, bias=gne[:, 0:1], scale=1.0 / 64.0), reads=rt + [r_c], writes=rt)
                s.op("dve", lambda e: e.reciprocal(out=sm[:, 24:32], in_=sm[:, 16:24]), reads=rt, writes=rt)
                s.op("dve", lambda e: e.tensor_tensor(out=v3(ta[:]), in0=v3(ta[:]), in1=sm[:, 24:32].unsqueeze(2).to_broadcast([128, 8, 64]), op=ALU.mult), reads=rt, writes=rt)
                s.op("pool", lambda e: e.tensor_tensor(out=ta[:], in0=ta[:], in1=GN[:, 0, :], op=ALU.mult), reads=rt + [r_c], writes=rt)
                s.op("pool", lambda e: e.tensor_tensor(out=ta[:], in0=ta[:], in1=GN[:, 1, :], op=ALU.add), reads=rt + [r_c], writes=rt)
                s.op("dve", lambda e, Y=Y: e.tensor_tensor(out=ta[:], in0=ta[:], in1=Y[:, 2, :], op=ALU.add), reads=rt + ry, writes=rt)
                s.op("dve", lambda e, Y=Y, k=k: e.tensor_tensor(out=ft[k][:, 1, :], in0=ta[:], in1=Y[:, 1, :], op=ALU.mult), reads=rt + ry + [r_ft[k]], writes=[r_ft[k]])
                for c in range(8):
                    s.op("pe", lambda e, k=k, c=c: e.transpose(out=tpb[:, c, :], in_=ft[k][:, c // 4, (c % 4) * 128:(c % 4 + 1) * 128],
                                                               identity=self.ident_b[:]), reads=[r_ft[k], self.rconst], writes=[r_tp])
                s.op("act", lambda e, k=k: e.copy(out=fT[k][:], in_=tpb[:]), reads=[r_tp], writes=[r_fT[k]])
                for h in range(2):
                    for kc in range(8):
                        s.op("pe", lambda e, h=h, kc=kc, k=k: e.matmul(po[h][:], lhsT=fT[k][:, kc, :], rhs=owout[:, kc, h * 512:(h + 1) * 512],
                                                                       start=(kc == 0), stop=(kc == 7)), reads=[r_fT[k], r_ow], writes=[r_po[h]])
                self.epilogue(m, [po[0][:], po[1][:]], r_po, xt[k][:], r_x[k], t1[k], r_t1[k], stt[k], r_st[k],
                              self.x_dst(False, nxt, i), None)
            s.barrier()

    def dump_dbg(self, cur):
        nc, s = self.nc, self.s
        with ExitStack() as es:
            xt = [es.enter_context(nc.sbuf_tensor(self.name("dx"), [128, D], F32)) for _ in range(2)]
            r = [Res(), Res()]
            for tile in range(NTILE):
                k = tile % 2
                s.dma("sp", lambda e, k=k, tile=tile: e.dma_start(out=xt[k][:], in_=self.XS[cur][tile * 128:(tile + 1) * 128, :]),
                      reads=[self.xres[cur][tile]], writes=[r[k]])
                s.dma("pool", lambda e, k=k, tile=tile: e.dma_start(out=self.dout["dbg"][tile * 128:(tile + 1) * 128, :], in_=xt[k][:]),
                      reads=[r[k]])
            s.barrier()

    def build(self):
        cfg = self.cfg
        self.setup_consts()
        self.cast_weights()
        cur = 0
        nsub = cfg.get("nsub", 6)
        isub_g = 0
        for l in range(DEPTH):
            self.adaln(l)
            for isub in range(3):
                if isub_g >= nsub:
                    break
                first = (isub_g == 0)
                last = (isub_g == 5)
                nxt = 1 - cur
                if isub == 1 and l == 0 and not cfg.get("stub0"):
                    parts = cfg.get("parts", "ASB")
                    if "A" in parts:
                        self.mixer0_A(cur)
                    if "S" in parts:
                        self.mixer0_S()
                    if "B" in parts:
                        self.mixer0_B(cur, nxt)
                    else:
                        self.stub_mixer(l, cur, nxt)
                elif isub == 1:
                    if not cfg.get("stub1"):
                        self.mixer1_A(cur)
                        if not cfg.get("nodil"):
                            self.mixer1_S()
                    if cfg.get("stub1") or cfg.get("norwkv"):
                        self.stub_mixer(l, cur, nxt)
                    else:
                        self.mixer1_P()
                        self.mixer1_R()
                        self.mixer1_Q(cur, nxt)
                else:
                    self.ffn_sublayer(l, 0 if isub == 0 else 1, isub, first, last, cur, nxt)
                cur = nxt
                isub_g += 1
        if cfg.get("dbg"):
            self.dump_dbg(cur)
        self.s.barrier()
        self.s.emit()
        return self.nc


_OUT_SHAPES = None


def make_in_maps(inputs, cfg):
    maps = []
    f32 = lambda a: np.ascontiguousarray(np.asarray(a, dtype=np.float32))
    shared = {
        "ada_w": f32(inputs["ada_w"]), "ada_b": f32(inputs["ada_b"]),
        "ln_g": f32(inputs["ln_g"]), "ln_b": f32(inputs["ln_b"]),
        "ffn_w_gate": f32(inputs["ffn_w_gate"]).reshape(4 * D, DFF),
        "ffn_w_up": f32(inputs["ffn_w_up"]).reshape(4 * D, DFF),
        "ffn_w_down": f32(inputs["ffn_w_down"]).reshape(4 * DFF, D),
        "even_w_in": f32(inputs["even_w_in"]).reshape(D, 2048),
        "even_w_out": f32(inputs["even_w_out"]).reshape(D, D),
        "odd_w_in": f32(inputs["odd_w_in"]).reshape(D, 3328),
        "odd_w_out": f32(inputs["odd_w_out"]).reshape(D, D),
        "diff_lambda": f32(inputs["diff_lambda"]).reshape(1, 256), "diff_subln": f32(inputs["diff_subln"]).reshape(1, 128),
        "s5_a_re": f32(inputs["s5_a_re"]).reshape(32, 64), "s5_a_im": f32(inputs["s5_a_im"]).reshape(32, 64),
        "s5_log_dt": f32(inputs["s5_log_dt"]).reshape(1, 32),
        "s5_b_re": f32(inputs["s5_b_re"]).reshape(32, 64, 16), "s5_b_im": f32(inputs["s5_b_im"]).reshape(32, 64, 16),
        "s5_c_re": f32(inputs["s5_c_re"]).reshape(32, 16, 64), "s5_c_im": f32(inputs["s5_c_im"]).reshape(32, 16, 64),
        "s5_d": f32(inputs["s5_d"]).reshape(1, 512), "s5_glu_w": f32(inputs["s5_glu_w"]).reshape(512, 512),
        "s5_glu_b": f32(inputs["s5_glu_b"]).reshape(1, 512),
        "cache_a_k": f32(inputs["cache_a_k"]).reshape(2560 * 128, 512), "cache_a_v": f32(inputs["cache_a_v"]).reshape(2560 * 128, 512),
    }
    cck = f32(inputs["cache_c_k"]).reshape(128, 2048, 512); ccv = f32(inputs["cache_c_v"]).reshape(128, 2048, 512)
    for nm in ("rwkv_mu", "rwkv_w0", "rwkv_a0", "rwkv_k_k", "rwkv_k_a", "rwkv_r_k", "rwkv_gn_w", "rwkv_gn_b"):
        shared[nm] = f32(inputs[nm]).reshape(1, -1)
    shared["rwkv_w2"] = f32(inputs["rwkv_w2"]).reshape(64, 512); shared["rwkv_a2"] = f32(inputs["rwkv_a2"]).reshape(64, 512)
    shared["rwkv_g2"] = f32(inputs["rwkv_g2"]).reshape(128, 512)
    dwkv = f32(inputs["state_d_wkv"]).reshape(128, 512, 64); dsh = f32(inputs["state_d_shift"]).reshape(128, 1792)
    s5r = f32(inputs["state_s5_re"]).reshape(128, 2048); s5i = f32(inputs["state_s5_im"]).reshape(128, 2048)
    ptab = np.ascontiguousarray(np.asarray(inputs["page_table"], dtype=np.int32))
    xp = f32(inputs["x_prompt"]); xs = f32(inputs["x_sample"])
    cp = f32(inputs["c_prompt"]); cs = f32(inputs["c_sample"])
    if "S" not in cfg.get("parts", "ASB"):
        shared["cache_a_k"] = shared["cache_a_k"][:128]
        shared["cache_a_v"] = shared["cache_a_v"][:128]
    for c in range(N_CORES):
        b = c % 4
        mp = dict(shared)
        mp["xp"] = xp[b]
        mp["xs"] = xs[c * NSB:(c + 1) * NSB].reshape(NS, D)
        mp["c17"] = np.ascontiguousarray(np.concatenate([cp[b:b + 1], cs[c * NSB:(c + 1) * NSB]], axis=0))
        mp["state_s5_re"] = s5r[c * NSB:(c + 1) * NSB]; mp["state_s5_im"] = s5i[c * NSB:(c + 1) * NSB]
        mp["page_table"] = ptab[c * NSB:(c + 1) * NSB].reshape(1, NSB * 16)
        mp["state_d_wkv"] = dwkv[c * NSB:(c + 1) * NSB].reshape(NSB * 512, 64)
        mp["state_d_shift"] = dsh[c * NSB:(c + 1) * NSB]
        mp["cache_c_k"] = cck[c * NSB:(c + 1) * NSB].reshape(NSB * 2048, 512)
        mp["cache_c_v"] = ccv[c * NSB:(c + 1) * NSB].reshape(NSB * 2048, 512)
        maps.append(mp)
    return maps


CFG = {}


def kernel(**inputs):
    cfg = dict(CFG)
    nc = Builder(cfg).build()
    maps = make_in_maps(inputs, cfg)
    res = run_bass_kernel_spmd(nc, maps, core_ids=list(range(N_CORES)))
    R = res.results
    yp = np.stack([R[b]["yp"] for b in range(4)], 0)
    ys = np.concatenate([R[c]["ys"].reshape(NSB, 4, D) for c in range(N_CORES)], 0)
    E, O, B, DB = 1, 1, 4, 128
    z = lambda *sh: np.zeros(sh, np.float32)
    catp = lambda k, sh: np.stack([R[b][k] for b in range(4)], 0).reshape(sh)
    cats = lambda k, sh: np.concatenate([R[c][k] for c in range(N_CORES)], 0).reshape(sh)
    return (yp, ys,
            catp("akp", (E, B, SEQ, 4, 2, 64)), cats("aks", (E, DB, 4, 4, 2, 64)),
            catp("avp", (E, B, SEQ, 4, 128)), cats("avs", (E, DB, 4, 4, 128)),
            catp("s5rp", (E, B, 32, 64)), cats("s5rs", (E, DB, 32, 64)), catp("s5ip", (E, B, 32, 64)), cats("s5is", (E, DB, 32, 64)),
            catp("ckp", (O, B, 2048, 8, 64)), cats("cks", (O, DB, 4, 8, 64)), catp("cvp", (O, B, 2048, 8, 64)), cats("cvs", (O, DB, 4, 8, 64)),
            catp("dwp", (O, B, 8, 64, 64)), cats("dws", (O, DB, 8, 64, 64)), catp("dsp", (O, B, 1792)), cats("dss", (O, DB, 1792)))
```

```python
from contextlib import ExitStack
import numpy as np
import concourse.bass as bass
import concourse.mybir as mybir
from concourse.bass_utils import run_bass_kernel_spmd

F32 = mybir.dt.float32
BF16 = mybir.dt.bfloat16
I32 = mybir.dt.int32
ALU = mybir.AluOpType
AF = mybir.ActivationFunctionType
AX = mybir.AxisListType

D = 1024
DFF = 2816
NKC = 8
NFC = 22
SEQ = 4096
NS = 64
NSB = 16
NTILE = 33
NROW = NTILE * 128
DEPTH = 2
ALPHA = (2.0 * DEPTH) ** 0.25
LN_EPS = 1e-5
N_CORES = 8


class Res:
    __slots__ = ("w", "r")

    def __init__(self):
        self.w = None
        self.r = []


class EngQ:
    def __init__(self, name, eng, sem):
        self.name = name
        self.eng = eng
        self.sem = sem
        self.count = 0
        self.ops = []
        self.waited = {}
        self.dma_slots = []
        self.dma_i = 0


class Sched:
    def __init__(self, nc, ndma_slots=8):
        self.nc = nc
        self.q = {}
        for name, eng in (("pe", nc.tensor), ("act", nc.scalar), ("dve", nc.vector),
                          ("pool", nc.gpsimd), ("sp", nc.sync)):
            q = EngQ(name, eng, nc.alloc_semaphore(name="s_" + name))
            for i in range(ndma_slots):
                q.dma_slots.append([nc.alloc_semaphore(name=f"d_{name}{i}"), 0])
            self.q[name] = q

    def _wait(self, q, tok):
        sem, val = tok
        key = id(sem)
        if q.waited.get(key, 0) >= val:
            return
        q.waited[key] = val
        q.ops.append(("wait", sem, val))

    def _deps(self, q, reads, writes, skip_same):
        deps = []
        for r in reads:
            if r.w is not None:
                deps.append(r.w)
        for w in writes:
            if w.w is not None:
                deps.append(w.w)
            deps.extend(w.r)
        for tok in deps:
            if skip_same and tok[0] is q.sem:
                continue
            self._wait(q, tok)

    @staticmethod
    def _commit(tok, reads, writes):
        for r in reads:
            r.r.append(tok)
            if len(r.r) > 64:
                best = {}
                for t in r.r:
                    k = id(t[0])
                    if k not in best or best[k][1] < t[1]:
                        best[k] = t
                r.r = list(best.values())
        for w in writes:
            w.w = tok
            w.r = []

    def op(self, qn, fn, reads=(), writes=()):
        q = self.q[qn]
        self._deps(q, reads, writes, skip_same=(qn == "pe"))
        q.count += 1
        tok = (q.sem, q.count)
        q.ops.append(("op", fn, q.sem, 1))
        self._commit(tok, reads, writes)
        return tok

    def dma(self, qn, fn, reads=(), writes=()):
        q = self.q[qn]
        self._deps(q, reads, writes, skip_same=False)
        slot = q.dma_slots[q.dma_i % len(q.dma_slots)]
        q.dma_i += 1
        if slot[1] > 0:
            self._wait(q, (slot[0], slot[1]))
        slot[1] += 16
        tok = (slot[0], slot[1])
        q.ops.append(("op", fn, slot[0], 16))
        self._commit(tok, reads, writes)
        return tok

    def barrier(self):
        toks = []
        for q in self.q.values():
            if q.count:
                toks.append((q.sem, q.count))
            for sl in q.dma_slots:
                if sl[1]:
                    toks.append((sl[0], sl[1]))
        for q in self.q.values():
            for t in toks:
                if t[0] is q.sem:
                    continue
                self._wait(q, t)

    def emit(self):
        nc = self.nc
        with nc.Block() as block:
            def mk(q):
                def body(eng):
                    for o in q.ops:
                        if o[0] == "wait":
                            eng.wait_ge(o[1], o[2])
                        else:
                            o[1](eng).then_inc(o[2], o[3])
                return body
            block.tensor(mk(self.q["pe"]))
            block.scalar(mk(self.q["act"]))
            block.vector(mk(self.q["dve"]))
            block.gpsimd(mk(self.q["pool"]))
            block.sync(mk(self.q["sp"]))


class Builder:
    def __init__(self, cfg):
        self.cfg = cfg
        nc = self.nc = bass.Bass("TRN2", target_bir_lowering=False)
        self.s = Sched(nc)
        self.uid = 0
        di = self.din = {}
        do = self.dout = {}

        def inp(name, shape, dt=F32):
            di[name] = nc.dram_tensor(name, list(shape), dt, kind="ExternalInput").ap()

        def outp(name, shape, dt=F32):
            do[name] = nc.dram_tensor(name, list(shape), dt, kind="ExternalOutput").ap()

        inp("xp", [SEQ, D]); inp("xs", [NS, D]); inp("c17", [17, D])
        inp("ada_w", [DEPTH, D, 9 * D]); inp("ada_b", [DEPTH, 9 * D])
        inp("ln_g", [DEPTH, 3, D]); inp("ln_b", [DEPTH, 3, D])
        inp("ffn_w_gate", [4 * D, DFF]); inp("ffn_w_up", [4 * D, DFF]); inp("ffn_w_down", [4 * DFF, D])
        inp("even_w_in", [D, 2048]); inp("even_w_out", [D, D])
        inp("odd_w_in", [D, 3328]); inp("odd_w_out", [D, D])
        inp("diff_lambda", [1, 256]); inp("diff_subln", [1, 128])
        inp("s5_a_re", [32, 64]); inp("s5_a_im", [32, 64]); inp("s5_log_dt", [1, 32])
        inp("s5_b_re", [32, 64, 16]); inp("s5_b_im", [32, 64, 16]); inp("s5_c_re", [32, 16, 64]); inp("s5_c_im", [32, 16, 64])
        inp("s5_d", [1, 512]); inp("s5_glu_w", [512, 512]); inp("s5_glu_b", [1, 512])
        inp("state_s5_re", [NSB, 2048]); inp("state_s5_im", [NSB, 2048])
        inp("page_table", [1, NSB * 16], I32)
        inp("cache_c_k", [NSB * 2048, 512]); inp("cache_c_v", [NSB * 2048, 512])
        inp("state_d_wkv", [NSB * 512, 64]); inp("state_d_shift", [NSB, 1792])
        inp("rwkv_mu", [1, 1792]); inp("rwkv_w0", [1, 512]); inp("rwkv_a0", [1, 512]); inp("rwkv_k_k", [1, 512]); inp("rwkv_k_a", [1, 512])
        inp("rwkv_r_k", [1, 512]); inp("rwkv_gn_w", [1, 512]); inp("rwkv_gn_b", [1, 512])
        inp("rwkv_w2", [64, 512]); inp("rwkv_a2", [64, 512]); inp("rwkv_g2", [128, 512])
        npool = 2560 * 128 if "S" in cfg.get("parts", "ASB") else 128
        inp("cache_a_k", [npool, 512]); inp("cache_a_v", [npool, 512])
        outp("yp", [SEQ, D]); outp("ys", [NS, D])
        outp("akp", [SEQ, 512]); outp("aks", [NS, 512]); outp("avp", [SEQ, 512]); outp("avs", [NS, 512])
        outp("ckp", [2048, 512]); outp("cks", [NS, 512]); outp("cvp", [2048, 512]); outp("cvs", [NS, 512])
        outp("dsp", [1, 1792]); outp("dss", [NSB, 1792])
        outp("dwp", [512, 64]); outp("dws", [NSB * 512, 64])
        outp("s5rp", [1, 2048]); outp("s5ip", [1, 2048]); outp("s5rs", [NSB, 2048]); outp("s5is", [NSB, 2048])
        if cfg.get("dbg"):
            outp("dbg", [NROW, D])
        self.wg_b = nc.dram_tensor("wg_b", [4 * D, DFF], BF16).ap()
        self.wu_b = nc.dram_tensor("wu_b", [4 * D, DFF], BF16).ap()
        self.wd_b = nc.dram_tensor("wd_b", [4 * DFF, D], BF16).ap()
        self.ewin_b = nc.dram_tensor("ewin_b", [D, 2048], BF16).ap()
        self.owin_b = nc.dram_tensor("owin_b", [D, 3328], BF16).ap()
        self.ewout_b = nc.dram_tensor("ewout_b", [D, D], BF16).ap()
        self.owout_b = nc.dram_tensor("owout_b", [D, D], BF16).ap()
        self.mod17 = nc.dram_tensor("mod17", [17, 9 * D], F32).ap()
        self.XS = [nc.dram_tensor(f"xscr{i}", [NROW, D], F32).ap() for i in range(2)]
        self.xres = [[Res() for _ in range(NTILE)] for _ in range(2)]
        self.mixbuf = nc.dram_tensor("mixbuf", [NROW, D], BF16).ap()
        self.mixres = [Res() for _ in range(NTILE)]
        self.ubuf = nc.dram_tensor("ubuf", [NROW, 512], BF16).ap()
        self.ures = [Res() for _ in range(NTILE)]
        self.qsbuf = nc.dram_tensor("qsbuf", [NS, 512], BF16).ap()
        self.r_aks = Res()
        self.r_cks = Res()
        self.rw = nc.dram_tensor("rwscr", [8, NROW, 512], F32).ap()
        self.rwres = [Res() for _ in range(NTILE)]
        self.ybuf = nc.dram_tensor("ybuf", [NROW, 512], F32).ap()
        self.yres = [Res() for _ in range(NTILE)]
        self.pdbuf = nc.dram_tensor("pdbuf", [NROW, 1792], F32).ap()
        self.pdres = [Res() for _ in range(NTILE)]

    def name(self, p):
        self.uid += 1
        return f"{p}{self.uid}"

    def setup_consts(self):
        nc, s = self.nc, self.s
        A = nc.alloc_sbuf_tensor
        self.ident_b = A("ident_b", [128, 128], BF16)
        self.ident_f = A("ident_f", [128, 128], F32)
        self.selP = A("selP", [17, 128], F32)
        self.selS = A("selS", [17, 128], F32)
        self.ones1 = A("ones1", [1, 128], F32)
        self.eps_t = A("eps_t", [128, 1], F32)
        self.rconst = Res()
        rc = [self.rconst]
        for idt in (self.ident_b, self.ident_f):
            s.op("pool", lambda e, t=idt: e.memset(t[:], 0.0), writes=rc)
            s.op("pool", lambda e, t=idt: e.affine_select(out=t[:], in_=t[:], pattern=[[-1, 128]],
                                                            compare_op=ALU.not_equal, fill=1.0, base=0,
                                                            channel_multiplier=1), reads=rc, writes=rc)
        s.op("pool", lambda e: e.memset(self.selP[:], 0.0), writes=rc)
        s.op("pool", lambda e: e.memset(self.selP[0:1, :], 1.0), writes=rc)
        s.op("pool", lambda e: e.memset(self.selS[:], 1.0), writes=rc)
        s.op("pool", lambda e: e.affine_select(out=self.selS[:], in_=self.selS[:], pattern=[[1, 128]],
                                                compare_op=ALU.is_ge, fill=0.0, base=4, channel_multiplier=-4),
             reads=rc, writes=rc)
        s.op("pool", lambda e: e.affine_select(out=self.selS[:], in_=self.selS[:], pattern=[[-1, 128]],
                                                compare_op=ALU.is_ge, fill=0.0, base=-1, channel_multiplier=4),
             reads=rc, writes=rc)
        s.op("pool", lambda e: e.memset(self.ones1[:], 1.0), writes=rc)
        s.op("pool", lambda e: e.memset(self.eps_t[:], LN_EPS), writes=rc)

    def cast_weights(self):
        nc, s, di = self.nc, self.s, self.din
        jobs = [(di["ffn_w_gate"], self.wg_b, 4 * D, DFF), (di["ffn_w_up"], self.wu_b, 4 * D, DFF),
                (di["ffn_w_down"], self.wd_b, 4 * DFF, D), (di["even_w_in"], self.ewin_b, D, 2048),
                (di["even_w_out"], self.ewout_b, D, D), (di["odd_w_in"], self.owin_b, D, 3328),
                (di["odd_w_out"], self.owout_b, D, D)]
        if self.cfg.get("nocast"):
            jobs = []
        with ExitStack() as es:
            NB = 3
            stg = [es.enter_context(nc.sbuf_tensor(f"cs{i}", [128, 2, 3328], F32)) for i in range(NB)]
            stb = [es.enter_context(nc.sbuf_tensor(f"cb{i}", [128, 2, 3328], BF16)) for i in range(NB)]
            rs = [Res() for _ in range(NB)]
            rb = [Res() for _ in range(NB)]
            i = 0
            for src, dst, R, C in jobs:
                nrt = R // 128
                G = 2
                for r0 in range(0, nrt, G):
                    g = min(G, nrt - r0)
                    k = i % NB
                    sv = src[r0 * 128:(r0 + g) * 128, :].rearrange("(g p) c -> p g c", p=128)
                    dv = dst[r0 * 128:(r0 + g) * 128, :].rearrange("(g p) c -> p g c", p=128)
                    s.dma("sp", lambda e, k=k, g=g, C=C, sv=sv: e.dma_start(out=stg[k][:, :g, :C], in_=sv),
                          writes=[rs[k]])
                    eng = ("act", "dve", "pool")[i % 3]
                    if eng == "act":
                        fn = lambda e, k=k, g=g, C=C: e.copy(out=stb[k][:, :g, :C], in_=stg[k][:, :g, :C])
                    else:
                        fn = lambda e, k=k, g=g, C=C: e.tensor_copy(out=stb[k][:, :g, :C], in_=stg[k][:, :g, :C])
                    s.op(eng, fn, reads=[rs[k]], writes=[rb[k]])
                    s.dma("pool", lambda e, k=k, g=g, C=C, dv=dv: e.dma_start(out=dv, in_=stb[k][:, :g, :C]),
                          reads=[rb[k]])
                    i += 1
            s.barrier()

    def adaln(self, l):
        nc, s, di = self.nc, self.s, self.din
        with ExitStack() as es:
            T = lambda n, sh, dt: es.enter_context(nc.sbuf_tensor(self.name(n), sh, dt))
            P = lambda n, sh, dt: es.enter_context(nc.psum_tensor(self.name(n), sh, dt))
            c_sb = T("c_sb", [17, D], F32)
            sc = T("sc", [17, D], F32)
            scT = T("scT", [128, NKC, 17], F32)
            w_sb = [T("adw", [128, NKC, 512], F32) for _ in range(2)]
            b_sb = [T("adb", [1, 512], F32) for _ in range(2)]
            o_sb = [T("ado", [17, 512], F32) for _ in range(2)]
            pt = P("adpt", [128, NKC, 32], F32)
            po = [P("adpo", [17, 512], F32) for _ in range(2)]
            r_c, r_sc, r_scT, r_pt = Res(), Res(), Res(), Res()
            r_w = [Res(), Res()]; r_b = [Res(), Res()]; r_o = [Res(), Res()]; r_po = [Res(), Res()]
            s.dma("sp", lambda e: e.dma_start(out=c_sb[:], in_=di["c17"][:, :]), writes=[r_c])
            s.op("act", lambda e: e.activation(out=sc[:], in_=c_sb[:], func=AF.Silu), reads=[r_c], writes=[r_sc])
            for kc in range(NKC):
                s.op("pe", lambda e, kc=kc: e.transpose(out=pt[:, kc, 0:17], in_=sc[:, kc * 128:(kc + 1) * 128],
                                                        identity=self.ident_f[0:17, 0:17]),
                     reads=[r_sc, self.rconst], writes=[r_pt])
            s.op("dve", lambda e: e.tensor_copy(out=scT[:], in_=pt[:, :, 0:17]), reads=[r_pt], writes=[r_scT])
            for cc in range(18):
                k = cc % 2
                wv = di["ada_w"][l, :, cc * 512:(cc + 1) * 512].rearrange("(kc p) n -> p kc n", p=128)
                s.dma("sp", lambda e, k=k, wv=wv: e.dma_start(out=w_sb[k][:], in_=wv), writes=[r_w[k]])
                bv = di["ada_b"][l:l + 1, cc * 512:(cc + 1) * 512]
                s.dma("sp", lambda e, k=k, bv=bv: e.dma_start(out=b_sb[k][:], in_=bv), writes=[r_b[k]])
                for kc in range(NKC):
                    s.op("pe", lambda e, k=k, kc=kc: e.matmul(po[k][:], lhsT=scT[:, kc, :], rhs=w_sb[k][:, kc, :],
                                                              start=(kc == 0), stop=False),
                         reads=[r_scT, r_w[k]], writes=[r_po[k]])
                s.op("pe", lambda e, k=k: e.matmul(po[k][:], lhsT=self.ones1[0:1, 0:17], rhs=b_sb[k][:],
                                                   start=False, stop=True),
                     reads=[r_b[k], self.rconst], writes=[r_po[k]])
                isub, kind = divmod(cc // 2, 3)
                if kind == 0:
                    fn = lambda e, k=k: e.tensor_copy(out=o_sb[k][:], in_=po[k][:])
                elif kind == 1:
                    fn = lambda e, k=k: e.tensor_scalar(out=o_sb[k][:], in0=po[k][:], scalar1=1.0, scalar2=None,
                                                        op0=ALU.add)
                else:
                    r = 1.0 if isub == 1 else 0.5
                    fn = lambda e, k=k, r=r: e.tensor_scalar(out=o_sb[k][:], in0=po[k][:], scalar1=1.0, scalar2=r,
                                                             op0=ALU.add, op1=ALU.mult)
                s.op("dve", fn, reads=[r_po[k]], writes=[r_o[k]])
                mv = self.mod17[:, cc * 512:(cc + 1) * 512]
                s.dma("pool", lambda e, k=k, mv=mv: e.dma_start(out=mv, in_=o_sb[k][:]), reads=[r_o[k]])
            s.barrier()

    def alloc_mod(self, es):
        nc = self.nc
        T = lambda n, sh, dt: es.enter_context(nc.sbuf_tensor(self.name(n), sh, dt))
        m = {}
        m["m3"] = T("m3", [17, 3, D], F32)
        m["SH"] = T("SH", [128, D], F32)
        m["SC"] = T("SC", [128, D], F32)
        m["G"] = T("G", [128, D], F32)
        m["LG"] = T("LG", [128, D], F32)
        m["LB"] = T("LB", [128, D], F32)
        m["r_m3"] = Res(); m["r_mod"] = Res(); m["r_ln"] = Res()
        return m

    def load_mod(self, m, l, isub, pmod):
        s, di = self.s, self.din
        mv = self.mod17[:, isub * 3 * D:(isub + 1) * 3 * D].rearrange("r (k d) -> r k d", k=3)
        s.dma("sp", lambda e: e.dma_start(out=m["m3"][:], in_=mv), writes=[m["r_m3"]])
        s.dma("sp", lambda e: e.dma_start(out=m["LG"][:], in_=di["ln_g"][l, isub:isub + 1, :].to_broadcast([128, D])),
              writes=[m["r_ln"]])
        s.dma("sp", lambda e: e.dma_start(out=m["LB"][:], in_=di["ln_b"][l, isub:isub + 1, :].to_broadcast([128, D])),
              writes=[m["r_ln"]])

    def bcast_mod(self, m, sample, pmod, r_pmod):
        s = self.s
        sel = self.selS if sample else self.selP
        for kind, key in enumerate(("SH", "SC", "G")):
            for h in range(2):
                s.op("pe", lambda e, kind=kind, h=h: e.matmul(pmod[:], lhsT=sel[:], rhs=m["m3"][:, kind, h * 512:(h + 1) * 512],
                                                              start=True, stop=True),
                     reads=[m["r_m3"], self.rconst], writes=[r_pmod])
                s.op("act", lambda e, key=key, h=h: e.copy(out=m[key][:, h * 512:(h + 1) * 512], in_=pmod[:]),
                     reads=[r_pmod], writes=[m["r_mod"]])

    def epilogue(self, m, po_halves, r_po, x_ap, r_x, t1, r_t1, st, r_st, dst_ap, dst_res, extra_dst=None):
        s = self.s
        if po_halves is not None:
            for h in range(2):
                s.op("dve", lambda e, h=h: e.tensor_tensor(out=t1[:, h * 512:(h + 1) * 512], in0=po_halves[h],
                                                           in1=m["G"][:, h * 512:(h + 1) * 512], op=ALU.mult),
                     reads=[r_po[h], m["r_mod"]], writes=[r_t1])
            s.op("dve", lambda e: e.scalar_tensor_tensor(out=t1[:], in0=x_ap, scalar=ALPHA, in1=t1[:],
                                                          op0=ALU.mult, op1=ALU.add),
                 reads=[r_x, r_t1], writes=[r_t1])
        else:
            s.op("pool", lambda e: e.tensor_scalar(out=t1[:], in0=x_ap, scalar1=ALPHA, scalar2=None, op0=ALU.mult),
                 reads=[r_x], writes=[r_t1])
        for h in range(2):
            s.op("dve", lambda e, h=h: e.bn_stats(out=st[:, h * 6:(h + 1) * 6], in_=t1[:, h * 512:(h + 1) * 512]),
                 reads=[r_t1], writes=[r_st])
        s.op("dve", lambda e: e.bn_aggr(out=st[:, 12:14], in_=st[:, 0:12]), reads=[r_st], writes=[r_st])
        s.op("act", lambda e: e.activation(out=st[:, 14:15], in_=st[:, 13:14], func=AF.Sqrt, bias=self.eps_t[:, 0:1],
                                           scale=1.0), reads=[r_st, self.rconst], writes=[r_st])
        s.op("dve", lambda e: e.reciprocal(out=st[:, 15:16], in_=st[:, 14:15]), reads=[r_st], writes=[r_st])
        s.op("dve", lambda e: e.tensor_scalar(out=t1[:], in0=t1[:], scalar1=st[:, 12:13], scalar2=st[:, 15:16],
                                              op0=ALU.subtract, op1=ALU.mult), reads=[r_st, r_t1], writes=[r_t1])
        s.op("pool", lambda e: e.tensor_tensor(out=t1[:], in0=t1[:], in1=m["LG"][:], op=ALU.mult),
             reads=[r_t1, m["r_ln"]], writes=[r_t1])
        s.op("dve", lambda e: e.tensor_tensor(out=t1[:], in0=t1[:], in1=m["LB"][:], op=ALU.add),
             reads=[r_t1, m["r_ln"]], writes=[r_t1])
        toks = []
        for ap, res, nrows in dst_ap:
            toks.append(s.dma("pool", lambda e, ap=ap, nrows=nrows: e.dma_start(out=ap, in_=t1[0:nrows, :]),
                              reads=[r_t1], writes=[res] if res is not None else []))
        return toks

    def x_src(self, first, cur, tile):
        if first:
            if tile < 32:
                return self.din["xp"][tile * 128:(tile + 1) * 128, :], 128, None
            return self.din["xs"][:, :], NS, None
        return self.XS[cur][tile * 128:(tile + 1) * 128, :], 128, self.xres[cur][tile]

    def x_dst(self, last, nxt, tile):
        if last:
            if tile < 32:
                return [(self.dout["yp"][tile * 128:(tile + 1) * 128, :], None, 128)]
            return [(self.dout["ys"][:, :], None, NS)]
        return [(self.XS[nxt][tile * 128:(tile + 1) * 128, :], self.xres[nxt][tile], 128)]

    def ffn_sublayer(self, l, f, isub, first, last, cur, nxt):
        nc, s = self.nc, self.s
        wrow = (l * 2 + f)
        nblk = self.cfg.get("nblk", 9)
        with ExitStack() as es:
            T = lambda n, sh, dt: es.enter_context(nc.sbuf_tensor(self.name(n), sh, dt))
            P = lambda n, sh, dt: es.enter_context(nc.psum_tensor(self.name(n), sh, dt))
            m = self.alloc_mod(es)
            xblk = [T("xblk", [128, 4, D], F32) for _ in range(2)]
            hb = [T("hb", [128, D], BF16) for _ in range(2)]
            htmp = [T("htmp", [128, D], F32) for _ in range(2)]
            hT = [T("hT", [128, NKC, 512], BF16) for _ in range(2)]
            NBW = 3
            wgu = [T("wgu", [128, 2, NKC, 256], BF16) for _ in range(NBW)]
            sg = [T("sg", [128, 512], BF16) for _ in range(2)]
            act = T("actb", [128, NFC, 512], BF16)
            wd = [T("wd", [128, NFC, 512], BF16) for _ in range(2)]
            t1 = [T("t1", [128, D], F32) for _ in range(2)]
            st = [T("st", [128, 16], F32) for _ in range(2)]
            tp = [P("tp", [128, NKC, 128], BF16) for _ in range(2)]
            pg = [P("pg", [128, 512], F32) for _ in range(2)]
            pu = [P("pu", [128, 512], F32) for _ in range(2)]
            po = [P("po", [128, 512], F32) for _ in range(2)]
            r_x = [Res(), Res()]; r_hb = [Res(), Res()]; r_htmp = [Res(), Res()]; r_hT = [Res(), Res()]
            r_wgu = [Res() for _ in range(NBW)]; r_sg = [Res(), Res()]; r_act = Res(); r_wd = [Res(), Res()]
            r_t1 = [Res(), Res()]; r_st = [Res(), Res()]; r_tp = [Res(), Res()]
            r_pg = [Res(), Res()]; r_pu = [Res(), Res()]; r_po = [Res(), Res()]
            for b in xblk:
                s.op("pool", lambda e, b=b: e.memset(b[:], 0.0), writes=[r_x[0], r_x[1]])
            self.load_mod(m, l, isub, None)
            self.bcast_mod(m, False, po[0], r_po[0])
            iw = 0
            iwd = 0
            ihb = 0
            igu = 0
            ipo = 0
            it1 = 0
            for blk in range(nblk):
                sample = (blk == 8)
                nt = 1 if sample else 4
                TB = nt * 128
                xb = blk % 2
                if sample:
                    self.bcast_mod(m, True, po[0], r_po[0])
                for t in range(nt):
                    ap, nrows, res = self.x_src(first, cur, blk * 4 + t)
                    s.dma("sp", lambda e, xb=xb, t=t, ap=ap, nrows=nrows: e.dma_start(out=xblk[xb][0:nrows, t, :], in_=ap),
                          reads=[res] if res is not None else [], writes=[r_x[xb]])
                for t in range(nt):
                    k = ihb % 2
                    ihb += 1
                    s.op("pool", lambda e, k=k, xb=xb, t=t: e.tensor_tensor(out=htmp[k][:], in0=xblk[xb][:, t, :], in1=m["SC"][:],
                                                                            op=ALU.mult),
                         reads=[r_x[xb], m["r_mod"]], writes=[r_htmp[k]])
                    s.op("dve", lambda e, k=k: e.tensor_tensor(out=hb[k][:], in0=htmp[k][:], in1=m["SH"][:], op=ALU.add),
                         reads=[r_htmp[k], m["r_mod"]], writes=[r_hb[k]])
                    for kc in range(NKC):
                        s.op("pe", lambda e, k=k, kc=kc: e.transpose(out=tp[k][:, kc, :], in_=hb[k][:, kc * 128:(kc + 1) * 128],
                                                                     identity=self.ident_b[:]),
                             reads=[r_hb[k], self.rconst], writes=[r_tp[k]])
                    s.op("act", lambda e, k=k, xb=xb, t=t: e.copy(out=hT[xb][:, :, t * 128:(t + 1) * 128], in_=tp[k][:]),
                         reads=[r_tp[k]], writes=[r_hT[xb]])
                for fp in range(NFC // 2):
                    kw = iw % NBW
                    iw += 1
                    gv = self.wg_b[wrow * D:(wrow + 1) * D, fp * 256:(fp + 1) * 256].rearrange("(kc p) n -> p kc n", p=128)
                    uv = self.wu_b[wrow * D:(wrow + 1) * D, fp * 256:(fp + 1) * 256].rearrange("(kc p) n -> p kc n", p=128)
                    s.dma("sp", lambda e, kw=kw, gv=gv: e.dma_start(out=wgu[kw][:, 0], in_=gv), writes=[r_wgu[kw]])
                    s.dma("sp", lambda e, kw=kw, uv=uv: e.dma_start(out=wgu[kw][:, 1], in_=uv), writes=[r_wgu[kw]])
                    for j in range(2):
                        fc = fp * 2 + j
                        kg = igu % 2
                        igu += 1
                        for which, pp, rr in ((0, pg, r_pg), (1, pu, r_pu)):
                            for kc in range(NKC):
                                s.op("pe", lambda e, kw=kw, which=which, kc=kc, j=j, pp=pp, kg=kg, xb=xb, TB=TB:
                                     e.matmul(pp[kg][:, 0:TB], lhsT=wgu[kw][:, which, kc, j * 128:(j + 1) * 128],
                                              rhs=hT[xb][:, kc, 0:TB], start=(kc == 0), stop=(kc == NKC - 1)),
                                     reads=[r_wgu[kw], r_hT[xb]], writes=[rr[kg]])
                        s.op("act", lambda e, kg=kg, TB=TB: e.activation(out=sg[kg][:, 0:TB], in_=pg[kg][:, 0:TB], func=AF.Silu),
                             reads=[r_pg[kg]], writes=[r_sg[kg]])
                        s.op("dve", lambda e, kg=kg, fc=fc, TB=TB: e.tensor_tensor(out=act[:, fc, 0:TB], in0=sg[kg][:, 0:TB],
                                                                                   in1=pu[kg][:, 0:TB], op=ALU.mult),
                             reads=[r_sg[kg], r_pu[kg]], writes=[r_act])
                for h in range(2):
                    dv = self.wd_b[wrow * DFF:(wrow + 1) * DFF, h * 512:(h + 1) * 512].rearrange("(fc p) n -> p fc n", p=128)
                    s.dma("sp", lambda e, h=h, dv=dv: e.dma_start(out=wd[h][:], in_=dv), writes=[r_wd[h]])
                for t in range(nt):
                    tile = blk * 4 + t
                    for h in range(2):
                        for fc in range(NFC):
                            s.op("pe", lambda e, h=h, fc=fc, t=t: e.matmul(po[h][:], lhsT=act[:, fc, t * 128:(t + 1) * 128],
                                                                           rhs=wd[h][:, fc, :], start=(fc == 0), stop=(fc == NFC - 1)),
                                 reads=[r_act, r_wd[h]], writes=[r_po[h]])
                    k1 = it1 % 2
                    it1 += 1
                    self.epilogue(m, [po[0][:], po[1][:]], r_po, xblk[xb][:, t, :], r_x[xb], t1[k1], r_t1[k1], st[k1], r_st[k1],
                                  self.x_dst(last, nxt, tile), None)
            s.barrier()

    def stub_mixer(self, l, cur, nxt):
        nc, s = self.nc, self.s
        with ExitStack() as es:
            T = lambda n, sh, dt: es.enter_context(nc.sbuf_tensor(self.name(n), sh, dt))
            m = self.alloc_mod(es)
            xt = [T("xt", [128, D], F32) for _ in range(2)]
            t1 = [T("t1", [128, D], F32) for _ in range(2)]
            st = [T("st", [128, 16], F32) for _ in range(2)]
            r_x = [Res(), Res()]; r_t1 = [Res(), Res()]; r_st = [Res(), Res()]
            self.load_mod(m, l, 1, None)
            for tile in range(NTILE):
                k = tile % 2
                ap, nrows, res = self.x_src(False, cur, tile)
                s.dma("sp", lambda e, k=k, ap=ap: e.dma_start(out=xt[k][:], in_=ap), reads=[res], writes=[r_x[k]])
                self.epilogue(m, None, None, xt[k][:], r_x[k], t1[k], r_t1[k], st[k], r_st[k],
                              self.x_dst(False, nxt, tile), None)
            s.barrier()

    def mixer0_A(self, cur):
        nc, s, di, do = self.nc, self.s, self.din, self.dout
        LAM_INIT = 0.8 - 0.6 * float(np.exp(-0.3 * 0))
        with ExitStack() as es:
            T = lambda n, sh, dt: es.enter_context(nc.sbuf_tensor(self.name(n), sh, dt))
            P = lambda n, sh, dt: es.enter_context(nc.psum_tensor(self.name(n), sh, dt))
            m = self.alloc_mod(es)
            ewin = T("ewin", [128, NKC, 2048], BF16)
            xt = [T("xt", [128, D], F32) for _ in range(2)]
            tmpf = T("tmpf", [128, D], F32)
            hb = [T("hb", [128, D], BF16) for _ in range(2)]
            hT = [T("hT", [128, NKC, 128], BF16) for _ in range(2)]
            q_b = [T("q_b", [128, 512], BF16) for _ in range(2)]
            k_b = [T("k_b", [128, 512], BF16) for _ in range(2)]
            u_b = [T("u_b", [128, 512], BF16) for _ in range(2)]
            kv_f = [T("kv_f", [128, 1024], F32) for _ in range(2)]
            qT = [T("qT", [128, 4, 128], BF16) for _ in range(2)]
            KT = T("KT", [128, 4, SEQ], BF16)
            VP = T("VP", [128, 32, 4, 132], BF16)
            E = [T("E", [128, 4, 128], BF16) for _ in range(3)]
            tri = T("tri", [128, 128], BF16)
            dl = T("dl", [128, 4, 64], F32)
            dlp = T("dlp", [128, 2, 64], F32)
            lamt = T("lamt", [128, 8], F32)
            subl = T("subl", [128, 128], F32)
            a1 = [T("a1", [128, 128], F32) for _ in range(2)]
            junk = T("junk", [128, 128], F32)
            rr = [T("rr", [128, 8], F32) for _ in range(2)]
            mixf = [T("mixf", [128, 512], BF16) for _ in range(2)]
            pp = [P("pp", [128, 512], F32) for _ in range(4)]
            tpb = P("tpb", [128, NKC, 128], BF16)
            st = [P("st", [128, 512], F32) for _ in range(2)]
            oaccs = [P("oacc", [128, 512], F32)]
            r_ewin = Res(); r_x = [Res(), Res()]; r_tmpf = Res(); r_hb = [Res(), Res()]; r_hT = [Res(), Res()]
            r_q = [Res(), Res()]; r_k = [Res(), Res()]; r_u = [Res(), Res()]; r_kv = [Res(), Res()]; r_qT = [Res(), Res()]
            r_KT = [Res() for _ in range(32)]; r_VP = [Res() for _ in range(32)]; r_E = [Res() for _ in range(3)]
            r_c = Res(); r_a1 = [Res(), Res()]; r_rr = [Res(), Res()]; r_mixf = [Res(), Res()]; r_junk = Res()
            r_pp = [Res() for _ in range(4)]; r_tp = Res(); r_st = [Res(), Res()]; r_oacc = [Res()]
            s.dma("sp", lambda e: e.dma_start(out=ewin[:], in_=self.ewin_b[:, :].rearrange("(kc p) n -> p kc n", p=128)),
                  writes=[r_ewin])
            s.op("pool", lambda e: e.memset(VP[:, :, :, 128:132], 1.0), writes=r_VP)
            s.op("pool", lambda e: e.memset(tri[:], 1.0), writes=[r_c])
            s.op("pool", lambda e: e.affine_select(out=tri[:], in_=tri[:], pattern=[[1, 128]], compare_op=ALU.is_ge,
                                                    fill=0.0, base=0, channel_multiplier=-1), reads=[r_c], writes=[r_c])
            self.diff_consts(dl, dlp, lamt, subl, junk, r_c, LAM_INIT)
            self.load_mod(m, 0, 1, None)
            self.bcast_mod(m, False, pp[0], r_pp[0])
            ist = 0
            ie = 0
            ihd = 0
            for i in (range(NTILE) if not self.cfg.get('tilesA') else self.cfg['tilesA']):
                sample = (i == 32)
                k = i % 2
                if sample:
                    self.bcast_mod(m, True, pp[0], r_pp[0])
                ap, nrows, res = self.x_src(False, cur, i)
                s.dma("sp", lambda e, k=k, ap=ap: e.dma_start(out=xt[k][:], in_=ap), reads=[res], writes=[r_x[k]])
                s.op("pool", lambda e, k=k: e.tensor_tensor(out=tmpf[:], in0=xt[k][:], in1=m["SC"][:], op=ALU.mult),
                     reads=[r_x[k], m["r_mod"]], writes=[r_tmpf])
                s.op("dve", lambda e, k=k: e.tensor_tensor(out=hb[k][:], in0=tmpf[:], in1=m["SH"][:], op=ALU.add),
                     reads=[r_tmpf, m["r_mod"]], writes=[r_hb[k]])
                for kc in range(NKC):
                    s.op("pe", lambda e, k=k, kc=kc: e.transpose(out=tpb[:, kc, :], in_=hb[k][:, kc * 128:(kc + 1) * 128],
                                                                 identity=self.ident_b[:]),
                         reads=[r_hb[k], self.rconst], writes=[r_tp])
                s.op("act", lambda e, k=k: e.copy(out=hT[k][:], in_=tpb[:]), reads=[r_tp], writes=[r_hT[k]])
                for c in range(4):
                    for kc in range(NKC):
                        s.op("pe", lambda e, k=k, kc=kc, c=c: e.matmul(pp[c][:], lhsT=hT[k][:, kc, :],
                                                                       rhs=ewin[:, kc, c * 512:(c + 1) * 512],
                                                                       start=(kc == 0), stop=(kc == NKC - 1)),
                             reads=[r_hT[k], r_ewin], writes=[r_pp[c]])
                if self.cfg.get("stopA") == 1:
                    continue
                s.op("act", lambda e, k=k: e.copy(out=q_b[k][:], in_=pp[0][:]), reads=[r_pp[0]], writes=[r_q[k]])
                s.op("dve", lambda e, k=k: e.tensor_copy(out=kv_f[k][:, 0:512], in_=pp[1][:]), reads=[r_pp[1]], writes=[r_kv[k]])
                s.op("dve", lambda e, k=k: e.tensor_copy(out=kv_f[k][:, 512:1024], in_=pp[2][:]), reads=[r_pp[2]], writes=[r_kv[k]])
                s.op("act", lambda e, k=k: e.copy(out=u_b[k][:], in_=pp[3][:]), reads=[r_pp[3]], writes=[r_u[k]])
                s.op("act", lambda e, k=k: e.copy(out=k_b[k][:], in_=kv_f[k][:, 0:512]), reads=[r_kv[k]], writes=[r_k[k]])
                if not sample:
                    s.op("act", lambda e, i=i, k=k: e.copy(out=VP[:, i, :, 0:128], in_=kv_f[k][:, 512:1024].rearrange("p (h e) -> p h e", h=4)),
                         reads=[r_kv[k]], writes=[r_VP[i]])
                rows = slice(i * 128, (i + 1) * 128)
                if self.cfg.get("stopA") == 2:
                    continue
                s.dma("pool", lambda e, k=k, rows=rows: e.dma_start(out=self.ubuf[rows, :], in_=u_b[k][:]),
                      reads=[r_u[k]], writes=[self.ures[i]])
                if not sample:
                    s.dma("pool", lambda e, k=k, rows=rows: e.dma_start(out=do["akp"][rows, :], in_=kv_f[k][:, 0:512]), reads=[r_kv[k]])
                    s.dma("pool", lambda e, k=k, rows=rows: e.dma_start(out=do["avp"][rows, :], in_=kv_f[k][:, 512:1024]), reads=[r_kv[k]])
                else:
                    s.dma("pool", lambda e, k=k: e.dma_start(out=do["aks"][:, :], in_=kv_f[k][0:NS, 0:512]), reads=[r_kv[k]],
                          writes=[self.r_aks])
                    s.dma("pool", lambda e, k=k: e.dma_start(out=do["avs"][:, :], in_=kv_f[k][0:NS, 512:1024]), reads=[r_kv[k]],
                          writes=[self.r_aks])
                    s.dma("pool", lambda e, k=k: e.dma_start(out=self.qsbuf[:, :], in_=q_b[k][0:NS, :]), reads=[r_q[k]],
                          writes=[self.r_aks])
                    continue
                if self.cfg.get("stopA") == 3:
                    continue
                for h in range(4):
                    s.op("pe", lambda e, k=k, h=h: e.transpose(out=tpb[:, h, :], in_=q_b[k][:, h * 128:(h + 1) * 128],
                                                               identity=self.ident_b[:]),
                         reads=[r_q[k], self.rconst], writes=[r_tp])
                    s.op("pe", lambda e, k=k, h=h: e.transpose(out=tpb[:, 4 + h, :], in_=k_b[k][:, h * 128:(h + 1) * 128],
                                                               identity=self.ident_b[:]),
                         reads=[r_k[k], self.rconst], writes=[r_tp])
                s.op("act", lambda e, k=k: e.copy(out=qT[k][:], in_=tpb[:, 0:4, :]), reads=[r_tp], writes=[r_qT[k]])
                s.op("act", lambda e, i=i: e.copy(out=KT[:, :, i * 128:(i + 1) * 128], in_=tpb[:, 4:8, :]), reads=[r_tp],
                     writes=[r_KT[i]])
                for h in range(4 if not self.cfg.get("noattn") else 0):
                    ob = 0
                    oacc = oaccs[ob]
                    for mm in range(2):
                        pr = slice(mm * 64, (mm + 1) * 64)
                        oc = mm * 132
                        for j0 in range(0, i + 1, 4):
                            js = list(range(j0, min(j0 + 4, i + 1)))
                            sb = ist % 2
                            ist += 1
                            for jj, j in enumerate(js):
                                s.op("pe", lambda e, sb=sb, jj=jj, j=j, h=h, pr=pr, k=k:
                                     e.matmul(st[sb][:, jj * 128:(jj + 1) * 128], lhsT=KT[pr, h, j * 128:(j + 1) * 128],
                                              rhs=qT[k][pr, h, :], start=True, stop=True),
                                     reads=[r_KT[j], r_qT[k]], writes=[r_st[sb]])
                            eb = ie % 3
                            ie += 1
                            w = len(js)
                            s.op("act", lambda e, eb=eb, sb=sb, w=w: e.activation(out=E[eb][:, 0:w, :],
                                                                                 in_=st[sb][:, 0:w * 128].rearrange("p (a b) -> p a b", a=w),
                                                                                 func=AF.Exp, scale=0.125),
                                 reads=[r_st[sb]], writes=[r_E[eb]])
                            if js[-1] == i:
                                jj = i - j0
                                s.op("pool", lambda e, eb=eb, jj=jj: e.tensor_tensor(out=E[eb][:, jj, :], in0=E[eb][:, jj, :],
                                                                                      in1=tri[:], op=ALU.mult),
                                     reads=[r_E[eb], r_c], writes=[r_E[eb]])
                            for jj, j in enumerate(js):
                                s.op("pe", lambda e, eb=eb, jj=jj, j=j, h=h, oc=oc, oacc=oacc, i=i:
                                     e.matmul(oacc[:, oc:oc + 129], lhsT=E[eb][:, jj, :], rhs=VP[:, j, h, 0:129],
                                              start=(j == 0), stop=(j == i)),
                                     reads=[r_E[eb], r_VP[j]], writes=[r_oacc[ob]])
                    kk = ihd % 2
                    ihd += 1
                    self.diff_finalize(oacc, r_oacc[ob], rr[kk], r_rr[kk], a1[kk], r_a1[kk], junk, r_junk, lamt, subl, r_c,
                                       mixf[k][:, h * 128:(h + 1) * 128], r_mixf[k], 128)
                s.dma("pool", lambda e, k=k, rows=rows: e.dma_start(out=self.mixbuf[rows, 0:512], in_=mixf[k][:]),
                      reads=[r_mixf[k]], writes=[self.mixres[i]])
            s.barrier()

    def diff_consts(self, dl, dlp, lamt, subl, junk, r_c, lam_init):
        s, di = self.s, self.din
        s.dma("sp", lambda e: e.dma_start(out=dl[:].rearrange("p a b -> p (a b)"),
                                          in_=di["diff_lambda"][0:1, :].to_broadcast([128, 256])), writes=[r_c])
        s.dma("sp", lambda e: e.dma_start(out=subl[:], in_=di["diff_subln"][0:1, :].to_broadcast([128, 128])), writes=[r_c])
        s.op("dve", lambda e: e.tensor_tensor(out=dlp[:, 0, :], in0=dl[:, 0, :], in1=dl[:, 1, :], op=ALU.mult), reads=[r_c], writes=[r_c])
        s.op("dve", lambda e: e.tensor_tensor(out=dlp[:, 1, :], in0=dl[:, 2, :], in1=dl[:, 3, :], op=ALU.mult), reads=[r_c], writes=[r_c])
        s.op("dve", lambda e: e.tensor_reduce(out=lamt[:, 0:2], in_=dlp[:], axis=AX.X, op=ALU.add), reads=[r_c], writes=[r_c])
        s.op("act", lambda e: e.activation(out=lamt[:, 4:6], in_=lamt[:, 0:2], func=AF.Exp), reads=[r_c], writes=[r_c])
        s.op("dve", lambda e: e.tensor_tensor(out=lamt[:, 2:3], in0=lamt[:, 4:5], in1=lamt[:, 5:6], op=ALU.subtract), reads=[r_c], writes=[r_c])
        s.op("dve", lambda e: e.tensor_scalar(out=lamt[:, 2:3], in0=lamt[:, 2:3], scalar1=lam_init, scalar2=None, op0=ALU.add),
             reads=[r_c], writes=[r_c])
        s.op("dve", lambda e: e.tensor_scalar(out=lamt[:, 3:4], in0=lamt[:, 2:3], scalar1=-1.0, scalar2=None, op0=ALU.mult),
             reads=[r_c], writes=[r_c])
        s.op("dve", lambda e: e.tensor_scalar(out=subl[:], in0=subl[:], scalar1=1.0 - lam_init, scalar2=None, op0=ALU.mult),
             reads=[r_c], writes=[r_c])

    def diff_finalize(self, oacc, r_oacc, rr, r_rr, a1, r_a1, junk, r_junk, lamt, subl, r_c, out_ap, r_out, np_):
        s = self.s
        P_ = slice(0, np_)
        s.op("dve", lambda e: e.reciprocal(out=rr[P_, 0:1], in_=oacc[P_, 128:129]), reads=[r_oacc], writes=[r_rr])
        s.op("dve", lambda e: e.reciprocal(out=rr[P_, 1:2], in_=oacc[P_, 260:261]), reads=[r_oacc], writes=[r_rr])
        s.op("dve", lambda e: e.tensor_tensor(out=rr[P_, 2:3], in0=rr[P_, 1:2], in1=lamt[P_, 3:4], op=ALU.mult),
             reads=[r_rr, r_c], writes=[r_rr])
        s.op("dve", lambda e: e.tensor_scalar(out=a1[P_, :], in0=oacc[P_, 0:128], scalar1=rr[P_, 0:1], scalar2=None, op0=ALU.mult),
             reads=[r_oacc, r_rr], writes=[r_a1])
        s.op("dve", lambda e: e.scalar_tensor_tensor(out=a1[P_, :], in0=oacc[P_, 132:260], scalar=rr[P_, 2:3], in1=a1[P_, :],
                                                     op0=ALU.mult, op1=ALU.add), reads=[r_oacc, r_rr, r_a1], writes=[r_a1])
        s.op("act", lambda e: e.activation(out=junk[P_, :], in_=a1[P_, :], func=AF.Square, accum_out=rr[P_, 3:4]),
             reads=[r_a1], writes=[r_junk, r_rr])
        s.op("act", lambda e: e.activation(out=rr[P_, 4:5], in_=rr[P_, 3:4], func=AF.Sqrt, bias=self.eps_t[P_, 0:1], scale=1.0 / 128.0),
             reads=[r_rr, self.rconst], writes=[r_rr])
        s.op("dve", lambda e: e.reciprocal(out=rr[P_, 5:6], in_=rr[P_, 4:5]), reads=[r_rr], writes=[r_rr])
        s.op("dve", lambda e: e.scalar_tensor_tensor(out=out_ap, in0=a1[P_, :], scalar=rr[P_, 5:6], in1=subl[P_, :],
                                                     op0=ALU.mult, op1=ALU.mult), reads=[r_a1, r_rr, r_c], writes=[r_out])

    def mixer0_S(self):
        nc, s, di, do = self.nc, self.s, self.din, self.dout
        LAM_INIT = 0.8 - 0.6 * float(np.exp(-0.3 * 0))
        NPG = 16
        with ExitStack() as es:
            T = lambda n, sh, dt: es.enter_context(nc.sbuf_tensor(self.name(n), sh, dt))
            P = lambda n, sh, dt: es.enter_context(nc.psum_tensor(self.name(n), sh, dt))
            pt_i = T("pt_i", [128, NSB * NPG], I32)
            io_i = T("io_i", [128, NSB * NPG], I32)
            idx = T("idx", [128, NSB * NPG], I32)
            pt_f = T("pt_f", [128, NSB * NPG], F32)
            io_f = T("io_f", [128, NSB * NPG], F32)
            selrows = T("selrows", [64, 64, 128], BF16)
            q_s = T("q_s", [64, 512], BF16)
            tri = T("tri", [128, 128], F32)
            dl = T("dl", [128, 4, 64], F32)
            dlp = T("dlp", [128, 2, 64], F32)
            lamt = T("lamt", [128, 8], F32)
            subl = T("subl", [128, 128], F32)
            junk = T("junk", [128, 128], F32)
            Qbc = [T("Qbc", [128, 4, 512], BF16) for _ in range(2)]
            Kg = [T("Kg", [128, 512], F32) for _ in range(4)]
            Vg = [T("Vg", [128, 512], F32) for _ in range(3)]
            VPb = [T("VPb", [128, NPG, 4, 132], BF16) for _ in range(2)]
            Kn = [T("Kn", [4, 512], F32) for _ in range(2)]
            Vn = [T("Vn", [4, 512], F32) for _ in range(2)]
            VPn = [T("VPn", [4, 4, 132], BF16) for _ in range(2)]
            prod = [T("prod", [128, 512], F32) for _ in range(2)]
            Sc = [T("Sc", [128, NPG + 1, 4, 2, 4], F32) for _ in range(2)]
            Eb = [T("Eb", [128, NPG + 1, 4, 2, 4], BF16) for _ in range(2)]
            rr = [T("rr", [128, 8], F32) for _ in range(2)]
            a1 = [T("a1", [128, 128], F32) for _ in range(2)]
            mixs = [T("mixs", [4, 512], BF16) for _ in range(2)]
            qps = [P("qps", [128, 512], F32) for _ in range(2)]
            ops_ = [P("ops", [128, 512], F32) for _ in range(4)]
            r_c = Res(); r_Qbc = [Res(), Res()]; r_Kg = [Res() for _ in range(4)]; r_Vg = [Res() for _ in range(3)]
            r_VPb = [Res(), Res()]; r_Kn = [Res(), Res()]; r_Vn = [Res(), Res()]; r_VPn = [Res(), Res()]
            r_prod = [Res(), Res()]; r_Sc = [Res(), Res()]; r_Eb = [Res(), Res()]; r_rr = [Res(), Res()]; r_a1 = [Res(), Res()]
            r_mixs = [Res(), Res()]; r_qps = [Res(), Res()]; r_ops = [Res() for _ in range(4)]; r_junk = Res()
            s.dma("sp", lambda e: e.dma_start(out=pt_i[:], in_=di["page_table"].rearrange("b g -> (b g)").rearrange("(o n) -> o n", o=1).to_broadcast([128, NSB * NPG])),
                  writes=[r_c])
            s.op("pool", lambda e: e.iota(io_i[:], pattern=[[0, NSB * NPG]], base=0, channel_multiplier=1), writes=[r_c])
            s.op("dve", lambda e: e.tensor_copy(out=pt_f[:], in_=pt_i[:]), reads=[r_c], writes=[r_c])
            s.op("dve", lambda e: e.tensor_copy(out=io_f[:], in_=io_i[:]), reads=[r_c], writes=[r_c])
            s.op("dve", lambda e: e.scalar_tensor_tensor(out=pt_f[:], in0=pt_f[:], scalar=128.0, in1=io_f[:], op0=ALU.mult, op1=ALU.add),
                 reads=[r_c], writes=[r_c])
            s.op("dve", lambda e: e.tensor_copy(out=idx[:], in_=pt_f[:]), reads=[r_c], writes=[r_c])
            s.op("pool", lambda e: e.memset(selrows[:], 1.0), writes=[r_c])
            s.op("pool", lambda e: e.affine_select(out=selrows[:], in_=selrows[:], pattern=[[-1, 64], [0, 128]],
                                                    compare_op=ALU.is_equal, fill=0.0, base=0, channel_multiplier=1),
                 reads=[r_c], writes=[r_c])
            s.op("pool", lambda e: e.memset(tri[:], 1.0), writes=[r_c])
            s.op("pool", lambda e: e.affine_select(out=tri[:], in_=tri[:], pattern=[[1, 128]], compare_op=ALU.is_ge,
                                                    fill=0.0, base=0, channel_multiplier=-1), reads=[r_c], writes=[r_c])
            s.dma("sp", lambda e: e.dma_start(out=q_s[:], in_=self.qsbuf[:, :]), reads=[self.r_aks], writes=[r_c])
            self.diff_consts(dl, dlp, lamt, subl, junk, r_c, LAM_INIT)
            for v in VPb:
                s.op("pool", lambda e, v=v: e.memset(v[:, :, :, 128:132], 1.0), writes=r_VPb)
            for v in VPn:
                s.op("pool", lambda e, v=v: e.memset(v[:, :, 128:132], 1.0), writes=r_VPn)
            for sc_ in Sc:
                s.op("pool", lambda e, sc_=sc_: e.memset(sc_[:], 0.0), writes=r_Sc)
            ikg = 0
            ivg = 0
            ipr = 0
            cak = di["cache_a_k"]
            cav = di["cache_a_v"]
            for b in range(NSB):
                kb = b % 2
                for qi in range(4):
                    kq = (b * 4 + qi) % 2
                    s.op("pe", lambda e, kq=kq, b=b, qi=qi: e.matmul(qps[kq][:], lhsT=selrows[:, 4 * b + qi, :], rhs=q_s[:, :],
                                                                     start=True, stop=True),
                         reads=[r_c], writes=[r_qps[kq]])
                    s.op("act", lambda e, kq=kq, kb=kb, qi=qi: e.copy(out=Qbc[kb][:, qi, :], in_=qps[kq][:]),
                         reads=[r_qps[kq]], writes=[r_Qbc[kb]])
                s.dma("sp", lambda e, kb=kb, b=b: e.dma_start(out=Kn[kb][:], in_=do["aks"][4 * b:4 * b + 4, :]),
                      reads=[self.r_aks], writes=[r_Kn[kb]])
                s.dma("sp", lambda e, kb=kb, b=b: e.dma_start(out=Vn[kb][:], in_=do["avs"][4 * b:4 * b + 4, :]),
                      reads=[self.r_aks], writes=[r_Vn[kb]])
                s.op("act", lambda e, kb=kb: e.copy(out=VPn[kb][:, :, 0:128], in_=Vn[kb][:].rearrange("p (h e) -> p h e", h=4)),
                     reads=[r_Vn[kb]], writes=[r_VPn[kb]])
                for pg in range(NPG):
                    kv = ivg % 3
                    ivg += 1
                    col = b * NPG + pg
                    s.dma("pool", lambda e, kv=kv, col=col: e.indirect_dma_start(
                        out=Vg[kv][:], out_offset=None, in_=cav[:, :],
                        in_offset=bass.IndirectOffsetOnAxis(ap=idx[:, col:col + 1], axis=0)),
                        reads=[r_c], writes=[r_Vg[kv]])
                    s.op("act", lambda e, kv=kv, kb=kb, pg=pg: e.copy(out=VPb[kb][:, pg, :, 0:128],
                                                                      in_=Vg[kv][:].rearrange("p (h e) -> p h e", h=4)),
                         reads=[r_Vg[kv]], writes=[r_VPb[kb]])
                for pg in range(NPG + 1):
                    if pg < NPG:
                        kk = ikg % 4
                        ikg += 1
                        col = b * NPG + pg
                        s.dma("pool", lambda e, kk=kk, col=col: e.indirect_dma_start(
                            out=Kg[kk][:], out_offset=None, in_=cak[:, :],
                            in_offset=bass.IndirectOffsetOnAxis(ap=idx[:, col:col + 1], axis=0)),
                            reads=[r_c], writes=[r_Kg[kk]])
                        ksrc, r_ks, npart = Kg[kk], r_Kg[kk], 128
                    else:
                        ksrc, r_ks, npart = Kn[kb], r_Kn[kb], 4
                    for qi in range(4):
                        kp = ipr % 2
                        ipr += 1
                        s.op("dve", lambda e, kp=kp, ksrc=ksrc, npart=npart, kb=kb, qi=qi:
                             e.tensor_tensor(out=prod[kp][0:npart, :], in0=ksrc[0:npart, :], in1=Qbc[kb][0:npart, qi, :], op=ALU.mult),
                             reads=[r_ks, r_Qbc[kb]], writes=[r_prod[kp]])
                        s.op("dve", lambda e, kp=kp, npart=npart, kb=kb, pg=pg, qi=qi:
                             e.tensor_reduce(out=Sc[kb][0:npart, pg, :, :, qi].rearrange("p h m -> p (h m)"),
                                             in_=prod[kp][0:npart, :].rearrange("p (g d) -> p g d", d=64), axis=AX.X, op=ALU.add),
                             reads=[r_prod[kp]], writes=[r_Sc[kb]])
                s.op("act", lambda e, kb=kb: e.activation(out=Eb[kb][:].rearrange("p a h m q -> p (a h m q)"),
                                                          in_=Sc[kb][:].rearrange("p a h m q -> p (a h m q)"),
                                                          func=AF.Exp, scale=0.125), reads=[r_Sc[kb]], writes=[r_Eb[kb]])
                s.op("pool", lambda e, kb=kb: e.tensor_tensor(out=Eb[kb][0:4, NPG].rearrange("p h m q -> p (h m) q"),
                                                               in0=Eb[kb][0:4, NPG].rearrange("p h m q -> p (h m) q"),
                                                               in1=tri[0:4, 0:4].unsqueeze(1).to_broadcast([4, 8, 4]), op=ALU.mult),
                     reads=[r_Eb[kb], r_c], writes=[r_Eb[kb]])
                for h in range(4):
                    for mm in range(2):
                        oc = mm * 132
                        for pg in range(NPG + 1):
                            if pg < NPG:
                                s.op("pe", lambda e, h=h, mm=mm, oc=oc, pg=pg, kb=kb:
                                     e.matmul(ops_[h][0:4, oc:oc + 129], lhsT=Eb[kb][:, pg, h, mm, :], rhs=VPb[kb][:, pg, h, 0:129],
                                              start=(pg == 0), stop=False),
                                     reads=[r_Eb[kb], r_VPb[kb]], writes=[r_ops[h]])
                            else:
                                s.op("pe", lambda e, h=h, mm=mm, oc=oc, pg=pg, kb=kb:
                                     e.matmul(ops_[h][0:4, oc:oc + 129], lhsT=Eb[kb][0:4, pg, h, mm, :], rhs=VPn[kb][0:4, h, 0:129],
                                              start=False, stop=True),
                                     reads=[r_Eb[kb], r_VPn[kb]], writes=[r_ops[h]])
                    kk2 = (b * 4 + h) % 2
                    self.diff_finalize(ops_[h], r_ops[h], rr[kk2], r_rr[kk2], a1[kk2], r_a1[kk2], junk, r_junk, lamt, subl, r_c,
                                       mixs[kb][0:4, h * 128:(h + 1) * 128], r_mixs[kb], 4)
                s.dma("sp", lambda e, kb=kb, b=b: e.dma_start(out=self.mixbuf[SEQ + 4 * b:SEQ + 4 * b + 4, 0:512], in_=mixs[kb][:]),
                      reads=[r_mixs[kb]], writes=[self.mixres[32]])
            s.barrier()

    def s5_setup(self, es, S):
        nc, s, di = self.nc, self.s, self.din
        T = lambda n, sh, dt: es.enter_context(nc.sbuf_tensor(self.name(n), sh, dt))
        PI = float(np.pi)
        r = S["r_c"] = Res()
        rc = [r]
        par = T("s5par", [128, 24, 16], F32)
        pari = T("s5pari", [128, 128], I32)
        S["par"] = par
        (A_RE, A_IM, LDT, DT, MAG, ANG, SN, CS, LRE, LIM, DEN, NR, FRE, FIM, T0, T1, NEGPI, TWOPI) = range(18)
        S["idx"] = dict(RHO=MAG, SN=SN, CS=CS, LRE=LRE, LIM=LIM)
        pv = lambda i: par[:, i, :]
        flat = lambda a: a.rearrange("g p -> (g p)").rearrange("(j q) -> q j", q=128)
        s.dma("sp", lambda e: e.dma_start(out=pv(A_RE), in_=flat(di["s5_a_re"]), allow_slow_non_contiguous=True), writes=rc)
        s.dma("sp", lambda e: e.dma_start(out=pv(A_IM), in_=flat(di["s5_a_im"]), allow_slow_non_contiguous=True), writes=rc)
        ld2 = di["s5_log_dt"].rearrange("o (j two) -> o two j", two=2)
        s.dma("sp", lambda e: e.dma_start(out=par[0:64, LDT, :], in_=ld2[0:1, 0, :].to_broadcast([64, 16]), allow_slow_non_contiguous=True), writes=rc)
        s.dma("sp", lambda e: e.dma_start(out=par[64:128, LDT, :], in_=ld2[0:1, 1, :].to_broadcast([64, 16]), allow_slow_non_contiguous=True), writes=rc)
        s.op("pool", lambda e: e.memset(pv(NEGPI), -PI), writes=rc)
        s.op("pool", lambda e: e.memset(pv(TWOPI), 2 * PI), writes=rc)
        V = lambda fn: s.op("dve", fn, reads=rc, writes=rc)
        Aop = lambda fn: s.op("act", fn, reads=rc, writes=rc)
        Aop(lambda e: e.activation(out=pv(DT), in_=pv(LDT), func=AF.Exp))
        V(lambda e: e.tensor_tensor(out=pv(T0), in0=pv(A_RE), in1=pv(DT), op=ALU.mult))
        Aop(lambda e: e.activation(out=pv(MAG), in_=pv(T0), func=AF.Exp))
        V(lambda e: e.tensor_tensor(out=pv(ANG), in0=pv(A_IM), in1=pv(DT), op=ALU.mult))

        def sincos(out_sin, out_cos, ang_ap, tmp_ap, tmpi_ap):
            TWO_PI_S = 2 * PI - 2e-6
            for dst, off in ((out_sin, 0.0), (out_cos, 0.25)):
                V(lambda e, off=off: e.tensor_scalar(out=tmp_ap, in0=ang_ap, scalar1=1.0 / (2 * PI), scalar2=off, op0=ALU.mult, op1=ALU.add))
                V(lambda e: e.tensor_copy(out=tmpi_ap, in_=tmp_ap))
                V(lambda e: e.tensor_tensor(out=tmp_ap, in0=tmp_ap, in1=tmpi_ap, op=ALU.subtract))
                Aop(lambda e, dst=dst: e.activation(out=dst, in_=tmp_ap, func=AF.Sin, scale=TWO_PI_S))
        S["sincos"] = sincos
        sincos(pv(SN), pv(CS), pv(ANG), pv(T0), pari[:, 0:16])
        V(lambda e: e.tensor_tensor(out=pv(LRE), in0=pv(MAG), in1=pv(CS), op=ALU.mult))
        V(lambda e: e.tensor_tensor(out=pv(LIM), in0=pv(MAG), in1=pv(SN), op=ALU.mult))
        V(lambda e: e.tensor_tensor(out=pv(DEN), in0=pv(A_RE), in1=pv(A_RE), op=ALU.mult))
        V(lambda e: e.tensor_tensor(out=pv(T0), in0=pv(A_IM), in1=pv(A_IM), op=ALU.mult))
        V(lambda e: e.tensor_tensor(out=pv(DEN), in0=pv(DEN), in1=pv(T0), op=ALU.add))
        V(lambda e: e.reciprocal(out=pv(DEN), in_=pv(DEN)))
        V(lambda e: e.tensor_scalar(out=pv(NR), in0=pv(LRE), scalar1=-1.0, scalar2=None, op0=ALU.add))
        V(lambda e: e.tensor_tensor(out=pv(T0), in0=pv(NR), in1=pv(A_RE), op=ALU.mult))
        V(lambda e: e.tensor_tensor(out=pv(T1), in0=pv(LIM), in1=pv(A_IM), op=ALU.mult))
        V(lambda e: e.tensor_tensor(out=pv(T0), in0=pv(T0), in1=pv(T1), op=ALU.add))
        V(lambda e: e.tensor_tensor(out=pv(FRE), in0=pv(T0), in1=pv(DEN), op=ALU.mult))
        V(lambda e: e.tensor_tensor(out=pv(T0), in0=pv(LIM), in1=pv(A_RE), op=ALU.mult))
        V(lambda e: e.tensor_tensor(out=pv(T1), in0=pv(NR), in1=pv(A_IM), op=ALU.mult))
        V(lambda e: e.tensor_tensor(out=pv(T0), in0=pv(T0), in1=pv(T1), op=ALU.subtract))
        V(lambda e: e.tensor_tensor(out=pv(FIM), in0=pv(T0), in1=pv(DEN), op=ALU.mult))
        M4 = T("M4", [128, 4, 8, 16], F32)
        s.op("pool", lambda e: e.memset(M4[:], 0.0), writes=rc)
        for v in range(4):
            s.op("pool", lambda e, v=v: e.memset(M4[0:64, v, 2 * v, :], 1.0), writes=rc)
            s.op("pool", lambda e, v=v: e.memset(M4[64:128, v, 2 * v + 1, :], 1.0), writes=rc)
        BBT = S["BBT"] = T("BBT", [128, 2, 16, 128], BF16)
        CX = S["CX"] = T("CX", [128, 2, 16, 128], BF16)
        with ExitStack() as es2:
            T2 = lambda n, sh, dt: es2.enter_context(nc.sbuf_tensor(self.name(n), sh, dt))
            P2 = lambda n, sh, dt: es2.enter_context(nc.psum_tensor(self.name(n), sh, dt))
            Bre = T2("Bre", [128, 16, 16], F32); Bim = T2("Bim", [128, 16, 16], F32)
            bbr = T2("bbr", [128, 16, 16], F32); bbi = T2("bbi", [128, 16, 16], F32); tt = T2("tt", [128, 16, 16], F32)
            BX = T2("BX", [128, 128], F32)
            Cnat = T2("Cnat", [128, 4, 128], F32)
            CT = [T2("CTd", [128, 4, 128], F32) for _ in range(2)]
            ptr = [P2("ptr", [128, 128], F32) for _ in range(2)]
            r_ptr = [Res(), Res()]; r_BX = Res()
            bflat = lambda a: a.rearrange("g p c -> (g p) c").rearrange("(j q) c -> q j c", q=128)
            s.dma("sp", lambda e: e.dma_start(out=Bre[:], in_=bflat(di["s5_b_re"])), writes=rc)
            s.dma("sp", lambda e: e.dma_start(out=Bim[:], in_=bflat(di["s5_b_im"])), writes=rc)
            fre_b = par[:, FRE, :].unsqueeze(2).to_broadcast([128, 16, 16])
            fim_b = par[:, FIM, :].unsqueeze(2).to_broadcast([128, 16, 16])
            V(lambda e: e.tensor_tensor(out=bbr[:], in0=Bre[:], in1=fre_b, op=ALU.mult))
            V(lambda e: e.tensor_tensor(out=tt[:], in0=Bim[:], in1=fim_b, op=ALU.mult))
            V(lambda e: e.tensor_tensor(out=bbr[:], in0=bbr[:], in1=tt[:], op=ALU.subtract))
            V(lambda e: e.tensor_tensor(out=bbi[:], in0=Bim[:], in1=fre_b, op=ALU.mult))
            V(lambda e: e.tensor_tensor(out=tt[:], in0=Bre[:], in1=fim_b, op=ALU.mult))
            V(lambda e: e.tensor_tensor(out=bbi[:], in0=bbi[:], in1=tt[:], op=ALU.add))
            it = 0
            for ri, bb in enumerate((bbr, bbi)):
                for j in range(16):
                    k = it % 2
                    it += 1
                    s.op("dve", lambda e, bb=bb, j=j: e.tensor_tensor(out=BX[:].rearrange("p (g c) -> p g c", g=8),
                                                                      in0=bb[:, j, :].unsqueeze(1).to_broadcast([128, 8, 16]),
                                                                      in1=M4[:, j % 4, :, :], op=ALU.mult),
                         reads=rc + [r_BX], writes=[r_BX])
                    s.op("pe", lambda e, k=k: e.transpose(out=ptr[k][:], in_=BX[:], identity=self.ident_f[:]),
                         reads=[r_BX, self.rconst], writes=[r_ptr[k]])
                    s.op("act", lambda e, k=k, ri=ri, j=j: e.copy(out=BBT[:, ri, j, :], in_=ptr[k][:]), reads=[r_ptr[k]], writes=rc)
            for ri, nm in enumerate(("s5_c_re", "s5_c_im")):
                cnatv = di[nm].rearrange("g c p -> (g c) p").rearrange("(m q) p -> q m p", q=128)
                s.dma("sp", lambda e, cnatv=cnatv: e.dma_start(out=Cnat[:, :, 0:64], in_=cnatv), writes=rc)
                s.dma("sp", lambda e, cnatv=cnatv: e.dma_start(out=Cnat[:, :, 64:128], in_=cnatv), writes=rc)
                for mch in range(4):
                    k = it % 2
                    it += 1
                    s.op("pe", lambda e, k=k, mch=mch: e.transpose(out=ptr[k][:], in_=Cnat[:, mch, :], identity=self.ident_f[:]),
                         reads=rc + [self.rconst], writes=[r_ptr[k]])
                    s.op("act", lambda e, k=k, ri=ri, mch=mch: e.copy(out=CT[ri][:, mch, :], in_=ptr[k][:]), reads=[r_ptr[k]], writes=rc)
                sgn = 1.0 if ri == 0 else -1.0
                for j in range(16):
                    V(lambda e, ri=ri, j=j, sgn=sgn: e.scalar_tensor_tensor(out=CX[:, ri, j, :], in0=CT[ri][:, j // 4, :], scalar=sgn,
                                                                            in1=M4[:, j % 4, :, :].rearrange("p g c -> p (g c)"),
                                                                            op0=ALU.mult, op1=ALU.mult))
            s.barrier()
        COS = S["COS"] = T("COS", [128, 16, 128], F32)
        SIN = S["SIN"] = T("SIN", [128, 16, 128], F32)
        jl = T("jl", [128, 128], F32)
        tb = T("tb", [128, 128], F32)
        tb2 = T("tb2", [128, 128], F32)
        jli = T("jli", [128, 128], I32)
        s.op("pool", lambda e: e.iota(jli[:], pattern=[[1, 128]], base=0, channel_multiplier=0), writes=rc)
        V(lambda e: e.tensor_copy(out=jl[:], in_=jli[:]))
        for j in range(16):
            V(lambda e, j=j: e.tensor_scalar(out=tb[:], in0=jl[:], scalar1=par[:, ANG, j:j + 1], scalar2=None, op0=ALU.mult))
            sincos(SIN[:, j, :], COS[:, j, :], tb[:], tb2[:], pari[:, :])
        dg = S["dg"] = T("dg", [128, 2, 4], F32)
        s.dma("sp", lambda e: e.dma_start(out=dg[:, 0, :], in_=di["s5_d"].rearrange("o (m q) -> q (o m)", q=128), allow_slow_non_contiguous=True), writes=rc)
        s.dma("sp", lambda e: e.dma_start(out=dg[:, 1, :], in_=di["s5_glu_b"].rearrange("o (m q) -> q (o m)", q=128), allow_slow_non_contiguous=True), writes=rc)
        gwf = T("gwf", [128, 4, 512], F32)
        gw = S["gw"] = T("gw", [128, 4, 512], BF16)
        s.dma("sp", lambda e: e.dma_start(out=gwf[:], in_=di["s5_glu_w"].rearrange("(m q) n -> q m n", q=128)), writes=rc)
        V(lambda e: e.tensor_copy(out=gw[:], in_=gwf[:]))

    def mixer0_B(self, cur, nxt):
        nc, s, di, do = self.nc, self.s, self.din, self.dout
        with ExitStack() as es:
            T = lambda n, sh, dt: es.enter_context(nc.sbuf_tensor(self.name(n), sh, dt))
            P = lambda n, sh, dt: es.enter_context(nc.psum_tensor(self.name(n), sh, dt))
            S = {}
            self.s5_setup(es, S)
            par, COS, SIN, BBT, CX, dg, gw = S["par"], S["COS"], S["SIN"], S["BBT"], S["CX"], S["dg"], S["gw"]
            ix = S["idx"]
            r_c = S["r_c"]
            m = self.alloc_mod(es)
            ewout = T("ewout", [128, NKC, D], BF16)
            xt = [T("xt", [128, D], F32) for _ in range(2)]
            ua = [T("ua", [128, 2, 512], BF16) for _ in range(2)]
            fT = [T("fT", [128, 8, 128], BF16) for _ in range(2)]
            y5T = [T("y5T", [128, 4, 128], BF16) for _ in range(2)]
            z = [T("z", [128, 4, 128], BF16) for _ in range(2)]
            NW = 6
            w = [T("w", [128, 4, 128], F32) for _ in range(NW)]
            gr = T("gr", [128, 4, 128], F32); gi = T("gi", [128, 4, 128], F32)
            hr = T("hr", [128, 4, 128], F32); hi = T("hi", [128, 4, 128], F32)
            hrb = [T("hrb", [128, 4, 128], BF16) for _ in range(2)]; hib = [T("hib", [128, 4, 128], BF16) for _ in range(2)]
            ysb = T("ysb", [128, 128], F32); ysq = T("ysq", [128, 128], F32); ysg = T("ysg", [128, 128], F32)
            carry = T("carry", [128, 4, 16], F32)
            ctmp = T("ctmp", [128, 2, 16], F32)
            t1 = [T("t1", [128, D], F32) for _ in range(2)]
            stt = [T("st", [128, 16], F32) for _ in range(2)]
            COSs = T("COSs", [128, 16, 64], F32); SINs = T("SINs", [128, 16, 64], F32); RHOm = T("RHOm", [128, 16, 64], F32)
            bmask = T("bmask", [128, 16, 4], F32)
            h0 = T("h0", [16, 2, 2048], F32); h0n = T("h0n", [128, 2, 16, 16], F32); inj = T("inj", [128, 2, 16, 16], F32)
            hout = T("hout", [128, 2, 16, 16], F32); houtT = T("houtT", [16, 2, 2048], F32)
            xre = P("xre", [128, 4, 128], F32); xim = P("xim", [128, 4, 128], F32)
            py = P("py", [128, 512], F32); pgl = P("pgl", [128, 512], F32)
            tpb = P("tpb", [128, 8, 128], BF16)
            po = [P("po", [128, 512], F32) for _ in range(2)]
            pmisc = P("pmisc", [128, 512], F32)
            r_ew = Res(); r_x = [Res(), Res()]; r_ua = [Res(), Res()]; r_fT = [Res(), Res()]; r_y5 = [Res(), Res()]; r_z = [Res(), Res()]
            r_w = [Res() for _ in range(NW)]; r_gr = Res(); r_gi = Res(); r_hr = Res(); r_hi = Res(); r_hb = [Res(), Res()]
            r_ys = Res(); r_carry = Res(); r_t1 = [Res(), Res()]; r_st = [Res(), Res()]
            r_xre = Res(); r_xim = Res(); r_py = Res(); r_pgl = Res(); r_tp = Res(); r_po = [Res(), Res()]; r_pm = Res(); r_s = Res()
            s.dma("sp", lambda e: e.dma_start(out=ewout[:], in_=self.ewout_b[:, :].rearrange("(kc p) n -> p kc n", p=128)), writes=[r_ew])
            self.load_mod(m, 0, 1, None)
            self.bcast_mod(m, False, pmisc, r_pm)
            s.op("pool", lambda e: e.memset(carry[:], 0.0), writes=[r_carry])
            Vs = lambda fn: s.op("dve", fn, reads=[r_c, r_s], writes=[r_s])
            Vs(lambda e: e.tensor_copy(out=COSs[:].rearrange("p j (b t) -> p j b t", t=4), in_=COS[:, :, 0:4].unsqueeze(2).to_broadcast([128, 16, 16, 4])))
            Vs(lambda e: e.tensor_copy(out=SINs[:].rearrange("p j (b t) -> p j b t", t=4), in_=SIN[:, :, 0:4].unsqueeze(2).to_broadcast([128, 16, 16, 4])))
            s.op("pool", lambda e: e.memset(bmask[:], 1.0), reads=[r_s], writes=[r_s])
            s.op("pool", lambda e: e.memset(bmask[:, :, 0:1], 0.0), reads=[r_s], writes=[r_s])
            Vs(lambda e: e.tensor_tensor(out=RHOm[:], in0=par[:, ix["RHO"], :].unsqueeze(2).to_broadcast([128, 16, 64]),
                                         in1=bmask[:].rearrange("p b t -> p (b t)").unsqueeze(1).to_broadcast([128, 16, 64]), op=ALU.mult))
            s.dma("sp", lambda e: e.dma_start(out=h0[:, 0, :], in_=di["state_s5_re"][:, :]), writes=[r_s])
            s.dma("sp", lambda e: e.dma_start(out=h0[:, 1, :], in_=di["state_s5_im"][:, :]), writes=[r_s])
            for ri in range(2):
                for j in range(16):
                    s.op("pe", lambda e, ri=ri, j=j: e.transpose(out=pmisc[:, (ri * 16 + j) * 16:(ri * 16 + j + 1) * 16],
                                                                 in_=h0[:, ri, j * 128:(j + 1) * 128], identity=self.ident_f[0:16, 0:16]),
                         reads=[r_s, self.rconst], writes=[r_pm])
            Vs2 = lambda fn: s.op("dve", fn, reads=[r_c, r_s, r_pm], writes=[r_s])
            Vs2(lambda e: e.tensor_copy(out=h0n[:].rearrange("p a j b -> p (a j b)"), in_=pmisc[:, 0:512]))
            lre_b = par[:, ix["LRE"], :].unsqueeze(2).to_broadcast([128, 16, 16])
            lim_b = par[:, ix["LIM"], :].unsqueeze(2).to_broadcast([128, 16, 16])
            Vs(lambda e: e.tensor_tensor(out=inj[:, 0], in0=h0n[:, 0], in1=lre_b, op=ALU.mult))
            Vs(lambda e: e.tensor_tensor(out=hout[:, 0], in0=h0n[:, 1], in1=lim_b, op=ALU.mult))
            Vs(lambda e: e.tensor_tensor(out=inj[:, 0], in0=inj[:, 0], in1=hout[:, 0], op=ALU.subtract))
            Vs(lambda e: e.tensor_tensor(out=inj[:, 1], in0=h0n[:, 1], in1=lre_b, op=ALU.mult))
            Vs(lambda e: e.tensor_tensor(out=hout[:, 0], in0=h0n[:, 0], in1=lim_b, op=ALU.mult))
            Vs(lambda e: e.tensor_tensor(out=inj[:, 1], in0=inj[:, 1], in1=hout[:, 0], op=ALU.add))
            iw = [0]

            def W():
                k = iw[0] % NW
                iw[0] += 1
                return w[k], r_w[k]

            for i in range(NTILE):
                sample = (i == 32)
                k = i % 2
                Wd = 64 if sample else 128
                if sample:
                    self.bcast_mod(m, True, pmisc, r_pm)
                ap, nrows, res = self.x_src(False, cur, i)
                rows = slice(i * 128, (i + 1) * 128)
                s.dma("sp", lambda e, k=k, ap=ap: e.dma_start(out=xt[k][:], in_=ap), reads=[res], writes=[r_x[k]])
                s.dma("sp", lambda e, k=k, rows=rows: e.dma_start(out=ua[k][:, 0, :], in_=self.ubuf[rows, :]), reads=[self.ures[i]], writes=[r_ua[k]])
                s.dma("sp", lambda e, k=k, rows=rows: e.dma_start(out=ua[k][:, 1, :], in_=self.mixbuf[rows, 0:512]), reads=[self.mixres[i]], writes=[r_ua[k]])
                for c in range(8):
                    s.op("pe", lambda e, k=k, c=c: e.transpose(out=tpb[:, c, :], in_=ua[k][:, c // 4, (c % 4) * 128:(c % 4 + 1) * 128],
                                                               identity=self.ident_b[:]), reads=[r_ua[k], self.rconst], writes=[r_tp])
                s.op("act", lambda e, k=k: e.copy(out=fT[k][:], in_=tpb[:]), reads=[r_tp], writes=[r_fT[k]])
                Ct = COSs if sample else COS
                St = SINs if sample else SIN
                for jg in range(4):
                    js = slice(jg * 4, jg * 4 + 4)
                    for jj in range(4):
                        j = jg * 4 + jj
                        s.op("pe", lambda e, k=k, j=j, jj=jj, jg=jg, Wd=Wd: e.matmul(xre[:, jj, 0:Wd], lhsT=BBT[:, 0, j, :], rhs=fT[k][:, jg, 0:Wd],
                                                                                   start=True, stop=True), reads=[r_c, r_fT[k]], writes=[r_xre])
                        s.op("pe", lambda e, k=k, j=j, jj=jj, jg=jg, Wd=Wd: e.matmul(xim[:, jj, 0:Wd], lhsT=BBT[:, 1, j, :], rhs=fT[k][:, jg, 0:Wd],
                                                                                   start=True, stop=True), reads=[r_c, r_fT[k]], writes=[r_xim])
                    wa, r_wa = W(); wb, r_wb = W(); xr, r_xr = W(); xi, r_xi = W()
                    Cv = Ct[:, js, 0:Wd]; Sv = St[:, js, 0:Wd]
                    rcs = [r_c, r_s]
                    s.op("dve", lambda e, wa=wa, Cv=Cv, Wd=Wd: e.tensor_tensor(out=wa[:, :, 0:Wd], in0=xre[:, :, 0:Wd], in1=Cv, op=ALU.mult), reads=[r_xre] + rcs, writes=[r_wa])
                    s.op("dve", lambda e, wb=wb, Sv=Sv, Wd=Wd: e.tensor_tensor(out=wb[:, :, 0:Wd], in0=xim[:, :, 0:Wd], in1=Sv, op=ALU.mult), reads=[r_xim] + rcs, writes=[r_wb])
                    s.op("pool", lambda e, wa=wa, wb=wb, xr=xr, Wd=Wd: e.tensor_tensor(out=xr[:, :, 0:Wd], in0=wa[:, :, 0:Wd], in1=wb[:, :, 0:Wd], op=ALU.add), reads=[r_wa, r_wb], writes=[r_xr])
                    wa2, r_wa2 = W(); wb2, r_wb2 = W()
                    s.op("dve", lambda e, wa2=wa2, Cv=Cv, Wd=Wd: e.tensor_tensor(out=wa2[:, :, 0:Wd], in0=xim[:, :, 0:Wd], in1=Cv, op=ALU.mult), reads=[r_xim] + rcs, writes=[r_wa2])
                    s.op("dve", lambda e, wb2=wb2, Sv=Sv, Wd=Wd: e.tensor_tensor(out=wb2[:, :, 0:Wd], in0=xre[:, :, 0:Wd], in1=Sv, op=ALU.mult), reads=[r_xre] + rcs, writes=[r_wb2])
                    s.op("pool", lambda e, wa2=wa2, wb2=wb2, xi=xi, Wd=Wd: e.tensor_tensor(out=xi[:, :, 0:Wd], in0=wa2[:, :, 0:Wd], in1=wb2[:, :, 0:Wd], op=ALU.subtract), reads=[r_wa2, r_wb2], writes=[r_xi])
                    if sample:
                        s.op("pool", lambda e, xr=xr, js=js: e.tensor_tensor(out=xr[:, :, 0:64].rearrange("p j (b t) -> p j b t", t=4)[:, :, :, 0],
                                                                             in0=xr[:, :, 0:64].rearrange("p j (b t) -> p j b t", t=4)[:, :, :, 0],
                                                                             in1=inj[:, 0, js, :], op=ALU.add), reads=[r_xr, r_s], writes=[r_xr])
                        s.op("pool", lambda e, xi=xi, js=js: e.tensor_tensor(out=xi[:, :, 0:64].rearrange("p j (b t) -> p j b t", t=4)[:, :, :, 0],
                                                                             in0=xi[:, :, 0:64].rearrange("p j (b t) -> p j b t", t=4)[:, :, :, 0],
                                                                             in1=inj[:, 1, js, :], op=ALU.add), reads=[r_xi, r_s], writes=[r_xi])
                    for jj in range(4):
                        j = jg * 4 + jj
                        if sample:
                            d0 = RHOm[:, j, :]
                            ini_r = 0.0
                            ini_i = 0.0
                        else:
                            d0 = par[:, ix["RHO"], j:j + 1].to_broadcast([128, 128])
                            ini_r = carry[:, 2, j:j + 1]
                            ini_i = carry[:, 3, j:j + 1]
                        s.op("dve", lambda e, xr=xr, jj=jj, d0=d0, ini_r=ini_r, Wd=Wd: e.tensor_tensor_scan(out=gr[:, jj, 0:Wd], data0=d0, data1=xr[:, jj, 0:Wd],
                                                                                                       initial=ini_r, op0=ALU.mult, op1=ALU.add),
                             reads=[r_xr, r_c, r_s, r_carry], writes=[r_gr])
                        s.op("dve", lambda e, xi=xi, jj=jj, d0=d0, ini_i=ini_i, Wd=Wd: e.tensor_tensor_scan(out=gi[:, jj, 0:Wd], data0=d0, data1=xi[:, jj, 0:Wd],
                                                                                                       initial=ini_i, op0=ALU.mult, op1=ALU.add),
                             reads=[r_xi, r_c, r_s, r_carry], writes=[r_gi])
                    wa, r_wa = W(); wb, r_wb = W()
                    s.op("dve", lambda e, wa=wa, Cv=Cv, Wd=Wd: e.tensor_tensor(out=wa[:, :, 0:Wd], in0=gr[:, :, 0:Wd], in1=Cv, op=ALU.mult), reads=[r_gr] + rcs, writes=[r_wa])
                    s.op("pool", lambda e, wb=wb, Sv=Sv, Wd=Wd: e.tensor_tensor(out=wb[:, :, 0:Wd], in0=gi[:, :, 0:Wd], in1=Sv, op=ALU.mult), reads=[r_gi] + rcs, writes=[r_wb])
                    s.op("dve", lambda e, wa=wa, wb=wb, Wd=Wd: e.tensor_tensor(out=hr[:, :, 0:Wd], in0=wa[:, :, 0:Wd], in1=wb[:, :, 0:Wd], op=ALU.subtract), reads=[r_wa, r_wb], writes=[r_hr])
                    wa2, r_wa2 = W(); wb2, r_wb2 = W()
                    s.op("pool", lambda e, wa2=wa2, Sv=Sv, Wd=Wd: e.tensor_tensor(out=wa2[:, :, 0:Wd], in0=gr[:, :, 0:Wd], in1=Sv, op=ALU.mult), reads=[r_gr] + rcs, writes=[r_wa2])
                    s.op("dve", lambda e, wb2=wb2, Cv=Cv, Wd=Wd: e.tensor_tensor(out=wb2[:, :, 0:Wd], in0=gi[:, :, 0:Wd], in1=Cv, op=ALU.mult), reads=[r_gi] + rcs, writes=[r_wb2])
                    s.op("dve", lambda e, wa2=wa2, wb2=wb2, Wd=Wd: e.tensor_tensor(out=hi[:, :, 0:Wd], in0=wa2[:, :, 0:Wd], in1=wb2[:, :, 0:Wd], op=ALU.add), reads=[r_wa2, r_wb2], writes=[r_hi])
                    kh = (i * 4 + jg) % 2
                    s.op("act", lambda e, kh=kh, Wd=Wd: e.copy(out=hrb[kh][:, :, 0:Wd], in_=hr[:, :, 0:Wd]), reads=[r_hr], writes=[r_hb[kh]])
                    s.op("act", lambda e, kh=kh, Wd=Wd: e.copy(out=hib[kh][:, :, 0:Wd], in_=hi[:, :, 0:Wd]), reads=[r_hi], writes=[r_hb[kh]])
                    if not sample:
                        s.op("pool", lambda e, js=js: e.tensor_copy(out=carry[:, 0, js], in_=hr[:, :, 127]), reads=[r_hr], writes=[r_carry])
                        s.op("pool", lambda e, js=js: e.tensor_copy(out=carry[:, 1, js], in_=hi[:, :, 127]), reads=[r_hi], writes=[r_carry])
                        cs_ = par[:, ix["CS"], js]; sn_ = par[:, ix["SN"], js]
                        s.op("dve", lambda e, js=js, cs_=cs_: e.tensor_tensor(out=ctmp[:, 0, js], in0=carry[:, 0, js], in1=cs_, op=ALU.mult), reads=[r_carry, r_c], writes=[r_carry])
                        s.op("dve", lambda e, js=js, sn_=sn_: e.tensor_tensor(out=ctmp[:, 1, js], in0=carry[:, 1, js], in1=sn_, op=ALU.mult), reads=[r_carry, r_c], writes=[r_carry])
                        s.op("dve", lambda e, js=js: e.tensor_tensor(out=carry[:, 2, js], in0=ctmp[:, 0, js], in1=ctmp[:, 1, js], op=ALU.subtract), reads=[r_carry], writes=[r_carry])
                        s.op("dve", lambda e, js=js, sn_=sn_: e.tensor_tensor(out=ctmp[:, 0, js], in0=carry[:, 0, js], in1=sn_, op=ALU.mult), reads=[r_carry, r_c], writes=[r_carry])
                        s.op("dve", lambda e, js=js, cs_=cs_: e.tensor_tensor(out=ctmp[:, 1, js], in0=carry[:, 1, js], in1=cs_, op=ALU.mult), reads=[r_carry, r_c], writes=[r_carry])
                        s.op("dve", lambda e, js=js: e.tensor_tensor(out=carry[:, 3, js], in0=ctmp[:, 0, js], in1=ctmp[:, 1, js], op=ALU.add), reads=[r_carry], writes=[r_carry])
                    else:
                        s.op("pool", lambda e, js=js: e.tensor_copy(out=hout[:, 0, js, :], in_=hr[:, :, 0:64].rearrange("p j (b t) -> p j b t", t=4)[:, :, :, 3]),
                             reads=[r_hr, r_s], writes=[r_s])
                        s.op("pool", lambda e, js=js: e.tensor_copy(out=hout[:, 1, js, :], in_=hi[:, :, 0:64].rearrange("p j (b t) -> p j b t", t=4)[:, :, :, 3]),
                             reads=[r_hi, r_s], writes=[r_s])
                    for jj in range(4):
                        j = jg * 4 + jj
                        s.op("pe", lambda e, kh=kh, j=j, jj=jj, Wd=Wd: e.matmul(py[:, 0:Wd], lhsT=CX[:, 0, j, :], rhs=hrb[kh][:, jj, 0:Wd],
                                                                                start=(jj == 0), stop=False), reads=[r_c, r_hb[kh]], writes=[r_py])
                        s.op("pe", lambda e, kh=kh, j=j, jj=jj, Wd=Wd: e.matmul(py[:, 0:Wd], lhsT=CX[:, 1, j, :], rhs=hib[kh][:, jj, 0:Wd],
                                                                                start=False, stop=(jj == 3)), reads=[r_c, r_hb[kh]], writes=[r_py])
                    s.op("dve", lambda e, k=k, jg=jg, Wd=Wd: e.scalar_tensor_tensor(out=ysb[:, 0:Wd], in0=fT[k][:, jg, 0:Wd], scalar=dg[:, 0, jg:jg + 1],
                                                                                     in1=py[:, 0:Wd], op0=ALU.mult, op1=ALU.add),
                         reads=[r_fT[k], r_py, r_c], writes=[r_ys])
                    s.op("pool", lambda e, Wd=Wd: e.tensor_tensor(out=ysq[:, 0:Wd], in0=ysb[:, 0:Wd], in1=ysb[:, 0:Wd], op=ALU.mult), reads=[r_ys], writes=[r_ys])
                    s.op("pool", lambda e, Wd=Wd: e.tensor_scalar(out=ysq[:, 0:Wd], in0=ysq[:, 0:Wd], scalar1=0.044715, scalar2=1.0, op0=ALU.mult, op1=ALU.add), reads=[r_ys], writes=[r_ys])
                    s.op("pool", lambda e, Wd=Wd: e.tensor_tensor(out=ysq[:, 0:Wd], in0=ysq[:, 0:Wd], in1=ysb[:, 0:Wd], op=ALU.mult), reads=[r_ys], writes=[r_ys])
                    s.op("act", lambda e, Wd=Wd: e.activation(out=ysg[:, 0:Wd], in_=ysq[:, 0:Wd], func=AF.Sigmoid, scale=1.5957691216057308), reads=[r_ys], writes=[r_ys])
                    s.op("dve", lambda e, k=k, jg=jg, Wd=Wd: e.tensor_tensor(out=z[k][:, jg, 0:Wd], in0=ysb[:, 0:Wd], in1=ysg[:, 0:Wd], op=ALU.mult), reads=[r_ys], writes=[r_z[k]])
                for mo in range(4):
                    for mi in range(4):
                        s.op("pe", lambda e, k=k, mo=mo, mi=mi, Wd=Wd: e.matmul(pgl[:, 0:Wd], lhsT=gw[:, mi, mo * 128:(mo + 1) * 128], rhs=z[k][:, mi, 0:Wd],
                                                                                start=(mi == 0), stop=(mi == 3)), reads=[r_c, r_z[k]], writes=[r_pgl])
                    s.op("act", lambda e, mo=mo, Wd=Wd: e.activation(out=ysg[:, 0:Wd], in_=pgl[:, 0:Wd], func=AF.Sigmoid, bias=dg[:, 1, mo:mo + 1], scale=1.0),
                         reads=[r_pgl, r_c, r_ys], writes=[r_ys])
                    s.op("dve", lambda e, k=k, mo=mo, Wd=Wd: e.tensor_tensor(out=y5T[k][:, mo, 0:Wd], in0=z[k][:, mo, 0:Wd], in1=ysg[:, 0:Wd], op=ALU.mult),
                         reads=[r_z[k], r_ys], writes=[r_y5[k]])
                if sample:
                    s.op("pool", lambda e, k=k: e.memset(y5T[k][:, :, 64:128], 0.0), reads=[r_y5[k]], writes=[r_y5[k]])
                for h in range(2):
                    for kc in range(8):
                        lhs = fT[k][:, 4 + kc, :] if kc < 4 else y5T[k][:, kc - 4, :]
                        rd = r_fT[k] if kc < 4 else r_y5[k]
                        s.op("pe", lambda e, h=h, kc=kc, lhs=lhs: e.matmul(po[h][:], lhsT=lhs, rhs=ewout[:, kc, h * 512:(h + 1) * 512],
                                                                           start=(kc == 0), stop=(kc == 7)), reads=[rd, r_ew], writes=[r_po[h]])
                self.epilogue(m, [po[0][:], po[1][:]], r_po, xt[k][:], r_x[k], t1[k], r_t1[k], stt[k], r_st[k],
                              self.x_dst(False, nxt, i), None)
            s.dma("pool", lambda e: e.dma_start(out=do["s5rp"].rearrange("o (j q) -> q (o j)", q=128), in_=carry[:, 0, :], allow_slow_non_contiguous=True), reads=[r_carry])
            s.dma("pool", lambda e: e.dma_start(out=do["s5ip"].rearrange("o (j q) -> q (o j)", q=128), in_=carry[:, 1, :], allow_slow_non_contiguous=True), reads=[r_carry])
            for ri in range(2):
                for j in range(16):
                    s.op("pe", lambda e, ri=ri, j=j: e.transpose(out=pmisc[0:16, ((ri * 16 + j) % 4) * 128:((ri * 16 + j) % 4 + 1) * 128],
                                                                 in_=hout[:, ri, j, :], identity=self.ident_f[:]),
                         reads=[r_s, self.rconst], writes=[r_pm])
                    s.op("act", lambda e, ri=ri, j=j: e.copy(out=houtT[:, ri, j * 128:(j + 1) * 128],
                                                             in_=pmisc[0:16, ((ri * 16 + j) % 4) * 128:((ri * 16 + j) % 4 + 1) * 128]),
                         reads=[r_pm], writes=[r_s])
            s.dma("pool", lambda e: e.dma_start(out=do["s5rs"][:, :], in_=houtT[:, 0, :]), reads=[r_s])
            s.dma("pool", lambda e: e.dma_start(out=do["s5is"][:, :], in_=houtT[:, 1, :]), reads=[r_s])
            s.barrier()

    def dil_masks(self, Wm_out, nfree, pattern, base, es2, r_c):
        nc, s = self.nc, self.s
        T2 = lambda n, sh, dt: es2.enter_context(nc.sbuf_tensor(self.name(n), sh, dt))
        Di = T2("Di", [128, nfree], I32); Df = T2("Df", [128, nfree], F32); Ti = T2("Ti", [128, nfree], I32)
        ge0 = T2("ge0", [128, nfree], F32); acc = T2("accm", [128, nfree], F32); tf = T2("tfm", [128, nfree], F32); t2 = T2("t2m", [128, nfree], F32)
        rc = [r_c]
        V = lambda fn: s.op("dve", fn, reads=rc, writes=rc)
        s.op("pool", lambda e: e.iota(Di[:], pattern=pattern, base=base, channel_multiplier=-1), reads=rc, writes=rc)
        V(lambda e: e.tensor_copy(out=Df[:], in_=Di[:]))
        V(lambda e: e.tensor_scalar(out=ge0[:], in0=Df[:], scalar1=0.0, scalar2=None, op0=ALU.is_ge))
        V(lambda e: e.tensor_scalar(out=acc[:], in0=Df[:], scalar1=128.0, scalar2=None, op0=ALU.is_le))
        for msk, lim in ((3, 512.0), (15, 2048.0)):
            V(lambda e, msk=msk: e.tensor_scalar(out=Ti[:], in0=Di[:], scalar1=msk, scalar2=None, op0=ALU.bitwise_and))
            V(lambda e: e.tensor_copy(out=tf[:], in_=Ti[:]))
            V(lambda e: e.tensor_scalar(out=tf[:], in0=tf[:], scalar1=0.0, scalar2=None, op0=ALU.is_equal))
            V(lambda e, lim=lim: e.tensor_scalar(out=t2[:], in0=Df[:], scalar1=lim, scalar2=None, op0=ALU.is_le))
            V(lambda e: e.tensor_tensor(out=tf[:], in0=tf[:], in1=t2[:], op=ALU.mult))
            V(lambda e: e.tensor_tensor(out=acc[:], in0=acc[:], in1=tf[:], op=ALU.add))
        V(lambda e: e.tensor_tensor(out=Wm_out, in0=acc[:], in1=ge0[:], op=ALU.mult))

    def mixer1_A(self, cur):
        nc, s, di, do = self.nc, self.s, self.din, self.dout
        ND = 17
        with ExitStack() as es:
            T = lambda n, sh, dt: es.enter_context(nc.sbuf_tensor(self.name(n), sh, dt))
            P = lambda n, sh, dt: es.enter_context(nc.psum_tensor(self.name(n), sh, dt))
            r_c = Res()
            Wm = T("Wm", [128, ND, 128], BF16)
            with ExitStack() as es2:
                self.dil_masks(Wm[:].rearrange("p a b -> p (a b)"), ND * 128, [[128, ND], [1, 128]], 0, es2, r_c)
                s.barrier()
            m = self.alloc_mod(es)
            owin = T("owin", [128, NKC, 3328], BF16)
            xt = [T("xt", [128, D], F32) for _ in range(2)]
            tmpf = T("tmpf", [128, D], F32)
            hb = [T("hb", [128, D], BF16) for _ in range(2)]
            hT = [T("hT", [128, NKC, 128], BF16) for _ in range(2)]
            kv_f = [T("kv_f", [128, 1024], F32) for _ in range(2)]
            pd_f1 = T("pd_f", [128, 1792], F32)
            pd_f = [pd_f1, pd_f1]
            q_b = [T("q_b", [128, 512], BF16) for _ in range(2)]
            k_b = [T("k_b", [128, 512], BF16) for _ in range(2)]
            qT = [T("qT", [128, 4, 128], BF16) for _ in range(2)]
            KT = T("KT1", [128, 4, SEQ], BF16)
            VP = T("VP1", [128, 32, 8, 66], BF16)
            E = [T("E", [128, 4, 128], BF16) for _ in range(3)]
            rr = [T("rr", [128, 2], F32) for _ in range(2)]
            mixf = [T("mixf", [128, 512], BF16) for _ in range(2)]
            pp = [P("pp", [128, 512], F32) for _ in range(4)]
            tpb = P("tpb", [128, NKC, 128], BF16)
            st = [P("st", [128, 512], F32) for _ in range(2)]
            oacc = P("oacc", [128, 512], F32)
            r_ow = Res(); r_x = [Res(), Res()]; r_tmpf = Res(); r_hb = [Res(), Res()]; r_hT = [Res(), Res()]
            r_kv = [Res(), Res()]; r_pd1 = Res(); r_pd = [r_pd1, r_pd1]; r_pp = [Res() for _ in range(4)]; r_tp = Res()
            r_q = [Res(), Res()]; r_k = [Res(), Res()]; r_qT = [Res(), Res()]
            r_KT = [Res() for _ in range(32)]; r_VP = [Res() for _ in range(32)]; r_E = [Res() for _ in range(3)]
            r_rr = [Res(), Res()]; r_mixf = [Res(), Res()]; r_st = [Res(), Res()]; r_oacc = Res()
            s.dma("sp", lambda e: e.dma_start(out=owin[:], in_=self.owin_b[:, :].rearrange("(kc p) n -> p kc n", p=128)), writes=[r_ow])
            s.op("pool", lambda e: e.memset(VP[:].rearrange("p a h e -> p (a h e)"), 1.0), writes=r_VP)
            self.load_mod(m, 1, 1, None)
            self.bcast_mod(m, False, pp[0], r_pp[0])
            widths = [512] * 6 + [256]
            ist = 0; ie = 0; ihd = 0
            for i in range(NTILE):
                sample = (i == 32)
                k = i % 2
                if sample:
                    self.bcast_mod(m, True, pp[0], r_pp[0])
                ap, nrows, res = self.x_src(False, cur, i)
                s.dma("sp", lambda e, k=k, ap=ap: e.dma_start(out=xt[k][:], in_=ap), reads=[res], writes=[r_x[k]])
                s.op("pool", lambda e, k=k: e.tensor_tensor(out=tmpf[:], in0=xt[k][:], in1=m["SC"][:], op=ALU.mult),
                     reads=[r_x[k], m["r_mod"]], writes=[r_tmpf])
                s.op("dve", lambda e, k=k: e.tensor_tensor(out=hb[k][:], in0=tmpf[:], in1=m["SH"][:], op=ALU.add),
                     reads=[r_tmpf, m["r_mod"]], writes=[r_hb[k]])
                for kc in range(NKC):
                    s.op("pe", lambda e, k=k, kc=kc: e.transpose(out=tpb[:, kc, :], in_=hb[k][:, kc * 128:(kc + 1) * 128],
                                                                 identity=self.ident_b[:]),
                         reads=[r_hb[k], self.rconst], writes=[r_tp])
                s.op("act", lambda e, k=k: e.copy(out=hT[k][:], in_=tpb[:]), reads=[r_tp], writes=[r_hT[k]])

                def proj(c, bank):
                    wdt = widths[c]
                    for kc in range(NKC):
                        s.op("pe", lambda e, kc=kc, k=k, wdt=wdt, c=c, bank=bank: e.matmul(pp[bank][:, 0:wdt], lhsT=hT[k][:, kc, :],
                                                             rhs=owin[:, kc, c * 512:c * 512 + wdt],
                                                             start=(kc == 0), stop=(kc == NKC - 1)),
                             reads=[r_hT[k], r_ow], writes=[r_pp[bank]])
                for c in range(3):
                    proj(c, c)
                s.op("act", lambda e, k=k: e.copy(out=q_b[k][:], in_=pp[0][:]), reads=[r_pp[0]], writes=[r_q[k]])
                s.op("dve", lambda e, k=k: e.tensor_copy(out=kv_f[k][:, 0:512], in_=pp[1][:]), reads=[r_pp[1]], writes=[r_kv[k]])
                s.op("dve", lambda e, k=k: e.tensor_copy(out=kv_f[k][:, 512:1024], in_=pp[2][:]), reads=[r_pp[2]], writes=[r_kv[k]])
                for c in range(3, 7):
                    proj(c, c - 3)
                    wdt = widths[c]
                    s.op("dve", lambda e, c=c, wdt=wdt, k=k: e.tensor_copy(out=pd_f[k][:, (c - 3) * 512:(c - 3) * 512 + wdt], in_=pp[c - 3][:, 0:wdt]),
                         reads=[r_pp[c - 3]], writes=[r_pd[k]])
                rows_all = slice(i * 128, (i + 1) * 128)
                s.dma("pool", lambda e, k=k, rows_all=rows_all: e.dma_start(out=self.pdbuf[rows_all, :], in_=pd_f[k][:]),
                      reads=[r_pd[k]], writes=[self.pdres[i]])
                if not sample:
                    if i >= 16:
                        rows = slice((i - 16) * 128, (i - 15) * 128)
                        s.dma("pool", lambda e, k=k, rows=rows: e.dma_start(out=do["ckp"][rows, :], in_=kv_f[k][:, 0:512]), reads=[r_kv[k]])
                        s.dma("pool", lambda e, k=k, rows=rows: e.dma_start(out=do["cvp"][rows, :], in_=kv_f[k][:, 512:1024]), reads=[r_kv[k]])
                    if i == 31:
                        s.dma("pool", lambda e, k=k: e.dma_start(out=do["dsp"][0:1, :], in_=pd_f[k][127:128, :]), reads=[r_pd[k]])
                else:
                    s.dma("pool", lambda e, k=k: e.dma_start(out=do["cks"][:, :], in_=kv_f[k][0:NS, 0:512]), reads=[r_kv[k]], writes=[self.r_cks])
                    s.dma("pool", lambda e, k=k: e.dma_start(out=do["cvs"][:, :], in_=kv_f[k][0:NS, 512:1024]), reads=[r_kv[k]], writes=[self.r_cks])
                    s.dma("pool", lambda e, k=k: e.dma_start(out=self.qsbuf[:, :], in_=q_b[k][0:NS, :]), reads=[r_q[k]], writes=[self.r_cks])
                    for b in range(NSB):
                        s.dma("pool", lambda e, b=b, k=k: e.dma_start(out=do["dss"][b:b + 1, :], in_=pd_f[k][4 * b + 3:4 * b + 4, :]), reads=[r_pd[k]])
                    continue
                if self.cfg.get("nodil"):
                    continue
                s.op("act", lambda e, k=k: e.copy(out=k_b[k][:], in_=kv_f[k][:, 0:512]), reads=[r_kv[k]], writes=[r_k[k]])
                s.op("act", lambda e, i=i, k=k: e.copy(out=VP[:, i, :, 0:64], in_=kv_f[k][:, 512:1024].rearrange("p (h e) -> p h e", h=8)),
                     reads=[r_kv[k]], writes=[r_VP[i]])
                for h in range(4):
                    s.op("pe", lambda e, k=k, h=h: e.transpose(out=tpb[:, h, :], in_=q_b[k][:, h * 128:(h + 1) * 128],
                                                               identity=self.ident_b[:]), reads=[r_q[k], self.rconst], writes=[r_tp])
                    s.op("pe", lambda e, k=k, h=h: e.transpose(out=tpb[:, 4 + h, :], in_=k_b[k][:, h * 128:(h + 1) * 128],
                                                               identity=self.ident_b[:]), reads=[r_k[k], self.rconst], writes=[r_tp])
                s.op("act", lambda e, k=k: e.copy(out=qT[k][:], in_=tpb[:, 0:4, :]), reads=[r_tp], writes=[r_qT[k]])
                s.op("act", lambda e, i=i: e.copy(out=KT[:, :, i * 128:(i + 1) * 128], in_=tpb[:, 4:8, :]), reads=[r_tp], writes=[r_KT[i]])
                nd = min(ND, i + 1)
                for h in range(8):
                    hp = h // 2
                    pr = slice((h % 2) * 64, (h % 2) * 64 + 64)
                    oc = (h % 2) * 66
                    for d0 in range(0, nd, 4):
                        ds_ = list(range(d0, min(d0 + 4, nd)))
                        sb = ist % 2; ist += 1
                        for jj, dlt in enumerate(ds_):
                            j = i - dlt
                            s.op("pe", lambda e, sb=sb, jj=jj, j=j, hp=hp, pr=pr, k=k:
                                 e.matmul(st[sb][:, jj * 128:(jj + 1) * 128], lhsT=KT[pr, hp, j * 128:(j + 1) * 128],
                                          rhs=qT[k][pr, hp, :], start=True, stop=True),
                                 reads=[r_KT[j], r_qT[k]], writes=[r_st[sb]])
                        eb = ie % 3; ie += 1
                        w = len(ds_)
                        s.op("act", lambda e, eb=eb, sb=sb, w=w: e.activation(out=E[eb][:, 0:w, :],
                                                                             in_=st[sb][:, 0:w * 128].rearrange("p (a b) -> p a b", a=w),
                                                                             func=AF.Exp, scale=0.125), reads=[r_st[sb]], writes=[r_E[eb]])
                        s.op("pool", lambda e, eb=eb, w=w, d0=d0: e.tensor_tensor(out=E[eb][:, 0:w, :], in0=E[eb][:, 0:w, :],
                                                                                   in1=Wm[:, d0:d0 + w, :], op=ALU.mult),
                             reads=[r_E[eb], r_c], writes=[r_E[eb]])
                        for jj, dlt in enumerate(ds_):
                            j = i - dlt
                            s.op("pe", lambda e, eb=eb, jj=jj, j=j, h=h, oc=oc, dlt=dlt, nd=nd:
                                 e.matmul(oacc[:, oc:oc + 65], lhsT=E[eb][:, jj, :], rhs=VP[:, j, h, 0:65],
                                          start=(dlt == 0), stop=(dlt == nd - 1)),
                                 reads=[r_E[eb], r_VP[j]], writes=[r_oacc])
                    kk = ihd % 2; ihd += 1
                    s.op("dve", lambda e, kk=kk, oc=oc: e.reciprocal(out=rr[kk][:, 0:1], in_=oacc[:, oc + 64:oc + 65]), reads=[r_oacc], writes=[r_rr[kk]])
                    s.op("dve", lambda e, kk=kk, oc=oc, h=h, k=k: e.tensor_scalar(out=mixf[k][:, h * 64:(h + 1) * 64], in0=oacc[:, oc:oc + 64],
                                                                                  scalar1=rr[kk][:, 0:1], scalar2=None, op0=ALU.mult),
                         reads=[r_oacc, r_rr[kk]], writes=[r_mixf[k]])
                s.dma("pool", lambda e, k=k, rows_all=rows_all: e.dma_start(out=self.mixbuf[rows_all, 0:512], in_=mixf[k][:]),
                      reads=[r_mixf[k]], writes=[self.mixres[i]])
            s.barrier()

    def mixer1_S(self):
        nc, s, di, do = self.nc, self.s, self.din, self.dout
        NPG = 16
        with ExitStack() as es:
            T = lambda n, sh, dt: es.enter_context(nc.sbuf_tensor(self.name(n), sh, dt))
            P = lambda n, sh, dt: es.enter_context(nc.psum_tensor(self.name(n), sh, dt))
            r_c = Res()
            Ws = T("Ws", [128, NPG + 1, 4], BF16)
            with ExitStack() as es2:
                self.dil_masks(Ws[:].rearrange("p a b -> p (a b)"), (NPG + 1) * 4, [[-128, NPG + 1], [1, 4]], 2048, es2, r_c)
                s.barrier()
            selrows = T("selrows", [64, 64, 128], BF16)
            q_s = T("q_s", [64, 512], BF16)
            Qbc = [T("Qbc", [128, 4, 512], BF16) for _ in range(2)]
            Kg = [T("Kg", [128, 512], F32) for _ in range(4)]
            Vg = [T("Vg", [128, 512], F32) for _ in range(3)]
            VPb = [T("VPb", [128, NPG, 8, 66], BF16) for _ in range(2)]
            Kn = [T("Kn", [4, 512], F32) for _ in range(2)]
            Vn = [T("Vn", [4, 512], F32) for _ in range(2)]
            VPn = [T("VPn", [4, 8, 66], BF16) for _ in range(2)]
            prod = [T("prod", [128, 512], F32) for _ in range(2)]
            Sc = [T("Sc", [128, NPG + 1, 8, 4], F32) for _ in range(2)]
            Eb = [T("Eb", [128, NPG + 1, 8, 4], BF16) for _ in range(2)]
            rr = [T("rr", [128, 2], F32) for _ in range(2)]
            mixs = [T("mixs", [4, 512], BF16) for _ in range(2)]
            qps = [P("qps", [128, 512], F32) for _ in range(2)]
            ops_ = [P("ops", [128, 512], F32) for _ in range(2)]
            r_Qbc = [Res(), Res()]; r_Kg = [Res() for _ in range(4)]; r_Vg = [Res() for _ in range(3)]
            r_VPb = [Res(), Res()]; r_Kn = [Res(), Res()]; r_Vn = [Res(), Res()]; r_VPn = [Res(), Res()]
            r_prod = [Res(), Res()]; r_Sc = [Res(), Res()]; r_Eb = [Res(), Res()]; r_rr = [Res(), Res()]
            r_mixs = [Res(), Res()]; r_qps = [Res(), Res()]; r_ops = [Res(), Res()]
            s.op("pool", lambda e: e.memset(selrows[:], 1.0), writes=[r_c])
            s.op("pool", lambda e: e.affine_select(out=selrows[:], in_=selrows[:], pattern=[[-1, 64], [0, 128]],
                                                    compare_op=ALU.is_equal, fill=0.0, base=0, channel_multiplier=1),
                 reads=[r_c], writes=[r_c])
            s.dma("sp", lambda e: e.dma_start(out=q_s[:], in_=self.qsbuf[:, :]), reads=[self.r_cks], writes=[r_c])
            for v in VPb:
                s.op("pool", lambda e, v=v: e.memset(v[:].rearrange("p a h e -> p (a h e)"), 1.0), writes=r_VPb)
            for v in VPn:
                s.op("pool", lambda e, v=v: e.memset(v[:].rearrange("p h e -> p (h e)"), 1.0), writes=r_VPn)
            for sc_ in Sc:
                s.op("pool", lambda e, sc_=sc_: e.memset(sc_[:].rearrange("p a h q -> p (a h q)"), 0.0), writes=r_Sc)
            ikg = 0; ivg = 0; ipr = 0
            cck = di["cache_c_k"]; ccv = di["cache_c_v"]
            for b in range(NSB):
                kb = b % 2
                for qi in range(4):
                    kq = (b * 4 + qi) % 2
                    s.op("pe", lambda e, kq=kq, b=b, qi=qi: e.matmul(qps[kq][:], lhsT=selrows[:, 4 * b + qi, :], rhs=q_s[:, :],
                                                                     start=True, stop=True), reads=[r_c], writes=[r_qps[kq]])
                    s.op("act", lambda e, kq=kq, kb=kb, qi=qi: e.copy(out=Qbc[kb][:, qi, :], in_=qps[kq][:]),
                         reads=[r_qps[kq]], writes=[r_Qbc[kb]])
                s.dma("sp", lambda e, kb=kb, b=b: e.dma_start(out=Kn[kb][:], in_=do["cks"][4 * b:4 * b + 4, :]),
                      reads=[self.r_cks], writes=[r_Kn[kb]])
                s.dma("sp", lambda e, kb=kb, b=b: e.dma_start(out=Vn[kb][:], in_=do["cvs"][4 * b:4 * b + 4, :]),
                      reads=[self.r_cks], writes=[r_Vn[kb]])
                s.op("act", lambda e, kb=kb: e.copy(out=VPn[kb][:, :, 0:64], in_=Vn[kb][:].rearrange("p (h e) -> p h e", h=8)),
                     reads=[r_Vn[kb]], writes=[r_VPn[kb]])
                for pg in range(NPG):
                    kv = ivg % 3; ivg += 1
                    r0 = b * 2048 + pg * 128
                    s.dma("sp", lambda e, kv=kv, r0=r0: e.dma_start(out=Vg[kv][:], in_=ccv[r0:r0 + 128, :]), writes=[r_Vg[kv]])
                    s.op("act", lambda e, kv=kv, kb=kb, pg=pg: e.copy(out=VPb[kb][:, pg, :, 0:64],
                                                                      in_=Vg[kv][:].rearrange("p (h e) -> p h e", h=8)),
                         reads=[r_Vg[kv]], writes=[r_VPb[kb]])
                for pg in range(NPG + 1):
                    if pg < NPG:
                        kk = ikg % 4; ikg += 1
                        r0 = b * 2048 + pg * 128
                        s.dma("sp", lambda e, kk=kk, r0=r0: e.dma_start(out=Kg[kk][:], in_=cck[r0:r0 + 128, :]), writes=[r_Kg[kk]])
                        ksrc, r_ks, npart = Kg[kk], r_Kg[kk], 128
                    else:
                        ksrc, r_ks, npart = Kn[kb], r_Kn[kb], 4
                    for qi in range(4):
                        kp = ipr % 2; ipr += 1
                        s.op("dve", lambda e, kp=kp, ksrc=ksrc, npart=npart, kb=kb, qi=qi:
                             e.tensor_tensor(out=prod[kp][0:npart, :], in0=ksrc[0:npart, :], in1=Qbc[kb][0:npart, qi, :], op=ALU.mult),
                             reads=[r_ks, r_Qbc[kb]], writes=[r_prod[kp]])
                        s.op("dve", lambda e, kp=kp, npart=npart, kb=kb, pg=pg, qi=qi:
                             e.tensor_reduce(out=Sc[kb][0:npart, pg, :, qi],
                                             in_=prod[kp][0:npart, :].rearrange("p (g d) -> p g d", d=64), axis=AX.X, op=ALU.add),
                             reads=[r_prod[kp]], writes=[r_Sc[kb]])
                s.op("act", lambda e, kb=kb: e.activation(out=Eb[kb][:].rearrange("p a h q -> p (a h q)"),
                                                          in_=Sc[kb][:].rearrange("p a h q -> p (a h q)"),
                                                          func=AF.Exp, scale=0.125), reads=[r_Sc[kb]], writes=[r_Eb[kb]])
                s.op("pool", lambda e, kb=kb: e.tensor_tensor(out=Eb[kb][:], in0=Eb[kb][:],
                                                               in1=Ws[:].unsqueeze(2).to_broadcast([128, NPG + 1, 8, 4]), op=ALU.mult),
                     reads=[r_Eb[kb], r_c], writes=[r_Eb[kb]])
                for h in range(8):
                    ob = h // 4
                    oc = (h % 4) * 66
                    for pg in range(NPG + 1):
                        if pg < NPG:
                            s.op("pe", lambda e, h=h, oc=oc, ob=ob, pg=pg, kb=kb:
                                 e.matmul(ops_[ob][0:4, oc:oc + 65], lhsT=Eb[kb][:, pg, h, :], rhs=VPb[kb][:, pg, h, 0:65],
                                          start=(pg == 0), stop=False), reads=[r_Eb[kb], r_VPb[kb]], writes=[r_ops[ob]])
                        else:
                            s.op("pe", lambda e, h=h, oc=oc, ob=ob, pg=pg, kb=kb:
                                 e.matmul(ops_[ob][0:4, oc:oc + 65], lhsT=Eb[kb][0:4, pg, h, :], rhs=VPn[kb][0:4, h, 0:65],
                                          start=False, stop=True), reads=[r_Eb[kb], r_VPn[kb]], writes=[r_ops[ob]])
                    k2 = (b * 8 + h) % 2
                    s.op("dve", lambda e, k2=k2, ob=ob, oc=oc: e.reciprocal(out=rr[k2][0:4, 0:1], in_=ops_[ob][0:4, oc + 64:oc + 65]),
                         reads=[r_ops[ob]], writes=[r_rr[k2]])
                    s.op("dve", lambda e, k2=k2, ob=ob, oc=oc, h=h, kb=kb: e.tensor_scalar(out=mixs[kb][0:4, h * 64:(h + 1) * 64], in0=ops_[ob][0:4, oc:oc + 64],
                                                                                           scalar1=rr[k2][0:4, 0:1], scalar2=None, op0=ALU.mult),
                         reads=[r_ops[ob], r_rr[k2]], writes=[r_mixs[kb]])
                s.dma("sp", lambda e, kb=kb, b=b: e.dma_start(out=self.mixbuf[SEQ + 4 * b:SEQ + 4 * b + 4, 0:512], in_=mixs[kb][:]),
                      reads=[r_mixs[kb]], writes=[self.mixres[32]])
            s.barrier()

    def mixer1_P(self):
        nc, s, di, do = self.nc, self.s, self.din, self.dout
        with ExitStack() as es:
            T = lambda n, sh, dt: es.enter_context(nc.sbuf_tensor(self.name(n), sh, dt))
            P = lambda n, sh, dt: es.enter_context(nc.psum_tensor(self.name(n), sh, dt))
            MU = T("MU", [128, 1792], F32)
            CB = T("CB", [128, 5, 512], F32)
            W12 = T("W12", [128, 512], F32); G2 = T("G2", [128, 512], F32)
            pd = [T("pdt", [128, 1792], F32) for _ in range(2)]
            pv = [T("pvt", [128, 1792], F32) for _ in range(2)]
            xm = [T("xm", [128, 1792], F32) for _ in range(2)]
            X3 = T("X3", [128, 256], F32); T12 = T("T12", [128, 2, 128], F32)
            o = [T("rwo", [128, 8, 512], F32) for _ in range(2)]
            tA = T("tA", [128, 512], F32); tB = T("tB", [128, 512], F32); av = T("av", [128, 512], F32)
            sm = T("sm", [128, 32], F32)
            ptr = [P("ptr", [128, 128], F32) for _ in range(2)]
            pL = [P("pL", [128, 512], F32) for _ in range(3)]
            r_c = Res(); r_pd = [Res(), Res()]; r_pv = [Res(), Res()]; r_xm = [Res(), Res()]; r_X3 = Res(); r_T12 = Res()
            r_o = [Res(), Res()]; r_t = Res(); r_ptr = [Res(), Res()]; r_pL = [Res() for _ in range(3)]
            bc = lambda ap, n: ap.to_broadcast([128, n])
            s.dma("sp", lambda e: e.dma_start(out=MU[:], in_=bc(di["rwkv_mu"][0:1, :], 1792)), writes=[r_c])
            for j, nm in enumerate(("rwkv_w0", "rwkv_a0", "rwkv_k_k", "rwkv_k_a", "rwkv_r_k")):
                s.dma("sp", lambda e, j=j, nm=nm: e.dma_start(out=CB[:, j, :], in_=bc(di[nm][0:1, :], 512)), writes=[r_c])
            s.dma("sp", lambda e: e.dma_start(out=W12[0:64, :], in_=di["rwkv_w2"][:, :]), writes=[r_c])
            s.dma("sp", lambda e: e.dma_start(out=W12[64:128, :], in_=di["rwkv_a2"][:, :]), writes=[r_c])
            s.dma("sp", lambda e: e.dma_start(out=G2[:], in_=di["rwkv_g2"][:, :]), writes=[r_c])
            for b_ in pv:
                s.op("pool", lambda e, b_=b_: e.memset(b_[:], 0.0), writes=r_pv)
            for i in range(NTILE):
                k = i % 2
                sample = (i == 32)
                rows = slice(i * 128, (i + 1) * 128)
                s.dma("sp", lambda e, k=k, rows=rows: e.dma_start(out=pd[k][:], in_=self.pdbuf[rows, :]), reads=[self.pdres[i]], writes=[r_pd[k]])
                if i == 0:
                    s.dma("sp", lambda e, k=k: e.dma_start(out=pv[k][1:128, :], in_=self.pdbuf[0:127, :]), reads=[self.pdres[0]], writes=[r_pv[k]])
                elif not sample:
                    s.dma("sp", lambda e, k=k, i=i: e.dma_start(out=pv[k][:, :], in_=self.pdbuf[i * 128 - 1:i * 128 + 127, :]),
                          reads=[self.pdres[i], self.pdres[i - 1]], writes=[r_pv[k]])
                else:
                    s.dma("sp", lambda e, k=k: e.dma_start(out=pv[k][1:64, :], in_=self.pdbuf[SEQ:SEQ + 63, :]), reads=[self.pdres[32]], writes=[r_pv[k]])
                    for b in range(NSB):
                        s.dma("sp", lambda e, k=k, b=b: e.dma_start(out=pv[k][4 * b:4 * b + 1, :], in_=di["state_d_shift"][b:b + 1, :]), writes=[r_pv[k]])
                X, PD, PV_ = xm[k], pd[k], pv[k]
                s.op("pool", lambda e, X=X, PD=PD, PV_=PV_: e.tensor_tensor(out=X[:], in0=PV_[:], in1=PD[:], op=ALU.subtract), reads=[r_pd[k], r_pv[k]], writes=[r_xm[k]])
                s.op("dve", lambda e, X=X: e.tensor_tensor(out=X[:], in0=X[:], in1=MU[:], op=ALU.mult), reads=[r_xm[k], r_c], writes=[r_xm[k]])
                s.op("pool", lambda e, X=X, PD=PD: e.tensor_tensor(out=X[:], in0=X[:], in1=PD[:], op=ALU.add), reads=[r_xm[k], r_pd[k]], writes=[r_xm[k]])
                s.op("act", lambda e, X=X: e.activation(out=X3[:, 0:64], in_=X[:, 1536:1600], func=AF.Tanh), reads=[r_xm[k]], writes=[r_X3])
                s.op("act", lambda e, X=X: e.copy(out=X3[:, 64:128], in_=X[:, 1600:1664]), reads=[r_xm[k]], writes=[r_X3])
                s.op("act", lambda e, X=X: e.activation(out=X3[:, 128:256], in_=X[:, 1664:1792], func=AF.Sigmoid), reads=[r_xm[k]], writes=[r_X3])
                for j in range(2):
                    s.op("pe", lambda e, j=j: e.transpose(out=ptr[j][:], in_=X3[:, j * 128:(j + 1) * 128], identity=self.ident_f[:]),
                         reads=[r_X3, self.rconst], writes=[r_ptr[j]])
                    s.op("dve", lambda e, j=j: e.tensor_copy(out=T12[:, j, :], in_=ptr[j][:]), reads=[r_ptr[j]], writes=[r_T12])
                s.op("pe", lambda e: e.matmul(pL[0][:], lhsT=T12[0:64, 0, :], rhs=W12[0:64, :], start=True, stop=True), reads=[r_T12, r_c], writes=[r_pL[0]])
                s.op("pe", lambda e: e.matmul(pL[1][:], lhsT=T12[64:128, 0, :], rhs=W12[64:128, :], start=True, stop=True), reads=[r_T12, r_c], writes=[r_pL[1]])
                s.op("pe", lambda e: e.matmul(pL[2][:], lhsT=T12[:, 1, :], rhs=G2[:, :], start=True, stop=True), reads=[r_T12, r_c], writes=[r_pL[2]])
                O = o[k]
                ro = [r_o[k]]
                rt = [r_t]
                s.op("dve", lambda e: e.tensor_tensor(out=tA[:], in0=pL[0][:], in1=CB[:, 0, :], op=ALU.add), reads=[r_pL[0], r_c], writes=rt)
                s.op("act", lambda e: e.activation(out=tA[:], in_=tA[:], func=AF.Sigmoid), reads=rt, writes=rt)
                s.op("act", lambda e, O=O: e.activation(out=O[:, 1, :], in_=tA[:], func=AF.Exp, scale=-0.6065306597126334), reads=rt, writes=ro)
                s.op("dve", lambda e: e.tensor_tensor(out=av[:], in0=pL[1][:], in1=CB[:, 1, :], op=ALU.add), reads=[r_pL[1], r_c], writes=rt)
                s.op("act", lambda e: e.activation(out=av[:], in_=av[:], func=AF.Sigmoid), reads=rt, writes=rt)
                s.op("act", lambda e, O=O: e.copy(out=O[:, 6, :], in_=pL[2][:]), reads=[r_pL[2]], writes=ro)
                s.op("act", lambda e, O=O, X=X: e.copy(out=O[:, 0, :], in_=X[:, 0:512]), reads=[r_xm[k]], writes=ro)
                s.op("act", lambda e, O=O, X=X: e.copy(out=O[:, 3, :], in_=X[:, 1024:1536]), reads=[r_xm[k]], writes=ro)
                s.op("pool", lambda e, X=X: e.tensor_tensor(out=tA[:], in0=X[:, 512:1024], in1=CB[:, 2, :], op=ALU.mult), reads=[r_xm[k], r_c] + rt, writes=rt)
                s.op("pool", lambda e: e.tensor_tensor(out=tB[:], in0=tA[:], in1=tA[:], op=ALU.mult), reads=rt, writes=rt)
                s.op("dve", lambda e: e.tensor_reduce(out=sm[:, 0:8], in_=tB[:].rearrange("p (h k) -> p h k", h=8), axis=AX.X, op=ALU.add), reads=rt, writes=rt)
                s.op("act", lambda e: e.activation(out=sm[:, 8:16], in_=sm[:, 0:8], func=AF.Sqrt), reads=rt, writes=rt)
                s.op("dve", lambda e: e.tensor_scalar(out=sm[:, 8:16], in0=sm[:, 8:16], scalar1=1e-12, scalar2=None, op0=ALU.max), reads=rt, writes=rt)
                s.op("dve", lambda e: e.reciprocal(out=sm[:, 16:24], in_=sm[:, 8:16]), reads=rt, writes=rt)
                s.op("dve", lambda e, O=O: e.tensor_tensor(out=O[:, 4, :].rearrange("p (h k) -> p h k", h=8), in0=tA[:].rearrange("p (h k) -> p h k", h=8),
                                                           in1=sm[:, 16:24].unsqueeze(2).to_broadcast([128, 8, 64]), op=ALU.mult), reads=rt, writes=ro)
                s.op("dve", lambda e: e.scalar_tensor_tensor(out=tB[:], in0=av[:], scalar=-1.0, in1=CB[:, 3, :], op0=ALU.add, op1=ALU.mult), reads=rt + [r_c], writes=rt)
                s.op("dve", lambda e, O=O, X=X: e.scalar_tensor_tensor(out=O[:, 2, :], in0=tB[:], scalar=1.0, in1=X[:, 512:1024], op0=ALU.add, op1=ALU.mult),
                     reads=rt + [r_xm[k]], writes=ro)
                s.op("dve", lambda e, O=O: e.scalar_tensor_tensor(out=O[:, 5, :], in0=O[:, 4, :], scalar=-1.0, in1=av[:], op0=ALU.mult, op1=ALU.mult), reads=rt + ro, writes=ro)
                s.op("pool", lambda e, O=O: e.tensor_tensor(out=tA[:], in0=O[:, 0, :], in1=O[:, 2, :], op=ALU.mult), reads=ro + rt, writes=rt)
                s.op("pool", lambda e: e.tensor_tensor(out=tA[:], in0=tA[:], in1=CB[:, 4, :], op=ALU.mult), reads=rt + [r_c], writes=rt)
                s.op("dve", lambda e: e.tensor_reduce(out=sm[:, 24:32], in_=tA[:].rearrange("p (h k) -> p h k", h=8), axis=AX.X, op=ALU.add), reads=rt, writes=rt)
                s.op("dve", lambda e, O=O: e.tensor_tensor(out=O[:, 7, :].rearrange("p (h k) -> p h k", h=8), in0=O[:, 3, :].rearrange("p (h k) -> p h k", h=8),
                                                           in1=sm[:, 24:32].unsqueeze(2).to_broadcast([128, 8, 64]), op=ALU.mult), reads=rt + ro, writes=ro)
                s.dma("pool", lambda e, O=O, rows=rows: e.dma_start(out=self.rw[:, rows, :].rearrange("a t c -> t a c"), in_=O[:]),
                      reads=ro, writes=[self.rwres[i]])
            s.barrier()

    def mixer1_R(self):
        nc, s, di, do = self.nc, self.s, self.din, self.dout
        with ExitStack() as es:
            T = lambda n, sh, dt: es.enter_context(nc.sbuf_tensor(self.name(n), sh, dt))
            P = lambda n, sh, dt: es.enter_context(nc.psum_tensor(self.name(n), sh, dt))
            bm = T("bm8", [8, 8, 64], F32)
            A = T("Ast", [64, 8, 64], F32)
            tok = [T("tokm", [128, 3, 512], F32) for _ in range(2)]
            cols = [T("cols", [64, 3, 128, 8], F32) for _ in range(2)]
            hm1 = T("hmK", [40, 2, 128, 64], F32)
            hm = [hm1, hm1]
            bm40 = T("bm40", [40, 8, 64], F32)
            UV = [T("UV", [40, 512], F32) for _ in range(2)]
            Ym = [T("Ym", [8, 8, 64], F32) for _ in range(2)]
            yb1 = T("yb", [8, 128, 64], F32)
            yb = [yb1, yb1]
            Sio = T("Sio", [64, 8, 64], F32)
            ptr = [P("ptr", [64, 4, 128], F32) for _ in range(2)]
            pu = [P("pu", [8, 512], F32) for _ in range(2)]
            pdA = [P("pdA", [64, 512], F32) for _ in range(2)]
            py = [P("py", [8, 512], F32) for _ in range(2)]
            r_c = Res(); r_A = Res(); r_tok = [Res(), Res()]; r_cols = [Res(), Res()]; r_hm1 = Res(); r_hm = [r_hm1, r_hm1]
            r_Um = [Res(), Res()]; r_Vm = [Res(), Res()]; r_Ym = [Res(), Res()]; r_yb1 = Res(); r_yb = [r_yb1, r_yb1]; r_S = Res()
            r_ptr = [Res(), Res()]; r_pu = [Res(), Res()]; r_pdA = [Res(), Res()]; r_py = [Res(), Res()]
            s.op("pool", lambda e: e.memset(bm[:], 1.0), writes=[r_c])
            s.op("pool", lambda e: e.affine_select(out=bm[:], in_=bm[:], pattern=[[-1, 8], [0, 64]], compare_op=ALU.is_equal, fill=0.0,
                                                    base=0, channel_multiplier=1), reads=[r_c], writes=[r_c])
            s.op("pool", lambda e: e.memset(A[:], 0.0), writes=[r_A])
            s.op("pool", lambda e: e.memset(hm1[:], 0.0), writes=[r_hm1])
            s.op("pool", lambda e: e.memset(bm40[:], 1.0), writes=[r_c])
            s.op("pool", lambda e: e.affine_select(out=bm40[:], in_=bm40[:], pattern=[[-1, 8], [0, 64]], compare_op=ALU.is_equal, fill=0.0,
                                                    base=-32, channel_multiplier=1), reads=[r_c], writes=[r_c])
            for c_ in range(2):
                s.op("pool", lambda e, c_=c_: e.memset(UV[c_][:], 0.0), writes=[r_Um[c_], r_Vm[c_]])
            Af = A[:].rearrange("p h v -> p (h v)")
            bmf = bm[:].rearrange("p h v -> p (h v)")
            cnt = [0]

            def load_tile(i, k):
                rows = slice(i * 128, (i + 1) * 128)
                for j, src in enumerate((4, 0, 1)):
                    s.dma("sp", lambda e, j=j, src=src: e.dma_start(out=tok[k][:, j, :], in_=self.rw[src, rows, :]), reads=[self.rwres[i]], writes=[r_tok[k]])
                for j, ro, src in ((0, 0, 5), (0, 32, 2), (1, 32, 3)):
                    s.dma("sp", lambda e, j=j, ro=ro, src=src: e.dma_start(out=hm[k][ro:ro + 8, j, :, :], in_=self.rw[src, rows, :].rearrange("t (h c) -> h t c", h=8)),
                          reads=[self.rwres[i]], writes=[r_hm[k]])
                it = 0
                for j in range(3):
                    for h0 in (0, 4):
                        kp = it % 2; it += 1
                        for hh in range(4):
                            h = h0 + hh
                            s.op("pe", lambda e, j=j, h=h, hh=hh, kp=kp: e.transpose(out=ptr[kp][:, hh, :], in_=tok[k][:, j, h * 64:(h + 1) * 64], identity=self.ident_f[:]),
                                 reads=[r_tok[k], self.rconst], writes=[r_ptr[kp]])
                        s.op("act", lambda e, j=j, h0=h0, kp=kp: e.copy(out=cols[k][:, j, :, h0:h0 + 4], in_=ptr[kp][:].rearrange("p h t -> p t h")),
                             reads=[r_ptr[kp]], writes=[r_cols[k]])

            def step(k, t):
                c = cnt[0] % 2
                cnt[0] += 1
                s.op("pe", lambda e: e.matmul(pu[c][:], lhsT=cols[k][:, 0, t, :], rhs=Af, start=True, stop=True), reads=[r_cols[k], r_A], writes=[r_pu[c]])
                s.op("dve", lambda e: e.tensor_tensor(out=UV[c][0:8, :], in0=pu[c][:], in1=bmf, op=ALU.mult), reads=[r_pu[c], r_c], writes=[r_Um[c]])
                s.op("pool", lambda e: e.tensor_tensor(out=UV[c][32:40, :].rearrange("p (h v) -> p h v", h=8),
                                                       in0=hm[k][32:40, 1, t, :].unsqueeze(1).to_broadcast([8, 8, 64]),
                                                       in1=bm40[32:40, :, :], op=ALU.mult), reads=[r_hm[k], r_c], writes=[r_Vm[c]])
                s.op("pe", lambda e: e.matmul(pdA[c][:], lhsT=hm[k][0:40, 0, t, :], rhs=UV[c][0:40, :], start=True, stop=True),
                     reads=[r_hm[k], r_Um[c], r_Vm[c]], writes=[r_pdA[c]])
                s.op("dve", lambda e: e.tensor_tensor(out=A[:], in0=A[:], in1=cols[k][:, 2, t, :].unsqueeze(2).to_broadcast([64, 8, 64]), op=ALU.mult),
                     reads=[r_A, r_cols[k]], writes=[r_A])
                s.op("dve", lambda e: e.tensor_tensor(out=Af, in0=Af, in1=pdA[c][:], op=ALU.add), reads=[r_A, r_pdA[c]], writes=[r_A])
                s.op("pe", lambda e: e.matmul(py[c][:], lhsT=cols[k][:, 1, t, :], rhs=Af, start=True, stop=True), reads=[r_cols[k], r_A], writes=[r_py[c]])
                s.op("dve", lambda e: e.tensor_tensor(out=Ym[c][:].rearrange("p h v -> p (h v)"), in0=py[c][:], in1=bmf, op=ALU.mult), reads=[r_py[c], r_c], writes=[r_Ym[c]])
                s.op("dve", lambda e: e.tensor_reduce(out=yb[k][:, t, :], in_=Ym[c][:].rearrange("p h v -> p v h"), axis=AX.X, op=ALU.add),
                     reads=[r_Ym[c]], writes=[r_yb[k]])

            def store_y(i, k, nt):
                rows = slice(i * 128, i * 128 + nt)
                s.dma("pool", lambda e: e.dma_start(out=self.ybuf[rows, :].rearrange("t (h v) -> h t v", h=8), in_=yb[k][:, 0:nt, :]),
                      reads=[r_yb[k]], writes=[self.yres[i]])

            def state_out(dst_ap):
                for h0 in (0, 4):
                    kp = (h0 // 4)
                    for hh in range(4):
                        s.op("pe", lambda e, hh=hh, h0=h0, kp=kp: e.transpose(out=ptr[kp][:, hh, 0:64], in_=A[:, h0 + hh, :], identity=self.ident_f[0:64, 0:64]),
                             reads=[r_A, self.rconst], writes=[r_ptr[kp]])
                    s.op("act", lambda e, h0=h0, kp=kp: e.copy(out=Sio[:, h0:h0 + 4, :], in_=ptr[kp][:, :, 0:64]), reads=[r_ptr[kp]], writes=[r_S])
                s.dma("pool", lambda e: e.dma_start(out=dst_ap.rearrange("(h v) c -> v h c", h=8), in_=Sio[:]), reads=[r_S])

            ntp = self.cfg.get("rw_tiles", 32)
            for i in range(ntp):
                k = i % 2
                load_tile(i, k)
                for t in range(128):
                    step(k, t)
                store_y(i, k, 128)
            state_out(do["dwp"][:, :])
            k = 0
            load_tile(32, k)
            for b in range(NSB):
                s.dma("sp", lambda e, b=b: e.dma_start(out=Sio[:], in_=di["state_d_wkv"][b * 512:(b + 1) * 512, :].rearrange("(h v) c -> v h c", h=8)),
                      reads=[r_S], writes=[r_S])
                for h0 in (0, 4):
                    kp = (h0 // 4)
                    for hh in range(4):
                        s.op("pe", lambda e, hh=hh, h0=h0, kp=kp: e.transpose(out=ptr[kp][:, hh, 0:64], in_=Sio[:, h0 + hh, :], identity=self.ident_f[0:64, 0:64]),
                             reads=[r_S, self.rconst], writes=[r_ptr[kp]])
                    s.op("act", lambda e, h0=h0, kp=kp: e.copy(out=A[:, h0:h0 + 4, :], in_=ptr[kp][:, :, 0:64]), reads=[r_ptr[kp]], writes=[r_A])
                for j in range(4):
                    step(k, 4 * b + j)
                state_out(do["dws"][b * 512:(b + 1) * 512, :])
            store_y(32, k, 64)
            s.barrier()

    def mixer1_Q(self, cur, nxt):
        nc, s, di, do = self.nc, self.s, self.din, self.dout
        GN_EPS = 64e-5
        with ExitStack() as es:
            T = lambda n, sh, dt: es.enter_context(nc.sbuf_tensor(self.name(n), sh, dt))
            P = lambda n, sh, dt: es.enter_context(nc.psum_tensor(self.name(n), sh, dt))
            m = self.alloc_mod(es)
            owout = T("owout", [128, NKC, D], BF16)
            GN = T("GNc", [128, 2, 512], F32)
            gne = T("gne", [128, 1], F32)
            xt = [T("xt", [128, D], F32) for _ in range(2)]
            yv = [T("yv", [128, 3, 512], F32) for _ in range(2)]
            ft = [T("ft", [128, 2, 512], BF16) for _ in range(2)]
            fT = [T("fT", [128, 8, 128], BF16) for _ in range(2)]
            ta = T("ta", [128, 512], F32); tb = T("tb", [128, 512], F32); sm = T("smq", [128, 32], F32)
            t1 = [T("t1", [128, D], F32) for _ in range(2)]
            stt = [T("st", [128, 16], F32) for _ in range(2)]
            tpb = P("tpb", [128, 8, 128], BF16)
            po = [P("po", [128, 512], F32) for _ in range(2)]
            pmisc = P("pmisc", [128, 512], F32)
            r_c = Res(); r_x = [Res(), Res()]; r_yv = [Res(), Res()]; r_ft = [Res(), Res()]; r_fT = [Res(), Res()]; r_t = Res()
            r_t1 = [Res(), Res()]; r_st = [Res(), Res()]; r_tp = Res(); r_po = [Res(), Res()]; r_pm = Res(); r_ow = Res()
            s.dma("sp", lambda e: e.dma_start(out=owout[:], in_=self.owout_b[:, :].rearrange("(kc p) n -> p kc n", p=128)), writes=[r_ow])
            s.dma("sp", lambda e: e.dma_start(out=GN[:, 0, :], in_=di["rwkv_gn_w"][0:1, :].to_broadcast([128, 512])), writes=[r_c])
            s.dma("sp", lambda e: e.dma_start(out=GN[:, 1, :], in_=di["rwkv_gn_b"][0:1, :].to_broadcast([128, 512])), writes=[r_c])
            s.op("pool", lambda e: e.memset(gne[:], GN_EPS), writes=[r_c])
            self.load_mod(m, 1, 1, None)
            self.bcast_mod(m, False, pmisc, r_pm)
            rt = [r_t]
            v3 = lambda ap: ap.rearrange("p (h k) -> p h k", h=8)
            for i in range(NTILE):
                k = i % 2
                sample = (i == 32)
                if sample:
                    self.bcast_mod(m, True, pmisc, r_pm)
                rows = slice(i * 128, (i + 1) * 128)
                ap, nrows, res = self.x_src(False, cur, i)
                s.dma("sp", lambda e, k=k, ap=ap: e.dma_start(out=xt[k][:], in_=ap), reads=[res], writes=[r_x[k]])
                s.dma("sp", lambda e, k=k, rows=rows: e.dma_start(out=yv[k][:, 0, :], in_=self.ybuf[rows, :]), reads=[self.yres[i]], writes=[r_yv[k]])
                s.dma("sp", lambda e, k=k, rows=rows: e.dma_start(out=yv[k][:, 1, :], in_=self.rw[6, rows, :]), reads=[self.rwres[i]], writes=[r_yv[k]])
                s.dma("sp", lambda e, k=k, rows=rows: e.dma_start(out=yv[k][:, 2, :], in_=self.rw[7, rows, :]), reads=[self.rwres[i]], writes=[r_yv[k]])
                s.dma("sp", lambda e, k=k, rows=rows: e.dma_start(out=ft[k][:, 0, :], in_=self.mixbuf[rows, 0:512]), reads=[self.mixres[i]], writes=[r_ft[k]])
                Y = yv[k]
                ry = [r_yv[k]]
                s.op("dve", lambda e, Y=Y: e.tensor_reduce(out=sm[:, 0:8], in_=v3(Y[:, 0, :]), axis=AX.X, op=ALU.add), reads=ry + rt, writes=rt)
                s.op("dve", lambda e: e.tensor_scalar(out=sm[:, 0:8], in0=sm[:, 0:8], scalar1=1.0 / 64.0, scalar2=None, op0=ALU.mult), reads=rt, writes=rt)
                s.op("dve", lambda e, Y=Y: e.tensor_tensor(out=v3(ta[:]), in0=v3(Y[:, 0, :]), in1=sm[:, 0:8].unsqueeze(2).to_broadcast([128, 8, 64]), op=ALU.subtract),
                     reads=ry + rt, writes=rt)
                s.op("pool", lambda e: e.tensor_tensor(out=tb[:], in0=ta[:], in1=ta[:], op=ALU.mult), reads=rt, writes=rt)
                s.op("dve", lambda e: e.tensor_reduce(out=sm[:, 8:16], in_=v3(tb[:]), axis=AX.X, op=ALU.add), reads=rt, writes=rt)
                s.op("act", lambda e: e.activation(out=sm[:, 16:24], in_=sm[:, 8:16], func=AF.Sqrt, bias=gne[:, 0:1], scale=1.0 / 64.0), reads=rt + [r_c], writes=rt)
                s.op("dve", lambda e: e.reciprocal(out=sm[:, 24:32], in_=sm[:, 16:24]), reads=rt, writes=rt)
                s.op("dve", lambda e: e.tensor_tensor(out=v3(ta[:]), in0=v3(ta[:]), in1=sm[:, 24:32].unsqueeze(2).to_broadcast([128, 8, 64]), op=ALU.mult), reads=rt, writes=rt)
                s.op("pool", lambda e: e.tensor_tensor(out=ta[:], in0=ta[:], in1=GN[:, 0, :], op=ALU.mult), reads=rt + [r_c], writes=rt)
                s.op("pool", lambda e: e.tensor_tensor(out=ta[:], in0=ta[:], in1=GN[:, 1, :], op=ALU.add), reads=rt + [r_c], writes=rt)
                s.op("dve", lambda e, Y=Y: e.tensor_tensor(out=ta[:], in0=ta[:], in1=Y[:, 2, :], op=ALU.add), reads=rt + ry, writes=rt)
                s.op("dve", lambda e, Y=Y, k=k: e.tensor_tensor(out=ft[k][:, 1, :], in0=ta[:], in1=Y[:, 1, :], op=ALU.mult), reads=rt + ry + [r_ft[k]], writes=[r_ft[k]])
                for c in range(8):
                    s.op("pe", lambda e, k=k, c=c: e.transpose(out=tpb[:, c, :], in_=ft[k][:, c // 4, (c % 4) * 128:(c % 4 + 1) * 128],
                                                               identity=self.ident_b[:]), reads=[r_ft[k], self.rconst], writes=[r_tp])
                s.op("act", lambda e, k=k: e.copy(out=fT[k][:], in_=tpb[:]), reads=[r_tp], writes=[r_fT[k]])
                for h in range(2):
                    for kc in range(8):
                        s.op("pe", lambda e, h=h, kc=kc, k=k: e.matmul(po[h][:], lhsT=fT[k][:, kc, :], rhs=owout[:, kc, h * 512:(h + 1) * 512],
                                                                       start=(kc == 0), stop=(kc == 7)), reads=[r_fT[k], r_ow], writes=[r_po[h]])
                self.epilogue(m, [po[0][:], po[1][:]], r_po, xt[k][:], r_x[k], t1[k], r_t1[k], stt[k], r_st[k],
                              self.x_dst(False, nxt, i), None)
            s.barrier()

    def dump_dbg(self, cur):
        nc, s = self.nc, self.s
        with ExitStack() as es:
            xt = [es.enter_context(nc.sbuf_tensor(self.name("dx"), [128, D], F32)) for _ in range(2)]
            r = [Res(), Res()]
            for tile in range(NTILE):
                k = tile % 2
                s.dma("sp", lambda e, k=k, tile=tile: e.dma_start(out=xt[k][:], in_=self.XS[cur][tile * 128:(tile + 1) * 128, :]),
                      reads=[self.xres[cur][tile]], writes=[r[k]])
                s.dma("pool", lambda e, k=k, tile=tile: e.dma_start(out=self.dout["dbg"][tile * 128:(tile + 1) * 128, :], in_=xt[k][:]),
                      reads=[r[k]])
            s.barrier()

    def build(self):
        cfg = self.cfg
        self.setup_consts()
        self.cast_weights()
        cur = 0
        nsub = cfg.get("nsub", 6)
        isub_g = 0
        for l in range(DEPTH):
            self.adaln(l)
            for isub in range(3):
                if isub_g >= nsub:
                    break
                first = (isub_g == 0)
                last = (isub_g == 5)
                nxt = 1 - cur
                if isub == 1 and l == 0 and not cfg.get("stub0"):
                    parts = cfg.get("parts", "ASB")
                    if "A" in parts:
                        self.mixer0_A(cur)
                    if "S" in parts:
                        self.mixer0_S()
                    if "B" in parts:
                        self.mixer0_B(cur, nxt)
                    else:
                        self.stub_mixer(l, cur, nxt)
                elif isub == 1:
                    if not cfg.get("stub1"):
                        self.mixer1_A(cur)
                        if not cfg.get("nodil"):
                            self.mixer1_S()
                    if cfg.get("stub1") or cfg.get("norwkv"):
                        self.stub_mixer(l, cur, nxt)
                    else:
                        self.mixer1_P()
                        self.mixer1_R()
                        self.mixer1_Q(cur, nxt)
                else:
                    self.ffn_sublayer(l, 0 if isub == 0 else 1, isub, first, last, cur, nxt)
                cur = nxt
                isub_g += 1
        if cfg.get("dbg"):
            self.dump_dbg(cur)
        self.s.barrier()
        self.s.emit()
        return self.nc


_OUT_SHAPES = None


def make_in_maps(inputs, cfg):
    maps = []
    f32 = lambda a: np.ascontiguousarray(np.asarray(a, dtype=np.float32))
    shared = {
        "ada_w": f32(inputs["ada_w"]), "ada_b": f32(inputs["ada_b"]),
        "ln_g": f32(inputs["ln_g"]), "ln_b": f32(inputs["ln_b"]),
        "ffn_w_gate": f32(inputs["ffn_w_gate"]).reshape(4 * D, DFF),
        "ffn_w_up": f32(inputs["ffn_w_up"]).reshape(4 * D, DFF),
        "ffn_w_down": f32(inputs["ffn_w_down"]).reshape(4 * DFF, D),
        "even_w_in": f32(inputs["even_w_in"]).reshape(D, 2048),
        "even_w_out": f32(inputs["even_w_out"]).reshape(D, D),
        "odd_w_in": f32(inputs["odd_w_in"]).reshape(D, 3328),
        "odd_w_out": f32(inputs["odd_w_out"]).reshape(D, D),
        "diff_lambda": f32(inputs["diff_lambda"]).reshape(1, 256), "diff_subln": f32(inputs["diff_subln"]).reshape(1, 128),
        "s5_a_re": f32(inputs["s5_a_re"]).reshape(32, 64), "s5_a_im": f32(inputs["s5_a_im"]).reshape(32, 64),
        "s5_log_dt": f32(inputs["s5_log_dt"]).reshape(1, 32),
        "s5_b_re": f32(inputs["s5_b_re"]).reshape(32, 64, 16), "s5_b_im": f32(inputs["s5_b_im"]).reshape(32, 64, 16),
        "s5_c_re": f32(inputs["s5_c_re"]).reshape(32, 16, 64), "s5_c_im": f32(inputs["s5_c_im"]).reshape(32, 16, 64),
        "s5_d": f32(inputs["s5_d"]).reshape(1, 512), "s5_glu_w": f32(inputs["s5_glu_w"]).reshape(512, 512),
        "s5_glu_b": f32(inputs["s5_glu_b"]).reshape(1, 512),
        "cache_a_k": f32(inputs["cache_a_k"]).reshape(2560 * 128, 512), "cache_a_v": f32(inputs["cache_a_v"]).reshape(2560 * 128, 512),
    }
    cck = f32(inputs["cache_c_k"]).reshape(128, 2048, 512); ccv = f32(inputs["cache_c_v"]).reshape(128, 2048, 512)
    for nm in ("rwkv_mu", "rwkv_w0", "rwkv_a0", "rwkv_k_k", "rwkv_k_a", "rwkv_r_k", "rwkv_gn_w", "rwkv_gn_b"):
        shared[nm] = f32(inputs[nm]).reshape(1, -1)
    shared["rwkv_w2"] = f32(inputs["rwkv_w2"]).reshape(64, 512); shared["rwkv_a2"] = f32(inputs["rwkv_a2"]).reshape(64, 512)
    shared["rwkv_g2"] = f32(inputs["rwkv_g2"]).reshape(128, 512)
    dwkv = f32(inputs["state_d_wkv"]).reshape(128, 512, 64); dsh = f32(inputs["state_d_shift"]).reshape(128, 1792)
    s5r = f32(inputs["state_s5_re"]).reshape(128, 2048); s5i = f32(inputs["state_s5_im"]).reshape(128, 2048)
    ptab = np.ascontiguousarray(np.asarray(inputs["page_table"], dtype=np.int32))
    xp = f32(inputs["x_prompt"]); xs = f32(inputs["x_sample"])
    cp = f32(inputs["c_prompt"]); cs = f32(inputs["c_sample"])
    if "S" not in cfg.get("parts", "ASB"):
        shared["cache_a_k"] = shared["cache_a_k"][:128]
        shared["cache_a_v"] = shared["cache_a_v"][:128]
    for c in range(N_CORES):
        b = c % 4
        mp = dict(shared)
        mp["xp"] = xp[b]
        mp["xs"] = xs[c * NSB:(c + 1) * NSB].reshape(NS, D)
        mp["c17"] = np.ascontiguousarray(np.concatenate([cp[b:b + 1], cs[c * NSB:(c + 1) * NSB]], axis=0))
        mp["state_s5_re"] = s5r[c * NSB:(c + 1) * NSB]; mp["state_s5_im"] = s5i[c * NSB:(c + 1) * NSB]
        mp["page_table"] = ptab[c * NSB:(c + 1) * NSB].reshape(1, NSB * 16)
        mp["state_d_wkv"] = dwkv[c * NSB:(c + 1) * NSB].reshape(NSB * 512, 64)
        mp["state_d_shift"] = dsh[c * NSB:(c + 1) * NSB]
        mp["cache_c_k"] = cck[c * NSB:(c + 1) * NSB].reshape(NSB * 2048, 512)
        mp["cache_c_v"] = ccv[c * NSB:(c + 1) * NSB].reshape(NSB * 2048, 512)
        maps.append(mp)
    return maps


CFG = {}


def kernel(**inputs):
    cfg = dict(CFG)
    nc = Builder(cfg).build()
    maps = make_in_maps(inputs, cfg)
    res = run_bass_kernel_spmd(nc, maps, core_ids=list(range(N_CORES)))
    R = res.results
    yp = np.stack([R[b]["yp"] for b in range(4)], 0)
    ys = np.concatenate([R[c]["ys"].reshape(NSB, 4, D) for c in range(N_CORES)], 0)
    E, O, B, DB = 1, 1, 4, 128
    z = lambda *sh: np.zeros(sh, np.float32)
    catp = lambda k, sh: np.stack([R[b][k] for b in range(4)], 0).reshape(sh)
    cats = lambda k, sh: np.concatenate([R[c][k] for c in range(N_CORES)], 0).reshape(sh)
    return (yp, ys,
            catp("akp", (E, B, SEQ, 4, 2, 64)), cats("aks", (E, DB, 4, 4, 2, 64)),
            catp("avp", (E, B, SEQ, 4, 128)), cats("avs", (E, DB, 4, 4, 128)),
            catp("s5rp", (E, B, 32, 64)), cats("s5rs", (E, DB, 32, 64)), catp("s5ip", (E, B, 32, 64)), cats("s5is", (E, DB, 32, 64)),
            catp("ckp", (O, B, 2048, 8, 64)), cats("cks", (O, DB, 4, 8, 64)), catp("cvp", (O, B, 2048, 8, 64)), cats("cvs", (O, DB, 4, 8, 64)),
            catp("dwp", (O, B, 8, 64, 64)), cats("dws", (O, DB, 8, 64, 64)), catp("dsp", (O, B, 1792)), cats("dss", (O, DB, 1792)))
```

```python
from contextlib import ExitStack
import numpy as np
import concourse.bass as bass
import concourse.mybir as mybir
from concourse.bass_utils import run_bass_kernel_spmd

F32 = mybir.dt.float32
BF16 = mybir.dt.bfloat16
I32 = mybir.dt.int32
ALU = mybir.AluOpType
AF = mybir.ActivationFunctionType
AX = mybir.AxisListType

D = 1024
DFF = 2816
NKC = 8
NFC = 22
SEQ = 4096
NS = 64
NSB = 16
NTILE = 33
NROW = NTILE * 128
DEPTH = 2
ALPHA = (2.0 * DEPTH) ** 0.25
LN_EPS = 1e-5
N_CORES = 8


class Res:
    __slots__ = ("w", "r")

    def __init__(self):
        self.w = None
        self.r = []


class EngQ:
    def __init__(self, name, eng, sem):
        self.name = name
        self.eng = eng
        self.sem = sem
        self.count = 0
        self.ops = []
        self.waited = {}
        self.dma_slots = []
        self.dma_i = 0


class Sched:
    def __init__(self, nc, ndma_slots=8):
        self.nc = nc
        self.q = {}
        for name, eng in (("pe", nc.tensor), ("act", nc.scalar), ("dve", nc.vector),
                          ("pool", nc.gpsimd), ("sp", nc.sync)):
            q = EngQ(name, eng, nc.alloc_semaphore(name="s_" + name))
            for i in range(ndma_slots):
                q.dma_slots.append([nc.alloc_semaphore(name=f"d_{name}{i}"), 0])
            self.q[name] = q

    def _wait(self, q, tok):
        sem, val = tok
        key = id(sem)
        if q.waited.get(key, 0) >= val:
            return
        q.waited[key] = val
        q.ops.append(("wait", sem, val))

    def _deps(self, q, reads, writes, skip_same):
        deps = []
        for r in reads:
            if r.w is not None:
                deps.append(r.w)
        for w in writes:
            if w.w is not None:
                deps.append(w.w)
            deps.extend(w.r)
        for tok in deps:
            if skip_same and tok[0] is q.sem:
                continue
            self._wait(q, tok)

    @staticmethod
    def _commit(tok, reads, writes):
        for r in reads:
            r.r.append(tok)
            if len(r.r) > 64:
                best = {}
                for t in r.r:
                    k = id(t[0])
                    if k not in best or best[k][1] < t[1]:
                        best[k] = t
                r.r = list(best.values())
        for w in writes:
            w.w = tok
            w.r = []

    def op(self, qn, fn, reads=(), writes=()):
        q = self.q[qn]
        self._deps(q, reads, writes, skip_same=(qn == "pe"))
        q.count += 1
        tok = (q.sem, q.count)
        q.ops.append(("op", fn, q.sem, 1))
        self._commit(tok, reads, writes)
        return tok

    def dma(self, qn, fn, reads=(), writes=()):
        q = self.q[qn]
        self._deps(q, reads, writes, skip_same=False)
        slot = q.dma_slots[q.dma_i % len(q.dma_slots)]
        q.dma_i += 1
        if slot[1] > 0:
            self._wait(q, (slot[0], slot[1]))
        slot[1] += 16
        tok = (slot[0], slot[1])
        q.ops.append(("op", fn, slot[0], 16))
        self._commit(tok, reads, writes)
        return tok

    def barrier(self):
        toks = []
        for q in self.q.values():
            if q.count:
                toks.append((q.sem, q.count))
            for sl in q.dma_slots:
                if sl[1]:
                    toks.append((sl[0], sl[1]))
        for q in self.q.values():
            for t in toks:
                if t[0] is q.sem:
                    continue
                self._wait(q, t)

    def emit(self):
        nc = self.nc
        with nc.Block() as block:
            def mk(q):
                def body(eng):
                    for o in q.ops:
                        if o[0] == "wait":
                            eng.wait_ge(o[1], o[2])
                        else:
                            o[1](eng).then_inc(o[2], o[3])
                return body
            block.tensor(mk(self.q["pe"]))
            block.scalar(mk(self.q["act"]))
            block.vector(mk(self.q["dve"]))
            block.gpsimd(mk(self.q["pool"]))
            block.sync(mk(self.q["sp"]))


class Builder:
    def __init__(self, cfg):
        self.cfg = cfg
        nc = self.nc = bass.Bass("TRN2", target_bir_lowering=False)
        self.s = Sched(nc)
        self.uid = 0
        di = self.din = {}
        do = self.dout = {}

        def inp(name, shape, dt=F32):
            di[name] = nc.dram_tensor(name, list(shape), dt, kind="ExternalInput").ap()

        def outp(name, shape, dt=F32):
            do[name] = nc.dram_tensor(name, list(shape), dt, kind="ExternalOutput").ap()

        inp("xp", [SEQ, D]); inp("xs", [NS, D]); inp("c17", [17, D])
        inp("ada_w", [DEPTH, D, 9 * D]); inp("ada_b", [DEPTH, 9 * D])
        inp("ln_g", [DEPTH, 3, D]); inp("ln_b", [DEPTH, 3, D])
        inp("ffn_w_gate", [4 * D, DFF]); inp("ffn_w_up", [4 * D, DFF]); inp("ffn_w_down", [4 * DFF, D])
        inp("even_w_in", [D, 2048]); inp("even_w_out", [D, D])
        inp("odd_w_in", [D, 3328]); inp("odd_w_out", [D, D])
        inp("diff_lambda", [1, 256]); inp("diff_subln", [1, 128])
        inp("s5_a_re", [32, 64]); inp("s5_a_im", [32, 64]); inp("s5_log_dt", [1, 32])
        inp("s5_b_re", [32, 64, 16]); inp("s5_b_im", [32, 64, 16]); inp("s5_c_re", [32, 16, 64]); inp("s5_c_im", [32, 16, 64])
        inp("s5_d", [1, 512]); inp("s5_glu_w", [512, 512]); inp("s5_glu_b", [1, 512])
        inp("state_s5_re", [NSB, 2048]); inp("state_s5_im", [NSB, 2048])
        inp("page_table", [1, NSB * 16], I32)
        inp("cache_c_k", [NSB * 2048, 512]); inp("cache_c_v", [NSB * 2048, 512])
        inp("state_d_wkv", [NSB * 512, 64]); inp("state_d_shift", [NSB, 1792])
        inp("rwkv_mu", [1, 1792]); inp("rwkv_w0", [1, 512]); inp("rwkv_a0", [1, 512]); inp("rwkv_k_k", [1, 512]); inp("rwkv_k_a", [1, 512])
        inp("rwkv_r_k", [1, 512]); inp("rwkv_gn_w", [1, 512]); inp("rwkv_gn_b", [1, 512])
        inp("rwkv_w2", [64, 512]); inp("rwkv_a2", [64, 512]); inp("rwkv_g2", [128, 512])
        npool = 2560 * 128 if "S" in cfg.get("parts", "ASB") else 128
        inp("cache_a_k", [npool, 512]); inp("cache_a_v", [npool, 512])
        outp("yp", [SEQ, D]); outp("ys", [NS, D])
        outp("akp", [SEQ, 512]); outp("aks", [NS, 512]); outp("avp", [SEQ, 512]); outp("avs", [NS, 512])
        outp("ckp", [2048, 512]); outp("cks", [NS, 512]); outp("cvp", [2048, 512]); outp("cvs", [NS, 512])
        outp("dsp", [1, 1792]); outp("dss", [NSB, 1792])
        outp("dwp", [512, 64]); outp("dws", [NSB * 512, 64])
        outp("s5rp", [1, 2048]); outp("s5ip", [1, 2048]); outp("s5rs", [NSB, 2048]); outp("s5is", [NSB, 2048])
        if cfg.get("dbg"):
            outp("dbg", [NROW, D])
        self.wg_b = nc.dram_tensor("wg_b", [4 * D, DFF], BF16).ap()
        self.wu_b = nc.dram_tensor("wu_b", [4 * D, DFF], BF16).ap()
        self.wd_b = nc.dram_tensor("wd_b", [4 * DFF, D], BF16).ap()
        self.ewin_b = nc.dram_tensor("ewin_b", [D, 2048], BF16).ap()
        self.owin_b = nc.dram_tensor("owin_b", [D, 3328], BF16).ap()
        self.ewout_b = nc.dram_tensor("ewout_b", [D, D], BF16).ap()
        self.owout_b = nc.dram_tensor("owout_b", [D, D], BF16).ap()
        self.mod17 = nc.dram_tensor("mod17", [17, 9 * D], F32).ap()
        self.XS = [nc.dram_tensor(f"xscr{i}", [NROW, D], F32).ap() for i in range(2)]
        self.xres = [[Res() for _ in range(NTILE)] for _ in range(2)]
        self.mixbuf = nc.dram_tensor("mixbuf", [NROW, D], BF16).ap()
        self.mixres = [Res() for _ in range(NTILE)]
        self.ubuf = nc.dram_tensor("ubuf", [NROW, 512], BF16).ap()
        self.ures = [Res() for _ in range(NTILE)]
        self.qsbuf = nc.dram_tensor("qsbuf", [NS, 512], BF16).ap()
        self.r_aks = Res()
        self.r_cks = Res()
        self.rw = nc.dram_tensor("rwscr", [8, NROW, 512], F32).ap()
        self.rwres = [Res() for _ in range(NTILE)]
        self.ybuf = nc.dram_tensor("ybuf", [NROW, 512], F32).ap()
        self.yres = [Res() for _ in range(NTILE)]
        self.pdbuf = nc.dram_tensor("pdbuf", [NROW, 1792], F32).ap()
        self.pdres = [Res() for _ in range(NTILE)]

    def name(self, p):
        self.uid += 1
        return f"{p}{self.uid}"

    def setup_consts(self):
        nc, s = self.nc, self.s
        A = nc.alloc_sbuf_tensor
        self.ident_b = A("ident_b", [128, 128], BF16)
        self.ident_f = A("ident_f", [128, 128], F32)
        self.selP = A("selP", [17, 128], F32)
        self.selS = A("selS", [17, 128], F32)
        self.ones1 = A("ones1", [1, 128], F32)
        self.eps_t = A("eps_t", [128, 1], F32)
        self.rconst = Res()
        rc = [self.rconst]
        for idt in (self.ident_b, self.ident_f):
            s.op("pool", lambda e, t=idt: e.memset(t[:], 0.0), writes=rc)
            s.op("pool", lambda e, t=idt: e.affine_select(out=t[:], in_=t[:], pattern=[[-1, 128]],
                                                            compare_op=ALU.not_equal, fill=1.0, base=0,
                                                            channel_multiplier=1), reads=rc, writes=rc)
        s.op("pool", lambda e: e.memset(self.selP[:], 0.0), writes=rc)
        s.op("pool", lambda e: e.memset(self.selP[0:1, :], 1.0), writes=rc)
        s.op("pool", lambda e: e.memset(self.selS[:], 1.0), writes=rc)
        s.op("pool", lambda e: e.affine_select(out=self.selS[:], in_=self.selS[:], pattern=[[1, 128]],
                                                compare_op=ALU.is_ge, fill=0.0, base=4, channel_multiplier=-4),
             reads=rc, writes=rc)
        s.op("pool", lambda e: e.affine_select(out=self.selS[:], in_=self.selS[:], pattern=[[-1, 128]],
                                                compare_op=ALU.is_ge, fill=0.0, base=-1, channel_multiplier=4),
             reads=rc, writes=rc)
        s.op("pool", lambda e: e.memset(self.ones1[:], 1.0), writes=rc)
        s.op("pool", lambda e: e.memset(self.eps_t[:], LN_EPS), writes=rc)

    def cast_weights(self):
        nc, s, di = self.nc, self.s, self.din
        jobs = [(di["ffn_w_gate"], self.wg_b, 4 * D, DFF), (di["ffn_w_up"], self.wu_b, 4 * D, DFF),
                (di["ffn_w_down"], self.wd_b, 4 * DFF, D), (di["even_w_in"], self.ewin_b, D, 2048),
                (di["even_w_out"], self.ewout_b, D, D), (di["odd_w_in"], self.owin_b, D, 3328),
                (di["odd_w_out"], self.owout_b, D, D)]
        if self.cfg.get("nocast"):
            jobs = []
        with ExitStack() as es:
            NB = 3
            stg = [es.enter_context(nc.sbuf_tensor(f"cs{i}", [128, 2, 3328], F32)) for i in range(NB)]
            stb = [es.enter_context(nc.sbuf_tensor(f"cb{i}", [128, 2, 3328], BF16)) for i in range(NB)]
            rs = [Res() for _ in range(NB)]
            rb = [Res() for _ in range(NB)]
            i = 0
            for src, dst, R, C in jobs:
                nrt = R // 128
                G = 2
                for r0 in range(0, nrt, G):
                    g = min(G, nrt - r0)
                    k = i % NB
                    sv = src[r0 * 128:(r0 + g) * 128, :].rearrange("(g p) c -> p g c", p=128)
                    dv = dst[r0 * 128:(r0 + g) * 128, :].rearrange("(g p) c -> p g c", p=128)
                    s.dma("sp", lambda e, k=k, g=g, C=C, sv=sv: e.dma_start(out=stg[k][:, :g, :C], in_=sv),
                          writes=[rs[k]])
                    eng = ("act", "dve", "pool")[i % 3]
                    if eng == "act":
                        fn = lambda e, k=k, g=g, C=C: e.copy(out=stb[k][:, :g, :C], in_=stg[k][:, :g, :C])
                    else:
                        fn = lambda e, k=k, g=g, C=C: e.tensor_copy(out=stb[k][:, :g, :C], in_=stg[k][:, :g, :C])
                    s.op(eng, fn, reads=[rs[k]], writes=[rb[k]])
                    s.dma("pool", lambda e, k=k, g=g, C=C, dv=dv: e.dma_start(out=dv, in_=stb[k][:, :g, :C]),
                          reads=[rb[k]])
                    i += 1
            s.barrier()

    def adaln(self, l):
        nc, s, di = self.nc, self.s, self.din
        with ExitStack() as es:
            T = lambda n, sh, dt: es.enter_context(nc.sbuf_tensor(self.name(n), sh, dt))
            P = lambda n, sh, dt: es.enter_context(nc.psum_tensor(self.name(n), sh, dt))
            c_sb = T("c_sb", [17, D], F32)
            sc = T("sc", [17, D], F32)
            scT = T("scT", [128, NKC, 17], F32)
            w_sb = [T("adw", [128, NKC, 512], F32) for _ in range(2)]
            b_sb = [T("adb", [1, 512], F32) for _ in range(2)]
            o_sb = [T("ado", [17, 512], F32) for _ in range(2)]
            pt = P("adpt", [128, NKC, 32], F32)
            po = [P("adpo", [17, 512], F32) for _ in range(2)]
            r_c, r_sc, r_scT, r_pt = Res(), Res(), Res(), Res()
            r_w = [Res(), Res()]; r_b = [Res(), Res()]; r_o = [Res(), Res()]; r_po = [Res(), Res()]
            s.dma("sp", lambda e: e.dma_start(out=c_sb[:], in_=di["c17"][:, :]), writes=[r_c])
            s.op("act", lambda e: e.activation(out=sc[:], in_=c_sb[:], func=AF.Silu), reads=[r_c], writes=[r_sc])
            for kc in range(NKC):
                s.op("pe", lambda e, kc=kc: e.transpose(out=pt[:, kc, 0:17], in_=sc[:, kc * 128:(kc + 1) * 128],
                                                        identity=self.ident_f[0:17, 0:17]),
                     reads=[r_sc, self.rconst], writes=[r_pt])
            s.op("dve", lambda e: e.tensor_copy(out=scT[:], in_=pt[:, :, 0:17]), reads=[r_pt], writes=[r_scT])
            for cc in range(18):
                k = cc % 2
                wv = di["ada_w"][l, :, cc * 512:(cc + 1) * 512].rearrange("(kc p) n -> p kc n", p=128)
                s.dma("sp", lambda e, k=k, wv=wv: e.dma_start(out=w_sb[k][:], in_=wv), writes=[r_w[k]])
                bv = di["ada_b"][l:l + 1, cc * 512:(cc + 1) * 512]
                s.dma("sp", lambda e, k=k, bv=bv: e.dma_start(out=b_sb[k][:], in_=bv), writes=[r_b[k]])
                for kc in range(NKC):
                    s.op("pe", lambda e, k=k, kc=kc: e.matmul(po[k][:], lhsT=scT[:, kc, :], rhs=w_sb[k][:, kc, :],
                                                              start=(kc == 0), stop=False),
                         reads=[r_scT, r_w[k]], writes=[r_po[k]])
                s.op("pe", lambda e, k=k: e.matmul(po[k][:], lhsT=self.ones1[0:1, 0:17], rhs=b_sb[k][:],
                                                   start=False, stop=True),
                     reads=[r_b[k], self.rconst], writes=[r_po[k]])
                isub, kind = divmod(cc // 2, 3)
                if kind == 0:
                    fn = lambda e, k=k: e.tensor_copy(out=o_sb[k][:], in_=po[k][:])
                elif kind == 1:
                    fn = lambda e, k=k: e.tensor_scalar(out=o_sb[k][:], in0=po[k][:], scalar1=1.0, scalar2=None,
                                                        op0=ALU.add)
                else:
                    r = 1.0 if isub == 1 else 0.5
                    fn = lambda e, k=k, r=r: e.tensor_scalar(out=o_sb[k][:], in0=po[k][:], scalar1=1.0, scalar2=r,
                                                             op0=ALU.add, op1=ALU.mult)
                s.op("dve", fn, reads=[r_po[k]], writes=[r_o[k]])
                mv = self.mod17[:, cc * 512:(cc + 1) * 512]
                s.dma("pool", lambda e, k=k, mv=mv: e.dma_start(out=mv, in_=o_sb[k][:]), reads=[r_o[k]])
            s.barrier()

    def alloc_mod(self, es):
        nc = self.nc
        T = lambda n, sh, dt: es.enter_context(nc.sbuf_tensor(self.name(n), sh, dt))
        m = {}
        m["m3"] = T("m3", [17, 3, D], F32)
        m["SH"] = T("SH", [128, D], F32)
        m["SC"] = T("SC", [128, D], F32)
        m["G"] = T("G", [128, D], F32)
        m["LG"] = T("LG", [128, D], F32)
        m["LB"] = T("LB", [128, D], F32)
        m["r_m3"] = Res(); m["r_mod"] = Res(); m["r_ln"] = Res()
        return m

    def load_mod(self, m, l, isub, pmod):
        s, di = self.s, self.din
        mv = self.mod17[:, isub * 3 * D:(isub + 1) * 3 * D].rearrange("r (k d) -> r k d", k=3)
        s.dma("sp", lambda e: e.dma_start(out=m["m3"][:], in_=mv), writes=[m["r_m3"]])
        s.dma("sp", lambda e: e.dma_start(out=m["LG"][:], in_=di["ln_g"][l, isub:isub + 1, :].to_broadcast([128, D])),
              writes=[m["r_ln"]])
        s.dma("sp", lambda e: e.dma_start(out=m["LB"][:], in_=di["ln_b"][l, isub:isub + 1, :].to_broadcast([128, D])),
              writes=[m["r_ln"]])

    def bcast_mod(self, m, sample, pmod, r_pmod):
        s = self.s
        sel = self.selS if sample else self.selP
        for kind, key in enumerate(("SH", "SC", "G")):
            for h in range(2):
                s.op("pe", lambda e, kind=kind, h=h: e.matmul(pmod[:], lhsT=sel[:], rhs=m["m3"][:, kind, h * 512:(h + 1) * 512],
                                                              start=True, stop=True),
                     reads=[m["r_m3"], self.rconst], writes=[r_pmod])
                s.op("act", lambda e, key=key, h=h: e.copy(out=m[key][:, h * 512:(h + 1) * 512], in_=pmod[:]),
                     reads=[r_pmod], writes=[m["r_mod"]])

    def epilogue(self, m, po_halves, r_po, x_ap, r_x, t1, r_t1, st, r_st, dst_ap, dst_res, extra_dst=None):
        s = self.s
        if po_halves is not None:
            for h in range(2):
                s.op("dve", lambda e, h=h: e.tensor_tensor(out=t1[:, h * 512:(h + 1) * 512], in0=po_halves[h],
                                                           in1=m["G"][:, h * 512:(h + 1) * 512], op=ALU.mult),
                     reads=[r_po[h], m["r_mod"]], writes=[r_t1])
            s.op("dve", lambda e: e.scalar_tensor_tensor(out=t1[:], in0=x_ap, scalar=ALPHA, in1=t1[:],
                                                          op0=ALU.mult, op1=ALU.add),
                 reads=[r_x, r_t1], writes=[r_t1])
        else:
            s.op("pool", lambda e: e.tensor_scalar(out=t1[:], in0=x_ap, scalar1=ALPHA, scalar2=None, op0=ALU.mult),
                 reads=[r_x], writes=[r_t1])
        for h in range(2):
            s.op("dve", lambda e, h=h: e.bn_stats(out=st[:, h * 6:(h + 1) * 6], in_=t1[:, h * 512:(h + 1) * 512]),
                 reads=[r_t1], writes=[r_st])
        s.op("dve", lambda e: e.bn_aggr(out=st[:, 12:14], in_=st[:, 0:12]), reads=[r_st], writes=[r_st])
        s.op("act", lambda e: e.activation(out=st[:, 14:15], in_=st[:, 13:14], func=AF.Sqrt, bias=self.eps_t[:, 0:1],
                                           scale=1.0), reads=[r_st, self.rconst], writes=[r_st])
        s.op("dve", lambda e: e.reciprocal(out=st[:, 15:16], in_=st[:, 14:15]), reads=[r_st], writes=[r_st])
        s.op("dve", lambda e: e.tensor_scalar(out=t1[:], in0=t1[:], scalar1=st[:, 12:13], scalar2=st[:, 15:16],
                                              op0=ALU.subtract, op1=ALU.mult), reads=[r_st, r_t1], writes=[r_t1])
        s.op("pool", lambda e: e.tensor_tensor(out=t1[:], in0=t1[:], in1=m["LG"][:], op=ALU.mult),
             reads=[r_t1, m["r_ln"]], writes=[r_t1])
        s.op("dve", lambda e: e.tensor_tensor(out=t1[:], in0=t1[:], in1=m["LB"][:], op=ALU.add),
             reads=[r_t1, m["r_ln"]], writes=[r_t1])
        toks = []
        for ap, res, nrows in dst_ap:
            toks.append(s.dma("pool", lambda e, ap=ap, nrows=nrows: e.dma_start(out=ap, in_=t1[0:nrows, :]),
                              reads=[r_t1], writes=[res] if res is not None else []))
        return toks

    def x_src(self, first, cur, tile):
        if first:
            if tile < 32:
                return self.din["xp"][tile * 128:(tile + 1) * 128, :], 128, None
            return self.din["xs"][:, :], NS, None
        return self.XS[cur][tile * 128:(tile + 1) * 128, :], 128, self.xres[cur][tile]

    def x_dst(self, last, nxt, tile):
        if last:
            if tile < 32:
                return [(self.dout["yp"][tile * 128:(tile + 1) * 128, :], None, 128)]
            return [(self.dout["ys"][:, :], None, NS)]
        return [(self.XS[nxt][tile * 128:(tile + 1) * 128, :], self.xres[nxt][tile], 128)]

    def ffn_sublayer(self, l, f, isub, first, last, cur, nxt):
        nc, s = self.nc, self.s
        wrow = (l * 2 + f)
        nblk = self.cfg.get("nblk", 9)
        with ExitStack() as es:
            T = lambda n, sh, dt: es.enter_context(nc.sbuf_tensor(self.name(n), sh, dt))
            P = lambda n, sh, dt: es.enter_context(nc.psum_tensor(self.name(n), sh, dt))
            m = self.alloc_mod(es)
            xblk = [T("xblk", [128, 4, D], F32) for _ in range(2)]
            hb = [T("hb", [128, D], BF16) for _ in range(2)]
            htmp = [T("htmp", [128, D], F32) for _ in range(2)]
            hT = [T("hT", [128, NKC, 512], BF16) for _ in range(2)]
            NBW = 3
            wgu = [T("wgu", [128, 2, NKC, 256], BF16) for _ in range(NBW)]
            sg = [T("sg", [128, 512], BF16) for _ in range(2)]
            act = T("actb", [128, NFC, 512], BF16)
            wd = [T("wd", [128, NFC, 512], BF16) for _ in range(2)]
            t1 = [T("t1", [128, D], F32) for _ in range(2)]
            st = [T("st", [128, 16], F32) for _ in range(2)]
            tp = [P("tp", [128, NKC, 128], BF16) for _ in range(2)]
            pg = [P("pg", [128, 512], F32) for _ in range(2)]
            pu = [P("pu", [128, 512], F32) for _ in range(2)]
            po = [P("po", [128, 512], F32) for _ in range(2)]
            r_x = [Res(), Res()]; r_hb = [Res(), Res()]; r_htmp = [Res(), Res()]; r_hT = [Res(), Res()]
            r_wgu = [Res() for _ in range(NBW)]; r_sg = [Res(), Res()]; r_act = Res(); r_wd = [Res(), Res()]
            r_t1 = [Res(), Res()]; r_st = [Res(), Res()]; r_tp = [Res(), Res()]
            r_pg = [Res(), Res()]; r_pu = [Res(), Res()]; r_po = [Res(), Res()]
            for b in xblk:
                s.op("pool", lambda e, b=b: e.memset(b[:], 0.0), writes=[r_x[0], r_x[1]])
            self.load_mod(m, l, isub, None)
            self.bcast_mod(m, False, po[0], r_po[0])
            iw = 0
            iwd = 0
            ihb = 0
            igu = 0
            ipo = 0
            it1 = 0
            for blk in range(nblk):
                sample = (blk == 8)
                nt = 1 if sample else 4
                TB = nt * 128
                xb = blk % 2
                if sample:
                    self.bcast_mod(m, True, po[0], r_po[0])
                for t in range(nt):
                    ap, nrows, res = self.x_src(first, cur, blk * 4 + t)
                    s.dma("sp", lambda e, xb=xb, t=t, ap=ap, nrows=nrows: e.dma_start(out=xblk[xb][0:nrows, t, :], in_=ap),
                          reads=[res] if res is not None else [], writes=[r_x[xb]])
                for t in range(nt):
                    k = ihb % 2
                    ihb += 1
                    s.op("pool", lambda e, k=k, xb=xb, t=t: e.tensor_tensor(out=htmp[k][:], in0=xblk[xb][:, t, :], in1=m["SC"][:],
                                                                            op=ALU.mult),
                         reads=[r_x[xb], m["r_mod"]], writes=[r_htmp[k]])
                    s.op("dve", lambda e, k=k: e.tensor_tensor(out=hb[k][:], in0=htmp[k][:], in1=m["SH"][:], op=ALU.add),
                         reads=[r_htmp[k], m["r_mod"]], writes=[r_hb[k]])
                    for kc in range(NKC):
                        s.op("pe", lambda e, k=k, kc=kc: e.transpose(out=tp[k][:, kc, :], in_=hb[k][:, kc * 128:(kc + 1) * 128],
                                                                     identity=self.ident_b[:]),
                             reads=[r_hb[k], self.rconst], writes=[r_tp[k]])
                    s.op("act", lambda e, k=k, xb=xb, t=t: e.copy(out=hT[xb][:, :, t * 128:(t + 1) * 128], in_=tp[k][:]),
                         reads=[r_tp[k]], writes=[r_hT[xb]])
                for fp in range(NFC // 2):
                    kw = iw % NBW
                    iw += 1
                    gv = self.wg_b[wrow * D:(wrow + 1) * D, fp * 256:(fp + 1) * 256].rearrange("(kc p) n -> p kc n", p=128)
                    uv = self.wu_b[wrow * D:(wrow + 1) * D, fp * 256:(fp + 1) * 256].rearrange("(kc p) n -> p kc n", p=128)
                    s.dma("sp", lambda e, kw=kw, gv=gv: e.dma_start(out=wgu[kw][:, 0], in_=gv), writes=[r_wgu[kw]])
                    s.dma("sp", lambda e, kw=kw, uv=uv: e.dma_start(out=wgu[kw][:, 1], in_=uv), writes=[r_wgu[kw]])
                    for j in range(2):
                        fc = fp * 2 + j
                        kg = igu % 2
                        igu += 1
                        for which, pp, rr in ((0, pg, r_pg), (1, pu, r_pu)):
                            for kc in range(NKC):
                                s.op("pe", lambda e, kw=kw, which=which, kc=kc, j=j, pp=pp, kg=kg, xb=xb, TB=TB:
                                     e.matmul(pp[kg][:, 0:TB], lhsT=wgu[kw][:, which, kc, j * 128:(j + 1) * 128],
                                              rhs=hT[xb][:, kc, 0:TB], start=(kc == 0), stop=(kc == NKC - 1)),
                                     reads=[r_wgu[kw], r_hT[xb]], writes=[rr[kg]])
                        s.op("act", lambda e, kg=kg, TB=TB: e.activation(out=sg[kg][:, 0:TB], in_=pg[kg][:, 0:TB], func=AF.Silu),
                             reads=[r_pg[kg]], writes=[r_sg[kg]])
                        s.op("dve", lambda e, kg=kg, fc=fc, TB=TB: e.tensor_tensor(out=act[:, fc, 0:TB], in0=sg[kg][:, 0:TB],
                                                                                   in1=pu[kg][:, 0:TB], op=ALU.mult),
                             reads=[r_sg[kg], r_pu[kg]], writes=[r_act])
                for h in range(2):
                    dv = self.wd_b[wrow * DFF:(wrow + 1) * DFF, h * 512:(h + 1) * 512].rearrange("(fc p) n -> p fc n", p=128)
                    s.dma("sp", lambda e, h=h, dv=dv: e.dma_start(out=wd[h][:], in_=dv), writes=[r_wd[h]])
                for t in range(nt):
                    tile = blk * 4 + t
                    for h in range(2):
                        for fc in range(NFC):
                            s.op("pe", lambda e, h=h, fc=fc, t=t: e.matmul(po[h][:], lhsT=act[:, fc, t * 128:(t + 1) * 128],
                                                                           rhs=wd[h][:, fc, :], start=(fc == 0), stop=(fc == NFC - 1)),
                                 reads=[r_act, r_wd[h]], writes=[r_po[h]])
                    k1 = it1 % 2
                    it1 += 1
                    self.epilogue(m, [po[0][:], po[1][:]], r_po, xblk[xb][:, t, :], r_x[xb], t1[k1], r_t1[k1], st[k1], r_st[k1],
                                  self.x_dst(last, nxt, tile), None)
            s.barrier()

    def stub_mixer(self, l, cur, nxt):
        nc, s = self.nc, self.s
        with ExitStack() as es:
            T = lambda n, sh, dt: es.enter_context(nc.sbuf_tensor(self.name(n), sh, dt))
            m = self.alloc_mod(es)
            xt = [T("xt", [128, D], F32) for _ in range(2)]
            t1 = [T("t1", [128, D], F32) for _ in range(2)]
            st = [T("st", [128, 16], F32) for _ in range(2)]
            r_x = [Res(), Res()]; r_t1 = [Res(), Res()]; r_st = [Res(), Res()]
            self.load_mod(m, l, 1, None)
            for tile in range(NTILE):
                k = tile % 2
                ap, nrows, res = self.x_src(False, cur, tile)
                s.dma("sp", lambda e, k=k, ap=ap: e.dma_start(out=xt[k][:], in_=ap), reads=[res], writes=[r_x[k]])
                self.epilogue(m, None, None, xt[k][:], r_x[k], t1[k], r_t1[k], st[k], r_st[k],
                              self.x_dst(False, nxt, tile), None)
            s.barrier()

    def mixer0_A(self, cur):
        nc, s, di, do = self.nc, self.s, self.din, self.dout
        LAM_INIT = 0.8 - 0.6 * float(np.exp(-0.3 * 0))
        with ExitStack() as es:
            T = lambda n, sh, dt: es.enter_context(nc.sbuf_tensor(self.name(n), sh, dt))
            P = lambda n, sh, dt: es.enter_context(nc.psum_tensor(self.name(n), sh, dt))
            m = self.alloc_mod(es)
            ewin = T("ewin", [128, NKC, 2048], BF16)
            xt = [T("xt", [128, D], F32) for _ in range(2)]
            tmpf = T("tmpf", [128, D], F32)
            hb = [T("hb", [128, D], BF16) for _ in range(2)]
            hT = [T("hT", [128, NKC, 128], BF16) for _ in range(2)]
            q_b = [T("q_b", [128, 512], BF16) for _ in range(2)]
            k_b = [T("k_b", [128, 512], BF16) for _ in range(2)]
            u_b = [T("u_b", [128, 512], BF16) for _ in range(2)]
            kv_f = [T("kv_f", [128, 1024], F32) for _ in range(2)]
            qT = [T("qT", [128, 4, 128], BF16) for _ in range(2)]
            KT = T("KT", [128, 4, SEQ], BF16)
            VP = T("VP", [128, 32, 4, 132], BF16)
            E = [T("E", [128, 4, 128], BF16) for _ in range(3)]
            tri = T("tri", [128, 128], BF16)
            dl = T("dl", [128, 4, 64], F32)
            dlp = T("dlp", [128, 2, 64], F32)
            lamt = T("lamt", [128, 8], F32)
            subl = T("subl", [128, 128], F32)
            a1 = [T("a1", [128, 128], F32) for _ in range(2)]
            junk = T("junk", [128, 128], F32)
            rr = [T("rr", [128, 8], F32) for _ in range(2)]
            mixf = [T("mixf", [128, 512], BF16) for _ in range(2)]
            pp = [P("pp", [128, 512], F32) for _ in range(4)]
            tpb = P("tpb", [128, NKC, 128], BF16)
            st = [P("st", [128, 512], F32) for _ in range(2)]
            oaccs = [P("oacc", [128, 512], F32)]
            r_ewin = Res(); r_x = [Res(), Res()]; r_tmpf = Res(); r_hb = [Res(), Res()]; r_hT = [Res(), Res()]
            r_q = [Res(), Res()]; r_k = [Res(), Res()]; r_u = [Res(), Res()]; r_kv = [Res(), Res()]; r_qT = [Res(), Res()]
            r_KT = [Res() for _ in range(32)]; r_VP = [Res() for _ in range(32)]; r_E = [Res() for _ in range(3)]
            r_c = Res(); r_a1 = [Res(), Res()]; r_rr = [Res(), Res()]; r_mixf = [Res(), Res()]; r_junk = Res()
            r_pp = [Res() for _ in range(4)]; r_tp = Res(); r_st = [Res(), Res()]; r_oacc = [Res()]
            s.dma("sp", lambda e: e.dma_start(out=ewin[:], in_=self.ewin_b[:, :].rearrange("(kc p) n -> p kc n", p=128)),
                  writes=[r_ewin])
            s.op("pool", lambda e: e.memset(VP[:, :, :, 128:132], 1.0), writes=r_VP)
            s.op("pool", lambda e: e.memset(tri[:], 1.0), writes=[r_c])
            s.op("pool", lambda e: e.affine_select(out=tri[:], in_=tri[:], pattern=[[1, 128]], compare_op=ALU.is_ge,
                                                    fill=0.0, base=0, channel_multiplier=-1), reads=[r_c], writes=[r_c])
            self.diff_consts(dl, dlp, lamt, subl, junk, r_c, LAM_INIT)
            self.load_mod(m, 0, 1, None)
            self.bcast_mod(m, False, pp[0], r_pp[0])
            ist = 0
            ie = 0
            ihd = 0
            for i in (range(NTILE) if not self.cfg.get('tilesA') else self.cfg['tilesA']):
                sample = (i == 32)
                k = i % 2
                if sample:
                    self.bcast_mod(m, True, pp[0], r_pp[0])
                ap, nrows, res = self.x_src(False, cur, i)
                s.dma("sp", lambda e, k=k, ap=ap: e.dma_start(out=xt[k][:], in_=ap), reads=[res], writes=[r_x[k]])
                s.op("pool", lambda e, k=k: e.tensor_tensor(out=tmpf[:], in0=xt[k][:], in1=m["SC"][:], op=ALU.mult),
                     reads=[r_x[k], m["r_mod"]], writes=[r_tmpf])
                s.op("dve", lambda e, k=k: e.tensor_tensor(out=hb[k][:], in0=tmpf[:], in1=m["SH"][:], op=ALU.add),
                     reads=[r_tmpf, m["r_mod"]], writes=[r_hb[k]])
                for kc in range(NKC):
                    s.op("pe", lambda e, k=k, kc=kc: e.transpose(out=tpb[:, kc, :], in_=hb[k][:, kc * 128:(kc + 1) * 128],
                                                                 identity=self.ident_b[:]),
                         reads=[r_hb[k], self.rconst], writes=[r_tp])
                s.op("act", lambda e, k=k: e.copy(out=hT[k][:], in_=tpb[:]), reads=[r_tp], writes=[r_hT[k]])
                for c in range(4):
                    for kc in range(NKC):
                        s.op("pe", lambda e, k=k, kc=kc, c=c: e.matmul(pp[c][:], lhsT=hT[k][:, kc, :],
                                                                       rhs=ewin[:, kc, c * 512:(c + 1) * 512],
                                                                       start=(kc == 0), stop=(kc == NKC - 1)),
                             reads=[r_hT[k], r_ewin], writes=[r_pp[c]])
                if self.cfg.get("stopA") == 1:
                    continue
                s.op("act", lambda e, k=k: e.copy(out=q_b[k][:], in_=pp[0][:]), reads=[r_pp[0]], writes=[r_q[k]])
                s.op("dve", lambda e, k=k: e.tensor_copy(out=kv_f[k][:, 0:512], in_=pp[1][:]), reads=[r_pp[1]], writes=[r_kv[k]])
                s.op("dve", lambda e, k=k: e.tensor_copy(out=kv_f[k][:, 512:1024], in_=pp[2][:]), reads=[r_pp[2]], writes=[r_kv[k]])
                s.op("act", lambda e, k=k: e.copy(out=u_b[k][:], in_=pp[3][:]), reads=[r_pp[3]], writes=[r_u[k]])
                s.op("act", lambda e, k=k: e.copy(out=k_b[k][:], in_=kv_f[k][:, 0:512]), reads=[r_kv[k]], writes=[r_k[k]])
                if not sample:
                    s.op("act", lambda e, i=i, k=k: e.copy(out=VP[:, i, :, 0:128], in_=kv_f[k][:, 512:1024].rearrange("p (h e) -> p h e", h=4)),
                         reads=[r_kv[k]], writes=[r_VP[i]])
                rows = slice(i * 128, (i + 1) * 128)
                if self.cfg.get("stopA") == 2:
                    continue
                s.dma("pool", lambda e, k=k, rows=rows: e.dma_start(out=self.ubuf[rows, :], in_=u_b[k][:]),
                      reads=[r_u[k]], writes=[self.ures[i]])
                if not sample:
                    s.dma("pool", lambda e, k=k, rows=rows: e.dma_start(out=do["akp"][rows, :], in_=kv_f[k][:, 0:512]), reads=[r_kv[k]])
                    s.dma("pool", lambda e, k=k, rows=rows: e.dma_start(out=do["avp"][rows, :], in_=kv_f[k][:, 512:1024]), reads=[r_kv[k]])
                else:
                    s.dma("pool", lambda e, k=k: e.dma_start(out=do["aks"][:, :], in_=kv_f[k][0:NS, 0:512]), reads=[r_kv[k]],
                          writes=[self.r_aks])
                    s.dma("pool", lambda e, k=k: e.dma_start(out=do["avs"][:, :], in_=kv_f[k][0:NS, 512:1024]), reads=[r_kv[k]],
                          writes=[self.r_aks])
                    s.dma("pool", lambda e, k=k: e.dma_start(out=self.qsbuf[:, :], in_=q_b[k][0:NS, :]), reads=[r_q[k]],
                          writes=[self.r_aks])
                    continue
                if self.cfg.get("stopA") == 3:
                    continue
                for h in range(4):
                    s.op("pe", lambda e, k=k, h=h: e.transpose(out=tpb[:, h, :], in_=q_b[k][:, h * 128:(h + 1) * 128],
                                                               identity=self.ident_b[:]),
                         reads=[r_q[k], self.rconst], writes=[r_tp])
                    s.op("pe", lambda e, k=k, h=h: e.transpose(out=tpb[:, 4 + h, :], in_=k_b[k][:, h * 128:(h + 1) * 128],
                                                               identity=self.ident_b[:]),
                         reads=[r_k[k], self.rconst], writes=[r_tp])
                s.op("act", lambda e, k=k: e.copy(out=qT[k][:], in_=tpb[:, 0:4, :]), reads=[r_tp], writes=[r_qT[k]])
                s.op("act", lambda e, i=i: e.copy(out=KT[:, :, i * 128:(i + 1) * 128], in_=tpb[:, 4:8, :]), reads=[r_tp],
                     writes=[r_KT[i]])
                for h in range(4 if not self.cfg.get("noattn") else 0):
                    ob = 0
                    oacc = oaccs[ob]
                    for mm in range(2):
                        pr = slice(mm * 64, (mm + 1) * 64)
                        oc = mm * 132
                        for j0 in range(0, i + 1, 4):
                            js = list(range(j0, min(j0 + 4, i + 1)))
                            sb = ist % 2
                            ist += 1
                            for jj, j in enumerate(js):
                                s.op("pe", lambda e, sb=sb, jj=jj, j=j, h=h, pr=pr, k=k:
                                     e.matmul(st[sb][:, jj * 128:(jj + 1) * 128], lhsT=KT[pr, h, j * 128:(j + 1) * 128],
                                              rhs=qT[k][pr, h, :], start=True, stop=True),
                                     reads=[r_KT[j], r_qT[k]], writes=[r_st[sb]])
                            eb = ie % 3
                            ie += 1
                            w = len(js)
                            s.op("act", lambda e, eb=eb, sb=sb, w=w: e.activation(out=E[eb][:, 0:w, :],
                                                                                 in_=st[sb][:, 0:w * 128].rearrange("p (a b) -> p a b", a=w),
                                                                                 func=AF.Exp, scale=0.125),
                                 reads=[r_st[sb]], writes=[r_E[eb]])
                            if js[-1] == i:
                                jj = i - j0
                                s.op("pool", lambda e, eb=eb, jj=jj: e.tensor_tensor(out=E[eb][:, jj, :], in0=E[eb][:, jj, :],
                                                                                      in1=tri[:], op=ALU.mult),
                                     reads=[r_E[eb], r_c], writes=[r_E[eb]])
                            for jj, j in enumerate(js):
                                s.op("pe", lambda e, eb=eb, jj=jj, j=j, h=h, oc=oc, oacc=oacc, i=i:
                                     e.matmul(oacc[:, oc:oc + 129], lhsT=E[eb][:, jj, :], rhs=VP[:, j, h, 0:129],
                                              start=(j == 0), stop=(j == i)),
                                     reads=[r_E[eb], r_VP[j]], writes=[r_oacc[ob]])
                    kk = ihd % 2
                    ihd += 1
                    self.diff_finalize(oacc, r_oacc[ob], rr[kk], r_rr[kk], a1[kk], r_a1[kk], junk, r_junk, lamt, subl, r_c,
                                       mixf[k][:, h * 128:(h + 1) * 128], r_mixf[k], 128)
                s.dma("pool", lambda e, k=k, rows=rows: e.dma_start(out=self.mixbuf[rows, 0:512], in_=mixf[k][:]),
                      reads=[r_mixf[k]], writes=[self.mixres[i]])
            s.barrier()

    def diff_consts(self, dl, dlp, lamt, subl, junk, r_c, lam_init):
        s, di = self.s, self.din
        s.dma("sp", lambda e: e.dma_start(out=dl[:].rearrange("p a b -> p (a b)"),
                                          in_=di["diff_lambda"][0:1, :].to_broadcast([128, 256])), writes=[r_c])
        s.dma("sp", lambda e: e.dma_start(out=subl[:], in_=di["diff_subln"][0:1, :].to_broadcast([128, 128])), writes=[r_c])
        s.op("dve", lambda e: e.tensor_tensor(out=dlp[:, 0, :], in0=dl[:, 0, :], in1=dl[:, 1, :], op=ALU.mult), reads=[r_c], writes=[r_c])
        s.op("dve", lambda e: e.tensor_tensor(out=dlp[:, 1, :], in0=dl[:, 2, :], in1=dl[:, 3, :], op=ALU.mult), reads=[r_c], writes=[r_c])
        s.op("dve", lambda e: e.tensor_reduce(out=lamt[:, 0:2], in_=dlp[:], axis=AX.X, op=ALU.add), reads=[r_c], writes=[r_c])
        s.op("act", lambda e: e.activation(out=lamt[:, 4:6], in_=lamt[:, 0:2], func=AF.Exp), reads=[r_c], writes=[r_c])
        s.op("dve", lambda e: e.tensor_tensor(out=lamt[:, 2:3], in0=lamt[:, 4:5], in1=lamt[:, 5:6], op=ALU.subtract), reads=[r_c], writes=[r_c])
        s.op("dve", lambda e: e.tensor_scalar(out=lamt[:, 2:3], in0=lamt[:, 2:3], scalar1=lam_init, scalar2=None, op0=ALU.add),
             reads=[r_c], writes=[r_c])
        s.op("dve", lambda e: e.tensor_scalar(out=lamt[:, 3:4], in0=lamt[:, 2:3], scalar1=-1.0, scalar2=None, op0=ALU.mult),
             reads=[r_c], writes=[r_c])
        s.op("dve", lambda e: e.tensor_scalar(out=subl[:], in0=subl[:], scalar1=1.0 - lam_init, scalar2=None, op0=ALU.mult),
             reads=[r_c], writes=[r_c])

    def diff_finalize(self, oacc, r_oacc, rr, r_rr, a1, r_a1, junk, r_junk, lamt, subl, r_c, out_ap, r_out, np_):
        s = self.s
        P_ = slice(0, np_)
        s.op("dve", lambda e: e.reciprocal(out=rr[P_, 0:1], in_=oacc[P_, 128:129]), reads=[r_oacc], writes=[r_rr])
        s.op("dve", lambda e: e.reciprocal(out=rr[P_, 1:2], in_=oacc[P_, 260:261]), reads=[r_oacc], writes=[r_rr])
        s.op("dve", lambda e: e.tensor_tensor(out=rr[P_, 2:3], in0=rr[P_, 1:2], in1=lamt[P_, 3:4], op=ALU.mult),
             reads=[r_rr, r_c], writes=[r_rr])
        s.op("dve", lambda e: e.tensor_scalar(out=a1[P_, :], in0=oacc[P_, 0:128], scalar1=rr[P_, 0:1], scalar2=None, op0=ALU.mult),
             reads=[r_oacc, r_rr], writes=[r_a1])
        s.op("dve", lambda e: e.scalar_tensor_tensor(out=a1[P_, :], in0=oacc[P_, 132:260], scalar=rr[P_, 2:3], in1=a1[P_, :],
                                                     op0=ALU.mult, op1=ALU.add), reads=[r_oacc, r_rr, r_a1], writes=[r_a1])
        s.op("act", lambda e: e.activation(out=junk[P_, :], in_=a1[P_, :], func=AF.Square, accum_out=rr[P_, 3:4]),
             reads=[r_a1], writes=[r_junk, r_rr])
        s.op("act", lambda e: e.activation(out=rr[P_, 4:5], in_=rr[P_, 3:4], func=AF.Sqrt, bias=self.eps_t[P_, 0:1], scale=1.0 / 128.0),
             reads=[r_rr, self.rconst], writes=[r_rr])
        s.op("dve", lambda e: e.reciprocal(out=rr[P_, 5:6], in_=rr[P_, 4:5]), reads=[r_rr], writes=[r_rr])
        s.op("dve", lambda e: e.scalar_tensor_tensor(out=out_ap, in0=a1[P_, :], scalar=rr[P_, 5:6], in1=subl[P_, :],
                                                     op0=ALU.mult, op1=ALU.mult), reads=[r_a1, r_rr, r_c], writes=[r_out])

    def mixer0_S(self):
        nc, s, di, do = self.nc, self.s, self.din, self.dout
        LAM_INIT = 0.8 - 0.6 * float(np.exp(-0.3 * 0))
        NPG = 16
        with ExitStack() as es:
            T = lambda n, sh, dt: es.enter_context(nc.sbuf_tensor(self.name(n), sh, dt))
            P = lambda n, sh, dt: es.enter_context(nc.psum_tensor(self.name(n), sh, dt))
            pt_i = T("pt_i", [128, NSB * NPG], I32)
            io_i = T("io_i", [128, NSB * NPG], I32)
            idx = T("idx", [128, NSB * NPG], I32)
            pt_f = T("pt_f", [128, NSB * NPG], F32)
            io_f = T("io_f", [128, NSB * NPG], F32)
            selrows = T("selrows", [64, 64, 128], BF16)
            q_s = T("q_s", [64, 512], BF16)
            tri = T("tri", [128, 128], F32)
            dl = T("dl", [128, 4, 64], F32)
            dlp = T("dlp", [128, 2, 64], F32)
            lamt = T("lamt", [128, 8], F32)
            subl = T("subl", [128, 128], F32)
            junk = T("junk", [128, 128], F32)
            Qbc = [T("Qbc", [128, 4, 512], BF16) for _ in range(2)]
            Kg = [T("Kg", [128, 512], F32) for _ in range(4)]
            Vg = [T("Vg", [128, 512], F32) for _ in range(3)]
            VPb = [T("VPb", [128, NPG, 4, 132], BF16) for _ in range(2)]
            Kn = [T("Kn", [4, 512], F32) for _ in range(2)]
            Vn = [T("Vn", [4, 512], F32) for _ in range(2)]
            VPn = [T("VPn", [4, 4, 132], BF16) for _ in range(2)]
            prod = [T("prod", [128, 512], F32) for _ in range(2)]
            Sc = [T("Sc", [128, NPG + 1, 4, 2, 4], F32) for _ in range(2)]
            Eb = [T("Eb", [128, NPG + 1, 4, 2, 4], BF16) for _ in range(2)]
            rr = [T("rr", [128, 8], F32) for _ in range(2)]
            a1 = [T("a1", [128, 128], F32) for _ in range(2)]
            mixs = [T("mixs", [4, 512], BF16) for _ in range(2)]
            qps = [P("qps", [128, 512], F32) for _ in range(2)]
            ops_ = [P("ops", [128, 512], F32) for _ in range(4)]
            r_c = Res(); r_Qbc = [Res(), Res()]; r_Kg = [Res() for _ in range(4)]; r_Vg = [Res() for _ in range(3)]
            r_VPb = [Res(), Res()]; r_Kn = [Res(), Res()]; r_Vn = [Res(), Res()]; r_VPn = [Res(), Res()]
            r_prod = [Res(), Res()]; r_Sc = [Res(), Res()]; r_Eb = [Res(), Res()]; r_rr = [Res(), Res()]; r_a1 = [Res(), Res()]
            r_mixs = [Res(), Res()]; r_qps = [Res(), Res()]; r_ops = [Res() for _ in range(4)]; r_junk = Res()
            s.dma("sp", lambda e: e.dma_start(out=pt_i[:], in_=di["page_table"].rearrange("b g -> (b g)").rearrange("(o n) -> o n", o=1).to_broadcast([128, NSB * NPG])),
                  writes=[r_c])
            s.op("pool", lambda e: e.iota(io_i[:], pattern=[[0, NSB * NPG]], base=0, channel_multiplier=1), writes=[r_c])
            s.op("dve", lambda e: e.tensor_copy(out=pt_f[:], in_=pt_i[:]), reads=[r_c], writes=[r_c])
            s.op("dve", lambda e: e.tensor_copy(out=io_f[:], in_=io_i[:]), reads=[r_c], writes=[r_c])
            s.op("dve", lambda e: e.scalar_tensor_tensor(out=pt_f[:], in0=pt_f[:], scalar=128.0, in1=io_f[:], op0=ALU.mult, op1=ALU.add),
                 reads=[r_c], writes=[r_c])
            s.op("dve", lambda e: e.tensor_copy(out=idx[:], in_=pt_f[:]), reads=[r_c], writes=[r_c])
            s.op("pool", lambda e: e.memset(selrows[:], 1.0), writes=[r_c])
            s.op("pool", lambda e: e.affine_select(out=selrows[:], in_=selrows[:], pattern=[[-1, 64], [0, 128]],
                                                    compare_op=ALU.is_equal, fill=0.0, base=0, channel_multiplier=1),
                 reads=[r_c], writes=[r_c])
            s.op("pool", lambda e: e.memset(tri[:], 1.0), writes=[r_c])
            s.op("pool", lambda e: e.affine_select(out=tri[:], in_=tri[:], pattern=[[1, 128]], compare_op=ALU.is_ge,
                                                    fill=0.0, base=0, channel_multiplier=-1), reads=[r_c], writes=[r_c])
            s.dma("sp", lambda e: e.dma_start(out=q_s[:], in_=self.qsbuf[:, :]), reads=[self.r_aks], writes=[r_c])
            self.diff_consts(dl, dlp, lamt, subl, junk, r_c, LAM_INIT)
            for v in VPb:
                s.op("pool", lambda e, v=v: e.memset(v[:, :, :, 128:132], 1.0), writes=r_VPb)
            for v in VPn:
                s.op("pool", lambda e, v=v: e.memset(v[:, :, 128:132], 1.0), writes=r_VPn)
            for sc_ in Sc:
                s.op("pool", lambda e, sc_=sc_: e.memset(sc_[:], 0.0), writes=r_Sc)
            ikg = 0
            ivg = 0
            ipr = 0
            cak = di["cache_a_k"]
            cav = di["cache_a_v"]
            for b in range(NSB):
                kb = b % 2
                for qi in range(4):
                    kq = (b * 4 + qi) % 2
                    s.op("pe", lambda e, kq=kq, b=b, qi=qi: e.matmul(qps[kq][:], lhsT=selrows[:, 4 * b + qi, :], rhs=q_s[:, :],
                                                                     start=True, stop=True),
                         reads=[r_c], writes=[r_qps[kq]])
                    s.op("act", lambda e, kq=kq, kb=kb, qi=qi: e.copy(out=Qbc[kb][:, qi, :], in_=qps[kq][:]),
                         reads=[r_qps[kq]], writes=[r_Qbc[kb]])
                s.dma("sp", lambda e, kb=kb, b=b: e.dma_start(out=Kn[kb][:], in_=do["aks"][4 * b:4 * b + 4, :]),
                      reads=[self.r_aks], writes=[r_Kn[kb]])
                s.dma("sp", lambda e, kb=kb, b=b: e.dma_start(out=Vn[kb][:], in_=do["avs"][4 * b:4 * b + 4, :]),
                      reads=[self.r_aks], writes=[r_Vn[kb]])
                s.op("act", lambda e, kb=kb: e.copy(out=VPn[kb][:, :, 0:128], in_=Vn[kb][:].rearrange("p (h e) -> p h e", h=4)),
                     reads=[r_Vn[kb]], writes=[r_VPn[kb]])
                for pg in range(NPG):
                    kv = ivg % 3
                    ivg += 1
                    col = b * NPG + pg
                    s.dma("pool", lambda e, kv=kv, col=col: e.indirect_dma_start(
                        out=Vg[kv][:], out_offset=None, in_=cav[:, :],
                        in_offset=bass.IndirectOffsetOnAxis(ap=idx[:, col:col + 1], axis=0)),
                        reads=[r_c], writes=[r_Vg[kv]])
                    s.op("act", lambda e, kv=kv, kb=kb, pg=pg: e.copy(out=VPb[kb][:, pg, :, 0:128],
                                                                      in_=Vg[kv][:].rearrange("p (h e) -> p h e", h=4)),
                         reads=[r_Vg[kv]], writes=[r_VPb[kb]])
                for pg in range(NPG + 1):
                    if pg < NPG:
                        kk = ikg % 4
                        ikg += 1
                        col = b * NPG + pg
                        s.dma("pool", lambda e, kk=kk, col=col: e.indirect_dma_start(
                            out=Kg[kk][:], out_offset=None, in_=cak[:, :],
                            in_offset=bass.IndirectOffsetOnAxis(ap=idx[:, col:col + 1], axis=0)),
                            reads=[r_c], writes=[r_Kg[kk]])
                        ksrc, r_ks, npart = Kg[kk], r_Kg[kk], 128
                    else:
                        ksrc, r_ks, npart = Kn[kb], r_Kn[kb], 4
                    for qi in range(4):
                        kp = ipr % 2
                        ipr += 1
                        s.op("dve", lambda e, kp=kp, ksrc=ksrc, npart=npart, kb=kb, qi=qi:
                             e.tensor_tensor(out=prod[kp][0:npart, :], in0=ksrc[0:npart, :], in1=Qbc[kb][0:npart, qi, :], op=ALU.mult),
                             reads=[r_ks, r_Qbc[kb]], writes=[r_prod[kp]])
                        s.op("dve", lambda e, kp=kp, npart=npart, kb=kb, pg=pg, qi=qi:
                             e.tensor_reduce(out=Sc[kb][0:npart, pg, :, :, qi].rearrange("p h m -> p (h m)"),
                                             in_=prod[kp][0:npart, :].rearrange("p (g d) -> p g d", d=64), axis=AX.X, op=ALU.add),
                             reads=[r_prod[kp]], writes=[r_Sc[kb]])
                s.op("act", lambda e, kb=kb: e.activation(out=Eb[kb][:].rearrange("p a h m q -> p (a h m q)"),
                                                          in_=Sc[kb][:].rearrange("p a h m q -> p (a h m q)"),
                                                          func=AF.Exp, scale=0.125), reads=[r_Sc[kb]], writes=[r_Eb[kb]])
                s.op("pool", lambda e, kb=kb: e.tensor_tensor(out=Eb[kb][0:4, NPG].rearrange("p h m q -> p (h m) q"),
                                                               in0=Eb[kb][0:4, NPG].rearrange("p h m q -> p (h m) q"),
                                                               in1=tri[0:4, 0:4].unsqueeze(1).to_broadcast([4, 8, 4]), op=ALU.mult),
                     reads=[r_Eb[kb], r_c], writes=[r_Eb[kb]])
                for h in range(4):
                    for mm in range(2):
                        oc = mm * 132
                        for pg in range(NPG + 1):
                            if pg < NPG:
                                s.op("pe", lambda e, h=h, mm=mm, oc=oc, pg=pg, kb=kb:
                                     e.matmul(ops_[h][0:4, oc:oc + 129], lhsT=Eb[kb][:, pg, h, mm, :], rhs=VPb[kb][:, pg, h, 0:129],
                                              start=(pg == 0), stop=False),
                                     reads=[r_Eb[kb], r_VPb[kb]], writes=[r_ops[h]])
                            else:
                                s.op("pe", lambda e, h=h, mm=mm, oc=oc, pg=pg, kb=kb:
                                     e.matmul(ops_[h][0:4, oc:oc + 129], lhsT=Eb[kb][0:4, pg, h, mm, :], rhs=VPn[kb][0:4, h, 0:129],
                                              start=False, stop=True),
                                     reads=[r_Eb[kb], r_VPn[kb]], writes=[r_ops[h]])
                    kk2 = (b * 4 + h) % 2
                    self.diff_finalize(ops_[h], r_ops[h], rr[kk2], r_rr[kk2], a1[kk2], r_a1[kk2], junk, r_junk, lamt, subl, r_c,
                                       mixs[kb][0:4, h * 128:(h + 1) * 128], r_mixs[kb], 4)
                s.dma("sp", lambda e, kb=kb, b=b: e.dma_start(out=self.mixbuf[SEQ + 4 * b:SEQ + 4 * b + 4, 0:512], in_=mixs[kb][:]),
                      reads=[r_mixs[kb]], writes=[self.mixres[32]])
            s.barrier()

    def s5_setup(self, es, S):
        nc, s, di = self.nc, self.s, self.din
        T = lambda n, sh, dt: es.enter_context(nc.sbuf_tensor(self.name(n), sh, dt))
        PI = float(np.pi)
        r = S["r_c"] = Res()
        rc = [r]
        par = T("s5par", [128, 24, 16], F32)
        pari = T("s5pari", [128, 128], I32)
        S["par"] = par
        (A_RE, A_IM, LDT, DT, MAG, ANG, SN, CS, LRE, LIM, DEN, NR, FRE, FIM, T0, T1, NEGPI, TWOPI) = range(18)
        S["idx"] = dict(RHO=MAG, SN=SN, CS=CS, LRE=LRE, LIM=LIM)
        pv = lambda i: par[:, i, :]
        flat = lambda a: a.rearrange("g p -> (g p)").rearrange("(j q) -> q j", q=128)
        s.dma("sp", lambda e: e.dma_start(out=pv(A_RE), in_=flat(di["s5_a_re"]), allow_slow_non_contiguous=True), writes=rc)
        s.dma("sp", lambda e: e.dma_start(out=pv(A_IM), in_=flat(di["s5_a_im"]), allow_slow_non_contiguous=True), writes=rc)
        ld2 = di["s5_log_dt"].rearrange("o (j two) -> o two j", two=2)
        s.dma("sp", lambda e: e.dma_start(out=par[0:64, LDT, :], in_=ld2[0:1, 0, :].to_broadcast([64, 16]), allow_slow_non_contiguous=True), writes=rc)
        s.dma("sp", lambda e: e.dma_start(out=par[64:128, LDT, :], in_=ld2[0:1, 1, :].to_broadcast([64, 16]), allow_slow_non_contiguous=True), writes=rc)
        s.op("pool", lambda e: e.memset(pv(NEGPI), -PI), writes=rc)
        s.op("pool", lambda e: e.memset(pv(TWOPI), 2 * PI), writes=rc)
        V = lambda fn: s.op("dve", fn, reads=rc, writes=rc)
        Aop = lambda fn: s.op("act", fn, reads=rc, writes=rc)
        Aop(lambda e: e.activation(out=pv(DT), in_=pv(LDT), func=AF.Exp))
        V(lambda e: e.tensor_tensor(out=pv(T0), in0=pv(A_RE), in1=pv(DT), op=ALU.mult))
        Aop(lambda e: e.activation(out=pv(MAG), in_=pv(T0), func=AF.Exp))
        V(lambda e: e.tensor_tensor(out=pv(ANG), in0=pv(A_IM), in1=pv(DT), op=ALU.mult))

        def sincos(out_sin, out_cos, ang_ap, tmp_ap, tmpi_ap):
            TWO_PI_S = 2 * PI - 2e-6
            for dst, off in ((out_sin, 0.0), (out_cos, 0.25)):
                V(lambda e, off=off: e.tensor_scalar(out=tmp_ap, in0=ang_ap, scalar1=1.0 / (2 * PI), scalar2=off, op0=ALU.mult, op1=ALU.add))
                V(lambda e: e.tensor_copy(out=tmpi_ap, in_=tmp_ap))
                V(lambda e: e.tensor_tensor(out=tmp_ap, in0=tmp_ap, in1=tmpi_ap, op=ALU.subtract))
                Aop(lambda e, dst=dst: e.activation(out=dst, in_=tmp_ap, func=AF.Sin, scale=TWO_PI_S))
        S["sincos"] = sincos
        sincos(pv(SN), pv(CS), pv(ANG), pv(T0), pari[:, 0:16])
        V(lambda e: e.tensor_tensor(out=pv(LRE), in0=pv(MAG), in1=pv(CS), op=ALU.mult))
        V(lambda e: e.tensor_tensor(out=pv(LIM), in0=pv(MAG), in1=pv(SN), op=ALU.mult))
        V(lambda e: e.tensor_tensor(out=pv(DEN), in0=pv(A_RE), in1=pv(A_RE), op=ALU.mult))
        V(lambda e: e.tensor_tensor(out=pv(T0), in0=pv(A_IM), in1=pv(A_IM), op=ALU.mult))
        V(lambda e: e.tensor_tensor(out=pv(DEN), in0=pv(DEN), in1=pv(T0), op=ALU.add))
        V(lambda e: e.reciprocal(out=pv(DEN), in_=pv(DEN)))
        V(lambda e: e.tensor_scalar(out=pv(NR), in0=pv(LRE), scalar1=-1.0, scalar2=None, op0=ALU.add))
        V(lambda e: e.tensor_tensor(out=pv(T0), in0=pv(NR), in1=pv(A_RE), op=ALU.mult))
        V(lambda e: e.tensor_tensor(out=pv(T1), in0=pv(LIM), in1=pv(A_IM), op=ALU.mult))
        V(lambda e: e.tensor_tensor(out=pv(T0), in0=pv(T0), in1=pv(T1), op=ALU.add))
        V(lambda e: e.tensor_tensor(out=pv(FRE), in0=pv(T0), in1=pv(DEN), op=ALU.mult))
        V(lambda e: e.tensor_tensor(out=pv(T0), in0=pv(LIM), in1=pv(A_RE), op=ALU.mult))
        V(lambda e: e.tensor_tensor(out=pv(T1), in0=pv(NR), in1=pv(A_IM), op=ALU.mult))
        V(lambda e: e.tensor_tensor(out=pv(T0), in0=pv(T0), in1=pv(T1), op=ALU.subtract))
        V(lambda e: e.tensor_tensor(out=pv(FIM), in0=pv(T0), in1=pv(DEN), op=ALU.mult))
        M4 = T("M4", [128, 4, 8, 16], F32)
        s.op("pool", lambda e: e.memset(M4[:], 0.0), writes=rc)
        for v in range(4):
            s.op("pool", lambda e, v=v: e.memset(M4[0:64, v, 2 * v, :], 1.0), writes=rc)
            s.op("pool", lambda e, v=v: e.memset(M4[64:128, v, 2 * v + 1, :], 1.0), writes=rc)
        BBT = S["BBT"] = T("BBT", [128, 2, 16, 128], BF16)
        CX = S["CX"] = T("CX", [128, 2, 16, 128], BF16)
        with ExitStack() as es2:
            T2 = lambda n, sh, dt: es2.enter_context(nc.sbuf_tensor(self.name(n), sh, dt))
            P2 = lambda n, sh, dt: es2.enter_context(nc.psum_tensor(self.name(n), sh, dt))
            Bre = T2("Bre", [128, 16, 16], F32); Bim = T2("Bim", [128, 16, 16], F32)
            bbr = T2("bbr", [128, 16, 16], F32); bbi = T2("bbi", [128, 16, 16], F32); tt = T2("tt", [128, 16, 16], F32)
            BX = T2("BX", [128, 128], F32)
            Cnat = T2("Cnat", [128, 4, 128], F32)
            CT = [T2("CTd", [128, 4, 128], F32) for _ in range(2)]
            ptr = [P2("ptr", [128, 128], F32) for _ in range(2)]
            r_ptr = [Res(), Res()]; r_BX = Res()
            bflat = lambda a: a.rearrange("g p c -> (g p) c").rearrange("(j q) c -> q j c", q=128)
            s.dma("sp", lambda e: e.dma_start(out=Bre[:], in_=bflat(di["s5_b_re"])), writes=rc)
            s.dma("sp", lambda e: e.dma_start(out=Bim[:], in_=bflat(di["s5_b_im"])), writes=rc)
            fre_b = par[:, FRE, :].unsqueeze(2).to_broadcast([128, 16, 16])
            fim_b = par[:, FIM, :].unsqueeze(2).to_broadcast([128, 16, 16])
            V(lambda e: e.tensor_tensor(out=bbr[:], in0=Bre[:], in1=fre_b, op=ALU.mult))
            V(lambda e: e.tensor_tensor(out=tt[:], in0=Bim[:], in1=fim_b, op=ALU.mult))
            V(lambda e: e.tensor_tensor(out=bbr[:], in0=bbr[:], in1=tt[:], op=ALU.subtract))
            V(lambda e: e.tensor_tensor(out=bbi[:], in0=Bim[:], in1=fre_b, op=ALU.mult))
            V(lambda e: e.tensor_tensor(out=tt[:], in0=Bre[:], in1=fim_b, op=ALU.mult))
            V(lambda e: e.tensor_tensor(out=bbi[:], in0=bbi[:], in1=tt[:], op=ALU.add))
            it = 0
            for ri, bb in enumerate((bbr, bbi)):
                for j in range(16):
                    k = it % 2
                    it += 1
                    s.op("dve", lambda e, bb=bb, j=j: e.tensor_tensor(out=BX[:].rearrange("p (g c) -> p g c", g=8),
                                                                      in0=bb[:, j, :].unsqueeze(1).to_broadcast([128, 8, 16]),
                                                                      in1=M4[:, j % 4, :, :], op=ALU.mult),
                         reads=rc + [r_BX], writes=[r_BX])
                    s.op("pe", lambda e, k=k: e.transpose(out=ptr[k][:], in_=BX[:], identity=self.ident_f[:]),
                         reads=[r_BX, self.rconst], writes=[r_ptr[k]])
                    s.op("act", lambda e, k=k, ri=ri, j=j: e.copy(out=BBT[:, ri, j, :], in_=ptr[k][:]), reads=[r_ptr[k]], writes=rc)
            for ri, nm in enumerate(("s5_c_re", "s5_c_im")):
                cnatv = di[nm].rearrange("g c p -> (g c) p").rearrange("(m q) p -> q m p", q=128)
                s.dma("sp", lambda e, cnatv=cnatv: e.dma_start(out=Cnat[:, :, 0:64], in_=cnatv), writes=rc)
                s.dma("sp", lambda e, cnatv=cnatv: e.dma_start(out=Cnat[:, :, 64:128], in_=cnatv), writes=rc)
                for mch in range(4):
                    k = it % 2
                    it += 1
                    s.op("pe", lambda e, k=k, mch=mch: e.transpose(out=ptr[k][:], in_=Cnat[:, mch, :], identity=self.ident_f[:]),
                         reads=rc + [self.rconst], writes=[r_ptr[k]])
                    s.op("act", lambda e, k=k, ri=ri, mch=mch: e.copy(out=CT[ri][:, mch, :], in_=ptr[k][:]), reads=[r_ptr[k]], writes=rc)
                sgn = 1.0 if ri == 0 else -1.0
                for j in range(16):
                    V(lambda e, ri=ri, j=j, sgn=sgn: e.scalar_tensor_tensor(out=CX[:, ri, j, :], in0=CT[ri][:, j // 4, :], scalar=sgn,
                                                                            in1=M4[:, j % 4, :, :].rearrange("p g c -> p (g c)"),
                                                                            op0=ALU.mult, op1=ALU.mult))
            s.barrier()
        COS = S["COS"] = T("COS", [128, 16, 128], F32)
        SIN = S["SIN"] = T("SIN", [128, 16, 128], F32)
        jl = T("jl", [128, 128], F32)
        tb = T("tb", [128, 128], F32)
        tb2 = T("tb2", [128, 128], F32)
        jli = T("jli", [128, 128], I32)
        s.op("pool", lambda e: e.iota(jli[:], pattern=[[1, 128]], base=0, channel_multiplier=0), writes=rc)
        V(lambda e: e.tensor_copy(out=jl[:], in_=jli[:]))
        for j in range(16):
            V(lambda e, j=j: e.tensor_scalar(out=tb[:], in0=jl[:], scalar1=par[:, ANG, j:j + 1], scalar2=None, op0=ALU.mult))
            sincos(SIN[:, j, :], COS[:, j, :], tb[:], tb2[:], pari[:, :])
        dg = S["dg"] = T("dg", [128, 2, 4], F32)
        s.dma("sp", lambda e: e.dma_start(out=dg[:, 0, :], in_=di["s5_d"].rearrange("o (m q) -> q (o m)", q=128), allow_slow_non_contiguous=True), writes=rc)
        s.dma("sp", lambda e: e.dma_start(out=dg[:, 1, :], in_=di["s5_glu_b"].rearrange("o (m q) -> q (o m)", q=128), allow_slow_non_contiguous=True), writes=rc)
        gwf = T("gwf", [128, 4, 512], F32)
        gw = S["gw"] = T("gw", [128, 4, 512], BF16)
        s.dma("sp", lambda e: e.dma_start(out=gwf[:], in_=di["s5_glu_w"].rearrange("(m q) n -> q m n", q=128)), writes=rc)
        V(lambda e: e.tensor_copy(out=gw[:], in_=gwf[:]))

    def mixer0_B(self, cur, nxt):
        nc, s, di, do = self.nc, self.s, self.din, self.dout
        with ExitStack() as es:
            T = lambda n, sh, dt: es.enter_context(nc.sbuf_tensor(self.name(n), sh, dt))
            P = lambda n, sh, dt: es.enter_context(nc.psum_tensor(self.name(n), sh, dt))
            S = {}
            self.s5_setup(es, S)
            par, COS, SIN, BBT, CX, dg, gw = S["par"], S["COS"], S["SIN"], S["BBT"], S["CX"], S["dg"], S["gw"]
            ix = S["idx"]
            r_c = S["r_c"]
            m = self.alloc_mod(es)
            ewout = T("ewout", [128, NKC, D], BF16)
            xt = [T("xt", [128, D], F32) for _ in range(2)]
            ua = [T("ua", [128, 2, 512], BF16) for _ in range(2)]
            fT = [T("fT", [128, 8, 128], BF16) for _ in range(2)]
            y5T = [T("y5T", [128, 4, 128], BF16) for _ in range(2)]
            z = [T("z", [128, 4, 128], BF16) for _ in range(2)]
            NW = 6
            w = [T("w", [128, 4, 128], F32) for _ in range(NW)]
            gr = T("gr", [128, 4, 128], F32); gi = T("gi", [128, 4, 128], F32)
            hr = T("hr", [128, 4, 128], F32); hi = T("hi", [128, 4, 128], F32)
            hrb = [T("hrb", [128, 4, 128], BF16) for _ in range(2)]; hib = [T("hib", [128, 4, 128], BF16) for _ in range(2)]
            ysb = T("ysb", [128, 128], F32); ysq = T("ysq", [128, 128], F32); ysg = T("ysg", [128, 128], F32)
            carry = T("carry", [128, 4, 16], F32)
            ctmp = T("ctmp", [128, 2, 16], F32)
            t1 = [T("t1", [128, D], F32) for _ in range(2)]
            stt = [T("st", [128, 16], F32) for _ in range(2)]
            COSs = T("COSs", [128, 16, 64], F32); SINs = T("SINs", [128, 16, 64], F32); RHOm = T("RHOm", [128, 16, 64], F32)
            bmask = T("bmask", [128, 16, 4], F32)
            h0 = T("h0", [16, 2, 2048], F32); h0n = T("h0n", [128, 2, 16, 16], F32); inj = T("inj", [128, 2, 16, 16], F32)
            hout = T("hout", [128, 2, 16, 16], F32); houtT = T("houtT", [16, 2, 2048], F32)
            xre = P("xre", [128, 4, 128], F32); xim = P("xim", [128, 4, 128], F32)
            py = P("py", [128, 512], F32); pgl = P("pgl", [128, 512], F32)
            tpb = P("tpb", [128, 8, 128], BF16)
            po = [P("po", [128, 512], F32) for _ in range(2)]
            pmisc = P("pmisc", [128, 512], F32)
            r_ew = Res(); r_x = [Res(), Res()]; r_ua = [Res(), Res()]; r_fT = [Res(), Res()]; r_y5 = [Res(), Res()]; r_z = [Res(), Res()]
            r_w = [Res() for _ in range(NW)]; r_gr = Res(); r_gi = Res(); r_hr = Res(); r_hi = Res(); r_hb = [Res(), Res()]
            r_ys = Res(); r_carry = Res(); r_t1 = [Res(), Res()]; r_st = [Res(), Res()]
            r_xre = Res(); r_xim = Res(); r_py = Res(); r_pgl = Res(); r_tp = Res(); r_po = [Res(), Res()]; r_pm = Res(); r_s = Res()
            s.dma("sp", lambda e: e.dma_start(out=ewout[:], in_=self.ewout_b[:, :].rearrange("(kc p) n -> p kc n", p=128)), writes=[r_ew])
            self.load_mod(m, 0, 1, None)
            self.bcast_mod(m, False, pmisc, r_pm)
            s.op("pool", lambda e: e.memset(carry[:], 0.0), writes=[r_carry])
            Vs = lambda fn: s.op("dve", fn, reads=[r_c, r_s], writes=[r_s])
            Vs(lambda e: e.tensor_copy(out=COSs[:].rearrange("p j (b t) -> p j b t", t=4), in_=COS[:, :, 0:4].unsqueeze(2).to_broadcast([128, 16, 16, 4])))
            Vs(lambda e: e.tensor_copy(out=SINs[:].rearrange("p j (b t) -> p j b t", t=4), in_=SIN[:, :, 0:4].unsqueeze(2).to_broadcast([128, 16, 16, 4])))
            s.op("pool", lambda e: e.memset(bmask[:], 1.0), reads=[r_s], writes=[r_s])
            s.op("pool", lambda e: e.memset(bmask[:, :, 0:1], 0.0), reads=[r_s], writes=[r_s])
            Vs(lambda e: e.tensor_tensor(out=RHOm[:], in0=par[:, ix["RHO"], :].unsqueeze(2).to_broadcast([128, 16, 64]),
                                         in1=bmask[:].rearrange("p b t -> p (b t)").unsqueeze(1).to_broadcast([128, 16, 64]), op=ALU.mult))
            s.dma("sp", lambda e: e.dma_start(out=h0[:, 0, :], in_=di["state_s5_re"][:, :]), writes=[r_s])
            s.dma("sp", lambda e: e.dma_start(out=h0[:, 1, :], in_=di["state_s5_im"][:, :]), writes=[r_s])
            for ri in range(2):
                for j in range(16):
                    s.op("pe", lambda e, ri=ri, j=j: e.transpose(out=pmisc[:, (ri * 16 + j) * 16:(ri * 16 + j + 1) * 16],
                                                                 in_=h0[:, ri, j * 128:(j + 1) * 128], identity=self.ident_f[0:16, 0:16]),
                         reads=[r_s, self.rconst], writes=[r_pm])
            Vs2 = lambda fn: s.op("dve", fn, reads=[r_c, r_s, r_pm], writes=[r_s])
            Vs2(lambda e: e.tensor_copy(out=h0n[:].rearrange("p a j b -> p (a j b)"), in_=pmisc[:, 0:512]))
            lre_b = par[:, ix["LRE"], :].unsqueeze(2).to_broadcast([128, 16, 16])
            lim_b = par[:, ix["LIM"], :].unsqueeze(2).to_broadcast([128, 16, 16])
            Vs(lambda e: e.tensor_tensor(out=inj[:, 0], in0=h0n[:, 0], in1=lre_b, op=ALU.mult))
            Vs(lambda e: e.tensor_tensor(out=hout[:, 0], in0=h0n[:, 1], in1=lim_b, op=ALU.mult))
            Vs(lambda e: e.tensor_tensor(out=inj[:, 0], in0=inj[:, 0], in1=hout[:, 0], op=ALU.subtract))
            Vs(lambda e: e.tensor_tensor(out=inj[:, 1], in0=h0n[:, 1], in1=lre_b, op=ALU.mult))
            Vs(lambda e: e.tensor_tensor(out=hout[:, 0], in0=h0n[:, 0], in1=lim_b, op=ALU.mult))
            Vs(lambda e: e.tensor_tensor(out=inj[:, 1], in0=inj[:, 1], in1=hout[:, 0], op=ALU.add))
            iw = [0]

            def W():
                k = iw[0] % NW
                iw[0] += 1
                return w[k], r_w[k]

            for i in range(NTILE):
                sample = (i == 32)
                k = i % 2
                Wd = 64 if sample else 128
                if sample:
                    self.bcast_mod(m, True, pmisc, r_pm)
                ap, nrows, res = self.x_src(False, cur, i)
                rows = slice(i * 128, (i + 1) * 128)
                s.dma("sp", lambda e, k=k, ap=ap: e.dma_start(out=xt[k][:], in_=ap), reads=[res], writes=[r_x[k]])
                s.dma("sp", lambda e, k=k, rows=rows: e.dma_start(out=ua[k][:, 0, :], in_=self.ubuf[rows, :]), reads=[self.ures[i]], writes=[r_ua[k]])
                s.dma("sp", lambda e, k=k, rows=rows: e.dma_start(out=ua[k][:, 1, :], in_=self.mixbuf[rows, 0:512]), reads=[self.mixres[i]], writes=[r_ua[k]])
                for c in range(8):
                    s.op("pe", lambda e, k=k, c=c: e.transpose(out=tpb[:, c, :], in_=ua[k][:, c // 4, (c % 4) * 128:(c % 4 + 1) * 128],
                                                               identity=self.ident_b[:]), reads=[r_ua[k], self.rconst], writes=[r_tp])
                s.op("act", lambda e, k=k: e.copy(out=fT[k][:], in_=tpb[:]), reads=[r_tp], writes=[r_fT[k]])
                Ct = COSs if sample else COS
                St = SINs if sample else SIN
                for jg in range(4):
                    js = slice(jg * 4, jg * 4 + 4)
                    for jj in range(4):
                        j = jg * 4 + jj
                        s.op("pe", lambda e, k=k, j=j, jj=jj, jg=jg, Wd=Wd: e.matmul(xre[:, jj, 0:Wd], lhsT=BBT[:, 0, j, :], rhs=fT[k][:, jg, 0:Wd],
                                                                                   start=True, stop=True), reads=[r_c, r_fT[k]], writes=[r_xre])
                        s.op("pe", lambda e, k=k, j=j, jj=jj, jg=jg, Wd=Wd: e.matmul(xim[:, jj, 0:Wd], lhsT=BBT[:, 1, j, :], rhs=fT[k][:, jg, 0:Wd],
                                                                                   start=True, stop=True), reads=[r_c, r_fT[k]], writes=[r_xim])
                    wa, r_wa = W(); wb, r_wb = W(); xr, r_xr = W(); xi, r_xi = W()
                    Cv = Ct[:, js, 0:Wd]; Sv = St[:, js, 0:Wd]
                    rcs = [r_c, r_s]
                    s.op("dve", lambda e, wa=wa, Cv=Cv, Wd=Wd: e.tensor_tensor(out=wa[:, :, 0:Wd], in0=xre[:, :, 0:Wd], in1=Cv, op=ALU.mult), reads=[r_xre] + rcs, writes=[r_wa])
                    s.op("dve", lambda e, wb=wb, Sv=Sv, Wd=Wd: e.tensor_tensor(out=wb[:, :, 0:Wd], in0=xim[:, :, 0:Wd], in1=Sv, op=ALU.mult), reads=[r_xim] + rcs, writes=[r_wb])
                    s.op("pool", lambda e, wa=wa, wb=wb, xr=xr, Wd=Wd: e.tensor_tensor(out=xr[:, :, 0:Wd], in0=wa[:, :, 0:Wd], in1=wb[:, :, 0:Wd], op=ALU.add), reads=[r_wa, r_wb], writes=[r_xr])
                    wa2, r_wa2 = W(); wb2, r_wb2 = W()
                    s.op("dve", lambda e, wa2=wa2, Cv=Cv, Wd=Wd: e.tensor_tensor(out=wa2[:, :, 0:Wd], in0=xim[:, :, 0:Wd], in1=Cv, op=ALU.mult), reads=[r_xim] + rcs, writes=[r_wa2])
                    s.op("dve", lambda e, wb2=wb2, Sv=Sv, Wd=Wd: e.tensor_tensor(out=wb2[:, :, 0:Wd], in0=xre[:, :, 0:Wd], in1=Sv, op=ALU.mult), reads=[r_xre] + rcs, writes=[r_wb2])
                    s.op("pool", lambda e, wa2=wa2, wb2=wb2, xi=xi, Wd=Wd: e.tensor_tensor(out=xi[:, :, 0:Wd], in0=wa2[:, :, 0:Wd], in1=wb2[:, :, 0:Wd], op=ALU.subtract), reads=[r_wa2, r_wb2], writes=[r_xi])
                    if sample:
                        s.op("pool", lambda e, xr=xr, js=js: e.tensor_tensor(out=xr[:, :, 0:64].rearrange("p j (b t) -> p j b t", t=4)[:, :, :, 0],
                                                                             in0=xr[:, :, 0:64].rearrange("p j (b t) -> p j b t", t=4)[:, :, :, 0],
                                                                             in1=inj[:, 0, js, :], op=ALU.add), reads=[r_xr, r_s], writes=[r_xr])
                        s.op("pool", lambda e, xi=xi, js=js: e.tensor_tensor(out=xi[:, :, 0:64].rearrange("p j (b t) -> p j b t", t=4)[:, :, :, 0],
                                                                             in0=xi[:, :, 0:64].rearrange("p j (b t) -> p j b t", t=4)[:, :, :, 0],
                                                                             in1=inj[:, 1, js, :], op=ALU.add), reads=[r_xi, r_s], writes=[r_xi])
                    for jj in range(4):
                        j = jg * 4 + jj
                        if sample:
                            d0 = RHOm[:, j, :]
                            ini_r = 0.0
                            ini_i = 0.0
                        else:
                            d0 = par[:, ix["RHO"], j:j + 1].to_broadcast([128, 128])
                            ini_r = carry[:, 2, j:j + 1]
                            ini_i = carry[:, 3, j:j + 1]
                        s.op("dve", lambda e, xr=xr, jj=jj, d0=d0, ini_r=ini_r, Wd=Wd: e.tensor_tensor_scan(out=gr[:, jj, 0:Wd], data0=d0, data1=xr[:, jj, 0:Wd],
                                                                                                       initial=ini_r, op0=ALU.mult, op1=ALU.add),
                             reads=[r_xr, r_c, r_s, r_carry], writes=[r_gr])
                        s.op("dve", lambda e, xi=xi, jj=jj, d0=d0, ini_i=ini_i, Wd=Wd: e.tensor_tensor_scan(out=gi[:, jj, 0:Wd], data0=d0, data1=xi[:, jj, 0:Wd],
                                                                                                       initial=ini_i, op0=ALU.mult, op1=ALU.add),
                             reads=[r_xi, r_c, r_s, r_carry], writes=[r_gi])
                    wa, r_wa = W(); wb, r_wb = W()
                    s.op("dve", lambda e, wa=wa, Cv=Cv, Wd=Wd: e.tensor_tensor(out=wa[:, :, 0:Wd], in0=gr[:, :, 0:Wd], in1=Cv, op=ALU.mult), reads=[r_gr] + rcs, writes=[r_wa])
                    s.op("pool", lambda e, wb=wb, Sv=Sv, Wd=Wd: e.tensor_tensor(out=wb[:, :, 0:Wd], in0=gi[:, :, 0:Wd], in1=Sv, op=ALU.mult), reads=[r_gi] + rcs, writes=[r_wb])
                    s.op("dve", lambda e, wa=wa, wb=wb, Wd=Wd: e.tensor_tensor(out=hr[:, :, 0:Wd], in0=wa[:, :, 0:Wd], in1=wb[:, :, 0:Wd], op=ALU.subtract), reads=[r_wa, r_wb], writes=[r_hr])
                    wa2, r_wa2 = W(); wb2, r_wb2 = W()
                    s.op("pool", lambda e, wa2=wa2, Sv=Sv, Wd=Wd: e.tensor_tensor(out=wa2[:, :, 0:Wd], in0=gr[:, :, 0:Wd], in1=Sv, op=ALU.mult), reads=[r_gr] + rcs, writes=[r_wa2])
                    s.op("dve", lambda e, wb2=wb2, Cv=Cv, Wd=Wd: e.tensor_tensor(out=wb2[:, :, 0:Wd], in0=gi[:, :, 0:Wd], in1=Cv, op=ALU.mult), reads=[r_gi] + rcs, writes=[r_wb2])
                    s.op("dve", lambda e, wa2=wa2, wb2=wb2, Wd=Wd: e.tensor_tensor(out=hi[:, :, 0:Wd], in0=wa2[:, :, 0:Wd], in1=wb2[:, :, 0:Wd], op=ALU.add), reads=[r_wa2, r_wb2], writes=[r_hi])
                    kh = (i * 4 + jg) % 2
                    s.op("act", lambda e, kh=kh, Wd=Wd: e.copy(out=hrb[kh][:, :, 0:Wd], in_=hr[:, :, 0:Wd]), reads=[r_hr], writes=[r_hb[kh]])
                    s.op("act", lambda e, kh=kh, Wd=Wd: e.copy(out=hib[kh][:, :, 0:Wd], in_=hi[:, :, 0:Wd]), reads=[r_hi], writes=[r_hb[kh]])
                    if not sample:
                        s.op("pool", lambda e, js=js: e.tensor_copy(out=carry[:, 0, js], in_=hr[:, :, 127]), reads=[r_hr], writes=[r_carry])
                        s.op("pool", lambda e, js=js: e.tensor_copy(out=carry[:, 1, js], in_=hi[:, :, 127]), reads=[r_hi], writes=[r_carry])
                        cs_ = par[:, ix["CS"], js]; sn_ = par[:, ix["SN"], js]
                        s.op("dve", lambda e, js=js, cs_=cs_: e.tensor_tensor(out=ctmp[:, 0, js], in0=carry[:, 0, js], in1=cs_, op=ALU.mult), reads=[r_carry, r_c], writes=[r_carry])
                        s.op("dve", lambda e, js=js, sn_=sn_: e.tensor_tensor(out=ctmp[:, 1, js], in0=carry[:, 1, js], in1=sn_, op=ALU.mult), reads=[r_carry, r_c], writes=[r_carry])
                        s.op("dve", lambda e, js=js: e.tensor_tensor(out=carry[:, 2, js], in0=ctmp[:, 0, js], in1=ctmp[:, 1, js], op=ALU.subtract), reads=[r_carry], writes=[r_carry])
                        s.op("dve", lambda e, js=js, sn_=sn_: e.tensor_tensor(out=ctmp[:, 0, js], in0=carry[:, 0, js], in1=sn_, op=ALU.mult), reads=[r_carry, r_c], writes=[r_carry])
                        s.op("dve", lambda e, js=js, cs_=cs_: e.tensor_tensor(out=ctmp[:, 1, js], in0=carry[:, 1, js], in1=cs_, op=ALU.mult), reads=[r_carry, r_c], writes=[r_carry])
                        s.op("dve", lambda e, js=js: e.tensor_tensor(out=carry[:, 3, js], in0=ctmp[:, 0, js], in1=ctmp[:, 1, js], op=ALU.add), reads=[r_carry], writes=[r_carry])
                    else:
                        s.op("pool", lambda e, js=js: e.tensor_copy(out=hout[:, 0, js, :], in_=hr[:, :, 0:64].rearrange("p j (b t) -> p j b t", t=4)[:, :, :, 3]),
                             reads=[r_hr, r_s], writes=[r_s])
                        s.op("pool", lambda e, js=js: e.tensor_copy(out=hout[:, 1, js, :], in_=hi[:, :, 0:64].rearrange("p j (b t) -> p j b t", t=4)[:, :, :, 3]),
                             reads=[r_hi, r_s], writes=[r_s])
                    for jj in range(4):
                        j = jg * 4 + jj
                        s.op("pe", lambda e, kh=kh, j=j, jj=jj, Wd=Wd: e.matmul(py[:, 0:Wd], lhsT=CX[:, 0, j, :], rhs=hrb[kh][:, jj, 0:Wd],
                                                                                start=(jj == 0), stop=False), reads=[r_c, r_hb[kh]], writes=[r_py])
                        s.op("pe", lambda e, kh=kh, j=j, jj=jj, Wd=Wd: e.matmul(py[:, 0:Wd], lhsT=CX[:, 1, j, :], rhs=hib[kh][:, jj, 0:Wd],
                                                                                start=False, stop=(jj == 3)), reads=[r_c, r_hb[kh]], writes=[r_py])
                    s.op("dve", lambda e, k=k, jg=jg, Wd=Wd: e.scalar_tensor_tensor(out=ysb[:, 0:Wd], in0=fT[k][:, jg, 0:Wd], scalar=dg[:, 0, jg:jg + 1],
                                                                                     in1=py[:, 0:Wd], op0=ALU.mult, op1=ALU.add),
                         reads=[r_fT[k], r_py, r_c], writes=[r_ys])
                    s.op("pool", lambda e, Wd=Wd: e.tensor_tensor(out=ysq[:, 0:Wd], in0=ysb[:, 0:Wd], in1=ysb[:, 0:Wd], op=ALU.mult), reads=[r_ys], writes=[r_ys])
                    s.op("pool", lambda e, Wd=Wd: e.tensor_scalar(out=ysq[:, 0:Wd], in0=ysq[:, 0:Wd], scalar1=0.044715, scalar2=1.0, op0=ALU.mult, op1=ALU.add), reads=[r_ys], writes=[r_ys])
                    s.op("pool", lambda e, Wd=Wd: e.tensor_tensor(out=ysq[:, 0:Wd], in0=ysq[:, 0:Wd], in1=ysb[:, 0:Wd], op=ALU.mult), reads=[r_ys], writes=[r_ys])
                    s.op("act", lambda e, Wd=Wd: e.activation(out=ysg[:, 0:Wd], in_=ysq[:, 0:Wd], func=AF.Sigmoid, scale=1.5957691216057308), reads=[r_ys], writes=[r_ys])
                    s.op("dve", lambda e, k=k, jg=jg, Wd=Wd: e.tensor_tensor(out=z[k][:, jg, 0:Wd], in0=ysb[:, 0:Wd], in1=ysg[:, 0:Wd], op=ALU.mult), reads=[r_ys], writes=[r_z[k]])
                for mo in range(4):
                    for mi in range(4):
                        s.op("pe", lambda e, k=k, mo=mo, mi=mi, Wd=Wd: e.matmul(pgl[:, 0:Wd], lhsT=gw[:, mi, mo * 128:(mo + 1) * 128], rhs=z[k][:, mi, 0:Wd],
                                                                                start=(mi == 0), stop=(mi == 3)), reads=[r_c, r_z[k]], writes=[r_pgl])
                    s.op("act", lambda e, mo=mo, Wd=Wd: e.activation(out=ysg[:, 0:Wd], in_=pgl[:, 0:Wd], func=AF.Sigmoid, bias=dg[:, 1, mo:mo + 1], scale=1.0),
                         reads=[r_pgl, r_c, r_ys], writes=[r_ys])
                    s.op("dve", lambda e, k=k, mo=mo, Wd=Wd: e.tensor_tensor(out=y5T[k][:, mo, 0:Wd], in0=z[k][:, mo, 0:Wd], in1=ysg[:, 0:Wd], op=ALU.mult),
                         reads=[r_z[k], r_ys], writes=[r_y5[k]])
                if sample:
                    s.op("pool", lambda e, k=k: e.memset(y5T[k][:, :, 64:128], 0.0), reads=[r_y5[k]], writes=[r_y5[k]])
                for h in range(2):
                    for kc in range(8):
                        lhs = fT[k][:, 4 + kc, :] if kc < 4 else y5T[k][:, kc - 4, :]
                        rd = r_fT[k] if kc < 4 else r_y5[k]
                        s.op("pe", lambda e, h=h, kc=kc, lhs=lhs: e.matmul(po[h][:], lhsT=lhs, rhs=ewout[:, kc, h * 512:(h + 1) * 512],
                                                                           start=(kc == 0), stop=(kc == 7)), reads=[rd, r_ew], writes=[r_po[h]])
                self.epilogue(m, [po[0][:], po[1][:]], r_po, xt[k][:], r_x[k], t1[k], r_t1[k], stt[k], r_st[k],
                              self.x_dst(False, nxt, i), None)
            s.dma("pool", lambda e: e.dma_start(out=do["s5rp"].rearrange("o (j q) -> q (o j)", q=128), in_=carry[:, 0, :], allow_slow_non_contiguous=True), reads=[r_carry])
            s.dma("pool", lambda e: e.dma_start(out=do["s5ip"].rearrange("o (j q) -> q (o j)", q=128), in_=carry[:, 1, :], allow_slow_non_contiguous=True), reads=[r_carry])
            for ri in range(2):
                for j in range(16):
                    s.op("pe", lambda e, ri=ri, j=j: e.transpose(out=pmisc[0:16, ((ri * 16 + j) % 4) * 128:((ri * 16 + j) % 4 + 1) * 128],
                                                                 in_=hout[:, ri, j, :], identity=self.ident_f[:]),
                         reads=[r_s, self.rconst], writes=[r_pm])
                    s.op("act", lambda e, ri=ri, j=j: e.copy(out=houtT[:, ri, j * 128:(j + 1) * 128],
                                                             in_=pmisc[0:16, ((ri * 16 + j) % 4) * 128:((ri * 16 + j) % 4 + 1) * 128]),
                         reads=[r_pm], writes=[r_s])
            s.dma("pool", lambda e: e.dma_start(out=do["s5rs"][:, :], in_=houtT[:, 0, :]), reads=[r_s])
            s.dma("pool", lambda e: e.dma_start(out=do["s5is"][:, :], in_=houtT[:, 1, :]), reads=[r_s])
            s.barrier()

    def dil_masks(self, Wm_out, nfree, pattern, base, es2, r_c):
        nc, s = self.nc, self.s
        T2 = lambda n, sh, dt: es2.enter_context(nc.sbuf_tensor(self.name(n), sh, dt))
        Di = T2("Di", [128, nfree], I32); Df = T2("Df", [128, nfree], F32); Ti = T2("Ti", [128, nfree], I32)
        ge0 = T2("ge0", [128, nfree], F32); acc = T2("accm", [128, nfree], F32); tf = T2("tfm", [128, nfree], F32); t2 = T2("t2m", [128, nfree], F32)
        rc = [r_c]
        V = lambda fn: s.op("dve", fn, reads=rc, writes=rc)
        s.op("pool", lambda e: e.iota(Di[:], pattern=pattern, base=base, channel_multiplier=-1), reads=rc, writes=rc)
        V(lambda e: e.tensor_copy(out=Df[:], in_=Di[:]))
        V(lambda e: e.tensor_scalar(out=ge0[:], in0=Df[:], scalar1=0.0, scalar2=None, op0=ALU.is_ge))
        V(lambda e: e.tensor_scalar(out=acc[:], in0=Df[:], scalar1=128.0, scalar2=None, op0=ALU.is_le))
        for msk, lim in ((3, 512.0), (15, 2048.0)):
            V(lambda e, msk=msk: e.tensor_scalar(out=Ti[:], in0=Di[:], scalar1=msk, scalar2=None, op0=ALU.bitwise_and))
            V(lambda e: e.tensor_copy(out=tf[:], in_=Ti[:]))
            V(lambda e: e.tensor_scalar(out=tf[:], in0=tf[:], scalar1=0.0, scalar2=None, op0=ALU.is_equal))
            V(lambda e, lim=lim: e.tensor_scalar(out=t2[:], in0=Df[:], scalar1=lim, scalar2=None, op0=ALU.is_le))
            V(lambda e: e.tensor_tensor(out=tf[:], in0=tf[:], in1=t2[:], op=ALU.mult))
            V(lambda e: e.tensor_tensor(out=acc[:], in0=acc[:], in1=tf[:], op=ALU.add))
        V(lambda e: e.tensor_tensor(out=Wm_out, in0=acc[:], in1=ge0[:], op=ALU.mult))

    def mixer1_A(self, cur):
        nc, s, di, do = self.nc, self.s, self.din, self.dout
        ND = 17
        with ExitStack() as es:
            T = lambda n, sh, dt: es.enter_context(nc.sbuf_tensor(self.name(n), sh, dt))
            P = lambda n, sh, dt: es.enter_context(nc.psum_tensor(self.name(n), sh, dt))
            r_c = Res()
            Wm = T("Wm", [128, ND, 128], BF16)
            with ExitStack() as es2:
                self.dil_masks(Wm[:].rearrange("p a b -> p (a b)"), ND * 128, [[128, ND], [1, 128]], 0, es2, r_c)
                s.barrier()
            m = self.alloc_mod(es)
            owin = T("owin", [128, NKC, 3328], BF16)
            xt = [T("xt", [128, D], F32) for _ in range(2)]
            tmpf = T("tmpf", [128, D], F32)
            hb = [T("hb", [128, D], BF16) for _ in range(2)]
            hT = [T("hT", [128, NKC, 128], BF16) for _ in range(2)]
            kv_f = [T("kv_f", [128, 1024], F32) for _ in range(2)]
            pd_f1 = T("pd_f", [128, 1792], F32)
            pd_f = [pd_f1, pd_f1]
            q_b = [T("q_b", [128, 512], BF16) for _ in range(2)]
            k_b = [T("k_b", [128, 512], BF16) for _ in range(2)]
            qT = [T("qT", [128, 4, 128], BF16) for _ in range(2)]
            KT = T("KT1", [128, 4, SEQ], BF16)
            VP = T("VP1", [128, 32, 8, 66], BF16)
            E = [T("E", [128, 4, 128], BF16) for _ in range(3)]
            rr = [T("rr", [128, 2], F32) for _ in range(2)]
            mixf = [T("mixf", [128, 512], BF16) for _ in range(2)]
            pp = [P("pp", [128, 512], F32) for _ in range(4)]
            tpb = P("tpb", [128, NKC, 128], BF16)
            st = [P("st", [128, 512], F32) for _ in range(2)]
            oacc = P("oacc", [128, 512], F32)
            r_ow = Res(); r_x = [Res(), Res()]; r_tmpf = Res(); r_hb = [Res(), Res()]; r_hT = [Res(), Res()]
            r_kv = [Res(), Res()]; r_pd1 = Res(); r_pd = [r_pd1, r_pd1]; r_pp = [Res() for _ in range(4)]; r_tp = Res()
            r_q = [Res(), Res()]; r_k = [Res(), Res()]; r_qT = [Res(), Res()]
            r_KT = [Res() for _ in range(32)]; r_VP = [Res() for _ in range(32)]; r_E = [Res() for _ in range(3)]
            r_rr = [Res(), Res()]; r_mixf = [Res(), Res()]; r_st = [Res(), Res()]; r_oacc = Res()
            s.dma("sp", lambda e: e.dma_start(out=owin[:], in_=self.owin_b[:, :].rearrange("(kc p) n -> p kc n", p=128)), writes=[r_ow])
            s.op("pool", lambda e: e.memset(VP[:].rearrange("p a h e -> p (a h e)"), 1.0), writes=r_VP)
            self.load_mod(m, 1, 1, None)
            self.bcast_mod(m, False, pp[0], r_pp[0])
            widths = [512] * 6 + [256]
            ist = 0; ie = 0; ihd = 0
            for i in range(NTILE):
                sample = (i == 32)
                k = i % 2
                if sample:
                    self.bcast_mod(m, True, pp[0], r_pp[0])
                ap, nrows, res = self.x_src(False, cur, i)
                s.dma("sp", lambda e, k=k, ap=ap: e.dma_start(out=xt[k][:], in_=ap), reads=[res], writes=[r_x[k]])
                s.op("pool", lambda e, k=k: e.tensor_tensor(out=tmpf[:], in0=xt[k][:], in1=m["SC"][:], op=ALU.mult),
                     reads=[r_x[k], m["r_mod"]], writes=[r_tmpf])
                s.op("dve", lambda e, k=k: e.tensor_tensor(out=hb[k][:], in0=tmpf[:], in1=m["SH"][:], op=ALU.add),
                     reads=[r_tmpf, m["r_mod"]], writes=[r_hb[k]])
                for kc in range(NKC):
                    s.op("pe", lambda e, k=k, kc=kc: e.transpose(out=tpb[:, kc, :], in_=hb[k][:, kc * 128:(kc + 1) * 128],
                                                                 identity=self.ident_b[:]),
                         reads=[r_hb[k], self.rconst], writes=[r_tp])
                s.op("act", lambda e, k=k: e.copy(out=hT[k][:], in_=tpb[:]), reads=[r_tp], writes=[r_hT[k]])

                def proj(c, bank):
                    wdt = widths[c]
                    for kc in range(NKC):
                        s.op("pe", lambda e, kc=kc, k=k, wdt=wdt, c=c, bank=bank: e.matmul(pp[bank][:, 0:wdt], lhsT=hT[k][:, kc, :],
                                                             rhs=owin[:, kc, c * 512:c * 512 + wdt],
                                                             start=(kc == 0), stop=(kc == NKC - 1)),
                             reads=[r_hT[k], r_ow], writes=[r_pp[bank]])
                for c in range(3):
                    proj(c, c)
                s.op("act", lambda e, k=k: e.copy(out=q_b[k][:], in_=pp[0][:]), reads=[r_pp[0]], writes=[r_q[k]])
                s.op("dve", lambda e, k=k: e.tensor_copy(out=kv_f[k][:, 0:512], in_=pp[1][:]), reads=[r_pp[1]], writes=[r_kv[k]])
                s.op("dve", lambda e, k=k: e.tensor_copy(out=kv_f[k][:, 512:1024], in_=pp[2][:]), reads=[r_pp[2]], writes=[r_kv[k]])
                for c in range(3, 7):
                    proj(c, c - 3)
                    wdt = widths[c]
                    s.op("dve", lambda e, c=c, wdt=wdt, k=k: e.tensor_copy(out=pd_f[k][:, (c - 3) * 512:(c - 3) * 512 + wdt], in_=pp[c - 3][:, 0:wdt]),
                         reads=[r_pp[c - 3]], writes=[r_pd[k]])
                rows_all = slice(i * 128, (i + 1) * 128)
                s.dma("pool", lambda e, k=k, rows_all=rows_all: e.dma_start(out=self.pdbuf[rows_all, :], in_=pd_f[k][:]),
                      reads=[r_pd[k]], writes=[self.pdres[i]])
                if not sample:
                    if i >= 16:
                        rows = slice((i - 16) * 128, (i - 15) * 128)
                        s.dma("pool", lambda e, k=k, rows=rows: e.dma_start(out=do["ckp"][rows, :], in_=kv_f[k][:, 0:512]), reads=[r_kv[k]])
                        s.dma("pool", lambda e, k=k, rows=rows: e.dma_start(out=do["cvp"][rows, :], in_=kv_f[k][:, 512:1024]), reads=[r_kv[k]])
                    if i == 31:
                        s.dma("pool", lambda e, k=k: e.dma_start(out=do["dsp"][0:1, :], in_=pd_f[k][127:128, :]), reads=[r_pd[k]])
                else:
                    s.dma("pool", lambda e, k=k: e.dma_start(out=do["cks"][:, :], in_=kv_f[k][0:NS, 0:512]), reads=[r_kv[k]], writes=[self.r_cks])
                    s.dma("pool", lambda e, k=k: e.dma_start(out=do["cvs"][:, :], in_=kv_f[k][0:NS, 512:1024]), reads=[r_kv[k]], writes=[self.r_cks])
                    s.dma("pool", lambda e, k=k: e.dma_start(out=self.qsbuf[:, :], in_=q_b[k][0:NS, :]), reads=[r_q[k]], writes=[self.r_cks])
                    for b in range(NSB):
                        s.dma("pool", lambda e, b=b, k=k: e.dma_start(out=do["dss"][b:b + 1, :], in_=pd_f[k][4 * b + 3:4 * b + 4, :]), reads=[r_pd[k]])
                    continue
                if self.cfg.get("nodil"):
                    continue
                s.op("act", lambda e, k=k: e.copy(out=k_b[k][:], in_=kv_f[k][:, 0:512]), reads=[r_kv[k]], writes=[r_k[k]])
                s.op("act", lambda e, i=i, k=k: e.copy(out=VP[:, i, :, 0:64], in_=kv_f[k][:, 512:1024].rearrange("p (h e) -> p h e", h=8)),
                     reads=[r_kv[k]], writes=[r_VP[i]])
                for h in range(4):
                    s.op("pe", lambda e, k=k, h=h: e.transpose(out=tpb[:, h, :], in_=q_b[k][:, h * 128:(h + 1) * 128],
                                                               identity=self.ident_b[:]), reads=[r_q[k], self.rconst], writes=[r_tp])
                    s.op("pe", lambda e, k=k, h=h: e.transpose(out=tpb[:, 4 + h, :], in_=k_b[k][:, h * 128:(h + 1) * 128],
                                                               identity=self.ident_b[:]), reads=[r_k[k], self.rconst], writes=[r_tp])
                s.op("act", lambda e, k=k: e.copy(out=qT[k][:], in_=tpb[:, 0:4, :]), reads=[r_tp], writes=[r_qT[k]])
                s.op("act", lambda e, i=i: e.copy(out=KT[:, :, i * 128:(i + 1) * 128], in_=tpb[:, 4:8, :]), reads=[r_tp], writes=[r_KT[i]])
                nd = min(ND, i + 1)
                for h in range(8):
                    hp = h // 2
                    pr = slice((h % 2) * 64, (h % 2) * 64 + 64)
                    oc = (h % 2) * 66
                    for d0 in range(0, nd, 4):
                        ds_ = list(range(d0, min(d0 + 4, nd)))
                        sb = ist % 2; ist += 1
                        for jj, dlt in enumerate(ds_):
                            j = i - dlt
                            s.op("pe", lambda e, sb=sb, jj=jj, j=j, hp=hp, pr=pr, k=k:
                                 e.matmul(st[sb][:, jj * 128:(jj + 1) * 128], lhsT=KT[pr, hp, j * 128:(j + 1) * 128],
                                          rhs=qT[k][pr, hp, :], start=True, stop=True),
                                 reads=[r_KT[j], r_qT[k]], writes=[r_st[sb]])
                        eb = ie % 3; ie += 1
                        w = len(ds_)
                        s.op("act", lambda e, eb=eb, sb=sb, w=w: e.activation(out=E[eb][:, 0:w, :],
                                                                             in_=st[sb][:, 0:w * 128].rearrange("p (a b) -> p a b", a=w),
                                                                             func=AF.Exp, scale=0.125), reads=[r_st[sb]], writes=[r_E[eb]])
                        s.op("pool", lambda e, eb=eb, w=w, d0=d0: e.tensor_tensor(out=E[eb][:, 0:w, :], in0=E[eb][:, 0:w, :],
                                                                                   in1=Wm[:, d0:d0 + w, :], op=ALU.mult),
                             reads=[r_E[eb], r_c], writes=[r_E[eb]])
                        for jj, dlt in enumerate(ds_):
                            j = i - dlt
                            s.op("pe", lambda e, eb=eb, jj=jj, j=j, h=h, oc=oc, dlt=dlt, nd=nd:
                                 e.matmul(oacc[:, oc:oc + 65], lhsT=E[eb][:, jj, :], rhs=VP[:, j, h, 0:65],
                                          start=(dlt == 0), stop=(dlt == nd - 1)),
                                 reads=[r_E[eb], r_VP[j]], writes=[r_oacc])
                    kk = ihd % 2; ihd += 1
                    s.op("dve", lambda e, kk=kk, oc=oc: e.reciprocal(out=rr[kk][:, 0:1], in_=oacc[:, oc + 64:oc + 65]), reads=[r_oacc], writes=[r_rr[kk]])
                    s.op("dve", lambda e, kk=kk, oc=oc, h=h, k=k: e.tensor_scalar(out=mixf[k][:, h * 64:(h + 1) * 64], in0=oacc[:, oc:oc + 64],
                                                                                  scalar1=rr[kk][:, 0:1], scalar2=None, op0=ALU.mult),
                         reads=[r_oacc, r_rr[kk]], writes=[r_mixf[k]])
                s.dma("pool", lambda e, k=k, rows_all=rows_all: e.dma_start(out=self.mixbuf[rows_all, 0:512], in_=mixf[k][:]),
                      reads=[r_mixf[k]], writes=[self.mixres[i]])
            s.barrier()

    def mixer1_S(self):
        nc, s, di, do = self.nc, self.s, self.din, self.dout
        NPG = 16
        with ExitStack() as es:
            T = lambda n, sh, dt: es.enter_context(nc.sbuf_tensor(self.name(n), sh, dt))
            P = lambda n, sh, dt: es.enter_context(nc.psum_tensor(self.name(n), sh, dt))
            r_c = Res()
            Ws = T("Ws", [128, NPG + 1, 4], BF16)
            with ExitStack() as es2:
                self.dil_masks(Ws[:].rearrange("p a b -> p (a b)"), (NPG + 1) * 4, [[-128, NPG + 1], [1, 4]], 2048, es2, r_c)
                s.barrier()
            selrows = T("selrows", [64, 64, 128], BF16)
            q_s = T("q_s", [64, 512], BF16)
            Qbc = [T("Qbc", [128, 4, 512], BF16) for _ in range(2)]
            Kg = [T("Kg", [128, 512], F32) for _ in range(4)]
            Vg = [T("Vg", [128, 512], F32) for _ in range(3)]
            VPb = [T("VPb", [128, NPG, 8, 66], BF16) for _ in range(2)]
            Kn = [T("Kn", [4, 512], F32) for _ in range(2)]
            Vn = [T("Vn", [4, 512], F32) for _ in range(2)]
            VPn = [T("VPn", [4, 8, 66], BF16) for _ in range(2)]
            prod = [T("prod", [128, 512], F32) for _ in range(2)]
            Sc = [T("Sc", [128, NPG + 1, 8, 4], F32) for _ in range(2)]
            Eb = [T("Eb", [128, NPG + 1, 8, 4], BF16) for _ in range(2)]
            rr = [T("rr", [128, 2], F32) for _ in range(2)]
            mixs = [T("mixs", [4, 512], BF16) for _ in range(2)]
            qps = [P("qps", [128, 512], F32) for _ in range(2)]
            ops_ = [P("ops", [128, 512], F32) for _ in range(2)]
            r_Qbc = [Res(), Res()]; r_Kg = [Res() for _ in range(4)]; r_Vg = [Res() for _ in range(3)]
            r_VPb = [Res(), Res()]; r_Kn = [Res(), Res()]; r_Vn = [Res(), Res()]; r_VPn = [Res(), Res()]
            r_prod = [Res(), Res()]; r_Sc = [Res(), Res()]; r_Eb = [Res(), Res()]; r_rr = [Res(), Res()]
            r_mixs = [Res(), Res()]; r_qps = [Res(), Res()]; r_ops = [Res(), Res()]
            s.op("pool", lambda e: e.memset(selrows[:], 1.0), writes=[r_c])
            s.op("pool", lambda e: e.affine_select(out=selrows[:], in_=selrows[:], pattern=[[-1, 64], [0, 128]],
                                                    compare_op=ALU.is_equal, fill=0.0, base=0, channel_multiplier=1),
                 reads=[r_c], writes=[r_c])
            s.dma("sp", lambda e: e.dma_start(out=q_s[:], in_=self.qsbuf[:, :]), reads=[self.r_cks], writes=[r_c])
            for v in VPb:
                s.op("pool", lambda e, v=v: e.memset(v[:].rearrange("p a h e -> p (a h e)"), 1.0), writes=r_VPb)
            for v in VPn:
                s.op("pool", lambda e, v=v: e.memset(v[:].rearrange("p h e -> p (h e)"), 1.0), writes=r_VPn)
            for sc_ in Sc:
                s.op("pool", lambda e, sc_=sc_: e.memset(sc_[:].rearrange("p a h q -> p (a h q)"), 0.0), writes=r_Sc)
            ikg = 0; ivg = 0; ipr = 0
            cck = di["cache_c_k"]; ccv = di["cache_c_v"]
            for b in range(NSB):
                kb = b % 2
                for qi in range(4):
                    kq = (b * 4 + qi) % 2
                    s.op("pe", lambda e, kq=kq, b=b, qi=qi: e.matmul(qps[kq][:], lhsT=selrows[:, 4 * b + qi, :], rhs=q_s[:, :],
                                                                     start=True, stop=True), reads=[r_c], writes=[r_qps[kq]])
                    s.op("act", lambda e, kq=kq, kb=kb, qi=qi: e.copy(out=Qbc[kb][:, qi, :], in_=qps[kq][:]),
                         reads=[r_qps[kq]], writes=[r_Qbc[kb]])
                s.dma("sp", lambda e, kb=kb, b=b: e.dma_start(out=Kn[kb][:], in_=do["cks"][4 * b:4 * b + 4, :]),
                      reads=[self.r_cks], writes=[r_Kn[kb]])
                s.dma("sp", lambda e, kb=kb, b=b: e.dma_start(out=Vn[kb][:], in_=do["cvs"][4 * b:4 * b + 4, :]),
                      reads=[self.r_cks], writes=[r_Vn[kb]])
                s.op("act", lambda e, kb=kb: e.copy(out=VPn[kb][:, :, 0:64], in_=Vn[kb][:].rearrange("p (h e) -> p h e", h=8)),
                     reads=[r_Vn[kb]], writes=[r_VPn[kb]])
                for pg in range(NPG):
                    kv = ivg % 3; ivg += 1
                    r0 = b * 2048 + pg * 128
                    s.dma("sp", lambda e, kv=kv, r0=r0: e.dma_start(out=Vg[kv][:], in_=ccv[r0:r0 + 128, :]), writes=[r_Vg[kv]])
                    s.op("act", lambda e, kv=kv, kb=kb, pg=pg: e.copy(out=VPb[kb][:, pg, :, 0:64],
                                                                      in_=Vg[kv][:].rearrange("p (h e) -> p h e", h=8)),
                         reads=[r_Vg[kv]], writes=[r_VPb[kb]])
                for pg in range(NPG + 1):
                    if pg < NPG:
                        kk = ikg % 4; ikg += 1
                        r0 = b * 2048 + pg * 128
                        s.dma("sp", lambda e, kk=kk, r0=r0: e.dma_start(out=Kg[kk][:], in_=cck[r0:r0 + 128, :]), writes=[r_Kg[kk]])
                        ksrc, r_ks, npart = Kg[kk], r_Kg[kk], 128
                    else:
                        ksrc, r_ks, npart = Kn[kb], r_Kn[kb], 4
                    for qi in range(4):
                        kp = ipr % 2; ipr += 1
                        s.op("dve", lambda e, kp=kp, ksrc=ksrc, npart=npart, kb=kb, qi=qi:
                             e.tensor_tensor(out=prod[kp][0:npart, :], in0=ksrc[0:npart, :], in1=Qbc[kb][0:npart, qi, :], op=ALU.mult),
                             reads=[r_ks, r_Qbc[kb]], writes=[r_prod[kp]])
                        s.op("dve", lambda e, kp=kp, npart=npart, kb=kb, pg=pg, qi=qi:
                             e.tensor_reduce(out=Sc[kb][0:npart, pg, :, qi],
                                             in_=prod[kp][0:npart, :].rearrange("p (g d) -> p g d", d=64), axis=AX.X, op=ALU.add),
                             reads=[r_prod[kp]], writes=[r_Sc[kb]])
                s.op("act", lambda e, kb=kb: e.activation(out=Eb[kb][:].rearrange("p a h q -> p (a h q)"),
                                                          in_=Sc[kb][:].rearrange("p a h q -> p (a h q)"),
                                                          func=AF.Exp, scale=0.125), reads=[r_Sc[kb]], writes=[r_Eb[kb]])
                s.op("pool", lambda e, kb=kb: e.tensor_tensor(out=Eb[kb][:], in0=Eb[kb][:],
                                                               in1=Ws[:].unsqueeze(2).to_broadcast([128, NPG + 1, 8, 4]), op=ALU.mult),
                     reads=[r_Eb[kb], r_c], writes=[r_Eb[kb]])
                for h in range(8):
                    ob = h // 4
                    oc = (h % 4) * 66
                    for pg in range(NPG + 1):
                        if pg < NPG:
                            s.op("pe", lambda e, h=h, oc=oc, ob=ob, pg=pg, kb=kb:
                                 e.matmul(ops_[ob][0:4, oc:oc + 65], lhsT=Eb[kb][:, pg, h, :], rhs=VPb[kb][:, pg, h, 0:65],
                                          start=(pg == 0), stop=False), reads=[r_Eb[kb], r_VPb[kb]], writes=[r_ops[ob]])
                        else:
                            s.op("pe", lambda e, h=h, oc=oc, ob=ob, pg=pg, kb=kb:
                                 e.matmul(ops_[ob][0:4, oc:oc + 65], lhsT=Eb[kb][0:4, pg, h, :], rhs=VPn[kb][0:4, h, 0:65],
                                          start=False, stop=True), reads=[r_Eb[kb], r_VPn[kb]], writes=[r_ops[ob]])
                    k2 = (b * 8 + h) % 2
                    s.op("dve", lambda e, k2=k2, ob=ob, oc=oc: e.reciprocal(out=rr[k2][0:4, 0:1], in_=ops_[ob][0:4, oc + 64:oc + 65]),
                         reads=[r_ops[ob]], writes=[r_rr[k2]])
                    s.op("dve", lambda e, k2=k2, ob=ob, oc=oc, h=h, kb=kb: e.tensor_scalar(out=mixs[kb][0:4, h * 64:(h + 1) * 64], in0=ops_[ob][0:4, oc:oc + 64],
                                                                                           scalar1=rr[k2][0:4, 0:1], scalar2=None, op0=ALU.mult),
                         reads=[r_ops[ob], r_rr[k2]], writes=[r_mixs[kb]])
                s.dma("sp", lambda e, kb=kb, b=b: e.dma_start(out=self.mixbuf[SEQ + 4 * b:SEQ + 4 * b + 4, 0:512], in_=mixs[kb][:]),
                      reads=[r_mixs[kb]], writes=[self.mixres[32]])
            s.barrier()

    def mixer1_P(self):
        nc, s, di, do = self.nc, self.s, self.din, self.dout
        with ExitStack() as es:
            T = lambda n, sh, dt: es.enter_context(nc.sbuf_tensor(self.name(n), sh, dt))
            P = lambda n, sh, dt: es.enter_context(nc.psum_tensor(self.name(n), sh, dt))
            MU = T("MU", [128, 1792], F32)
            CB = T("CB", [128, 5, 512], F32)
            W12 = T("W12", [128, 512], F32); G2 = T("G2", [128, 512], F32)
            pd = [T("pdt", [128, 1792], F32) for _ in range(2)]
            pv = [T("pvt", [128, 1792], F32) for _ in range(2)]
            xm = [T("xm", [128, 1792], F32) for _ in range(2)]
            X3 = T("X3", [128, 256], F32); T12 = T("T12", [128, 2, 128], F32)
            o = [T("rwo", [128, 8, 512], F32) for _ in range(2)]
            tA = T("tA", [128, 512], F32); tB = T("tB", [128, 512], F32); av = T("av", [128, 512], F32)
            sm = T("sm", [128, 32], F32)
            ptr = [P("ptr", [128, 128], F32) for _ in range(2)]
            pL = [P("pL", [128, 512], F32) for _ in range(3)]
            r_c = Res(); r_pd = [Res(), Res()]; r_pv = [Res(), Res()]; r_xm = [Res(), Res()]; r_X3 = Res(); r_T12 = Res()
            r_o = [Res(), Res()]; r_t = Res(); r_ptr = [Res(), Res()]; r_pL = [Res() for _ in range(3)]
            bc = lambda ap, n: ap.to_broadcast([128, n])
            s.dma("sp", lambda e: e.dma_start(out=MU[:], in_=bc(di["rwkv_mu"][0:1, :], 1792)), writes=[r_c])
            for j, nm in enumerate(("rwkv_w0", "rwkv_a0", "rwkv_k_k", "rwkv_k_a", "rwkv_r_k")):
                s.dma("sp", lambda e, j=j, nm=nm: e.dma_start(out=CB[:, j, :], in_=bc(di[nm][0:1, :], 512)), writes=[r_c])
            s.dma("sp", lambda e: e.dma_start(out=W12[0:64, :], in_=di["rwkv_w2"][:, :]), writes=[r_c])
            s.dma("sp", lambda e: e.dma_start(out=W12[64:128, :], in_=di["rwkv_a2"][:, :]), writes=[r_c])
            s.dma("sp", lambda e: e.dma_start(out=G2[:], in_=di["rwkv_g2"][:, :]), writes=[r_c])
            for b_ in pv:
                s.op("pool", lambda e, b_=b_: e.memset(b_[:], 0.0), writes=r_pv)
            for i in range(NTILE):
                k = i % 2
                sample = (i == 32)
                rows = slice(i * 128, (i + 1) * 128)
                s.dma("sp", lambda e, k=k, rows=rows: e.dma_start(out=pd[k][:], in_=self.pdbuf[rows, :]), reads=[self.pdres[i]], writes=[r_pd[k]])
                if i == 0:
                    s.dma("sp", lambda e, k=k: e.dma_start(out=pv[k][1:128, :], in_=self.pdbuf[0:127, :]), reads=[self.pdres[0]], writes=[r_pv[k]])
                elif not sample:
                    s.dma("sp", lambda e, k=k, i=i: e.dma_start(out=pv[k][:, :], in_=self.pdbuf[i * 128 - 1:i * 128 + 127, :]),
                          reads=[self.pdres[i], self.pdres[i - 1]], writes=[r_pv[k]])
                else:
                    s.dma("sp", lambda e, k=k: e.dma_start(out=pv[k][1:64, :], in_=self.pdbuf[SEQ:SEQ + 63, :]), reads=[self.pdres[32]], writes=[r_pv[k]])
                    for b in range(NSB):
                        s.dma("sp", lambda e, k=k, b=b: e.dma_start(out=pv[k][4 * b:4 * b + 1, :], in_=di["state_d_shift"][b:b + 1, :]), writes=[r_pv[k]])
                X, PD, PV_ = xm[k], pd[k], pv[k]
                s.op("pool", lambda e, X=X, PD=PD, PV_=PV_: e.tensor_tensor(out=X[:], in0=PV_[:], in1=PD[:], op=ALU.subtract), reads=[r_pd[k], r_pv[k]], writes=[r_xm[k]])
                s.op("dve", lambda e, X=X: e.tensor_tensor(out=X[:], in0=X[:], in1=MU[:], op=ALU.mult), reads=[r_xm[k], r_c], writes=[r_xm[k]])
                s.op("pool", lambda e, X=X, PD=PD: e.tensor_tensor(out=X[:], in0=X[:], in1=PD[:], op=ALU.add), reads=[r_xm[k], r_pd[k]], writes=[r_xm[k]])
                s.op("act", lambda e, X=X: e.activation(out=X3[:, 0:64], in_=X[:, 1536:1600], func=AF.Tanh), reads=[r_xm[k]], writes=[r_X3])
                s.op("act", lambda e, X=X: e.copy(out=X3[:, 64:128], in_=X[:, 1600:1664]), reads=[r_xm[k]], writes=[r_X3])
                s.op("act", lambda e, X=X: e.activation(out=X3[:, 128:256], in_=X[:, 1664:1792], func=AF.Sigmoid), reads=[r_xm[k]], writes=[r_X3])
                for j in range(2):
                    s.op("pe", lambda e, j=j: e.transpose(out=ptr[j][:], in_=X3[:, j * 128:(j + 1) * 128], identity=self.ident_f[:]),
                         reads=[r_X3, self.rconst], writes=[r_ptr[j]])
                    s.op("dve", lambda e, j=j: e.tensor_copy(out=T12[:, j, :], in_=ptr[j][:]), reads=[r_ptr[j]], writes=[r_T12])
                s.op("pe", lambda e: e.matmul(pL[0][:], lhsT=T12[0:64, 0, :], rhs=W12[0:64, :], start=True, stop=True), reads=[r_T12, r_c], writes=[r_pL[0]])
                s.op("pe", lambda e: e.matmul(pL[1][:], lhsT=T12[64:128, 0, :], rhs=W12[64:128, :], start=True, stop=True), reads=[r_T12, r_c], writes=[r_pL[1]])
                s.op("pe", lambda e: e.matmul(pL[2][:], lhsT=T12[:, 1, :], rhs=G2[:, :], start=True, stop=True), reads=[r_T12, r_c], writes=[r_pL[2]])
                O = o[k]
                ro = [r_o[k]]
                rt = [r_t]
                s.op("dve", lambda e: e.tensor_tensor(out=tA[:], in0=pL[0][:], in1=CB[:, 0, :], op=ALU.add), reads=[r_pL[0], r_c], writes=rt)
                s.op("act", lambda e: e.activation(out=tA[:], in_=tA[:], func=AF.Sigmoid), reads=rt, writes=rt)
                s.op("act", lambda e, O=O: e.activation(out=O[:, 1, :], in_=tA[:], func=AF.Exp, scale=-0.6065306597126334), reads=rt, writes=ro)
                s.op("dve", lambda e: e.tensor_tensor(out=av[:], in0=pL[1][:], in1=CB[:, 1, :], op=ALU.add), reads=[r_pL[1], r_c], writes=rt)
                s.op("act", lambda e: e.activation(out=av[:], in_=av[:], func=AF.Sigmoid), reads=rt, writes=rt)
                s.op("act", lambda e, O=O: e.copy(out=O[:, 6, :], in_=pL[2][:]), reads=[r_pL[2]], writes=ro)
                s.op("act", lambda e, O=O, X=X: e.copy(out=O[:, 0, :], in_=X[:, 0:512]), reads=[r_xm[k]], writes=ro)
                s.op("act", lambda e, O=O, X=X: e.copy(out=O[:, 3, :], in_=X[:, 1024:1536]), reads=[r_xm[k]], writes=ro)
                s.op("pool", lambda e, X=X: e.tensor_tensor(out=tA[:], in0=X[:, 512:1024], in1=CB[:, 2, :], op=ALU.mult), reads=[r_xm[k], r_c] + rt, writes=rt)
                s.op("pool", lambda e: e.tensor_tensor(out=tB[:], in0=tA[:], in1=tA[:], op=ALU.mult), reads=rt, writes=rt)
                s.op("dve", lambda e: e.tensor_reduce(out=sm[:, 0:8], in_=tB[:].rearrange("p (h k) -> p h k", h=8), axis=AX.X, op=ALU.add), reads=rt, writes=rt)
                s.op("act", lambda e: e.activation(out=sm[:, 8:16], in_=sm[:, 0:8], func=AF.Sqrt), reads=rt, writes=rt)
                s.op("dve", lambda e: e.tensor_scalar(out=sm[:, 8:16], in0=sm[:, 8:16], scalar1=1e-12, scalar2=None, op0=ALU.max), reads=rt, writes=rt)
                s.op("dve", lambda e: e.reciprocal(out=sm[:, 16:24], in_=sm[:, 8:16]), reads=rt, writes=rt)
                s.op("dve", lambda e, O=O: e.tensor_tensor(out=O[:, 4, :].rearrange("p (h k) -> p h k", h=8), in0=tA[:].rearrange("p (h k) -> p h k", h=8),
                                                           in1=sm[:, 16:24].unsqueeze(2).to_broadcast([128, 8, 64]), op=ALU.mult), reads=rt, writes=ro)
                s.op("dve", lambda e: e.scalar_tensor_tensor(out=tB[:], in0=av[:], scalar=-1.0, in1=CB[:, 3, :], op0=ALU.add, op1=ALU.mult), reads=rt + [r_c], writes=rt)
                s.op("dve", lambda e, O=O, X=X: e.scalar_tensor_tensor(out=O[:, 2, :], in0=tB[:], scalar=1.0, in1=X[:, 512:1024], op0=ALU.add, op1=ALU.mult),
                     reads=rt + [r_xm[k]], writes=ro)
                s.op("dve", lambda e, O=O: e.scalar_tensor_tensor(out=O[:, 5, :], in0=O[:, 4, :], scalar=-1.0, in1=av[:], op0=ALU.mult, op1=ALU.mult), reads=rt + ro, writes=ro)
                s.op("pool", lambda e, O=O: e.tensor_tensor(out=tA[:], in0=O[:, 0, :], in1=O[:, 2, :], op=ALU.mult), reads=ro + rt, writes=rt)
                s.op("pool", lambda e: e.tensor_tensor(out=tA[:], in0=tA[:], in1=CB[:, 4, :], op=ALU.mult), reads=rt + [r_c], writes=rt)
                s.op("dve", lambda e: e.tensor_reduce(out=sm[:, 24:32], in_=tA[:].rearrange("p (h k) -> p h k", h=8), axis=AX.X, op=ALU.add), reads=rt, writes=rt)
                s.op("dve", lambda e, O=O: e.tensor_tensor(out=O[:, 7, :].rearrange("p (h k) -> p h k", h=8), in0=O[:, 3, :].rearrange("p (h k) -> p h k", h=8),
                                                           in1=sm[:, 24:32].unsqueeze(2).to_broadcast([128, 8, 64]), op=ALU.mult), reads=rt + ro, writes=ro)
                s.dma("pool", lambda e, O=O, rows=rows: e.dma_start(out=self.rw[:, rows, :].rearrange("a t c -> t a c"), in_=O[:]),
                      reads=ro, writes=[self.rwres[i]])
            s.barrier()

    def mixer1_R(self):
        nc, s, di, do = self.nc, self.s, self.din, self.dout
        with ExitStack() as es:
            T = lambda n, sh, dt: es.enter_context(nc.sbuf_tensor(self.name(n), sh, dt))
            P = lambda n, sh, dt: es.enter_context(nc.psum_tensor(self.name(n), sh, dt))
            bm = T("bm8", [8, 8, 64], F32)
            A = T("Ast", [64, 8, 64], F32)
            tok = [T("tokm", [128, 3, 512], F32) for _ in range(2)]
            L = [T("Lkr", [64, 129, 40], F32) for _ in range(2)]
            Wt = [T("Wt", [64, 128, 8], F32) for _ in range(2)]
            hm1 = T("hmK", [40, 2, 128, 64], F32)
            hm = [hm1, hm1]
            bm40 = T("bm40", [40, 8, 64], F32)
            UV = [T("UV", [40, 512], F32) for _ in range(2)]
            Ym = [T("Ym", [40, 8, 64], F32) for _ in range(2)]
            yb1 = T("yb", [40, 128, 64], F32)
            yb = [yb1, yb1]
            Sio = T("Sio", [64, 8, 64], F32)
            ptr = [P("ptr", [64, 4, 128], F32) for _ in range(2)]
            pu = [P("pu", [40, 512], F32) for _ in range(2)]
            pdA = [P("pdA", [64, 512], F32) for _ in range(2)]
            r_c = Res(); r_A = Res(); r_tok = [Res(), Res()]; r_cols = [Res(), Res()]; r_hm1 = Res(); r_hm = [r_hm1, r_hm1]
            r_Um = [Res(), Res()]; r_Vm = [Res(), Res()]; r_Ym = [Res(), Res()]; r_yb1 = Res(); r_yb = [r_yb1, r_yb1]; r_S = Res()
            r_ptr = [Res(), Res()]; r_pu = [Res(), Res()]; r_pdA = [Res(), Res()]; r_py = [Res(), Res()]
            s.op("pool", lambda e: e.memset(bm[:], 1.0), writes=[r_c])
            s.op("pool", lambda e: e.affine_select(out=bm[:], in_=bm[:], pattern=[[-1, 8], [0, 64]], compare_op=ALU.is_equal, fill=0.0,
                                                    base=0, channel_multiplier=1), reads=[r_c], writes=[r_c])
            s.op("pool", lambda e: e.memset(A[:], 0.0), writes=[r_A])
            s.op("pool", lambda e: e.memset(hm1[:], 0.0), writes=[r_hm1])
            s.op("pool", lambda e: e.memset(bm40[:], 1.0), writes=[r_c])
            s.op("pool", lambda e: e.affine_select(out=bm40[:], in_=bm40[:], pattern=[[-1, 8], [0, 64]], compare_op=ALU.is_equal, fill=0.0,
                                                    base=-32, channel_multiplier=1), reads=[r_c], writes=[r_c])
            for c_ in range(2):
                s.op("pool", lambda e, c_=c_: e.memset(UV[c_][:], 0.0), writes=[r_Um[c_], r_Vm[c_]])
                s.op("pool", lambda e, c_=c_: e.memset(L[c_][:], 0.0), writes=[r_cols[c_]])
            bm40f = bm40[:].rearrange("p h v -> p (h v)")
            Af = A[:].rearrange("p h v -> p (h v)")
            bmf = bm[:].rearrange("p h v -> p (h v)")
            cnt = [0]

            def load_tile(i, k):
                rows = slice(i * 128, (i + 1) * 128)
                for j, src in enumerate((4, 0, 1)):
                    s.dma("sp", lambda e, j=j, src=src: e.dma_start(out=tok[k][:, j, :], in_=self.rw[src, rows, :]), reads=[self.rwres[i]], writes=[r_tok[k]])
                for j, ro, src in ((0, 0, 5), (0, 32, 2), (1, 32, 3)):
                    s.dma("sp", lambda e, j=j, ro=ro, src=src: e.dma_start(out=hm[k][ro:ro + 8, j, :, :], in_=self.rw[src, rows, :].rearrange("t (h c) -> h t c", h=8)),
                          reads=[self.rwres[i]], writes=[r_hm[k]])
                it = 0
                for j in range(3):
                    for h0 in (0, 4):
                        kp = it % 2; it += 1
                        for hh in range(4):
                            h = h0 + hh
                            s.op("pe", lambda e, j=j, h=h, hh=hh, kp=kp: e.transpose(out=ptr[kp][:, hh, :], in_=tok[k][:, j, h * 64:(h + 1) * 64], identity=self.ident_f[:]),
                                 reads=[r_tok[k], self.rconst], writes=[r_ptr[kp]])
                        if j == 0:
                            dst = L[k][:, 0:128, h0:h0 + 4]
                        elif j == 1:
                            dst = L[k][:, 1:129, 32 + h0:32 + h0 + 4]
                        else:
                            dst = Wt[k][:, :, h0:h0 + 4]
                        s.op("act", lambda e, dst=dst, kp=kp: e.copy(out=dst, in_=ptr[kp][:].rearrange("p h t -> p t h")),
                             reads=[r_ptr[kp]], writes=[r_cols[k]])

            def emit_y(k, c, t):
                s.op("dve", lambda e: e.tensor_tensor(out=Ym[c][32:40, :, :].rearrange("p h v -> p (h v)"), in0=pu[c][32:40, :], in1=bm40f[32:40, :], op=ALU.mult),
                     reads=[r_pu[c], r_c], writes=[r_Ym[c]])
                s.op("dve", lambda e: e.tensor_reduce(out=yb[k][32:40, t, :], in_=Ym[c][32:40, :, :].rearrange("p h v -> p v h"), axis=AX.X, op=ALU.add),
                     reads=[r_Ym[c]], writes=[r_yb[k]])

            def step(k, t, emit_prev):
                c = cnt[0] % 2
                cnt[0] += 1
                s.op("pe", lambda e: e.matmul(pu[c][:], lhsT=L[k][:, t, :], rhs=Af, start=True, stop=True), reads=[r_cols[k], r_A], writes=[r_pu[c]])
                s.op("dve", lambda e: e.tensor_tensor(out=UV[c][0:8, :], in0=pu[c][0:8, :], in1=bmf, op=ALU.mult), reads=[r_pu[c], r_c], writes=[r_Um[c]])
                s.op("pool", lambda e: e.tensor_tensor(out=UV[c][32:40, :].rearrange("p (h v) -> p h v", h=8),
                                                       in0=hm[k][32:40, 1, t, :].unsqueeze(1).to_broadcast([8, 8, 64]),
                                                       in1=bm40[32:40, :, :], op=ALU.mult), reads=[r_hm[k], r_c], writes=[r_Vm[c]])
                s.op("pe", lambda e: e.matmul(pdA[c][:], lhsT=hm[k][0:40, 0, t, :], rhs=UV[c][0:40, :], start=True, stop=True),
                     reads=[r_hm[k], r_Um[c], r_Vm[c]], writes=[r_pdA[c]])
                s.op("dve", lambda e: e.tensor_tensor(out=A[:], in0=A[:], in1=Wt[k][:, t, :].unsqueeze(2).to_broadcast([64, 8, 64]), op=ALU.mult),
                     reads=[r_A, r_cols[k]], writes=[r_A])
                if emit_prev:
                    emit_y(k, c, t - 1)
                s.op("dve", lambda e: e.tensor_tensor(out=Af, in0=Af, in1=pdA[c][:], op=ALU.add), reads=[r_A, r_pdA[c]], writes=[r_A])

            def flush(k, t):
                c = cnt[0] % 2
                cnt[0] += 1
                s.op("pe", lambda e: e.matmul(pu[c][:], lhsT=L[k][:, t + 1, :], rhs=Af, start=True, stop=True), reads=[r_cols[k], r_A], writes=[r_pu[c]])
                emit_y(k, c, t)

            def store_y(i, k, nt):
                rows = slice(i * 128, i * 128 + nt)
                s.dma("pool", lambda e: e.dma_start(out=self.ybuf[rows, :].rearrange("t (h v) -> h t v", h=8), in_=yb[k][32:40, 0:nt, :]),
                      reads=[r_yb[k]], writes=[self.yres[i]])

            def state_out(dst_ap):
                for h0 in (0, 4):
                    kp = (h0 // 4)
                    for hh in range(4):
                        s.op("pe", lambda e, hh=hh, h0=h0, kp=kp: e.transpose(out=ptr[kp][:, hh, 0:64], in_=A[:, h0 + hh, :], identity=self.ident_f[0:64, 0:64]),
                             reads=[r_A, self.rconst], writes=[r_ptr[kp]])
                    s.op("act", lambda e, h0=h0, kp=kp: e.copy(out=Sio[:, h0:h0 + 4, :], in_=ptr[kp][:, :, 0:64]), reads=[r_ptr[kp]], writes=[r_S])
                s.dma("pool", lambda e: e.dma_start(out=dst_ap.rearrange("(h v) c -> v h c", h=8), in_=Sio[:]), reads=[r_S])

            ntp = self.cfg.get("rw_tiles", 32)
            for i in range(ntp):
                k = i % 2
                load_tile(i, k)
                for t in range(128):
                    step(k, t, t > 0)
                flush(k, 127)
                store_y(i, k, 128)
            state_out(do["dwp"][:, :])
            k = 0
            load_tile(32, k)
            for b in range(NSB):
                s.dma("sp", lambda e, b=b: e.dma_start(out=Sio[:], in_=di["state_d_wkv"][b * 512:(b + 1) * 512, :].rearrange("(h v) c -> v h c", h=8)),
                      reads=[r_S], writes=[r_S])
                for h0 in (0, 4):
                    kp = (h0 // 4)
                    for hh in range(4):
                        s.op("pe", lambda e, hh=hh, h0=h0, kp=kp: e.transpose(out=ptr[kp][:, hh, 0:64], in_=Sio[:, h0 + hh, :], identity=self.ident_f[0:64, 0:64]),
                             reads=[r_S, self.rconst], writes=[r_ptr[kp]])
                    s.op("act", lambda e, h0=h0, kp=kp: e.copy(out=A[:, h0:h0 + 4, :], in_=ptr[kp][:, :, 0:64]), reads=[r_ptr[kp]], writes=[r_A])
                for j in range(4):
                    step(k, 4 * b + j, j > 0)
                flush(k, 4 * b + 3)
                state_out(do["dws"][b * 512:(b + 1) * 512, :])
            store_y(32, k, 64)
            s.barrier()

    def mixer1_Q(self, cur, nxt):
        nc, s, di, do = self.nc, self.s, self.din, self.dout
        GN_EPS = 64e-5
        with ExitStack() as es:
            T = lambda n, sh, dt: es.enter_context(nc.sbuf_tensor(self.name(n), sh, dt))
            P = lambda n, sh, dt: es.enter_context(nc.psum_tensor(self.name(n), sh, dt))
            m = self.alloc_mod(es)
            owout = T("owout", [128, NKC, D], BF16)
            GN = T("GNc", [128, 2, 512], F32)
            gne = T("gne", [128, 1], F32)
            xt = [T("xt", [128, D], F32) for _ in range(2)]
            yv = [T("yv", [128, 3, 512], F32) for _ in range(2)]
            ft = [T("ft", [128, 2, 512], BF16) for _ in range(2)]
            fT = [T("fT", [128, 8, 128], BF16) for _ in range(2)]
            ta = T("ta", [128, 512], F32); tb = T("tb", [128, 512], F32); sm = T("smq", [128, 32], F32)
            t1 = [T("t1", [128, D], F32) for _ in range(2)]
            stt = [T("st", [128, 16], F32) for _ in range(2)]
            tpb = P("tpb", [128, 8, 128], BF16)
            po = [P("po", [128, 512], F32) for _ in range(2)]
            pmisc = P("pmisc", [128, 512], F32)
            r_c = Res(); r_x = [Res(), Res()]; r_yv = [Res(), Res()]; r_ft = [Res(), Res()]; r_fT = [Res(), Res()]; r_t = Res()
            r_t1 = [Res(), Res()]; r_st = [Res(), Res()]; r_tp = Res(); r_po = [Res(), Res()]; r_pm = Res(); r_ow = Res()
            s.dma("sp", lambda e: e.dma_start(out=owout[:], in_=self.owout_b[:, :].rearrange("(kc p) n -> p kc n", p=128)), writes=[r_ow])
            s.dma("sp", lambda e: e.dma_start(out=GN[:, 0, :], in_=di["rwkv_gn_w"][0:1, :].to_broadcast([128, 512])), writes=[r_c])
            s.dma("sp", lambda e: e.dma_start(out=GN[:, 1, :], in_=di["rwkv_gn_b"][0:1, :].to_broadcast([128, 512])), writes=[r_c])
            s.op("pool", lambda e: e.memset(gne[:], GN_EPS), writes=[r_c])
            self.load_mod(m, 1, 1, None)
            self.bcast_mod(m, False, pmisc, r_pm)
            rt = [r_t]
            v3 = lambda ap: ap.rearrange("p (h k) -> p h k", h=8)
            for i in range(NTILE):
                k = i % 2
                sample = (i == 32)
                if sample:
                    self.bcast_mod(m, True, pmisc, r_pm)
                rows = slice(i * 128, (i + 1) * 128)
                ap, nrows, res = self.x_src(False, cur, i)
                s.dma("sp", lambda e, k=k, ap=ap: e.dma_start(out=xt[k][:], in_=ap), reads=[res], writes=[r_x[k]])
                s.dma("sp", lambda e, k=k, rows=rows: e.dma_start(out=yv[k][:, 0, :], in_=self.ybuf[rows, :]), reads=[self.yres[i]], writes=[r_yv[k]])
                s.dma("sp", lambda e, k=k, rows=rows: e.dma_start(out=yv[k][:, 1, :], in_=self.rw[6, rows, :]), reads=[self.rwres[i]], writes=[r_yv[k]])
                s.dma("sp", lambda e, k=k, rows=rows: e.dma_start(out=yv[k][:, 2, :], in_=self.rw[7, rows, :]), reads=[self.rwres[i]], writes=[r_yv[k]])
                s.dma("sp", lambda e, k=k, rows=rows: e.dma_start(out=ft[k][:, 0, :], in_=self.mixbuf[rows, 0:512]), reads=[self.mixres[i]], writes=[r_ft[k]])
                Y = yv[k]
                ry = [r_yv[k]]
                s.op("dve", lambda e, Y=Y: e.tensor_reduce(out=sm[:, 0:8], in_=v3(Y[:, 0, :]), axis=AX.X, op=ALU.add), reads=ry + rt, writes=rt)
                s.op("dve", lambda e: e.tensor_scalar(out=sm[:, 0:8], in0=sm[:, 0:8], scalar1=1.0 / 64.0, scalar2=None, op0=ALU.mult), reads=rt, writes=rt)
                s.op("dve", lambda e, Y=Y: e.tensor_tensor(out=v3(ta[:]), in0=v3(Y[:, 0, :]), in1=sm[:, 0:8].unsqueeze(2).to_broadcast([128, 8, 64]), op=ALU.subtract),
                     reads=ry + rt, writes=rt)
                s.op("pool", lambda e: e.tensor_tensor(out=tb[:], in0=ta[:], in1=ta[:], op=ALU.mult), reads=rt, writes=rt)
                s.op("dve", lambda e: e.tensor_reduce(out=sm[:, 8:16], in_=v3(tb[:]), axis=AX.X, op=ALU.add), reads=rt, writes=rt)
                s.op("act", lambda e: e.activation(out=sm[:, 16:24], in_=sm[:, 8:16], func=AF.Sqrt, bias=gne[:, 0:1], scale=1.0 / 64.0), reads=rt + [r_c], writes=rt)
                s.op("dve", lambda e: e.reciprocal(out=sm[:, 24:32], in_=sm[:, 16:24]), reads=rt, writes=rt)
                s.op("dve", lambda e: e.tensor_tensor(out=v3(ta[:]), in0=v3(ta[:]), in1=sm[:, 24:32].unsqueeze(2).to_broadcast([128, 8, 64]), op=ALU.mult), reads=rt, writes=rt)
                s.op("pool", lambda e: e.tensor_tensor(out=ta[:], in0=ta[:], in1=GN[:, 0, :], op=ALU.mult), reads=rt + [r_c], writes=rt)
                s.op("pool", lambda e: e.tensor_tensor(out=ta[:], in0=ta[:], in1=GN[:, 1, :], op=ALU.add), reads=rt + [r_c], writes=rt)
                s.op("dve", lambda e, Y=Y: e.tensor_tensor(out=ta[:], in0=ta[:], in1=Y[:, 2, :], op=ALU.add), reads=rt + ry, writes=rt)
                s.op("dve", lambda e, Y=Y, k=k: e.tensor_tensor(out=ft[k][:, 1, :], in0=ta[:], in1=Y[:, 1, :], op=ALU.mult), reads=rt + ry + [r_ft[k]], writes=[r_ft[k]])
                for c in range(8):
                    s.op("pe", lambda e, k=k, c=c: e.transpose(out=tpb[:, c, :], in_=ft[k][:, c // 4, (c % 4) * 128:(c % 4 + 1) * 128],
                                                               identity=self.ident_b[:]), reads=[r_ft[k], self.rconst], writes=[r_tp])
                s.op("act", lambda e, k=k: e.copy(out=fT[k][:], in_=tpb[:]), reads=[r_tp], writes=[r_fT[k]])
                for h in range(2):
                    for kc in range(8):
                        s.op("pe", lambda e, h=h, kc=kc, k=k: e.matmul(po[h][:], lhsT=fT[k][:, kc, :], rhs=owout[:, kc, h * 512:(h + 1) * 512],
                                                                       start=(kc == 0), stop=(kc == 7)), reads=[r_fT[k], r_ow], writes=[r_po[h]])
                self.epilogue(m, [po[0][:], po[1][:]], r_po, xt[k][:], r_x[k], t1[k], r_t1[k], stt[k], r_st[k],
                              self.x_dst(False, nxt, i), None)
            s.barrier()

    def dump_dbg(self, cur):
        nc, s = self.nc, self.s
        with ExitStack() as es:
            xt = [es.enter_context(nc.sbuf_tensor(self.name("dx"), [128, D], F32)) for _ in range(2)]
            r = [Res(), Res()]
            for tile in range(NTILE):
                k = tile % 2
                s.dma("sp", lambda e, k=k, tile=tile: e.dma_start(out=xt[k][:], in_=self.XS[cur][tile * 128:(tile + 1) * 128, :]),
                      reads=[self.xres[cur][tile]], writes=[r[k]])
                s.dma("pool", lambda e, k=k, tile=tile: e.dma_start(out=self.dout["dbg"][tile * 128:(tile + 1) * 128, :], in_=xt[k][:]),
                      reads=[r[k]])
            s.barrier()

    def build(self):
        cfg = self.cfg
        self.setup_consts()
        self.cast_weights()
        cur = 0
        nsub = cfg.get("nsub", 6)
        isub_g = 0
        for l in range(DEPTH):
            self.adaln(l)
            for isub in range(3):
                if isub_g >= nsub:
                    break
                first = (isub_g == 0)
                last = (isub_g == 5)
                nxt = 1 - cur
                if isub == 1 and l == 0 and not cfg.get("stub0"):
                    parts = cfg.get("parts", "ASB")
                    if "A" in parts:
                        self.mixer0_A(cur)
                    if "S" in parts:
                        self.mixer0_S()
                    if "B" in parts:
                        self.mixer0_B(cur, nxt)
                    else:
                        self.stub_mixer(l, cur, nxt)
                elif isub == 1:
                    if not cfg.get("stub1"):
                        self.mixer1_A(cur)
                        if not cfg.get("nodil"):
                            self.mixer1_S()
                    if cfg.get("stub1") or cfg.get("norwkv"):
                        self.stub_mixer(l, cur, nxt)
                    else:
                        self.mixer1_P()
                        self.mixer1_R()
                        self.mixer1_Q(cur, nxt)
                else:
                    self.ffn_sublayer(l, 0 if isub == 0 else 1, isub, first, last, cur, nxt)
                cur = nxt
                isub_g += 1
        if cfg.get("dbg"):
            self.dump_dbg(cur)
        self.s.barrier()
        self.s.emit()
        return self.nc


_OUT_SHAPES = None


def make_in_maps(inputs, cfg):
    maps = []
    f32 = lambda a: np.ascontiguousarray(np.asarray(a, dtype=np.float32))
    shared = {
        "ada_w": f32(inputs["ada_w"]), "ada_b": f32(inputs["ada_b"]),
        "ln_g": f32(inputs["ln_g"]), "ln_b": f32(inputs["ln_b"]),
        "ffn_w_gate": f32(inputs["ffn_w_gate"]).reshape(4 * D, DFF),
        "ffn_w_up": f32(inputs["ffn_w_up"]).reshape(4 * D, DFF),
        "ffn_w_down": f32(inputs["ffn_w_down"]).reshape(4 * DFF, D),
        "even_w_in": f32(inputs["even_w_in"]).reshape(D, 2048),
        "even_w_out": f32(inputs["even_w_out"]).reshape(D, D),
        "odd_w_in": f32(inputs["odd_w_in"]).reshape(D, 3328),
        "odd_w_out": f32(inputs["odd_w_out"]).reshape(D, D),
        "diff_lambda": f32(inputs["diff_lambda"]).reshape(1, 256), "diff_subln": f32(inputs["diff_subln"]).reshape(1, 128),
        "s5_a_re": f32(inputs["s5_a_re"]).reshape(32, 64), "s5_a_im": f32(inputs["s5_a_im"]).reshape(32, 64),
        "s5_log_dt": f32(inputs["s5_log_dt"]).reshape(1, 32),
        "s5_b_re": f32(inputs["s5_b_re"]).reshape(32, 64, 16), "s5_b_im": f32(inputs["s5_b_im"]).reshape(32, 64, 16),
        "s5_c_re": f32(inputs["s5_c_re"]).reshape(32, 16, 64), "s5_c_im": f32(inputs["s5_c_im"]).reshape(32, 16, 64),
        "s5_d": f32(inputs["s5_d"]).reshape(1, 512), "s5_glu_w": f32(inputs["s5_glu_w"]).reshape(512, 512),
        "s5_glu_b": f32(inputs["s5_glu_b"]).reshape(1, 512),
        "cache_a_k": f32(inputs["cache_a_k"]).reshape(2560 * 128, 512), "cache_a_v": f32(inputs["cache_a_v"]).reshape(2560 * 128, 512),
    }
    cck = f32(inputs["cache_c_k"]).reshape(128, 2048, 512); ccv = f32(inputs["cache_c_v"]).reshape(128, 2048, 512)
    for nm in ("rwkv_mu", "rwkv_w0", "rwkv_a0", "rwkv_k_k", "rwkv_k_a", "rwkv_r_k", "rwkv_gn_w", "rwkv_gn_b"):
        shared[nm] = f32(inputs[nm]).reshape(1, -1)
    shared["rwkv_w2"] = f32(inputs["rwkv_w2"]).reshape(64, 512); shared["rwkv_a2"] = f32(inputs["rwkv_a2"]).reshape(64, 512)
    shared["rwkv_g2"] = f32(inputs["rwkv_g2"]).reshape(128, 512)
    dwkv = f32(inputs["state_d_wkv"]).reshape(128, 512, 64); dsh = f32(inputs["state_d_shift"]).reshape(128, 1792)
    s5r = f32(inputs["state_s5_re"]).reshape(128, 2048); s5i = f32(inputs["state_s5_im"]).reshape(128, 2048)
    ptab = np.ascontiguousarray(np.asarray(inputs["page_table"], dtype=np.int32))
    xp = f32(inputs["x_prompt"]); xs = f32(inputs["x_sample"])
    cp = f32(inputs["c_prompt"]); cs = f32(inputs["c_sample"])
    if "S" not in cfg.get("parts", "ASB"):
        shared["cache_a_k"] = shared["cache_a_k"][:128]
        shared["cache_a_v"] = shared["cache_a_v"][:128]
    for c in range(N_CORES):
        b = c % 4
        mp = dict(shared)
        mp["xp"] = xp[b]
        mp["xs"] = xs[c * NSB:(c + 1) * NSB].reshape(NS, D)
        mp["c17"] = np.ascontiguousarray(np.concatenate([cp[b:b + 1], cs[c * NSB:(c + 1) * NSB]], axis=0))
        mp["state_s5_re"] = s5r[c * NSB:(c + 1) * NSB]; mp["state_s5_im"] = s5i[c * NSB:(c + 1) * NSB]
        mp["page_table"] = ptab[c * NSB:(c + 1) * NSB].reshape(1, NSB * 16)
        mp["state_d_wkv"] = dwkv[c * NSB:(c + 1) * NSB].reshape(NSB * 512, 64)
        mp["state_d_shift"] = dsh[c * NSB:(c + 1) * NSB]
        mp["cache_c_k"] = cck[c * NSB:(c + 1) * NSB].reshape(NSB * 2048, 512)
        mp["cache_c_v"] = ccv[c * NSB:(c + 1) * NSB].reshape(NSB * 2048, 512)
        maps.append(mp)
    return maps


CFG = {}


def kernel(**inputs):
    cfg = dict(CFG)
    nc = Builder(cfg).build()
    maps = make_in_maps(inputs, cfg)
    res = run_bass_kernel_spmd(nc, maps, core_ids=list(range(N_CORES)))
    R = res.results
    yp = np.stack([R[b]["yp"] for b in range(4)], 0)
    ys = np.concatenate([R[c]["ys"].reshape(NSB, 4, D) for c in range(N_CORES)], 0)
    E, O, B, DB = 1, 1, 4, 128
    z = lambda *sh: np.zeros(sh, np.float32)
    catp = lambda k, sh: np.stack([R[b][k] for b in range(4)], 0).reshape(sh)
    cats = lambda k, sh: np.concatenate([R[c][k] for c in range(N_CORES)], 0).reshape(sh)
    return (yp, ys,
            catp("akp", (E, B, SEQ, 4, 2, 64)), cats("aks", (E, DB, 4, 4, 2, 64)),
            catp("avp", (E, B, SEQ, 4, 128)), cats("avs", (E, DB, 4, 4, 128)),
            catp("s5rp", (E, B, 32, 64)), cats("s5rs", (E, DB, 32, 64)), catp("s5ip", (E, B, 32, 64)), cats("s5is", (E, DB, 32, 64)),
            catp("ckp", (O, B, 2048, 8, 64)), cats("cks", (O, DB, 4, 8, 64)), catp("cvp", (O, B, 2048, 8, 64)), cats("cvs", (O, DB, 4, 8, 64)),
            catp("dwp", (O, B, 8, 64, 64)), cats("dws", (O, DB, 8, 64, 64)), catp("dsp", (O, B, 1792)), cats("dss", (O, DB, 1792)))
```

```python
from contextlib import ExitStack
import numpy as np
import concourse.bass as bass
import concourse.mybir as mybir
from concourse.bass_utils import run_bass_kernel_spmd

F32 = mybir.dt.float32
BF16 = mybir.dt.bfloat16
I32 = mybir.dt.int32
ALU = mybir.AluOpType
AF = mybir.ActivationFunctionType
AX = mybir.AxisListType

D = 1024
DFF = 2816
NKC = 8
NFC = 22
SEQ = 4096
NS = 64
NSB = 16
NTILE = 33
NROW = NTILE * 128
DEPTH = 2
ALPHA = (2.0 * DEPTH) ** 0.25
LN_EPS = 1e-5
N_CORES = 8


class Res:
    __slots__ = ("w", "r")

    def __init__(self):
        self.w = None
        self.r = []


class EngQ:
    def __init__(self, name, eng, sem):
        self.name = name
        self.eng = eng
        self.sem = sem
        self.count = 0
        self.ops = []
        self.waited = {}
        self.dma_slots = []
        self.dma_i = 0


class Sched:
    def __init__(self, nc, ndma_slots=8):
        self.nc = nc
        self.q = {}
        for name, eng in (("pe", nc.tensor), ("act", nc.scalar), ("dve", nc.vector),
                          ("pool", nc.gpsimd), ("sp", nc.sync)):
            q = EngQ(name, eng, nc.alloc_semaphore(name="s_" + name))
            for i in range(ndma_slots):
                q.dma_slots.append([nc.alloc_semaphore(name=f"d_{name}{i}"), 0])
            self.q[name] = q

    def _wait(self, q, tok):
        sem, val = tok
        key = id(sem)
        if q.waited.get(key, 0) >= val:
            return
        q.waited[key] = val
        q.ops.append(("wait", sem, val))

    def _deps(self, q, reads, writes, skip_same):
        deps = []
        for r in reads:
            if r.w is not None:
                deps.append(r.w)
        for w in writes:
            if w.w is not None:
                deps.append(w.w)
            deps.extend(w.r)
        for tok in deps:
            if skip_same and tok[0] is q.sem:
                continue
            self._wait(q, tok)

    @staticmethod
    def _commit(tok, reads, writes):
        for r in reads:
            r.r.append(tok)
            if len(r.r) > 64:
                best = {}
                for t in r.r:
                    k = id(t[0])
                    if k not in best or best[k][1] < t[1]:
                        best[k] = t
                r.r = list(best.values())
        for w in writes:
            w.w = tok
            w.r = []

    def op(self, qn, fn, reads=(), writes=()):
        q = self.q[qn]
        self._deps(q, reads, writes, skip_same=(qn == "pe"))
        q.count += 1
        tok = (q.sem, q.count)
        q.ops.append(("op", fn, q.sem, 1))
        self._commit(tok, reads, writes)
        return tok

    def dma(self, qn, fn, reads=(), writes=()):
        q = self.q[qn]
        self._deps(q, reads, writes, skip_same=False)
        slot = q.dma_slots[q.dma_i % len(q.dma_slots)]
        q.dma_i += 1
        if slot[1] > 0:
            self._wait(q, (slot[0], slot[1]))
        slot[1] += 16
        tok = (slot[0], slot[1])
        q.ops.append(("op", fn, slot[0], 16))
        self._commit(tok, reads, writes)
        return tok

    def barrier(self):
        toks = []
        for q in self.q.values():
            if q.count:
                toks.append((q.sem, q.count))
            for sl in q.dma_slots:
                if sl[1]:
                    toks.append((sl[0], sl[1]))
        for q in self.q.values():
            for t in toks:
                if t[0] is q.sem:
                    continue
                self._wait(q, t)

    def emit(self):
        nc = self.nc
        with nc.Block() as block:
            def mk(q):
                def body(eng):
                    for o in q.ops:
                        if o[0] == "wait":
                            eng.wait_ge(o[1], o[2])
                        else:
                            o[1](eng).then_inc(o[2], o[3])
                return body
            block.tensor(mk(self.q["pe"]))
            block.scalar(mk(self.q["act"]))
            block.vector(mk(self.q["dve"]))
            block.gpsimd(mk(self.q["pool"]))
            block.sync(mk(self.q["sp"]))


class Builder:
    def __init__(self, cfg):
        self.cfg = cfg
        nc = self.nc = bass.Bass("TRN2", target_bir_lowering=False)
        self.s = Sched(nc)
        self.uid = 0
        di = self.din = {}
        do = self.dout = {}

        def inp(name, shape, dt=F32):
            di[name] = nc.dram_tensor(name, list(shape), dt, kind="ExternalInput").ap()

        def outp(name, shape, dt=F32):
            do[name] = nc.dram_tensor(name, list(shape), dt, kind="ExternalOutput").ap()

        inp("xp", [SEQ, D]); inp("xs", [NS, D]); inp("c17", [17, D])
        inp("ada_w", [DEPTH, D, 9 * D]); inp("ada_b", [DEPTH, 9 * D])
        inp("ln_g", [DEPTH, 3, D]); inp("ln_b", [DEPTH, 3, D])
        inp("ffn_w_gate", [4 * D, DFF]); inp("ffn_w_up", [4 * D, DFF]); inp("ffn_w_down", [4 * DFF, D])
        inp("even_w_in", [D, 2048]); inp("even_w_out", [D, D])
        inp("odd_w_in", [D, 3328]); inp("odd_w_out", [D, D])
        inp("diff_lambda", [1, 256]); inp("diff_subln", [1, 128])
        inp("s5_a_re", [32, 64]); inp("s5_a_im", [32, 64]); inp("s5_log_dt", [1, 32])
        inp("s5_b_re", [32, 64, 16]); inp("s5_b_im", [32, 64, 16]); inp("s5_c_re", [32, 16, 64]); inp("s5_c_im", [32, 16, 64])
        inp("s5_d", [1, 512]); inp("s5_glu_w", [512, 512]); inp("s5_glu_b", [1, 512])
        inp("state_s5_re", [NSB, 2048]); inp("state_s5_im", [NSB, 2048])
        inp("page_table", [1, NSB * 16], I32)
        inp("cache_c_k", [NSB * 2048, 512]); inp("cache_c_v", [NSB * 2048, 512])
        inp("state_d_wkv", [NSB * 512, 64]); inp("state_d_shift", [NSB, 1792])
        inp("rwkv_mu", [1, 1792]); inp("rwkv_w0", [1, 512]); inp("rwkv_a0", [1, 512]); inp("rwkv_k_k", [1, 512]); inp("rwkv_k_a", [1, 512])
        inp("rwkv_r_k", [1, 512]); inp("rwkv_gn_w", [1, 512]); inp("rwkv_gn_b", [1, 512])
        inp("rwkv_w2", [64, 512]); inp("rwkv_a2", [64, 512]); inp("rwkv_g2", [128, 512])
        npool = 2560 * 128 if "S" in cfg.get("parts", "ASB") else 128
        inp("cache_a_k", [npool, 512]); inp("cache_a_v", [npool, 512])
        outp("yp", [SEQ, D]); outp("ys", [NS, D])
        outp("akp", [SEQ, 512]); outp("aks", [NS, 512]); outp("avp", [SEQ, 512]); outp("avs", [NS, 512])
        outp("ckp", [2048, 512]); outp("cks", [NS, 512]); outp("cvp", [2048, 512]); outp("cvs", [NS, 512])
        outp("dsp", [1, 1792]); outp("dss", [NSB, 1792])
        outp("dwp", [512, 64]); outp("dws", [NSB * 512, 64])
        outp("s5rp", [1, 2048]); outp("s5ip", [1, 2048]); outp("s5rs", [NSB, 2048]); outp("s5is", [NSB, 2048])
        if cfg.get("dbg"):
            outp("dbg", [NROW, D])
        self.wg_b = nc.dram_tensor("wg_b", [4 * D, DFF], BF16).ap()
        self.wu_b = nc.dram_tensor("wu_b", [4 * D, DFF], BF16).ap()
        self.wd_b = nc.dram_tensor("wd_b", [4 * DFF, D], BF16).ap()
        self.ewin_b = nc.dram_tensor("ewin_b", [D, 2048], BF16).ap()
        self.owin_b = nc.dram_tensor("owin_b", [D, 3328], BF16).ap()
        self.ewout_b = nc.dram_tensor("ewout_b", [D, D], BF16).ap()
        self.owout_b = nc.dram_tensor("owout_b", [D, D], BF16).ap()
        self.mod17 = nc.dram_tensor("mod17", [17, 9 * D], F32).ap()
        self.XS = [nc.dram_tensor(f"xscr{i}", [NROW, D], F32).ap() for i in range(2)]
        self.xres = [[Res() for _ in range(NTILE)] for _ in range(2)]
        self.mixbuf = nc.dram_tensor("mixbuf", [NROW, D], BF16).ap()
        self.mixres = [Res() for _ in range(NTILE)]
        self.ubuf = nc.dram_tensor("ubuf", [NROW, 512], BF16).ap()
        self.ures = [Res() for _ in range(NTILE)]
        self.qsbuf = nc.dram_tensor("qsbuf", [NS, 512], BF16).ap()
        self.r_aks = Res()
        self.r_cks = Res()
        self.rw = nc.dram_tensor("rwscr", [8, NROW, 512], F32).ap()
        self.rwres = [Res() for _ in range(NTILE)]
        self.ybuf = nc.dram_tensor("ybuf", [NROW, 512], F32).ap()
        self.yres = [Res() for _ in range(NTILE)]
        self.pdbuf = nc.dram_tensor("pdbuf", [NROW, 1792], F32).ap()
        self.pdres = [Res() for _ in range(NTILE)]

    def name(self, p):
        self.uid += 1
        return f"{p}{self.uid}"

    def setup_consts(self):
        nc, s = self.nc, self.s
        A = nc.alloc_sbuf_tensor
        self.ident_b = A("ident_b", [128, 128], BF16)
        self.ident_f = A("ident_f", [128, 128], F32)
        self.selP = A("selP", [17, 128], F32)
        self.selS = A("selS", [17, 128], F32)
        self.ones1 = A("ones1", [1, 128], F32)
        self.eps_t = A("eps_t", [128, 1], F32)
        self.rconst = Res()
        rc = [self.rconst]
        for idt in (self.ident_b, self.ident_f):
            s.op("pool", lambda e, t=idt: e.memset(t[:], 0.0), writes=rc)
            s.op("pool", lambda e, t=idt: e.affine_select(out=t[:], in_=t[:], pattern=[[-1, 128]],
                                                            compare_op=ALU.not_equal, fill=1.0, base=0,
                                                            channel_multiplier=1), reads=rc, writes=rc)
        s.op("pool", lambda e: e.memset(self.selP[:], 0.0), writes=rc)
        s.op("pool", lambda e: e.memset(self.selP[0:1, :], 1.0), writes=rc)
        s.op("pool", lambda e: e.memset(self.selS[:], 1.0), writes=rc)
        s.op("pool", lambda e: e.affine_select(out=self.selS[:], in_=self.selS[:], pattern=[[1, 128]],
                                                compare_op=ALU.is_ge, fill=0.0, base=4, channel_multiplier=-4),
             reads=rc, writes=rc)
        s.op("pool", lambda e: e.affine_select(out=self.selS[:], in_=self.selS[:], pattern=[[-1, 128]],
                                                compare_op=ALU.is_ge, fill=0.0, base=-1, channel_multiplier=4),
             reads=rc, writes=rc)
        s.op("pool", lambda e: e.memset(self.ones1[:], 1.0), writes=rc)
        s.op("pool", lambda e: e.memset(self.eps_t[:], LN_EPS), writes=rc)

    def cast_weights(self):
        nc, s, di = self.nc, self.s, self.din
        jobs = [(di["ffn_w_gate"], self.wg_b, 4 * D, DFF), (di["ffn_w_up"], self.wu_b, 4 * D, DFF),
                (di["ffn_w_down"], self.wd_b, 4 * DFF, D), (di["even_w_in"], self.ewin_b, D, 2048),
                (di["even_w_out"], self.ewout_b, D, D), (di["odd_w_in"], self.owin_b, D, 3328),
                (di["odd_w_out"], self.owout_b, D, D)]
        if self.cfg.get("nocast"):
            jobs = []
        with ExitStack() as es:
            NB = 3
            stg = [es.enter_context(nc.sbuf_tensor(f"cs{i}", [128, 2, 3328], F32)) for i in range(NB)]
            stb = [es.enter_context(nc.sbuf_tensor(f"cb{i}", [128, 2, 3328], BF16)) for i in range(NB)]
            rs = [Res() for _ in range(NB)]
            rb = [Res() for _ in range(NB)]
            i = 0
            for src, dst, R, C in jobs:
                nrt = R // 128
                G = 2
                for r0 in range(0, nrt, G):
                    g = min(G, nrt - r0)
                    k = i % NB
                    sv = src[r0 * 128:(r0 + g) * 128, :].rearrange("(g p) c -> p g c", p=128)
                    dv = dst[r0 * 128:(r0 + g) * 128, :].rearrange("(g p) c -> p g c", p=128)
                    s.dma("sp", lambda e, k=k, g=g, C=C, sv=sv: e.dma_start(out=stg[k][:, :g, :C], in_=sv),
                          writes=[rs[k]])
                    eng = ("act", "dve", "pool")[i % 3]
                    if eng == "act":
                        fn = lambda e, k=k, g=g, C=C: e.copy(out=stb[k][:, :g, :C], in_=stg[k][:, :g, :C])
                    else:
                        fn = lambda e, k=k, g=g, C=C: e.tensor_copy(out=stb[k][:, :g, :C], in_=stg[k][:, :g, :C])
                    s.op(eng, fn, reads=[rs[k]], writes=[rb[k]])
                    s.dma("pool", lambda e, k=k, g=g, C=C, dv=dv: e.dma_start(out=dv, in_=stb[k][:, :g, :C]),
                          reads=[rb[k]])
                    i += 1
            s.barrier()

    def adaln(self, l):
        nc, s, di = self.nc, self.s, self.din
        with ExitStack() as es:
            T = lambda n, sh, dt: es.enter_context(nc.sbuf_tensor(self.name(n), sh, dt))
            P = lambda n, sh, dt: es.enter_context(nc.psum_tensor(self.name(n), sh, dt))
            c_sb = T("c_sb", [17, D], F32)
            sc = T("sc", [17, D], F32)
            scT = T("scT", [128, NKC, 17], F32)
            w_sb = [T("adw", [128, NKC, 512], F32) for _ in range(2)]
            b_sb = [T("adb", [1, 512], F32) for _ in range(2)]
            o_sb = [T("ado", [17, 512], F32) for _ in range(2)]
            pt = P("adpt", [128, NKC, 32], F32)
            po = [P("adpo", [17, 512], F32) for _ in range(2)]
            r_c, r_sc, r_scT, r_pt = Res(), Res(), Res(), Res()
            r_w = [Res(), Res()]; r_b = [Res(), Res()]; r_o = [Res(), Res()]; r_po = [Res(), Res()]
            s.dma("sp", lambda e: e.dma_start(out=c_sb[:], in_=di["c17"][:, :]), writes=[r_c])
            s.op("act", lambda e: e.activation(out=sc[:], in_=c_sb[:], func=AF.Silu), reads=[r_c], writes=[r_sc])
            for kc in range(NKC):
                s.op("pe", lambda e, kc=kc: e.transpose(out=pt[:, kc, 0:17], in_=sc[:, kc * 128:(kc + 1) * 128],
                                                        identity=self.ident_f[0:17, 0:17]),
                     reads=[r_sc, self.rconst], writes=[r_pt])
            s.op("dve", lambda e: e.tensor_copy(out=scT[:], in_=pt[:, :, 0:17]), reads=[r_pt], writes=[r_scT])
            for cc in range(18):
                k = cc % 2
                wv = di["ada_w"][l, :, cc * 512:(cc + 1) * 512].rearrange("(kc p) n -> p kc n", p=128)
                s.dma("sp", lambda e, k=k, wv=wv: e.dma_start(out=w_sb[k][:], in_=wv), writes=[r_w[k]])
                bv = di["ada_b"][l:l + 1, cc * 512:(cc + 1) * 512]
                s.dma("sp", lambda e, k=k, bv=bv: e.dma_start(out=b_sb[k][:], in_=bv), writes=[r_b[k]])
                for kc in range(NKC):
                    s.op("pe", lambda e, k=k, kc=kc: e.matmul(po[k][:], lhsT=scT[:, kc, :], rhs=w_sb[k][:, kc, :],
                                                              start=(kc == 0), stop=False),
                         reads=[r_scT, r_w[k]], writes=[r_po[k]])
                s.op("pe", lambda e, k=k: e.matmul(po[k][:], lhsT=self.ones1[0:1, 0:17], rhs=b_sb[k][:],
                                                   start=False, stop=True),
                     reads=[r_b[k], self.rconst], writes=[r_po[k]])
                isub, kind = divmod(cc // 2, 3)
                if kind == 0:
                    fn = lambda e, k=k: e.tensor_copy(out=o_sb[k][:], in_=po[k][:])
                elif kind == 1:
                    fn = lambda e, k=k: e.tensor_scalar(out=o_sb[k][:], in0=po[k][:], scalar1=1.0, scalar2=None,
                                                        op0=ALU.add)
                else:
                    r = 1.0 if isub == 1 else 0.5
                    fn = lambda e, k=k, r=r: e.tensor_scalar(out=o_sb[k][:], in0=po[k][:], scalar1=1.0, scalar2=r,
                                                             op0=ALU.add, op1=ALU.mult)
                s.op("dve", fn, reads=[r_po[k]], writes=[r_o[k]])
                mv = self.mod17[:, cc * 512:(cc + 1) * 512]
                s.dma("pool", lambda e, k=k, mv=mv: e.dma_start(out=mv, in_=o_sb[k][:]), reads=[r_o[k]])
            s.barrier()

    def alloc_mod(self, es):
        nc = self.nc
        T = lambda n, sh, dt: es.enter_context(nc.sbuf_tensor(self.name(n), sh, dt))
        m = {}
        m["m3"] = T("m3", [17, 3, D], F32)
        m["SH"] = T("SH", [128, D], F32)
        m["SC"] = T("SC", [128, D], F32)
        m["G"] = T("G", [128, D], F32)
        m["LG"] = T("LG", [128, D], F32)
        m["LB"] = T("LB", [128, D], F32)
        m["r_m3"] = Res(); m["r_mod"] = Res(); m["r_ln"] = Res()
        return m

    def load_mod(self, m, l, isub, pmod):
        s, di = self.s, self.din
        mv = self.mod17[:, isub * 3 * D:(isub + 1) * 3 * D].rearrange("r (k d) -> r k d", k=3)
        s.dma("sp", lambda e: e.dma_start(out=m["m3"][:], in_=mv), writes=[m["r_m3"]])
        s.dma("sp", lambda e: e.dma_start(out=m["LG"][:], in_=di["ln_g"][l, isub:isub + 1, :].to_broadcast([128, D])),
              writes=[m["r_ln"]])
        s.dma("sp", lambda e: e.dma_start(out=m["LB"][:], in_=di["ln_b"][l, isub:isub + 1, :].to_broadcast([128, D])),
              writes=[m["r_ln"]])

    def bcast_mod(self, m, sample, pmod, r_pmod):
        s = self.s
        sel = self.selS if sample else self.selP
        for kind, key in enumerate(("SH", "SC", "G")):
            for h in range(2):
                s.op("pe", lambda e, kind=kind, h=h: e.matmul(pmod[:], lhsT=sel[:], rhs=m["m3"][:, kind, h * 512:(h + 1) * 512],
                                                              start=True, stop=True),
                     reads=[m["r_m3"], self.rconst], writes=[r_pmod])
                s.op("act", lambda e, key=key, h=h: e.copy(out=m[key][:, h * 512:(h + 1) * 512], in_=pmod[:]),
                     reads=[r_pmod], writes=[m["r_mod"]])

    def epilogue(self, m, po_halves, r_po, x_ap, r_x, t1, r_t1, st, r_st, dst_ap, dst_res, extra_dst=None):
        s = self.s
        if po_halves is not None:
            for h in range(2):
                s.op("dve", lambda e, h=h: e.tensor_tensor(out=t1[:, h * 512:(h + 1) * 512], in0=po_halves[h],
                                                           in1=m["G"][:, h * 512:(h + 1) * 512], op=ALU.mult),
                     reads=[r_po[h], m["r_mod"]], writes=[r_t1])
            s.op("dve", lambda e: e.scalar_tensor_tensor(out=t1[:], in0=x_ap, scalar=ALPHA, in1=t1[:],
                                                          op0=ALU.mult, op1=ALU.add),
                 reads=[r_x, r_t1], writes=[r_t1])
        else:
            s.op("pool", lambda e: e.tensor_scalar(out=t1[:], in0=x_ap, scalar1=ALPHA, scalar2=None, op0=ALU.mult),
                 reads=[r_x], writes=[r_t1])
        for h in range(2):
            s.op("dve", lambda e, h=h: e.bn_stats(out=st[:, h * 6:(h + 1) * 6], in_=t1[:, h * 512:(h + 1) * 512]),
                 reads=[r_t1], writes=[r_st])
        s.op("dve", lambda e: e.bn_aggr(out=st[:, 12:14], in_=st[:, 0:12]), reads=[r_st], writes=[r_st])
        s.op("act", lambda e: e.activation(out=st[:, 14:15], in_=st[:, 13:14], func=AF.Sqrt, bias=self.eps_t[:, 0:1],
                                           scale=1.0), reads=[r_st, self.rconst], writes=[r_st])
        s.op("dve", lambda e: e.reciprocal(out=st[:, 15:16], in_=st[:, 14:15]), reads=[r_st], writes=[r_st])
        s.op("dve", lambda e: e.tensor_scalar(out=t1[:], in0=t1[:], scalar1=st[:, 12:13], scalar2=st[:, 15:16],
                                              op0=ALU.subtract, op1=ALU.mult), reads=[r_st, r_t1], writes=[r_t1])
        s.op("pool", lambda e: e.tensor_tensor(out=t1[:], in0=t1[:], in1=m["LG"][:], op=ALU.mult),
             reads=[r_t1, m["r_ln"]], writes=[r_t1])
        s.op("dve", lambda e: e.tensor_tensor(out=t1[:], in0=t1[:], in1=m["LB"][:], op=ALU.add),
             reads=[r_t1, m["r_ln"]], writes=[r_t1])
        toks = []
        for ap, res, nrows in dst_ap:
            toks.append(s.dma("pool", lambda e, ap=ap, nrows=nrows: e.dma_start(out=ap, in_=t1[0:nrows, :]),
                              reads=[r_t1], writes=[res] if res is not None else []))
        return toks

    def x_src(self, first, cur, tile):
        if first:
            if tile < 32:
                return self.din["xp"][tile * 128:(tile + 1) * 128, :], 128, None
            return self.din["xs"][:, :], NS, None
        return self.XS[cur][tile * 128:(tile + 1) * 128, :], 128, self.xres[cur][tile]

    def x_dst(self, last, nxt, tile):
        if last:
            if tile < 32:
                return [(self.dout["yp"][tile * 128:(tile + 1) * 128, :], None, 128)]
            return [(self.dout["ys"][:, :], None, NS)]
        return [(self.XS[nxt][tile * 128:(tile + 1) * 128, :], self.xres[nxt][tile], 128)]

    def ffn_sublayer(self, l, f, isub, first, last, cur, nxt):
        nc, s = self.nc, self.s
        wrow = (l * 2 + f)
        nblk = self.cfg.get("nblk", 9)
        with ExitStack() as es:
            T = lambda n, sh, dt: es.enter_context(nc.sbuf_tensor(self.name(n), sh, dt))
            P = lambda n, sh, dt: es.enter_context(nc.psum_tensor(self.name(n), sh, dt))
            m = self.alloc_mod(es)
            xblk = [T("xblk", [128, 4, D], F32) for _ in range(2)]
            hb = [T("hb", [128, D], BF16) for _ in range(2)]
            htmp = [T("htmp", [128, D], F32) for _ in range(2)]
            hT = [T("hT", [128, NKC, 512], BF16) for _ in range(2)]
            NBW = 3
            wgu = [T("wgu", [128, 2, NKC, 256], BF16) for _ in range(NBW)]
            sg = [T("sg", [128, 512], BF16) for _ in range(2)]
            act = T("actb", [128, NFC, 512], BF16)
            wd = [T("wd", [128, NFC, 512], BF16) for _ in range(2)]
            t1 = [T("t1", [128, D], F32) for _ in range(2)]
            st = [T("st", [128, 16], F32) for _ in range(2)]
            tp = [P("tp", [128, NKC, 128], BF16) for _ in range(2)]
            pg = [P("pg", [128, 512], F32) for _ in range(2)]
            pu = [P("pu", [128, 512], F32) for _ in range(2)]
            po = [P("po", [128, 512], F32) for _ in range(2)]
            r_x = [Res(), Res()]; r_hb = [Res(), Res()]; r_htmp = [Res(), Res()]; r_hT = [Res(), Res()]
            r_wgu = [Res() for _ in range(NBW)]; r_sg = [Res(), Res()]; r_act = Res(); r_wd = [Res(), Res()]
            r_t1 = [Res(), Res()]; r_st = [Res(), Res()]; r_tp = [Res(), Res()]
            r_pg = [Res(), Res()]; r_pu = [Res(), Res()]; r_po = [Res(), Res()]
            for b in xblk:
                s.op("pool", lambda e, b=b: e.memset(b[:], 0.0), writes=[r_x[0], r_x[1]])
            self.load_mod(m, l, isub, None)
            self.bcast_mod(m, False, po[0], r_po[0])
            iw = 0
            iwd = 0
            ihb = 0
            igu = 0
            ipo = 0
            it1 = 0
            for blk in range(nblk):
                sample = (blk == 8)
                nt = 1 if sample else 4
                TB = nt * 128
                xb = blk % 2
                if sample:
                    self.bcast_mod(m, True, po[0], r_po[0])
                for t in range(nt):
                    ap, nrows, res = self.x_src(first, cur, blk * 4 + t)
                    s.dma("sp", lambda e, xb=xb, t=t, ap=ap, nrows=nrows: e.dma_start(out=xblk[xb][0:nrows, t, :], in_=ap),
                          reads=[res] if res is not None else [], writes=[r_x[xb]])
                for t in range(nt):
                    k = ihb % 2
                    ihb += 1
                    s.op("pool", lambda e, k=k, xb=xb, t=t: e.tensor_tensor(out=htmp[k][:], in0=xblk[xb][:, t, :], in1=m["SC"][:],
                                                                            op=ALU.mult),
                         reads=[r_x[xb], m["r_mod"]], writes=[r_htmp[k]])
                    s.op("dve", lambda e, k=k: e.tensor_tensor(out=hb[k][:], in0=htmp[k][:], in1=m["SH"][:], op=ALU.add),
                         reads=[r_htmp[k], m["r_mod"]], writes=[r_hb[k]])
                    for kc in range(NKC):
                        s.op("pe", lambda e, k=k, kc=kc: e.transpose(out=tp[k][:, kc, :], in_=hb[k][:, kc * 128:(kc + 1) * 128],
                                                                     identity=self.ident_b[:]),
                             reads=[r_hb[k], self.rconst], writes=[r_tp[k]])
                    s.op("act", lambda e, k=k, xb=xb, t=t: e.copy(out=hT[xb][:, :, t * 128:(t + 1) * 128], in_=tp[k][:]),
                         reads=[r_tp[k]], writes=[r_hT[xb]])
                for fp in range(NFC // 2):
                    kw = iw % NBW
                    iw += 1
                    gv = self.wg_b[wrow * D:(wrow + 1) * D, fp * 256:(fp + 1) * 256].rearrange("(kc p) n -> p kc n", p=128)
                    uv = self.wu_b[wrow * D:(wrow + 1) * D, fp * 256:(fp + 1) * 256].rearrange("(kc p) n -> p kc n", p=128)
                    s.dma("sp", lambda e, kw=kw, gv=gv: e.dma_start(out=wgu[kw][:, 0], in_=gv), writes=[r_wgu[kw]])
                    s.dma("sp", lambda e, kw=kw, uv=uv: e.dma_start(out=wgu[kw][:, 1], in_=uv), writes=[r_wgu[kw]])
                    for j in range(2):
                        fc = fp * 2 + j
                        kg = igu % 2
                        igu += 1
                        for which, pp, rr in ((0, pg, r_pg), (1, pu, r_pu)):
                            for kc in range(NKC):
                                s.op("pe", lambda e, kw=kw, which=which, kc=kc, j=j, pp=pp, kg=kg, xb=xb, TB=TB:
                                     e.matmul(pp[kg][:, 0:TB], lhsT=wgu[kw][:, which, kc, j * 128:(j + 1) * 128],
                                              rhs=hT[xb][:, kc, 0:TB], start=(kc == 0), stop=(kc == NKC - 1)),
                                     reads=[r_wgu[kw], r_hT[xb]], writes=[rr[kg]])
                        s.op("act", lambda e, kg=kg, TB=TB: e.activation(out=sg[kg][:, 0:TB], in_=pg[kg][:, 0:TB], func=AF.Silu),
                             reads=[r_pg[kg]], writes=[r_sg[kg]])
                        s.op("dve", lambda e, kg=kg, fc=fc, TB=TB: e.tensor_tensor(out=act[:, fc, 0:TB], in0=sg[kg][:, 0:TB],
                                                                                   in1=pu[kg][:, 0:TB], op=ALU.mult),
                             reads=[r_sg[kg], r_pu[kg]], writes=[r_act])
                for h in range(2):
                    dv = self.wd_b[wrow * DFF:(wrow + 1) * DFF, h * 512:(h + 1) * 512].rearrange("(fc p) n -> p fc n", p=128)
                    s.dma("sp", lambda e, h=h, dv=dv: e.dma_start(out=wd[h][:], in_=dv), writes=[r_wd[h]])
                for t in range(nt):
                    tile = blk * 4 + t
                    for h in range(2):
                        for fc in range(NFC):
                            s.op("pe", lambda e, h=h, fc=fc, t=t: e.matmul(po[h][:], lhsT=act[:, fc, t * 128:(t + 1) * 128],
                                                                           rhs=wd[h][:, fc, :], start=(fc == 0), stop=(fc == NFC - 1)),
                                 reads=[r_act, r_wd[h]], writes=[r_po[h]])
                    k1 = it1 % 2
                    it1 += 1
                    self.epilogue(m, [po[0][:], po[1][:]], r_po, xblk[xb][:, t, :], r_x[xb], t1[k1], r_t1[k1], st[k1], r_st[k1],
                                  self.x_dst(last, nxt, tile), None)
            s.barrier()

    def stub_mixer(self, l, cur, nxt):
        nc, s = self.nc, self.s
        with ExitStack() as es:
            T = lambda n, sh, dt: es.enter_context(nc.sbuf_tensor(self.name(n), sh, dt))
            m = self.alloc_mod(es)
            xt = [T("xt", [128, D], F32) for _ in range(2)]
            t1 = [T("t1", [128, D], F32) for _ in range(2)]
            st = [T("st", [128, 16], F32) for _ in range(2)]
            r_x = [Res(), Res()]; r_t1 = [Res(), Res()]; r_st = [Res(), Res()]
            self.load_mod(m, l, 1, None)
            for tile in range(NTILE):
                k = tile % 2
                ap, nrows, res = self.x_src(False, cur, tile)
                s.dma("sp", lambda e, k=k, ap=ap: e.dma_start(out=xt[k][:], in_=ap), reads=[res], writes=[r_x[k]])
                self.epilogue(m, None, None, xt[k][:], r_x[k], t1[k], r_t1[k], st[k], r_st[k],
                              self.x_dst(False, nxt, tile), None)
            s.barrier()

    def mixer0_A(self, cur):
        nc, s, di, do = self.nc, self.s, self.din, self.dout
        LAM_INIT = 0.8 - 0.6 * float(np.exp(-0.3 * 0))
        with ExitStack() as es:
            T = lambda n, sh, dt: es.enter_context(nc.sbuf_tensor(self.name(n), sh, dt))
            P = lambda n, sh, dt: es.enter_context(nc.psum_tensor(self.name(n), sh, dt))
            m = self.alloc_mod(es)
            ewin = T("ewin", [128, NKC, 2048], BF16)
            xt = [T("xt", [128, D], F32) for _ in range(2)]
            tmpf = T("tmpf", [128, D], F32)
            hb = [T("hb", [128, D], BF16) for _ in range(2)]
            hT = [T("hT", [128, NKC, 128], BF16) for _ in range(2)]
            q_b = [T("q_b", [128, 512], BF16) for _ in range(2)]
            k_b = [T("k_b", [128, 512], BF16) for _ in range(2)]
            u_b = [T("u_b", [128, 512], BF16) for _ in range(2)]
            kv_f = [T("kv_f", [128, 1024], F32) for _ in range(2)]
            qT = [T("qT", [128, 4, 128], BF16) for _ in range(2)]
            KT = T("KT", [128, 4, SEQ], BF16)
            VP = T("VP", [128, 32, 4, 132], BF16)
            E = [T("E", [128, 4, 128], BF16) for _ in range(3)]
            tri = T("tri", [128, 128], BF16)
            dl = T("dl", [128, 4, 64], F32)
            dlp = T("dlp", [128, 2, 64], F32)
            lamt = T("lamt", [128, 8], F32)
            subl = T("subl", [128, 128], F32)
            a1 = [T("a1", [128, 128], F32) for _ in range(2)]
            junk = T("junk", [128, 128], F32)
            rr = [T("rr", [128, 8], F32) for _ in range(2)]
            mixf = [T("mixf", [128, 512], BF16) for _ in range(2)]
            pp = [P("pp", [128, 512], F32) for _ in range(4)]
            tpb = P("tpb", [128, NKC, 128], BF16)
            st = [P("st", [128, 512], F32) for _ in range(2)]
            oaccs = [P("oacc", [128, 512], F32)]
            r_ewin = Res(); r_x = [Res(), Res()]; r_tmpf = Res(); r_hb = [Res(), Res()]; r_hT = [Res(), Res()]
            r_q = [Res(), Res()]; r_k = [Res(), Res()]; r_u = [Res(), Res()]; r_kv = [Res(), Res()]; r_qT = [Res(), Res()]
            r_KT = [Res() for _ in range(32)]; r_VP = [Res() for _ in range(32)]; r_E = [Res() for _ in range(3)]
            r_c = Res(); r_a1 = [Res(), Res()]; r_rr = [Res(), Res()]; r_mixf = [Res(), Res()]; r_junk = Res()
            r_pp = [Res() for _ in range(4)]; r_tp = Res(); r_st = [Res(), Res()]; r_oacc = [Res()]
            s.dma("sp", lambda e: e.dma_start(out=ewin[:], in_=self.ewin_b[:, :].rearrange("(kc p) n -> p kc n", p=128)),
                  writes=[r_ewin])
            s.op("pool", lambda e: e.memset(VP[:, :, :, 128:132], 1.0), writes=r_VP)
            s.op("pool", lambda e: e.memset(tri[:], 1.0), writes=[r_c])
            s.op("pool", lambda e: e.affine_select(out=tri[:], in_=tri[:], pattern=[[1, 128]], compare_op=ALU.is_ge,
                                                    fill=0.0, base=0, channel_multiplier=-1), reads=[r_c], writes=[r_c])
            self.diff_consts(dl, dlp, lamt, subl, junk, r_c, LAM_INIT)
            self.load_mod(m, 0, 1, None)
            self.bcast_mod(m, False, pp[0], r_pp[0])
            ist = 0
            ie = 0
            ihd = 0
            for i in (range(NTILE) if not self.cfg.get('tilesA') else self.cfg['tilesA']):
                sample = (i == 32)
                k = i % 2
                if sample:
                    self.bcast_mod(m, True, pp[0], r_pp[0])
                ap, nrows, res = self.x_src(False, cur, i)
                s.dma("sp", lambda e, k=k, ap=ap: e.dma_start(out=xt[k][:], in_=ap), reads=[res], writes=[r_x[k]])
                s.op("pool", lambda e, k=k: e.tensor_tensor(out=tmpf[:], in0=xt[k][:], in1=m["SC"][:], op=ALU.mult),
                     reads=[r_x[k], m["r_mod"]], writes=[r_tmpf])
                s.op("dve", lambda e, k=k: e.tensor_tensor(out=hb[k][:], in0=tmpf[:], in1=m["SH"][:], op=ALU.add),
                     reads=[r_tmpf, m["r_mod"]], writes=[r_hb[k]])
                for kc in range(NKC):
                    s.op("pe", lambda e, k=k, kc=kc: e.transpose(out=tpb[:, kc, :], in_=hb[k][:, kc * 128:(kc + 1) * 128],
                                                                 identity=self.ident_b[:]),
                         reads=[r_hb[k], self.rconst], writes=[r_tp])
                s.op("act", lambda e, k=k: e.copy(out=hT[k][:], in_=tpb[:]), reads=[r_tp], writes=[r_hT[k]])
                for c in range(4):
                    for kc in range(NKC):
                        s.op("pe", lambda e, k=k, kc=kc, c=c: e.matmul(pp[c][:], lhsT=hT[k][:, kc, :],
                                                                       rhs=ewin[:, kc, c * 512:(c + 1) * 512],
                                                                       start=(kc == 0), stop=(kc == NKC - 1)),
                             reads=[r_hT[k], r_ewin], writes=[r_pp[c]])
                if self.cfg.get("stopA") == 1:
                    continue
                s.op("act", lambda e, k=k: e.copy(out=q_b[k][:], in_=pp[0][:]), reads=[r_pp[0]], writes=[r_q[k]])
                s.op("dve", lambda e, k=k: e.tensor_copy(out=kv_f[k][:, 0:512], in_=pp[1][:]), reads=[r_pp[1]], writes=[r_kv[k]])
                s.op("dve", lambda e, k=k: e.tensor_copy(out=kv_f[k][:, 512:1024], in_=pp[2][:]), reads=[r_pp[2]], writes=[r_kv[k]])
                s.op("act", lambda e, k=k: e.copy(out=u_b[k][:], in_=pp[3][:]), reads=[r_pp[3]], writes=[r_u[k]])
                s.op("act", lambda e, k=k: e.copy(out=k_b[k][:], in_=kv_f[k][:, 0:512]), reads=[r_kv[k]], writes=[r_k[k]])
                if not sample:
                    s.op("act", lambda e, i=i, k=k: e.copy(out=VP[:, i, :, 0:128], in_=kv_f[k][:, 512:1024].rearrange("p (h e) -> p h e", h=4)),
                         reads=[r_kv[k]], writes=[r_VP[i]])
                rows = slice(i * 128, (i + 1) * 128)
                if self.cfg.get("stopA") == 2:
                    continue
                s.dma("pool", lambda e, k=k, rows=rows: e.dma_start(out=self.ubuf[rows, :], in_=u_b[k][:]),
                      reads=[r_u[k]], writes=[self.ures[i]])
                if not sample:
                    s.dma("pool", lambda e, k=k, rows=rows: e.dma_start(out=do["akp"][rows, :], in_=kv_f[k][:, 0:512]), reads=[r_kv[k]])
                    s.dma("pool", lambda e, k=k, rows=rows: e.dma_start(out=do["avp"][rows, :], in_=kv_f[k][:, 512:1024]), reads=[r_kv[k]])
                else:
                    s.dma("pool", lambda e, k=k: e.dma_start(out=do["aks"][:, :], in_=kv_f[k][0:NS, 0:512]), reads=[r_kv[k]],
                          writes=[self.r_aks])
                    s.dma("pool", lambda e, k=k: e.dma_start(out=do["avs"][:, :], in_=kv_f[k][0:NS, 512:1024]), reads=[r_kv[k]],
                          writes=[self.r_aks])
                    s.dma("pool", lambda e, k=k: e.dma_start(out=self.qsbuf[:, :], in_=q_b[k][0:NS, :]), reads=[r_q[k]],
                          writes=[self.r_aks])
                    continue
                if self.cfg.get("stopA") == 3:
                    continue
                for h in range(4):
                    s.op("pe", lambda e, k=k, h=h: e.transpose(out=tpb[:, h, :], in_=q_b[k][:, h * 128:(h + 1) * 128],
                                                               identity=self.ident_b[:]),
                         reads=[r_q[k], self.rconst], writes=[r_tp])
                    s.op("pe", lambda e, k=k, h=h: e.transpose(out=tpb[:, 4 + h, :], in_=k_b[k][:, h * 128:(h + 1) * 128],
                                                               identity=self.ident_b[:]),
                         reads=[r_k[k], self.rconst], writes=[r_tp])
                s.op("act", lambda e, k=k: e.copy(out=qT[k][:], in_=tpb[:, 0:4, :]), reads=[r_tp], writes=[r_qT[k]])
                s.op("act", lambda e, i=i: e.copy(out=KT[:, :, i * 128:(i + 1) * 128], in_=tpb[:, 4:8, :]), reads=[r_tp],
                     writes=[r_KT[i]])
                for h in range(4 if not self.cfg.get("noattn") else 0):
                    ob = 0
                    oacc = oaccs[ob]
                    for mm in range(2):
                        pr = slice(mm * 64, (mm + 1) * 64)
                        oc = mm * 132
                        for j0 in range(0, i + 1, 4):
                            js = list(range(j0, min(j0 + 4, i + 1)))
                            sb = ist % 2
                            ist += 1
                            for jj, j in enumerate(js):
                                s.op("pe", lambda e, sb=sb, jj=jj, j=j, h=h, pr=pr, k=k:
                                     e.matmul(st[sb][:, jj * 128:(jj + 1) * 128], lhsT=KT[pr, h, j * 128:(j + 1) * 128],
                                              rhs=qT[k][pr, h, :], start=True, stop=True),
                                     reads=[r_KT[j], r_qT[k]], writes=[r_st[sb]])
                            eb = ie % 3
                            ie += 1
                            w = len(js)
                            s.op("act", lambda e, eb=eb, sb=sb, w=w: e.activation(out=E[eb][:, 0:w, :],
                                                                                 in_=st[sb][:, 0:w * 128].rearrange("p (a b) -> p a b", a=w),
                                                                                 func=AF.Exp, scale=0.125),
                                 reads=[r_st[sb]], writes=[r_E[eb]])
                            if js[-1] == i:
                                jj = i - j0
                                s.op("pool", lambda e, eb=eb, jj=jj: e.tensor_tensor(out=E[eb][:, jj, :], in0=E[eb][:, jj, :],
                                                                                      in1=tri[:], op=ALU.mult),
                                     reads=[r_E[eb], r_c], writes=[r_E[eb]])
                            for jj, j in enumerate(js):
                                s.op("pe", lambda e, eb=eb, jj=jj, j=j, h=h, oc=oc, oacc=oacc, i=i:
                                     e.matmul(oacc[:, oc:oc + 129], lhsT=E[eb][:, jj, :], rhs=VP[:, j, h, 0:129],
                                              start=(j == 0), stop=(j == i)),
                                     reads=[r_E[eb], r_VP[j]], writes=[r_oacc[ob]])
                    kk = ihd % 2
                    ihd += 1
                    self.diff_finalize(oacc, r_oacc[ob], rr[kk], r_rr[kk], a1[kk], r_a1[kk], junk, r_junk, lamt, subl, r_c,
                                       mixf[k][:, h * 128:(h + 1) * 128], r_mixf[k], 128)
                s.dma("pool", lambda e, k=k, rows=rows: e.dma_start(out=self.mixbuf[rows, 0:512], in_=mixf[k][:]),
                      reads=[r_mixf[k]], writes=[self.mixres[i]])
            s.barrier()

    def diff_consts(self, dl, dlp, lamt, subl, junk, r_c, lam_init):
        s, di = self.s, self.din
        s.dma("sp", lambda e: e.dma_start(out=dl[:].rearrange("p a b -> p (a b)"),
                                          in_=di["diff_lambda"][0:1, :].to_broadcast([128, 256])), writes=[r_c])
        s.dma("sp", lambda e: e.dma_start(out=subl[:], in_=di["diff_subln"][0:1, :].to_broadcast([128, 128])), writes=[r_c])
        s.op("dve", lambda e: e.tensor_tensor(out=dlp[:, 0, :], in0=dl[:, 0, :], in1=dl[:, 1, :], op=ALU.mult), reads=[r_c], writes=[r_c])
        s.op("dve", lambda e: e.tensor_tensor(out=dlp[:, 1, :], in0=dl[:, 2, :], in1=dl[:, 3, :], op=ALU.mult), reads=[r_c], writes=[r_c])
        s.op("dve", lambda e: e.tensor_reduce(out=lamt[:, 0:2], in_=dlp[:], axis=AX.X, op=ALU.add), reads=[r_c], writes=[r_c])
        s.op("act", lambda e: e.activation(out=lamt[:, 4:6], in_=lamt[:, 0:2], func=AF.Exp), reads=[r_c], writes=[r_c])
        s.op("dve", lambda e: e.tensor_tensor(out=lamt[:, 2:3], in0=lamt[:, 4:5], in1=lamt[:, 5:6], op=ALU.subtract), reads=[r_c], writes=[r_c])
        s.op("dve", lambda e: e.tensor_scalar(out=lamt[:, 2:3], in0=lamt[:, 2:3], scalar1=lam_init, scalar2=None, op0=ALU.add),
             reads=[r_c], writes=[r_c])
        s.op("dve", lambda e: e.tensor_scalar(out=lamt[:, 3:4], in0=lamt[:, 2:3], scalar1=-1.0, scalar2=None, op0=ALU.mult),
             reads=[r_c], writes=[r_c])
        s.op("dve", lambda e: e.tensor_scalar(out=subl[:], in0=subl[:], scalar1=1.0 - lam_init, scalar2=None, op0=ALU.mult),
             reads=[r_c], writes=[r_c])

    def diff_finalize(self, oacc, r_oacc, rr, r_rr, a1, r_a1, junk, r_junk, lamt, subl, r_c, out_ap, r_out, np_):
        s = self.s
        P_ = slice(0, np_)
        s.op("dve", lambda e: e.reciprocal(out=rr[P_, 0:1], in_=oacc[P_, 128:129]), reads=[r_oacc], writes=[r_rr])
        s.op("dve", lambda e: e.reciprocal(out=rr[P_, 1:2], in_=oacc[P_, 260:261]), reads=[r_oacc], writes=[r_rr])
        s.op("dve", lambda e: e.tensor_tensor(out=rr[P_, 2:3], in0=rr[P_, 1:2], in1=lamt[P_, 3:4], op=ALU.mult),
             reads=[r_rr, r_c], writes=[r_rr])
        s.op("dve", lambda e: e.tensor_scalar(out=a1[P_, :], in0=oacc[P_, 0:128], scalar1=rr[P_, 0:1], scalar2=None, op0=ALU.mult),
             reads=[r_oacc, r_rr], writes=[r_a1])
        s.op("dve", lambda e: e.scalar_tensor_tensor(out=a1[P_, :], in0=oacc[P_, 132:260], scalar=rr[P_, 2:3], in1=a1[P_, :],
                                                     op0=ALU.mult, op1=ALU.add), reads=[r_oacc, r_rr, r_a1], writes=[r_a1])
        s.op("act", lambda e: e.activation(out=junk[P_, :], in_=a1[P_, :], func=AF.Square, accum_out=rr[P_, 3:4]),
             reads=[r_a1], writes=[r_junk, r_rr])
        s.op("act", lambda e: e.activation(out=rr[P_, 4:5], in_=rr[P_, 3:4], func=AF.Sqrt, bias=self.eps_t[P_, 0:1], scale=1.0 / 128.0),
             reads=[r_rr, self.rconst], writes=[r_rr])
        s.op("dve", lambda e: e.reciprocal(out=rr[P_, 5:6], in_=rr[P_, 4:5]), reads=[r_rr], writes=[r_rr])
        s.op("dve", lambda e: e.scalar_tensor_tensor(out=out_ap, in0=a1[P_, :], scalar=rr[P_, 5:6], in1=subl[P_, :],
                                                     op0=ALU.mult, op1=ALU.mult), reads=[r_a1, r_rr, r_c], writes=[r_out])

    def mixer0_S(self):
        nc, s, di, do = self.nc, self.s, self.din, self.dout
        LAM_INIT = 0.8 - 0.6 * float(np.exp(-0.3 * 0))
        NPG = 16
        with ExitStack() as es:
            T = lambda n, sh, dt: es.enter_context(nc.sbuf_tensor(self.name(n), sh, dt))
            P = lambda n, sh, dt: es.enter_context(nc.psum_tensor(self.name(n), sh, dt))
            pt_i = T("pt_i", [128, NSB * NPG], I32)
            io_i = T("io_i", [128, NSB * NPG], I32)
            idx = T("idx", [128, NSB * NPG], I32)
            pt_f = T("pt_f", [128, NSB * NPG], F32)
            io_f = T("io_f", [128, NSB * NPG], F32)
            selrows = T("selrows", [64, 64, 128], BF16)
            q_s = T("q_s", [64, 512], BF16)
            tri = T("tri", [128, 128], F32)
            dl = T("dl", [128, 4, 64], F32)
            dlp = T("dlp", [128, 2, 64], F32)
            lamt = T("lamt", [128, 8], F32)
            subl = T("subl", [128, 128], F32)
            junk = T("junk", [128, 128], F32)
            Qbc = [T("Qbc", [128, 4, 512], BF16) for _ in range(2)]
            Kg = [T("Kg", [128, 512], F32) for _ in range(4)]
            Vg = [T("Vg", [128, 512], F32) for _ in range(3)]
            VPb = [T("VPb", [128, NPG, 4, 132], BF16) for _ in range(2)]
            Kn = [T("Kn", [4, 512], F32) for _ in range(2)]
            Vn = [T("Vn", [4, 512], F32) for _ in range(2)]
            VPn = [T("VPn", [4, 4, 132], BF16) for _ in range(2)]
            prod = [T("prod", [128, 512], F32) for _ in range(2)]
            Sc = [T("Sc", [128, NPG + 1, 4, 2, 4], F32) for _ in range(2)]
            Eb = [T("Eb", [128, NPG + 1, 4, 2, 4], BF16) for _ in range(2)]
            rr = [T("rr", [128, 8], F32) for _ in range(2)]
            a1 = [T("a1", [128, 128], F32) for _ in range(2)]
            mixs = [T("mixs", [4, 512], BF16) for _ in range(2)]
            qps = [P("qps", [128, 512], F32) for _ in range(2)]
            ops_ = [P("ops", [128, 512], F32) for _ in range(4)]
            r_c = Res(); r_Qbc = [Res(), Res()]; r_Kg = [Res() for _ in range(4)]; r_Vg = [Res() for _ in range(3)]
            r_VPb = [Res(), Res()]; r_Kn = [Res(), Res()]; r_Vn = [Res(), Res()]; r_VPn = [Res(), Res()]
            r_prod = [Res(), Res()]; r_Sc = [Res(), Res()]; r_Eb = [Res(), Res()]; r_rr = [Res(), Res()]; r_a1 = [Res(), Res()]
            r_mixs = [Res(), Res()]; r_qps = [Res(), Res()]; r_ops = [Res() for _ in range(4)]; r_junk = Res()
            s.dma("sp", lambda e: e.dma_start(out=pt_i[:], in_=di["page_table"].rearrange("b g -> (b g)").rearrange("(o n) -> o n", o=1).to_broadcast([128, NSB * NPG])),
                  writes=[r_c])
            s.op("pool", lambda e: e.iota(io_i[:], pattern=[[0, NSB * NPG]], base=0, channel_multiplier=1), writes=[r_c])
            s.op("dve", lambda e: e.tensor_copy(out=pt_f[:], in_=pt_i[:]), reads=[r_c], writes=[r_c])
            s.op("dve", lambda e: e.tensor_copy(out=io_f[:], in_=io_i[:]), reads=[r_c], writes=[r_c])
            s.op("dve", lambda e: e.scalar_tensor_tensor(out=pt_f[:], in0=pt_f[:], scalar=128.0, in1=io_f[:], op0=ALU.mult, op1=ALU.add),
                 reads=[r_c], writes=[r_c])
            s.op("dve", lambda e: e.tensor_copy(out=idx[:], in_=pt_f[:]), reads=[r_c], writes=[r_c])
            s.op("pool", lambda e: e.memset(selrows[:], 1.0), writes=[r_c])
            s.op("pool", lambda e: e.affine_select(out=selrows[:], in_=selrows[:], pattern=[[-1, 64], [0, 128]],
                                                    compare_op=ALU.is_equal, fill=0.0, base=0, channel_multiplier=1),
                 reads=[r_c], writes=[r_c])
            s.op("pool", lambda e: e.memset(tri[:], 1.0), writes=[r_c])
            s.op("pool", lambda e: e.affine_select(out=tri[:], in_=tri[:], pattern=[[1, 128]], compare_op=ALU.is_ge,
                                                    fill=0.0, base=0, channel_multiplier=-1), reads=[r_c], writes=[r_c])
            s.dma("sp", lambda e: e.dma_start(out=q_s[:], in_=self.qsbuf[:, :]), reads=[self.r_aks], writes=[r_c])
            self.diff_consts(dl, dlp, lamt, subl, junk, r_c, LAM_INIT)
            for v in VPb:
                s.op("pool", lambda e, v=v: e.memset(v[:, :, :, 128:132], 1.0), writes=r_VPb)
            for v in VPn:
                s.op("pool", lambda e, v=v: e.memset(v[:, :, 128:132], 1.0), writes=r_VPn)
            for sc_ in Sc:
                s.op("pool", lambda e, sc_=sc_: e.memset(sc_[:], 0.0), writes=r_Sc)
            ikg = 0
            ivg = 0
            ipr = 0
            cak = di["cache_a_k"]
            cav = di["cache_a_v"]
            for b in range(NSB):
                kb = b % 2
                for qi in range(4):
                    kq = (b * 4 + qi) % 2
                    s.op("pe", lambda e, kq=kq, b=b, qi=qi: e.matmul(qps[kq][:], lhsT=selrows[:, 4 * b + qi, :], rhs=q_s[:, :],
                                                                     start=True, stop=True),
                         reads=[r_c], writes=[r_qps[kq]])
                    s.op("act", lambda e, kq=kq, kb=kb, qi=qi: e.copy(out=Qbc[kb][:, qi, :], in_=qps[kq][:]),
                         reads=[r_qps[kq]], writes=[r_Qbc[kb]])
                s.dma("sp", lambda e, kb=kb, b=b: e.dma_start(out=Kn[kb][:], in_=do["aks"][4 * b:4 * b + 4, :]),
                      reads=[self.r_aks], writes=[r_Kn[kb]])
                s.dma("sp", lambda e, kb=kb, b=b: e.dma_start(out=Vn[kb][:], in_=do["avs"][4 * b:4 * b + 4, :]),
                      reads=[self.r_aks], writes=[r_Vn[kb]])
                s.op("act", lambda e, kb=kb: e.copy(out=VPn[kb][:, :, 0:128], in_=Vn[kb][:].rearrange("p (h e) -> p h e", h=4)),
                     reads=[r_Vn[kb]], writes=[r_VPn[kb]])
                for pg in range(NPG):
                    kv = ivg % 3
                    ivg += 1
                    col = b * NPG + pg
                    s.dma("pool", lambda e, kv=kv, col=col: e.indirect_dma_start(
                        out=Vg[kv][:], out_offset=None, in_=cav[:, :],
                        in_offset=bass.IndirectOffsetOnAxis(ap=idx[:, col:col + 1], axis=0)),
                        reads=[r_c], writes=[r_Vg[kv]])
                    s.op("act", lambda e, kv=kv, kb=kb, pg=pg: e.copy(out=VPb[kb][:, pg, :, 0:128],
                                                                      in_=Vg[kv][:].rearrange("p (h e) -> p h e", h=4)),
                         reads=[r_Vg[kv]], writes=[r_VPb[kb]])
                for pg in range(NPG + 1):
                    if pg < NPG:
                        kk = ikg % 4
                        ikg += 1
                        col = b * NPG + pg
                        s.dma("pool", lambda e, kk=kk, col=col: e.indirect_dma_start(
                            out=Kg[kk][:], out_offset=None, in_=cak[:, :],
                            in_offset=bass.IndirectOffsetOnAxis(ap=idx[:, col:col + 1], axis=0)),
                            reads=[r_c], writes=[r_Kg[kk]])
                        ksrc, r_ks, npart = Kg[kk], r_Kg[kk], 128
                    else:
                        ksrc, r_ks, npart = Kn[kb], r_Kn[kb], 4
                    for qi in range(4):
                        kp = ipr % 2
                        ipr += 1
                        s.op("dve", lambda e, kp=kp, ksrc=ksrc, npart=npart, kb=kb, qi=qi:
                             e.tensor_tensor(out=prod[kp][0:npart, :], in0=ksrc[0:npart, :], in1=Qbc[kb][0:npart, qi, :], op=ALU.mult),
                             reads=[r_ks, r_Qbc[kb]], writes=[r_prod[kp]])
                        s.op("dve", lambda e, kp=kp, npart=npart, kb=kb, pg=pg, qi=qi:
                             e.tensor_reduce(out=Sc[kb][0:npart, pg, :, :, qi].rearrange("p h m -> p (h m)"),
                                             in_=prod[kp][0:npart, :].rearrange("p (g d) -> p g d", d=64), axis=AX.X, op=ALU.add),
                             reads=[r_prod[kp]], writes=[r_Sc[kb]])
                s.op("act", lambda e, kb=kb: e.activation(out=Eb[kb][:].rearrange("p a h m q -> p (a h m q)"),
                                                          in_=Sc[kb][:].rearrange("p a h m q -> p (a h m q)"),
                                                          func=AF.Exp, scale=0.125), reads=[r_Sc[kb]], writes=[r_Eb[kb]])
                s.op("pool", lambda e, kb=kb: e.tensor_tensor(out=Eb[kb][0:4, NPG].rearrange("p h m q -> p (h m) q"),
                                                               in0=Eb[kb][0:4, NPG].rearrange("p h m q -> p (h m) q"),
                                                               in1=tri[0:4, 0:4].unsqueeze(1).to_broadcast([4, 8, 4]), op=ALU.mult),
                     reads=[r_Eb[kb], r_c], writes=[r_Eb[kb]])
                for h in range(4):
                    for mm in range(2):
                        oc = mm * 132
                        for pg in range(NPG + 1):
                            if pg < NPG:
                                s.op("pe", lambda e, h=h, mm=mm, oc=oc, pg=pg, kb=kb:
                                     e.matmul(ops_[h][0:4, oc:oc + 129], lhsT=Eb[kb][:, pg, h, mm, :], rhs=VPb[kb][:, pg, h, 0:129],
                                              start=(pg == 0), stop=False),
                                     reads=[r_Eb[kb], r_VPb[kb]], writes=[r_ops[h]])
                            else:
                                s.op("pe", lambda e, h=h, mm=mm, oc=oc, pg=pg, kb=kb:
                                     e.matmul(ops_[h][0:4, oc:oc + 129], lhsT=Eb[kb][0:4, pg, h, mm, :], rhs=VPn[kb][0:4, h, 0:129],
                                              start=False, stop=True),
                                     reads=[r_Eb[kb], r_VPn[kb]], writes=[r_ops[h]])
                    kk2 = (b * 4 + h) % 2
                    self.diff_finalize(ops_[h], r_ops[h], rr[kk2], r_rr[kk2], a1[kk2], r_a1[kk2], junk, r_junk, lamt, subl, r_c,
                                       mixs[kb][0:4, h * 128:(h + 1) * 128], r_mixs[kb], 4)
                s.dma("sp", lambda e, kb=kb, b=b: e.dma_start(out=self.mixbuf[SEQ + 4 * b:SEQ + 4 * b + 4, 0:512], in_=mixs[kb][:]),
                      reads=[r_mixs[kb]], writes=[self.mixres[32]])
            s.barrier()

    def s5_setup(self, es, S):
        nc, s, di = self.nc, self.s, self.din
        T = lambda n, sh, dt: es.enter_context(nc.sbuf_tensor(self.name(n), sh, dt))
        PI = float(np.pi)
        r = S["r_c"] = Res()
        rc = [r]
        par = T("s5par", [128, 24, 16], F32)
        pari = T("s5pari", [128, 128], I32)
        S["par"] = par
        (A_RE, A_IM, LDT, DT, MAG, ANG, SN, CS, LRE, LIM, DEN, NR, FRE, FIM, T0, T1, NEGPI, TWOPI) = range(18)
        S["idx"] = dict(RHO=MAG, SN=SN, CS=CS, LRE=LRE, LIM=LIM)
        pv = lambda i: par[:, i, :]
        flat = lambda a: a.rearrange("g p -> (g p)").rearrange("(j q) -> q j", q=128)
        s.dma("sp", lambda e: e.dma_start(out=pv(A_RE), in_=flat(di["s5_a_re"]), allow_slow_non_contiguous=True), writes=rc)
        s.dma("sp", lambda e: e.dma_start(out=pv(A_IM), in_=flat(di["s5_a_im"]), allow_slow_non_contiguous=True), writes=rc)
        ld2 = di["s5_log_dt"].rearrange("o (j two) -> o two j", two=2)
        s.dma("sp", lambda e: e.dma_start(out=par[0:64, LDT, :], in_=ld2[0:1, 0, :].to_broadcast([64, 16]), allow_slow_non_contiguous=True), writes=rc)
        s.dma("sp", lambda e: e.dma_start(out=par[64:128, LDT, :], in_=ld2[0:1, 1, :].to_broadcast([64, 16]), allow_slow_non_contiguous=True), writes=rc)
        s.op("pool", lambda e: e.memset(pv(NEGPI), -PI), writes=rc)
        s.op("pool", lambda e: e.memset(pv(TWOPI), 2 * PI), writes=rc)
        V = lambda fn: s.op("dve", fn, reads=rc, writes=rc)
        Aop = lambda fn: s.op("act", fn, reads=rc, writes=rc)
        Aop(lambda e: e.activation(out=pv(DT), in_=pv(LDT), func=AF.Exp))
        V(lambda e: e.tensor_tensor(out=pv(T0), in0=pv(A_RE), in1=pv(DT), op=ALU.mult))
        Aop(lambda e: e.activation(out=pv(MAG), in_=pv(T0), func=AF.Exp))
        V(lambda e: e.tensor_tensor(out=pv(ANG), in0=pv(A_IM), in1=pv(DT), op=ALU.mult))

        def sincos(out_sin, out_cos, ang_ap, tmp_ap, tmpi_ap):
            TWO_PI_S = 2 * PI - 2e-6
            for dst, off in ((out_sin, 0.0), (out_cos, 0.25)):
                V(lambda e, off=off: e.tensor_scalar(out=tmp_ap, in0=ang_ap, scalar1=1.0 / (2 * PI), scalar2=off, op0=ALU.mult, op1=ALU.add))
                V(lambda e: e.tensor_copy(out=tmpi_ap, in_=tmp_ap))
                V(lambda e: e.tensor_tensor(out=tmp_ap, in0=tmp_ap, in1=tmpi_ap, op=ALU.subtract))
                Aop(lambda e, dst=dst: e.activation(out=dst, in_=tmp_ap, func=AF.Sin, scale=TWO_PI_S))
        S["sincos"] = sincos
        sincos(pv(SN), pv(CS), pv(ANG), pv(T0), pari[:, 0:16])
        V(lambda e: e.tensor_tensor(out=pv(LRE), in0=pv(MAG), in1=pv(CS), op=ALU.mult))
        V(lambda e: e.tensor_tensor(out=pv(LIM), in0=pv(MAG), in1=pv(SN), op=ALU.mult))
        V(lambda e: e.tensor_tensor(out=pv(DEN), in0=pv(A_RE), in1=pv(A_RE), op=ALU.mult))
        V(lambda e: e.tensor_tensor(out=pv(T0), in0=pv(A_IM), in1=pv(A_IM), op=ALU.mult))
        V(lambda e: e.tensor_tensor(out=pv(DEN), in0=pv(DEN), in1=pv(T0), op=ALU.add))
        V(lambda e: e.reciprocal(out=pv(DEN), in_=pv(DEN)))
        V(lambda e: e.tensor_scalar(out=pv(NR), in0=pv(LRE), scalar1=-1.0, scalar2=None, op0=ALU.add))
        V(lambda e: e.tensor_tensor(out=pv(T0), in0=pv(NR), in1=pv(A_RE), op=ALU.mult))
        V(lambda e: e.tensor_tensor(out=pv(T1), in0=pv(LIM), in1=pv(A_IM), op=ALU.mult))
        V(lambda e: e.tensor_tensor(out=pv(T0), in0=pv(T0), in1=pv(T1), op=ALU.add))
        V(lambda e: e.tensor_tensor(out=pv(FRE), in0=pv(T0), in1=pv(DEN), op=ALU.mult))
        V(lambda e: e.tensor_tensor(out=pv(T0), in0=pv(LIM), in1=pv(A_RE), op=ALU.mult))
        V(lambda e: e.tensor_tensor(out=pv(T1), in0=pv(NR), in1=pv(A_IM), op=ALU.mult))
        V(lambda e: e.tensor_tensor(out=pv(T0), in0=pv(T0), in1=pv(T1), op=ALU.subtract))
        V(lambda e: e.tensor_tensor(out=pv(FIM), in0=pv(T0), in1=pv(DEN), op=ALU.mult))
        M4 = T("M4", [128, 4, 8, 16], F32)
        s.op("pool", lambda e: e.memset(M4[:], 0.0), writes=rc)
        for v in range(4):
            s.op("pool", lambda e, v=v: e.memset(M4[0:64, v, 2 * v, :], 1.0), writes=rc)
            s.op("pool", lambda e, v=v: e.memset(M4[64:128, v, 2 * v + 1, :], 1.0), writes=rc)
        BBT = S["BBT"] = T("BBT", [128, 2, 16, 128], BF16)
        CX = S["CX"] = T("CX", [128, 2, 16, 128], BF16)
        with ExitStack() as es2:
            T2 = lambda n, sh, dt: es2.enter_context(nc.sbuf_tensor(self.name(n), sh, dt))
            P2 = lambda n, sh, dt: es2.enter_context(nc.psum_tensor(self.name(n), sh, dt))
            Bre = T2("Bre", [128, 16, 16], F32); Bim = T2("Bim", [128, 16, 16], F32)
            bbr = T2("bbr", [128, 16, 16], F32); bbi = T2("bbi", [128, 16, 16], F32); tt = T2("tt", [128, 16, 16], F32)
            BX = T2("BX", [128, 128], F32)
            Cnat = T2("Cnat", [128, 4, 128], F32)
            CT = [T2("CTd", [128, 4, 128], F32) for _ in range(2)]
            ptr = [P2("ptr", [128, 128], F32) for _ in range(2)]
            r_ptr = [Res(), Res()]; r_BX = Res()
            bflat = lambda a: a.rearrange("g p c -> (g p) c").rearrange("(j q) c -> q j c", q=128)
            s.dma("sp", lambda e: e.dma_start(out=Bre[:], in_=bflat(di["s5_b_re"])), writes=rc)
            s.dma("sp", lambda e: e.dma_start(out=Bim[:], in_=bflat(di["s5_b_im"])), writes=rc)
            fre_b = par[:, FRE, :].unsqueeze(2).to_broadcast([128, 16, 16])
            fim_b = par[:, FIM, :].unsqueeze(2).to_broadcast([128, 16, 16])
            V(lambda e: e.tensor_tensor(out=bbr[:], in0=Bre[:], in1=fre_b, op=ALU.mult))
            V(lambda e: e.tensor_tensor(out=tt[:], in0=Bim[:], in1=fim_b, op=ALU.mult))
            V(lambda e: e.tensor_tensor(out=bbr[:], in0=bbr[:], in1=tt[:], op=ALU.subtract))
            V(lambda e: e.tensor_tensor(out=bbi[:], in0=Bim[:], in1=fre_b, op=ALU.mult))
            V(lambda e: e.tensor_tensor(out=tt[:], in0=Bre[:], in1=fim_b, op=ALU.mult))
            V(lambda e: e.tensor_tensor(out=bbi[:], in0=bbi[:], in1=tt[:], op=ALU.add))
            it = 0
            for ri, bb in enumerate((bbr, bbi)):
                for j in range(16):
                    k = it % 2
                    it += 1
                    s.op("dve", lambda e, bb=bb, j=j: e.tensor_tensor(out=BX[:].rearrange("p (g c) -> p g c", g=8),
                                                                      in0=bb[:, j, :].unsqueeze(1).to_broadcast([128, 8, 16]),
                                                                      in1=M4[:, j % 4, :, :], op=ALU.mult),
                         reads=rc + [r_BX], writes=[r_BX])
                    s.op("pe", lambda e, k=k: e.transpose(out=ptr[k][:], in_=BX[:], identity=self.ident_f[:]),
                         reads=[r_BX, self.rconst], writes=[r_ptr[k]])
                    s.op("act", lambda e, k=k, ri=ri, j=j: e.copy(out=BBT[:, ri, j, :], in_=ptr[k][:]), reads=[r_ptr[k]], writes=rc)
            for ri, nm in enumerate(("s5_c_re", "s5_c_im")):
                cnatv = di[nm].rearrange("g c p -> (g c) p").rearrange("(m q) p -> q m p", q=128)
                s.dma("sp", lambda e, cnatv=cnatv: e.dma_start(out=Cnat[:, :, 0:64], in_=cnatv), writes=rc)
                s.dma("sp", lambda e, cnatv=cnatv: e.dma_start(out=Cnat[:, :, 64:128], in_=cnatv), writes=rc)
                for mch in range(4):
                    k = it % 2
                    it += 1
                    s.op("pe", lambda e, k=k, mch=mch: e.transpose(out=ptr[k][:], in_=Cnat[:, mch, :], identity=self.ident_f[:]),
                         reads=rc + [self.rconst], writes=[r_ptr[k]])
                    s.op("act", lambda e, k=k, ri=ri, mch=mch: e.copy(out=CT[ri][:, mch, :], in_=ptr[k][:]), reads=[r_ptr[k]], writes=rc)
                sgn = 1.0 if ri == 0 else -1.0
                for j in range(16):
                    V(lambda e, ri=ri, j=j, sgn=sgn: e.scalar_tensor_tensor(out=CX[:, ri, j, :], in0=CT[ri][:, j // 4, :], scalar=sgn,
                                                                            in1=M4[:, j % 4, :, :].rearrange("p g c -> p (g c)"),
                                                                            op0=ALU.mult, op1=ALU.mult))
            s.barrier()
        COS = S["COS"] = T("COS", [128, 16, 128], F32)
        SIN = S["SIN"] = T("SIN", [128, 16, 128], F32)
        jl = T("jl", [128, 128], F32)
        tb = T("tb", [128, 128], F32)
        tb2 = T("tb2", [128, 128], F32)
        jli = T("jli", [128, 128], I32)
        s.op("pool", lambda e: e.iota(jli[:], pattern=[[1, 128]], base=0, channel_multiplier=0), writes=rc)
        V(lambda e: e.tensor_copy(out=jl[:], in_=jli[:]))
        for j in range(16):
            V(lambda e, j=j: e.tensor_scalar(out=tb[:], in0=jl[:], scalar1=par[:, ANG, j:j + 1], scalar2=None, op0=ALU.mult))
            sincos(SIN[:, j, :], COS[:, j, :], tb[:], tb2[:], pari[:, :])
        dg = S["dg"] = T("dg", [128, 2, 4], F32)
        s.dma("sp", lambda e: e.dma_start(out=dg[:, 0, :], in_=di["s5_d"].rearrange("o (m q) -> q (o m)", q=128), allow_slow_non_contiguous=True), writes=rc)
        s.dma("sp", lambda e: e.dma_start(out=dg[:, 1, :], in_=di["s5_glu_b"].rearrange("o (m q) -> q (o m)", q=128), allow_slow_non_contiguous=True), writes=rc)
        gwf = T("gwf", [128, 4, 512], F32)
        gw = S["gw"] = T("gw", [128, 4, 512], BF16)
        s.dma("sp", lambda e: e.dma_start(out=gwf[:], in_=di["s5_glu_w"].rearrange("(m q) n -> q m n", q=128)), writes=rc)
        V(lambda e: e.tensor_copy(out=gw[:], in_=gwf[:]))

    def mixer0_B(self, cur, nxt):
        nc, s, di, do = self.nc, self.s, self.din, self.dout
        with ExitStack() as es:
            T = lambda n, sh, dt: es.enter_context(nc.sbuf_tensor(self.name(n), sh, dt))
            P = lambda n, sh, dt: es.enter_context(nc.psum_tensor(self.name(n), sh, dt))
            S = {}
            self.s5_setup(es, S)
            par, COS, SIN, BBT, CX, dg, gw = S["par"], S["COS"], S["SIN"], S["BBT"], S["CX"], S["dg"], S["gw"]
            ix = S["idx"]
            r_c = S["r_c"]
            m = self.alloc_mod(es)
            ewout = T("ewout", [128, NKC, D], BF16)
            xt = [T("xt", [128, D], F32) for _ in range(2)]
            ua = [T("ua", [128, 2, 512], BF16) for _ in range(2)]
            fT = [T("fT", [128, 8, 128], BF16) for _ in range(2)]
            y5T = [T("y5T", [128, 4, 128], BF16) for _ in range(2)]
            z = [T("z", [128, 4, 128], BF16) for _ in range(2)]
            NW = 6
            w = [T("w", [128, 4, 128], F32) for _ in range(NW)]
            gr = T("gr", [128, 4, 128], F32); gi = T("gi", [128, 4, 128], F32)
            hr = T("hr", [128, 4, 128], F32); hi = T("hi", [128, 4, 128], F32)
            hrb = [T("hrb", [128, 4, 128], BF16) for _ in range(2)]; hib = [T("hib", [128, 4, 128], BF16) for _ in range(2)]
            ysb = T("ysb", [128, 128], F32); ysq = T("ysq", [128, 128], F32); ysg = T("ysg", [128, 128], F32)
            carry = T("carry", [128, 4, 16], F32)
            ctmp = T("ctmp", [128, 2, 16], F32)
            t1 = [T("t1", [128, D], F32) for _ in range(2)]
            stt = [T("st", [128, 16], F32) for _ in range(2)]
            COSs = T("COSs", [128, 16, 64], F32); SINs = T("SINs", [128, 16, 64], F32); RHOm = T("RHOm", [128, 16, 64], F32)
            bmask = T("bmask", [128, 16, 4], F32)
            h0 = T("h0", [16, 2, 2048], F32); h0n = T("h0n", [128, 2, 16, 16], F32); inj = T("inj", [128, 2, 16, 16], F32)
            hout = T("hout", [128, 2, 16, 16], F32); houtT = T("houtT", [16, 2, 2048], F32)
            xre = P("xre", [128, 4, 128], F32); xim = P("xim", [128, 4, 128], F32)
            py = P("py", [128, 512], F32); pgl = P("pgl", [128, 512], F32)
            tpb = P("tpb", [128, 8, 128], BF16)
            po = [P("po", [128, 512], F32) for _ in range(2)]
            pmisc = P("pmisc", [128, 512], F32)
            r_ew = Res(); r_x = [Res(), Res()]; r_ua = [Res(), Res()]; r_fT = [Res(), Res()]; r_y5 = [Res(), Res()]; r_z = [Res(), Res()]
            r_w = [Res() for _ in range(NW)]; r_gr = Res(); r_gi = Res(); r_hr = Res(); r_hi = Res(); r_hb = [Res(), Res()]
            r_ys = Res(); r_carry = Res(); r_t1 = [Res(), Res()]; r_st = [Res(), Res()]
            r_xre = Res(); r_xim = Res(); r_py = Res(); r_pgl = Res(); r_tp = Res(); r_po = [Res(), Res()]; r_pm = Res(); r_s = Res()
            s.dma("sp", lambda e: e.dma_start(out=ewout[:], in_=self.ewout_b[:, :].rearrange("(kc p) n -> p kc n", p=128)), writes=[r_ew])
            self.load_mod(m, 0, 1, None)
            self.bcast_mod(m, False, pmisc, r_pm)
            s.op("pool", lambda e: e.memset(carry[:], 0.0), writes=[r_carry])
            Vs = lambda fn: s.op("dve", fn, reads=[r_c, r_s], writes=[r_s])
            Vs(lambda e: e.tensor_copy(out=COSs[:].rearrange("p j (b t) -> p j b t", t=4), in_=COS[:, :, 0:4].unsqueeze(2).to_broadcast([128, 16, 16, 4])))
            Vs(lambda e: e.tensor_copy(out=SINs[:].rearrange("p j (b t) -> p j b t", t=4), in_=SIN[:, :, 0:4].unsqueeze(2).to_broadcast([128, 16, 16, 4])))
            s.op("pool", lambda e: e.memset(bmask[:], 1.0), reads=[r_s], writes=[r_s])
            s.op("pool", lambda e: e.memset(bmask[:, :, 0:1], 0.0), reads=[r_s], writes=[r_s])
            Vs(lambda e: e.tensor_tensor(out=RHOm[:], in0=par[:, ix["RHO"], :].unsqueeze(2).to_broadcast([128, 16, 64]),
                                         in1=bmask[:].rearrange("p b t -> p (b t)").unsqueeze(1).to_broadcast([128, 16, 64]), op=ALU.mult))
            s.dma("sp", lambda e: e.dma_start(out=h0[:, 0, :], in_=di["state_s5_re"][:, :]), writes=[r_s])
            s.dma("sp", lambda e: e.dma_start(out=h0[:, 1, :], in_=di["state_s5_im"][:, :]), writes=[r_s])
            for ri in range(2):
                for j in range(16):
                    s.op("pe", lambda e, ri=ri, j=j: e.transpose(out=pmisc[:, (ri * 16 + j) * 16:(ri * 16 + j + 1) * 16],
                                                                 in_=h0[:, ri, j * 128:(j + 1) * 128], identity=self.ident_f[0:16, 0:16]),
                         reads=[r_s, self.rconst], writes=[r_pm])
            Vs2 = lambda fn: s.op("dve", fn, reads=[r_c, r_s, r_pm], writes=[r_s])
            Vs2(lambda e: e.tensor_copy(out=h0n[:].rearrange("p a j b -> p (a j b)"), in_=pmisc[:, 0:512]))
            lre_b = par[:, ix["LRE"], :].unsqueeze(2).to_broadcast([128, 16, 16])
            lim_b = par[:, ix["LIM"], :].unsqueeze(2).to_broadcast([128, 16, 16])
            Vs(lambda e: e.tensor_tensor(out=inj[:, 0], in0=h0n[:, 0], in1=lre_b, op=ALU.mult))
            Vs(lambda e: e.tensor_tensor(out=hout[:, 0], in0=h0n[:, 1], in1=lim_b, op=ALU.mult))
            Vs(lambda e: e.tensor_tensor(out=inj[:, 0], in0=inj[:, 0], in1=hout[:, 0], op=ALU.subtract))
            Vs(lambda e: e.tensor_tensor(out=inj[:, 1], in0=h0n[:, 1], in1=lre_b, op=ALU.mult))
            Vs(lambda e: e.tensor_tensor(out=hout[:, 0], in0=h0n[:, 0], in1=lim_b, op=ALU.mult))
            Vs(lambda e: e.tensor_tensor(out=inj[:, 1], in0=inj[:, 1], in1=hout[:, 0], op=ALU.add))
            iw = [0]

            def W():
                k = iw[0] % NW
                iw[0] += 1
                return w[k], r_w[k]

            for i in range(NTILE):
                sample = (i == 32)
                k = i % 2
                Wd = 64 if sample else 128
                if sample:
                    self.bcast_mod(m, True, pmisc, r_pm)
                ap, nrows, res = self.x_src(False, cur, i)
                rows = slice(i * 128, (i + 1) * 128)
                s.dma("sp", lambda e, k=k, ap=ap: e.dma_start(out=xt[k][:], in_=ap), reads=[res], writes=[r_x[k]])
                s.dma("sp", lambda e, k=k, rows=rows: e.dma_start(out=ua[k][:, 0, :], in_=self.ubuf[rows, :]), reads=[self.ures[i]], writes=[r_ua[k]])
                s.dma("sp", lambda e, k=k, rows=rows: e.dma_start(out=ua[k][:, 1, :], in_=self.mixbuf[rows, 0:512]), reads=[self.mixres[i]], writes=[r_ua[k]])
                for c in range(8):
                    s.op("pe", lambda e, k=k, c=c: e.transpose(out=tpb[:, c, :], in_=ua[k][:, c // 4, (c % 4) * 128:(c % 4 + 1) * 128],
                                                               identity=self.ident_b[:]), reads=[r_ua[k], self.rconst], writes=[r_tp])
                s.op("act", lambda e, k=k: e.copy(out=fT[k][:], in_=tpb[:]), reads=[r_tp], writes=[r_fT[k]])
                Ct = COSs if sample else COS
                St = SINs if sample else SIN
                for jg in range(4):
                    js = slice(jg * 4, jg * 4 + 4)
                    for jj in range(4):
                        j = jg * 4 + jj
                        s.op("pe", lambda e, k=k, j=j, jj=jj, jg=jg, Wd=Wd: e.matmul(xre[:, jj, 0:Wd], lhsT=BBT[:, 0, j, :], rhs=fT[k][:, jg, 0:Wd],
                                                                                   start=True, stop=True), reads=[r_c, r_fT[k]], writes=[r_xre])
                        s.op("pe", lambda e, k=k, j=j, jj=jj, jg=jg, Wd=Wd: e.matmul(xim[:, jj, 0:Wd], lhsT=BBT[:, 1, j, :], rhs=fT[k][:, jg, 0:Wd],
                                                                                   start=True, stop=True), reads=[r_c, r_fT[k]], writes=[r_xim])
                    wa, r_wa = W(); wb, r_wb = W(); xr, r_xr = W(); xi, r_xi = W()
                    Cv = Ct[:, js, 0:Wd]; Sv = St[:, js, 0:Wd]
                    rcs = [r_c, r_s]
                    s.op("dve", lambda e, wa=wa, Cv=Cv, Wd=Wd: e.tensor_tensor(out=wa[:, :, 0:Wd], in0=xre[:, :, 0:Wd], in1=Cv, op=ALU.mult), reads=[r_xre] + rcs, writes=[r_wa])
                    s.op("dve", lambda e, wb=wb, Sv=Sv, Wd=Wd: e.tensor_tensor(out=wb[:, :, 0:Wd], in0=xim[:, :, 0:Wd], in1=Sv, op=ALU.mult), reads=[r_xim] + rcs, writes=[r_wb])
                    s.op("pool", lambda e, wa=wa, wb=wb, xr=xr, Wd=Wd: e.tensor_tensor(out=xr[:, :, 0:Wd], in0=wa[:, :, 0:Wd], in1=wb[:, :, 0:Wd], op=ALU.add), reads=[r_wa, r_wb], writes=[r_xr])
                    wa2, r_wa2 = W(); wb2, r_wb2 = W()
                    s.op("dve", lambda e, wa2=wa2, Cv=Cv, Wd=Wd: e.tensor_tensor(out=wa2[:, :, 0:Wd], in0=xim[:, :, 0:Wd], in1=Cv, op=ALU.mult), reads=[r_xim] + rcs, writes=[r_wa2])
                    s.op("dve", lambda e, wb2=wb2, Sv=Sv, Wd=Wd: e.tensor_tensor(out=wb2[:, :, 0:Wd], in0=xre[:, :, 0:Wd], in1=Sv, op=ALU.mult), reads=[r_xre] + rcs, writes=[r_wb2])
                    s.op("pool", lambda e, wa2=wa2, wb2=wb2, xi=xi, Wd=Wd: e.tensor_tensor(out=xi[:, :, 0:Wd], in0=wa2[:, :, 0:Wd], in1=wb2[:, :, 0:Wd], op=ALU.subtract), reads=[r_wa2, r_wb2], writes=[r_xi])
                    if sample:
                        s.op("pool", lambda e, xr=xr, js=js: e.tensor_tensor(out=xr[:, :, 0:64].rearrange("p j (b t) -> p j b t", t=4)[:, :, :, 0],
                                                                             in0=xr[:, :, 0:64].rearrange("p j (b t) -> p j b t", t=4)[:, :, :, 0],
                                                                             in1=inj[:, 0, js, :], op=ALU.add), reads=[r_xr, r_s], writes=[r_xr])
                        s.op("pool", lambda e, xi=xi, js=js: e.tensor_tensor(out=xi[:, :, 0:64].rearrange("p j (b t) -> p j b t", t=4)[:, :, :, 0],
                                                                             in0=xi[:, :, 0:64].rearrange("p j (b t) -> p j b t", t=4)[:, :, :, 0],
                                                                             in1=inj[:, 1, js, :], op=ALU.add), reads=[r_xi, r_s], writes=[r_xi])
                    for jj in range(4):
                        j = jg * 4 + jj
                        if sample:
                            d0 = RHOm[:, j, :]
                            ini_r = 0.0
                            ini_i = 0.0
                        else:
                            d0 = par[:, ix["RHO"], j:j + 1].to_broadcast([128, 128])
                            ini_r = carry[:, 2, j:j + 1]
                            ini_i = carry[:, 3, j:j + 1]
                        s.op("dve", lambda e, xr=xr, jj=jj, d0=d0, ini_r=ini_r, Wd=Wd: e.tensor_tensor_scan(out=gr[:, jj, 0:Wd], data0=d0, data1=xr[:, jj, 0:Wd],
                                                                                                       initial=ini_r, op0=ALU.mult, op1=ALU.add),
                             reads=[r_xr, r_c, r_s, r_carry], writes=[r_gr])
                        s.op("dve", lambda e, xi=xi, jj=jj, d0=d0, ini_i=ini_i, Wd=Wd: e.tensor_tensor_scan(out=gi[:, jj, 0:Wd], data0=d0, data1=xi[:, jj, 0:Wd],
                                                                                                       initial=ini_i, op0=ALU.mult, op1=ALU.add),
                             reads=[r_xi, r_c, r_s, r_carry], writes=[r_gi])
                    wa, r_wa = W(); wb, r_wb = W()
                    s.op("dve", lambda e, wa=wa, Cv=Cv, Wd=Wd: e.tensor_tensor(out=wa[:, :, 0:Wd], in0=gr[:, :, 0:Wd], in1=Cv, op=ALU.mult), reads=[r_gr] + rcs, writes=[r_wa])
                    s.op("pool", lambda e, wb=wb, Sv=Sv, Wd=Wd: e.tensor_tensor(out=wb[:, :, 0:Wd], in0=gi[:, :, 0:Wd], in1=Sv, op=ALU.mult), reads=[r_gi] + rcs, writes=[r_wb])
                    s.op("dve", lambda e, wa=wa, wb=wb, Wd=Wd: e.tensor_tensor(out=hr[:, :, 0:Wd], in0=wa[:, :, 0:Wd], in1=wb[:, :, 0:Wd], op=ALU.subtract), reads=[r_wa, r_wb], writes=[r_hr])
                    wa2, r_wa2 = W(); wb2, r_wb2 = W()
                    s.op("pool", lambda e, wa2=wa2, Sv=Sv, Wd=Wd: e.tensor_tensor(out=wa2[:, :, 0:Wd], in0=gr[:, :, 0:Wd], in1=Sv, op=ALU.mult), reads=[r_gr] + rcs, writes=[r_wa2])
                    s.op("dve", lambda e, wb2=wb2, Cv=Cv, Wd=Wd: e.tensor_tensor(out=wb2[:, :, 0:Wd], in0=gi[:, :, 0:Wd], in1=Cv, op=ALU.mult), reads=[r_gi] + rcs, writes=[r_wb2])
                    s.op("dve", lambda e, wa2=wa2, wb2=wb2, Wd=Wd: e.tensor_tensor(out=hi[:, :, 0:Wd], in0=wa2[:, :, 0:Wd], in1=wb2[:, :, 0:Wd], op=ALU.add), reads=[r_wa2, r_wb2], writes=[r_hi])
                    kh = (i * 4 + jg) % 2
                    s.op("act", lambda e, kh=kh, Wd=Wd: e.copy(out=hrb[kh][:, :, 0:Wd], in_=hr[:, :, 0:Wd]), reads=[r_hr], writes=[r_hb[kh]])
                    s.op("act", lambda e, kh=kh, Wd=Wd: e.copy(out=hib[kh][:, :, 0:Wd], in_=hi[:, :, 0:Wd]), reads=[r_hi], writes=[r_hb[kh]])
                    if not sample:
                        s.op("pool", lambda e, js=js: e.tensor_copy(out=carry[:, 0, js], in_=hr[:, :, 127]), reads=[r_hr], writes=[r_carry])
                        s.op("pool", lambda e, js=js: e.tensor_copy(out=carry[:, 1, js], in_=hi[:, :, 127]), reads=[r_hi], writes=[r_carry])
                        cs_ = par[:, ix["CS"], js]; sn_ = par[:, ix["SN"], js]
                        s.op("dve", lambda e, js=js, cs_=cs_: e.tensor_tensor(out=ctmp[:, 0, js], in0=carry[:, 0, js], in1=cs_, op=ALU.mult), reads=[r_carry, r_c], writes=[r_carry])
                        s.op("dve", lambda e, js=js, sn_=sn_: e.tensor_tensor(out=ctmp[:, 1, js], in0=carry[:, 1, js], in1=sn_, op=ALU.mult), reads=[r_carry, r_c], writes=[r_carry])
                        s.op("dve", lambda e, js=js: e.tensor_tensor(out=carry[:, 2, js], in0=ctmp[:, 0, js], in1=ctmp[:, 1, js], op=ALU.subtract), reads=[r_carry], writes=[r_carry])
                        s.op("dve", lambda e, js=js, sn_=sn_: e.tensor_tensor(out=ctmp[:, 0, js], in0=carry[:, 0, js], in1=sn_, op=ALU.mult), reads=[r_carry, r_c], writes=[r_carry])
                        s.op("dve", lambda e, js=js, cs_=cs_: e.tensor_tensor(out=ctmp[:, 1, js], in0=carry[:, 1, js], in1=cs_, op=ALU.mult), reads=[r_carry, r_c], writes=[r_carry])
                        s.op("dve", lambda e, js=js: e.tensor_tensor(out=carry[:, 3, js], in0=ctmp[:, 0, js], in1=ctmp[:, 1, js], op=ALU.add), reads=[r_carry], writes=[r_carry])
                    else:
                        s.op("pool", lambda e, js=js: e.tensor_copy(out=hout[:, 0, js, :], in_=hr[:, :, 0:64].rearrange("p j (b t) -> p j b t", t=4)[:, :, :, 3]),
                             reads=[r_hr, r_s], writes=[r_s])
                        s.op("pool", lambda e, js=js: e.tensor_copy(out=hout[:, 1, js, :], in_=hi[:, :, 0:64].rearrange("p j (b t) -> p j b t", t=4)[:, :, :, 3]),
                             reads=[r_hi, r_s], writes=[r_s])
                    for jj in range(4):
                        j = jg * 4 + jj
                        s.op("pe", lambda e, kh=kh, j=j, jj=jj, Wd=Wd: e.matmul(py[:, 0:Wd], lhsT=CX[:, 0, j, :], rhs=hrb[kh][:, jj, 0:Wd],
                                                                                start=(jj == 0), stop=False), reads=[r_c, r_hb[kh]], writes=[r_py])
                        s.op("pe", lambda e, kh=kh, j=j, jj=jj, Wd=Wd: e.matmul(py[:, 0:Wd], lhsT=CX[:, 1, j, :], rhs=hib[kh][:, jj, 0:Wd],
                                                                                start=False, stop=(jj == 3)), reads=[r_c, r_hb[kh]], writes=[r_py])
                    s.op("dve", lambda e, k=k, jg=jg, Wd=Wd: e.scalar_tensor_tensor(out=ysb[:, 0:Wd], in0=fT[k][:, jg, 0:Wd], scalar=dg[:, 0, jg:jg + 1],
                                                                                     in1=py[:, 0:Wd], op0=ALU.mult, op1=ALU.add),
                         reads=[r_fT[k], r_py, r_c], writes=[r_ys])
                    s.op("pool", lambda e, Wd=Wd: e.tensor_tensor(out=ysq[:, 0:Wd], in0=ysb[:, 0:Wd], in1=ysb[:, 0:Wd], op=ALU.mult), reads=[r_ys], writes=[r_ys])
                    s.op("pool", lambda e, Wd=Wd: e.tensor_scalar(out=ysq[:, 0:Wd], in0=ysq[:, 0:Wd], scalar1=0.044715, scalar2=1.0, op0=ALU.mult, op1=ALU.add), reads=[r_ys], writes=[r_ys])
                    s.op("pool", lambda e, Wd=Wd: e.tensor_tensor(out=ysq[:, 0:Wd], in0=ysq[:, 0:Wd], in1=ysb[:, 0:Wd], op=ALU.mult), reads=[r_ys], writes=[r_ys])
                    s.op("act", lambda e, Wd=Wd: e.activation(out=ysg[:, 0:Wd], in_=ysq[:, 0:Wd], func=AF.Sigmoid, scale=1.5957691216057308), reads=[r_ys], writes=[r_ys])
                    s.op("dve", lambda e, k=k, jg=jg, Wd=Wd: e.tensor_tensor(out=z[k][:, jg, 0:Wd], in0=ysb[:, 0:Wd], in1=ysg[:, 0:Wd], op=ALU.mult), reads=[r_ys], writes=[r_z[k]])
                for mo in range(4):
                    for mi in range(4):
                        s.op("pe", lambda e, k=k, mo=mo, mi=mi, Wd=Wd: e.matmul(pgl[:, 0:Wd], lhsT=gw[:, mi, mo * 128:(mo + 1) * 128], rhs=z[k][:, mi, 0:Wd],
                                                                                start=(mi == 0), stop=(mi == 3)), reads=[r_c, r_z[k]], writes=[r_pgl])
                    s.op("act", lambda e, mo=mo, Wd=Wd: e.activation(out=ysg[:, 0:Wd], in_=pgl[:, 0:Wd], func=AF.Sigmoid, bias=dg[:, 1, mo:mo + 1], scale=1.0),
                         reads=[r_pgl, r_c, r_ys], writes=[r_ys])
                    s.op("dve", lambda e, k=k, mo=mo, Wd=Wd: e.tensor_tensor(out=y5T[k][:, mo, 0:Wd], in0=z[k][:, mo, 0:Wd], in1=ysg[:, 0:Wd], op=ALU.mult),
                         reads=[r_z[k], r_ys], writes=[r_y5[k]])
                if sample:
                    s.op("pool", lambda e, k=k: e.memset(y5T[k][:, :, 64:128], 0.0), reads=[r_y5[k]], writes=[r_y5[k]])
                for h in range(2):
                    for kc in range(8):
                        lhs = fT[k][:, 4 + kc, :] if kc < 4 else y5T[k][:, kc - 4, :]
                        rd = r_fT[k] if kc < 4 else r_y5[k]
                        s.op("pe", lambda e, h=h, kc=kc, lhs=lhs: e.matmul(po[h][:], lhsT=lhs, rhs=ewout[:, kc, h * 512:(h + 1) * 512],
                                                                           start=(kc == 0), stop=(kc == 7)), reads=[rd, r_ew], writes=[r_po[h]])
                self.epilogue(m, [po[0][:], po[1][:]], r_po, xt[k][:], r_x[k], t1[k], r_t1[k], stt[k], r_st[k],
                              self.x_dst(False, nxt, i), None)
            s.dma("pool", lambda e: e.dma_start(out=do["s5rp"].rearrange("o (j q) -> q (o j)", q=128), in_=carry[:, 0, :], allow_slow_non_contiguous=True), reads=[r_carry])
            s.dma("pool", lambda e: e.dma_start(out=do["s5ip"].rearrange("o (j q) -> q (o j)", q=128), in_=carry[:, 1, :], allow_slow_non_contiguous=True), reads=[r_carry])
            for ri in range(2):
                for j in range(16):
                    s.op("pe", lambda e, ri=ri, j=j: e.transpose(out=pmisc[0:16, ((ri * 16 + j) % 4) * 128:((ri * 16 + j) % 4 + 1) * 128],
                                                                 in_=hout[:, ri, j, :], identity=self.ident_f[:]),
                         reads=[r_s, self.rconst], writes=[r_pm])
                    s.op("act", lambda e, ri=ri, j=j: e.copy(out=houtT[:, ri, j * 128:(j + 1) * 128],
                                                             in_=pmisc[0:16, ((ri * 16 + j) % 4) * 128:((ri * 16 + j) % 4 + 1) * 128]),
                         reads=[r_pm], writes=[r_s])
            s.dma("pool", lambda e: e.dma_start(out=do["s5rs"][:, :], in_=houtT[:, 0, :]), reads=[r_s])
            s.dma("pool", lambda e: e.dma_start(out=do["s5is"][:, :], in_=houtT[:, 1, :]), reads=[r_s])
            s.barrier()

    def dil_masks(self, Wm_out, nfree, pattern, base, es2, r_c):
        nc, s = self.nc, self.s
        T2 = lambda n, sh, dt: es2.enter_context(nc.sbuf_tensor(self.name(n), sh, dt))
        Di = T2("Di", [128, nfree], I32); Df = T2("Df", [128, nfree], F32); Ti = T2("Ti", [128, nfree], I32)
        ge0 = T2("ge0", [128, nfree], F32); acc = T2("accm", [128, nfree], F32); tf = T2("tfm", [128, nfree], F32); t2 = T2("t2m", [128, nfree], F32)
        rc = [r_c]
        V = lambda fn: s.op("dve", fn, reads=rc, writes=rc)
        s.op("pool", lambda e: e.iota(Di[:], pattern=pattern, base=base, channel_multiplier=-1), reads=rc, writes=rc)
        V(lambda e: e.tensor_copy(out=Df[:], in_=Di[:]))
        V(lambda e: e.tensor_scalar(out=ge0[:], in0=Df[:], scalar1=0.0, scalar2=None, op0=ALU.is_ge))
        V(lambda e: e.tensor_scalar(out=acc[:], in0=Df[:], scalar1=128.0, scalar2=None, op0=ALU.is_le))
        for msk, lim in ((3, 512.0), (15, 2048.0)):
            V(lambda e, msk=msk: e.tensor_scalar(out=Ti[:], in0=Di[:], scalar1=msk, scalar2=None, op0=ALU.bitwise_and))
            V(lambda e: e.tensor_copy(out=tf[:], in_=Ti[:]))
            V(lambda e: e.tensor_scalar(out=tf[:], in0=tf[:], scalar1=0.0, scalar2=None, op0=ALU.is_equal))
            V(lambda e, lim=lim: e.tensor_scalar(out=t2[:], in0=Df[:], scalar1=lim, scalar2=None, op0=ALU.is_le))
            V(lambda e: e.tensor_tensor(out=tf[:], in0=tf[:], in1=t2[:], op=ALU.mult))
            V(lambda e: e.tensor_tensor(out=acc[:], in0=acc[:], in1=tf[:], op=ALU.add))
        V(lambda e: e.tensor_tensor(out=Wm_out, in0=acc[:], in1=ge0[:], op=ALU.mult))

    def mixer1_A(self, cur):
        nc, s, di, do = self.nc, self.s, self.din, self.dout
        ND = 17
        with ExitStack() as es:
            T = lambda n, sh, dt: es.enter_context(nc.sbuf_tensor(self.name(n), sh, dt))
            P = lambda n, sh, dt: es.enter_context(nc.psum_tensor(self.name(n), sh, dt))
            r_c = Res()
            Wm = T("Wm", [128, ND, 128], BF16)
            with ExitStack() as es2:
                self.dil_masks(Wm[:].rearrange("p a b -> p (a b)"), ND * 128, [[128, ND], [1, 128]], 0, es2, r_c)
                s.barrier()
            m = self.alloc_mod(es)
            owin = T("owin", [128, NKC, 3328], BF16)
            xt = [T("xt", [128, D], F32) for _ in range(2)]
            tmpf = T("tmpf", [128, D], F32)
            hb = [T("hb", [128, D], BF16) for _ in range(2)]
            hT = [T("hT", [128, NKC, 128], BF16) for _ in range(2)]
            kv_f = [T("kv_f", [128, 1024], F32) for _ in range(2)]
            pd_f1 = T("pd_f", [128, 1792], F32)
            pd_f = [pd_f1, pd_f1]
            q_b = [T("q_b", [128, 512], BF16) for _ in range(2)]
            k_b = [T("k_b", [128, 512], BF16) for _ in range(2)]
            qT = [T("qT", [128, 4, 128], BF16) for _ in range(2)]
            KT = T("KT1", [128, 4, SEQ], BF16)
            VP = T("VP1", [128, 32, 8, 66], BF16)
            E = [T("E", [128, 4, 128], BF16) for _ in range(3)]
            rr = [T("rr", [128, 2], F32) for _ in range(2)]
            mixf = [T("mixf", [128, 512], BF16) for _ in range(2)]
            pp = [P("pp", [128, 512], F32) for _ in range(4)]
            tpb = P("tpb", [128, NKC, 128], BF16)
            st = [P("st", [128, 512], F32) for _ in range(2)]
            oacc = P("oacc", [128, 512], F32)
            r_ow = Res(); r_x = [Res(), Res()]; r_tmpf = Res(); r_hb = [Res(), Res()]; r_hT = [Res(), Res()]
            r_kv = [Res(), Res()]; r_pd1 = Res(); r_pd = [r_pd1, r_pd1]; r_pp = [Res() for _ in range(4)]; r_tp = Res()
            r_q = [Res(), Res()]; r_k = [Res(), Res()]; r_qT = [Res(), Res()]
            r_KT = [Res() for _ in range(32)]; r_VP = [Res() for _ in range(32)]; r_E = [Res() for _ in range(3)]
            r_rr = [Res(), Res()]; r_mixf = [Res(), Res()]; r_st = [Res(), Res()]; r_oacc = Res()
            s.dma("sp", lambda e: e.dma_start(out=owin[:], in_=self.owin_b[:, :].rearrange("(kc p) n -> p kc n", p=128)), writes=[r_ow])
            s.op("pool", lambda e: e.memset(VP[:].rearrange("p a h e -> p (a h e)"), 1.0), writes=r_VP)
            self.load_mod(m, 1, 1, None)
            self.bcast_mod(m, False, pp[0], r_pp[0])
            widths = [512] * 6 + [256]
            ist = 0; ie = 0; ihd = 0
            for i in range(NTILE):
                sample = (i == 32)
                k = i % 2
                if sample:
                    self.bcast_mod(m, True, pp[0], r_pp[0])
                ap, nrows, res = self.x_src(False, cur, i)
                s.dma("sp", lambda e, k=k, ap=ap: e.dma_start(out=xt[k][:], in_=ap), reads=[res], writes=[r_x[k]])
                s.op("pool", lambda e, k=k: e.tensor_tensor(out=tmpf[:], in0=xt[k][:], in1=m["SC"][:], op=ALU.mult),
                     reads=[r_x[k], m["r_mod"]], writes=[r_tmpf])
                s.op("dve", lambda e, k=k: e.tensor_tensor(out=hb[k][:], in0=tmpf[:], in1=m["SH"][:], op=ALU.add),
                     reads=[r_tmpf, m["r_mod"]], writes=[r_hb[k]])
                for kc in range(NKC):
                    s.op("pe", lambda e, k=k, kc=kc: e.transpose(out=tpb[:, kc, :], in_=hb[k][:, kc * 128:(kc + 1) * 128],
                                                                 identity=self.ident_b[:]),
                         reads=[r_hb[k], self.rconst], writes=[r_tp])
                s.op("act", lambda e, k=k: e.copy(out=hT[k][:], in_=tpb[:]), reads=[r_tp], writes=[r_hT[k]])

                def proj(c, bank):
                    wdt = widths[c]
                    for kc in range(NKC):
                        s.op("pe", lambda e, kc=kc, k=k, wdt=wdt, c=c, bank=bank: e.matmul(pp[bank][:, 0:wdt], lhsT=hT[k][:, kc, :],
                                                             rhs=owin[:, kc, c * 512:c * 512 + wdt],
                                                             start=(kc == 0), stop=(kc == NKC - 1)),
                             reads=[r_hT[k], r_ow], writes=[r_pp[bank]])
                for c in range(3):
                    proj(c, c)
                s.op("act", lambda e, k=k: e.copy(out=q_b[k][:], in_=pp[0][:]), reads=[r_pp[0]], writes=[r_q[k]])
                s.op("dve", lambda e, k=k: e.tensor_copy(out=kv_f[k][:, 0:512], in_=pp[1][:]), reads=[r_pp[1]], writes=[r_kv[k]])
                s.op("dve", lambda e, k=k: e.tensor_copy(out=kv_f[k][:, 512:1024], in_=pp[2][:]), reads=[r_pp[2]], writes=[r_kv[k]])
                for c in range(3, 7):
                    proj(c, c - 3)
                    wdt = widths[c]
                    s.op("dve", lambda e, c=c, wdt=wdt, k=k: e.tensor_copy(out=pd_f[k][:, (c - 3) * 512:(c - 3) * 512 + wdt], in_=pp[c - 3][:, 0:wdt]),
                         reads=[r_pp[c - 3]], writes=[r_pd[k]])
                rows_all = slice(i * 128, (i + 1) * 128)
                s.dma("pool", lambda e, k=k, rows_all=rows_all: e.dma_start(out=self.pdbuf[rows_all, :], in_=pd_f[k][:]),
                      reads=[r_pd[k]], writes=[self.pdres[i]])
                if not sample:
                    if i >= 16:
                        rows = slice((i - 16) * 128, (i - 15) * 128)
                        s.dma("pool", lambda e, k=k, rows=rows: e.dma_start(out=do["ckp"][rows, :], in_=kv_f[k][:, 0:512]), reads=[r_kv[k]])
                        s.dma("pool", lambda e, k=k, rows=rows: e.dma_start(out=do["cvp"][rows, :], in_=kv_f[k][:, 512:1024]), reads=[r_kv[k]])
                    if i == 31:
                        s.dma("pool", lambda e, k=k: e.dma_start(out=do["dsp"][0:1, :], in_=pd_f[k][127:128, :]), reads=[r_pd[k]])
                else:
                    s.dma("pool", lambda e, k=k: e.dma_start(out=do["cks"][:, :], in_=kv_f[k][0:NS, 0:512]), reads=[r_kv[k]], writes=[self.r_cks])
                    s.dma("pool", lambda e, k=k: e.dma_start(out=do["cvs"][:, :], in_=kv_f[k][0:NS, 512:1024]), reads=[r_kv[k]], writes=[self.r_cks])
                    s.dma("pool", lambda e, k=k: e.dma_start(out=self.qsbuf[:, :], in_=q_b[k][0:NS, :]), reads=[r_q[k]], writes=[self.r_cks])
                    for b in range(NSB):
                        s.dma("pool", lambda e, b=b, k=k: e.dma_start(out=do["dss"][b:b + 1, :], in_=pd_f[k][4 * b + 3:4 * b + 4, :]), reads=[r_pd[k]])
                    continue
                if self.cfg.get("nodil"):
                    continue
                s.op("act", lambda e, k=k: e.copy(out=k_b[k][:], in_=kv_f[k][:, 0:512]), reads=[r_kv[k]], writes=[r_k[k]])
                s.op("act", lambda e, i=i, k=k: e.copy(out=VP[:, i, :, 0:64], in_=kv_f[k][:, 512:1024].rearrange("p (h e) -> p h e", h=8)),
                     reads=[r_kv[k]], writes=[r_VP[i]])
                for h in range(4):
                    s.op("pe", lambda e, k=k, h=h: e.transpose(out=tpb[:, h, :], in_=q_b[k][:, h * 128:(h + 1) * 128],
                                                               identity=self.ident_b[:]), reads=[r_q[k], self.rconst], writes=[r_tp])
                    s.op("pe", lambda e, k=k, h=h: e.transpose(out=tpb[:, 4 + h, :], in_=k_b[k][:, h * 128:(h + 1) * 128],
                                                               identity=self.ident_b[:]), reads=[r_k[k], self.rconst], writes=[r_tp])
                s.op("act", lambda e, k=k: e.copy(out=qT[k][:], in_=tpb[:, 0:4, :]), reads=[r_tp], writes=[r_qT[k]])
                s.op("act", lambda e, i=i: e.copy(out=KT[:, :, i * 128:(i + 1) * 128], in_=tpb[:, 4:8, :]), reads=[r_tp], writes=[r_KT[i]])
                nd = min(ND, i + 1)
                for h in range(8):
                    hp = h // 2
                    pr = slice((h % 2) * 64, (h % 2) * 64 + 64)
                    oc = (h % 2) * 66
                    for d0 in range(0, nd, 4):
                        ds_ = list(range(d0, min(d0 + 4, nd)))
                        sb = ist % 2; ist += 1
                        for jj, dlt in enumerate(ds_):
                            j = i - dlt
                            s.op("pe", lambda e, sb=sb, jj=jj, j=j, hp=hp, pr=pr, k=k:
                                 e.matmul(st[sb][:, jj * 128:(jj + 1) * 128], lhsT=KT[pr, hp, j * 128:(j + 1) * 128],
                                          rhs=qT[k][pr, hp, :], start=True, stop=True),
                                 reads=[r_KT[j], r_qT[k]], writes=[r_st[sb]])
                        eb = ie % 3; ie += 1
                        w = len(ds_)
                        s.op("act", lambda e, eb=eb, sb=sb, w=w: e.activation(out=E[eb][:, 0:w, :],
                                                                             in_=st[sb][:, 0:w * 128].rearrange("p (a b) -> p a b", a=w),
                                                                             func=AF.Exp, scale=0.125), reads=[r_st[sb]], writes=[r_E[eb]])
                        s.op("pool", lambda e, eb=eb, w=w, d0=d0: e.tensor_tensor(out=E[eb][:, 0:w, :], in0=E[eb][:, 0:w, :],
                                                                                   in1=Wm[:, d0:d0 + w, :], op=ALU.mult),
                             reads=[r_E[eb], r_c], writes=[r_E[eb]])
                        for jj, dlt in enumerate(ds_):
                            j = i - dlt
                            s.op("pe", lambda e, eb=eb, jj=jj, j=j, h=h, oc=oc, dlt=dlt, nd=nd:
                                 e.matmul(oacc[:, oc:oc + 65], lhsT=E[eb][:, jj, :], rhs=VP[:, j, h, 0:65],
                                          start=(dlt == 0), stop=(dlt == nd - 1)),
                                 reads=[r_E[eb], r_VP[j]], writes=[r_oacc])
                    kk = ihd % 2; ihd += 1
                    s.op("dve", lambda e, kk=kk, oc=oc: e.reciprocal(out=rr[kk][:, 0:1], in_=oacc[:, oc + 64:oc + 65]), reads=[r_oacc], writes=[r_rr[kk]])
                    s.op("dve", lambda e, kk=kk, oc=oc, h=h, k=k: e.tensor_scalar(out=mixf[k][:, h * 64:(h + 1) * 64], in0=oacc[:, oc:oc + 64],
                                                                                  scalar1=rr[kk][:, 0:1], scalar2=None, op0=ALU.mult),
                         reads=[r_oacc, r_rr[kk]], writes=[r_mixf[k]])
                s.dma("pool", lambda e, k=k, rows_all=rows_all: e.dma_start(out=self.mixbuf[rows_all, 0:512], in_=mixf[k][:]),
                      reads=[r_mixf[k]], writes=[self.mixres[i]])
            s.barrier()

    def mixer1_S(self):
        nc, s, di, do = self.nc, self.s, self.din, self.dout
        NPG = 16
        with ExitStack() as es:
            T = lambda n, sh, dt: es.enter_context(nc.sbuf_tensor(self.name(n), sh, dt))
            P = lambda n, sh, dt: es.enter_context(nc.psum_tensor(self.name(n), sh, dt))
            r_c = Res()
            Ws = T("Ws", [128, NPG + 1, 4], BF16)
            with ExitStack() as es2:
                self.dil_masks(Ws[:].rearrange("p a b -> p (a b)"), (NPG + 1) * 4, [[-128, NPG + 1], [1, 4]], 2048, es2, r_c)
                s.barrier()
            selrows = T("selrows", [64, 64, 128], BF16)
            q_s = T("q_s", [64, 512], BF16)
            Qbc = [T("Qbc", [128, 4, 512], BF16) for _ in range(2)]
            Kg = [T("Kg", [128, 512], F32) for _ in range(4)]
            Vg = [T("Vg", [128, 512], F32) for _ in range(3)]
            VPb = [T("VPb", [128, NPG, 8, 66], BF16) for _ in range(2)]
            Kn = [T("Kn", [4, 512], F32) for _ in range(2)]
            Vn = [T("Vn", [4, 512], F32) for _ in range(2)]
            VPn = [T("VPn", [4, 8, 66], BF16) for _ in range(2)]
            prod = [T("prod", [128, 512], F32) for _ in range(2)]
            Sc = [T("Sc", [128, NPG + 1, 8, 4], F32) for _ in range(2)]
            Eb = [T("Eb", [128, NPG + 1, 8, 4], BF16) for _ in range(2)]
            rr = [T("rr", [128, 2], F32) for _ in range(2)]
            mixs = [T("mixs", [4, 512], BF16) for _ in range(2)]
            qps = [P("qps", [128, 512], F32) for _ in range(2)]
            ops_ = [P("ops", [128, 512], F32) for _ in range(2)]
            r_Qbc = [Res(), Res()]; r_Kg = [Res() for _ in range(4)]; r_Vg = [Res() for _ in range(3)]
            r_VPb = [Res(), Res()]; r_Kn = [Res(), Res()]; r_Vn = [Res(), Res()]; r_VPn = [Res(), Res()]
            r_prod = [Res(), Res()]; r_Sc = [Res(), Res()]; r_Eb = [Res(), Res()]; r_rr = [Res(), Res()]
            r_mixs = [Res(), Res()]; r_qps = [Res(), Res()]; r_ops = [Res(), Res()]
            s.op("pool", lambda e: e.memset(selrows[:], 1.0), writes=[r_c])
            s.op("pool", lambda e: e.affine_select(out=selrows[:], in_=selrows[:], pattern=[[-1, 64], [0, 128]],
                                                    compare_op=ALU.is_equal, fill=0.0, base=0, channel_multiplier=1),
                 reads=[r_c], writes=[r_c])
            s.dma("sp", lambda e: e.dma_start(out=q_s[:], in_=self.qsbuf[:, :]), reads=[self.r_cks], writes=[r_c])
            for v in VPb:
                s.op("pool", lambda e, v=v: e.memset(v[:].rearrange("p a h e -> p (a h e)"), 1.0), writes=r_VPb)
            for v in VPn:
                s.op("pool", lambda e, v=v: e.memset(v[:].rearrange("p h e -> p (h e)"), 1.0), writes=r_VPn)
            for sc_ in Sc:
                s.op("pool", lambda e, sc_=sc_: e.memset(sc_[:].rearrange("p a h q -> p (a h q)"), 0.0), writes=r_Sc)
            ikg = 0; ivg = 0; ipr = 0
            cck = di["cache_c_k"]; ccv = di["cache_c_v"]
            for b in range(NSB):
                kb = b % 2
                for qi in range(4):
                    kq = (b * 4 + qi) % 2
                    s.op("pe", lambda e, kq=kq, b=b, qi=qi: e.matmul(qps[kq][:], lhsT=selrows[:, 4 * b + qi, :], rhs=q_s[:, :],
                                                                     start=True, stop=True), reads=[r_c], writes=[r_qps[kq]])
                    s.op("act", lambda e, kq=kq, kb=kb, qi=qi: e.copy(out=Qbc[kb][:, qi, :], in_=qps[kq][:]),
                         reads=[r_qps[kq]], writes=[r_Qbc[kb]])
                s.dma("sp", lambda e, kb=kb, b=b: e.dma_start(out=Kn[kb][:], in_=do["cks"][4 * b:4 * b + 4, :]),
                      reads=[self.r_cks], writes=[r_Kn[kb]])
                s.dma("sp", lambda e, kb=kb, b=b: e.dma_start(out=Vn[kb][:], in_=do["cvs"][4 * b:4 * b + 4, :]),
                      reads=[self.r_cks], writes=[r_Vn[kb]])
                s.op("act", lambda e, kb=kb: e.copy(out=VPn[kb][:, :, 0:64], in_=Vn[kb][:].rearrange("p (h e) -> p h e", h=8)),
                     reads=[r_Vn[kb]], writes=[r_VPn[kb]])
                for pg in range(NPG):
                    kv = ivg % 3; ivg += 1
                    r0 = b * 2048 + pg * 128
                    s.dma("sp", lambda e, kv=kv, r0=r0: e.dma_start(out=Vg[kv][:], in_=ccv[r0:r0 + 128, :]), writes=[r_Vg[kv]])
                    s.op("act", lambda e, kv=kv, kb=kb, pg=pg: e.copy(out=VPb[kb][:, pg, :, 0:64],
                                                                      in_=Vg[kv][:].rearrange("p (h e) -> p h e", h=8)),
                         reads=[r_Vg[kv]], writes=[r_VPb[kb]])
                for pg in range(NPG + 1):
                    if pg < NPG:
                        kk = ikg % 4; ikg += 1
                        r0 = b * 2048 + pg * 128
                        s.dma("sp", lambda e, kk=kk, r0=r0: e.dma_start(out=Kg[kk][:], in_=cck[r0:r0 + 128, :]), writes=[r_Kg[kk]])
                        ksrc, r_ks, npart = Kg[kk], r_Kg[kk], 128
                    else:
                        ksrc, r_ks, npart = Kn[kb], r_Kn[kb], 4
                    for qi in range(4):
                        kp = ipr % 2; ipr += 1
                        s.op("dve", lambda e, kp=kp, ksrc=ksrc, npart=npart, kb=kb, qi=qi:
                             e.tensor_tensor(out=prod[kp][0:npart, :], in0=ksrc[0:npart, :], in1=Qbc[kb][0:npart, qi, :], op=ALU.mult),
                             reads=[r_ks, r_Qbc[kb]], writes=[r_prod[kp]])
                        s.op("dve", lambda e, kp=kp, npart=npart, kb=kb, pg=pg, qi=qi:
                             e.tensor_reduce(out=Sc[kb][0:npart, pg, :, qi],
                                             in_=prod[kp][0:npart, :].rearrange("p (g d) -> p g d", d=64), axis=AX.X, op=ALU.add),
                             reads=[r_prod[kp]], writes=[r_Sc[kb]])
                s.op("act", lambda e, kb=kb: e.activation(out=Eb[kb][:].rearrange("p a h q -> p (a h q)"),
                                                          in_=Sc[kb][:].rearrange("p a h q -> p (a h q)"),
                                                          func=AF.Exp, scale=0.125), reads=[r_Sc[kb]], writes=[r_Eb[kb]])
                s.op("pool", lambda e, kb=kb: e.tensor_tensor(out=Eb[kb][:], in0=Eb[kb][:],
                                                               in1=Ws[:].unsqueeze(2).to_broadcast([128, NPG + 1, 8, 4]), op=ALU.mult),
                     reads=[r_Eb[kb], r_c], writes=[r_Eb[kb]])
                for h in range(8):
                    ob = h // 4
                    oc = (h % 4) * 66
                    for pg in range(NPG + 1):
                        if pg < NPG:
                            s.op("pe", lambda e, h=h, oc=oc, ob=ob, pg=pg, kb=kb:
                                 e.matmul(ops_[ob][0:4, oc:oc + 65], lhsT=Eb[kb][:, pg, h, :], rhs=VPb[kb][:, pg, h, 0:65],
                                          start=(pg == 0), stop=False), reads=[r_Eb[kb], r_VPb[kb]], writes=[r_ops[ob]])
                        else:
                            s.op("pe", lambda e, h=h, oc=oc, ob=ob, pg=pg, kb=kb:
                                 e.matmul(ops_[ob][0:4, oc:oc + 65], lhsT=Eb[kb][0:4, pg, h, :], rhs=VPn[kb][0:4, h, 0:65],
                                          start=False, stop=True), reads=[r_Eb[kb], r_VPn[kb]], writes=[r_ops[ob]])
                    k2 = (b * 8 + h) % 2
                    s.op("dve", lambda e, k2=k2, ob=ob, oc=oc: e.reciprocal(out=rr[k2][0:4, 0:1], in_=ops_[ob][0:4, oc + 64:oc + 65]),
                         reads=[r_ops[ob]], writes=[r_rr[k2]])
                    s.op("dve", lambda e, k2=k2, ob=ob, oc=oc, h=h, kb=kb: e.tensor_scalar(out=mixs[kb][0:4, h * 64:(h + 1) * 64], in0=ops_[ob][0:4, oc:oc + 64],
                                                                                           scalar1=rr[k2][0:4, 0:1], scalar2=None, op0=ALU.mult),
                         reads=[r_ops[ob], r_rr[k2]], writes=[r_mixs[kb]])
                s.dma("sp", lambda e, kb=kb, b=b: e.dma_start(out=self.mixbuf[SEQ + 4 * b:SEQ + 4 * b + 4, 0:512], in_=mixs[kb][:]),
                      reads=[r_mixs[kb]], writes=[self.mixres[32]])
            s.barrier()

    def mixer1_P(self):
        nc, s, di, do = self.nc, self.s, self.din, self.dout
        with ExitStack() as es:
            T = lambda n, sh, dt: es.enter_context(nc.sbuf_tensor(self.name(n), sh, dt))
            P = lambda n, sh, dt: es.enter_context(nc.psum_tensor(self.name(n), sh, dt))
            MU = T("MU", [128, 1792], F32)
            CB = T("CB", [128, 5, 512], F32)
            W12 = T("W12", [128, 512], F32); G2 = T("G2", [128, 512], F32)
            pd = [T("pdt", [128, 1792], F32) for _ in range(2)]
            pv = [T("pvt", [128, 1792], F32) for _ in range(2)]
            xm = [T("xm", [128, 1792], F32) for _ in range(2)]
            X3 = T("X3", [128, 256], F32); T12 = T("T12", [128, 2, 128], F32)
            o = [T("rwo", [128, 8, 512], F32) for _ in range(2)]
            tA = T("tA", [128, 512], F32); tB = T("tB", [128, 512], F32); av = T("av", [128, 512], F32)
            sm = T("sm", [128, 32], F32)
            ptr = [P("ptr", [128, 128], F32) for _ in range(2)]
            pL = [P("pL", [128, 512], F32) for _ in range(3)]
            r_c = Res(); r_pd = [Res(), Res()]; r_pv = [Res(), Res()]; r_xm = [Res(), Res()]; r_X3 = Res(); r_T12 = Res()
            r_o = [Res(), Res()]; r_t = Res(); r_ptr = [Res(), Res()]; r_pL = [Res() for _ in range(3)]
            bc = lambda ap, n: ap.to_broadcast([128, n])
            s.dma("sp", lambda e: e.dma_start(out=MU[:], in_=bc(di["rwkv_mu"][0:1, :], 1792)), writes=[r_c])
            for j, nm in enumerate(("rwkv_w0", "rwkv_a0", "rwkv_k_k", "rwkv_k_a", "rwkv_r_k")):
                s.dma("sp", lambda e, j=j, nm=nm: e.dma_start(out=CB[:, j, :], in_=bc(di[nm][0:1, :], 512)), writes=[r_c])
            s.dma("sp", lambda e: e.dma_start(out=W12[0:64, :], in_=di["rwkv_w2"][:, :]), writes=[r_c])
            s.dma("sp", lambda e: e.dma_start(out=W12[64:128, :], in_=di["rwkv_a2"][:, :]), writes=[r_c])
            s.dma("sp", lambda e: e.dma_start(out=G2[:], in_=di["rwkv_g2"][:, :]), writes=[r_c])
            for b_ in pv:
                s.op("pool", lambda e, b_=b_: e.memset(b_[:], 0.0), writes=r_pv)
            for i in range(NTILE):
                k = i % 2
                sample = (i == 32)
                rows = slice(i * 128, (i + 1) * 128)
                s.dma("sp", lambda e, k=k, rows=rows: e.dma_start(out=pd[k][:], in_=self.pdbuf[rows, :]), reads=[self.pdres[i]], writes=[r_pd[k]])
                if i == 0:
                    s.dma("sp", lambda e, k=k: e.dma_start(out=pv[k][1:128, :], in_=self.pdbuf[0:127, :]), reads=[self.pdres[0]], writes=[r_pv[k]])
                elif not sample:
                    s.dma("sp", lambda e, k=k, i=i: e.dma_start(out=pv[k][:, :], in_=self.pdbuf[i * 128 - 1:i * 128 + 127, :]),
                          reads=[self.pdres[i], self.pdres[i - 1]], writes=[r_pv[k]])
                else:
                    s.dma("sp", lambda e, k=k: e.dma_start(out=pv[k][1:64, :], in_=self.pdbuf[SEQ:SEQ + 63, :]), reads=[self.pdres[32]], writes=[r_pv[k]])
                    for b in range(NSB):
                        s.dma("sp", lambda e, k=k, b=b: e.dma_start(out=pv[k][4 * b:4 * b + 1, :], in_=di["state_d_shift"][b:b + 1, :]), writes=[r_pv[k]])
                X, PD, PV_ = xm[k], pd[k], pv[k]
                s.op("pool", lambda e, X=X, PD=PD, PV_=PV_: e.tensor_tensor(out=X[:], in0=PV_[:], in1=PD[:], op=ALU.subtract), reads=[r_pd[k], r_pv[k]], writes=[r_xm[k]])
                s.op("dve", lambda e, X=X: e.tensor_tensor(out=X[:], in0=X[:], in1=MU[:], op=ALU.mult), reads=[r_xm[k], r_c], writes=[r_xm[k]])
                s.op("pool", lambda e, X=X, PD=PD: e.tensor_tensor(out=X[:], in0=X[:], in1=PD[:], op=ALU.add), reads=[r_xm[k], r_pd[k]], writes=[r_xm[k]])
                s.op("act", lambda e, X=X: e.activation(out=X3[:, 0:64], in_=X[:, 1536:1600], func=AF.Tanh), reads=[r_xm[k]], writes=[r_X3])
                s.op("act", lambda e, X=X: e.copy(out=X3[:, 64:128], in_=X[:, 1600:1664]), reads=[r_xm[k]], writes=[r_X3])
                s.op("act", lambda e, X=X: e.activation(out=X3[:, 128:256], in_=X[:, 1664:1792], func=AF.Sigmoid), reads=[r_xm[k]], writes=[r_X3])
                for j in range(2):
                    s.op("pe", lambda e, j=j: e.transpose(out=ptr[j][:], in_=X3[:, j * 128:(j + 1) * 128], identity=self.ident_f[:]),
                         reads=[r_X3, self.rconst], writes=[r_ptr[j]])
                    s.op("dve", lambda e, j=j: e.tensor_copy(out=T12[:, j, :], in_=ptr[j][:]), reads=[r_ptr[j]], writes=[r_T12])
                s.op("pe", lambda e: e.matmul(pL[0][:], lhsT=T12[0:64, 0, :], rhs=W12[0:64, :], start=True, stop=True), reads=[r_T12, r_c], writes=[r_pL[0]])
                s.op("pe", lambda e: e.matmul(pL[1][:], lhsT=T12[64:128, 0, :], rhs=W12[64:128, :], start=True, stop=True), reads=[r_T12, r_c], writes=[r_pL[1]])
                s.op("pe", lambda e: e.matmul(pL[2][:], lhsT=T12[:, 1, :], rhs=G2[:, :], start=True, stop=True), reads=[r_T12, r_c], writes=[r_pL[2]])
                O = o[k]
                ro = [r_o[k]]
                rt = [r_t]
                s.op("dve", lambda e: e.tensor_tensor(out=tA[:], in0=pL[0][:], in1=CB[:, 0, :], op=ALU.add), reads=[r_pL[0], r_c], writes=rt)
                s.op("act", lambda e: e.activation(out=tA[:], in_=tA[:], func=AF.Sigmoid), reads=rt, writes=rt)
                s.op("act", lambda e, O=O: e.activation(out=O[:, 1, :], in_=tA[:], func=AF.Exp, scale=-0.6065306597126334), reads=rt, writes=ro)
                s.op("dve", lambda e: e.tensor_tensor(out=av[:], in0=pL[1][:], in1=CB[:, 1, :], op=ALU.add), reads=[r_pL[1], r_c], writes=rt)
                s.op("act", lambda e: e.activation(out=av[:], in_=av[:], func=AF.Sigmoid), reads=rt, writes=rt)
                s.op("act", lambda e, O=O: e.copy(out=O[:, 6, :], in_=pL[2][:]), reads=[r_pL[2]], writes=ro)
                s.op("act", lambda e, O=O, X=X: e.copy(out=O[:, 0, :], in_=X[:, 0:512]), reads=[r_xm[k]], writes=ro)
                s.op("act", lambda e, O=O, X=X: e.copy(out=O[:, 3, :], in_=X[:, 1024:1536]), reads=[r_xm[k]], writes=ro)
                s.op("pool", lambda e, X=X: e.tensor_tensor(out=tA[:], in0=X[:, 512:1024], in1=CB[:, 2, :], op=ALU.mult), reads=[r_xm[k], r_c] + rt, writes=rt)
                s.op("pool", lambda e: e.tensor_tensor(out=tB[:], in0=tA[:], in1=tA[:], op=ALU.mult), reads=rt, writes=rt)
                s.op("dve", lambda e: e.tensor_reduce(out=sm[:, 0:8], in_=tB[:].rearrange("p (h k) -> p h k", h=8), axis=AX.X, op=ALU.add), reads=rt, writes=rt)
                s.op("act", lambda e: e.activation(out=sm[:, 8:16], in_=sm[:, 0:8], func=AF.Sqrt), reads=rt, writes=rt)
                s.op("dve", lambda e: e.tensor_scalar(out=sm[:, 8:16], in0=sm[:, 8:16], scalar1=1e-12, scalar2=None, op0=ALU.max), reads=rt, writes=rt)
                s.op("dve", lambda e: e.reciprocal(out=sm[:, 16:24], in_=sm[:, 8:16]), reads=rt, writes=rt)
                s.op("dve", lambda e, O=O: e.tensor_tensor(out=O[:, 4, :].rearrange("p (h k) -> p h k", h=8), in0=tA[:].rearrange("p (h k) -> p h k", h=8),
                                                           in1=sm[:, 16:24].unsqueeze(2).to_broadcast([128, 8, 64]), op=ALU.mult), reads=rt, writes=ro)
                s.op("dve", lambda e: e.scalar_tensor_tensor(out=tB[:], in0=av[:], scalar=-1.0, in1=CB[:, 3, :], op0=ALU.add, op1=ALU.mult), reads=rt + [r_c], writes=rt)
                s.op("dve", lambda e, O=O, X=X: e.scalar_tensor_tensor(out=O[:, 2, :], in0=tB[:], scalar=1.0, in1=X[:, 512:1024], op0=ALU.add, op1=ALU.mult),
                     reads=rt + [r_xm[k]], writes=ro)
                s.op("dve", lambda e, O=O: e.scalar_tensor_tensor(out=O[:, 5, :], in0=O[:, 4, :], scalar=-1.0, in1=av[:], op0=ALU.mult, op1=ALU.mult), reads=rt + ro, writes=ro)
                s.op("pool", lambda e, O=O: e.tensor_tensor(out=tA[:], in0=O[:, 0, :], in1=O[:, 2, :], op=ALU.mult), reads=ro + rt, writes=rt)
                s.op("pool", lambda e: e.tensor_tensor(out=tA[:], in0=tA[:], in1=CB[:, 4, :], op=ALU.mult), reads=rt + [r_c], writes=rt)
                s.op("dve", lambda e: e.tensor_reduce(out=sm[:, 24:32], in_=tA[:].rearrange("p (h k) -> p h k", h=8), axis=AX.X, op=ALU.add), reads=rt, writes=rt)
                s.op("dve", lambda e, O=O: e.tensor_tensor(out=O[:, 7, :].rearrange("p (h k) -> p h k", h=8), in0=O[:, 3, :].rearrange("p (h k) -> p h k", h=8),
                                                           in1=sm[:, 24:32].unsqueeze(2).to_broadcast([128, 8, 64]), op=ALU.mult), reads=rt + ro, writes=ro)
                s.dma("pool", lambda e, O=O, rows=rows: e.dma_start(out=self.rw[:, rows, :].rearrange("a t c -> t a c"), in_=O[:]),
                      reads=ro, writes=[self.rwres[i]])
            s.barrier()

    def mixer1_R(self):
        nc, s, di, do = self.nc, self.s, self.din, self.dout
        with ExitStack() as es:
            T = lambda n, sh, dt: es.enter_context(nc.sbuf_tensor(self.name(n), sh, dt))
            P = lambda n, sh, dt: es.enter_context(nc.psum_tensor(self.name(n), sh, dt))
            bm = T("bm8", [8, 8, 64], F32)
            A = T("Ast", [64, 8, 64], F32)
            tok = [T("tokm", [128, 3, 512], F32) for _ in range(2)]
            L = [T("Lkr", [64, 129, 40], F32) for _ in range(2)]
            Wt = [T("Wt", [64, 128, 8], F32) for _ in range(2)]
            hm1 = T("hmK", [40, 2, 128, 64], F32)
            hm = [hm1, hm1]
            bm40 = T("bm40", [40, 8, 64], F32)
            UV = [T("UV", [40, 512], F32) for _ in range(2)]
            Ym = [T("Ym", [40, 8, 64], F32) for _ in range(2)]
            yb1 = T("yb", [40, 128, 64], F32)
            yb = [yb1, yb1]
            Sio = T("Sio", [64, 8, 64], F32)
            ptr = [P("ptr", [64, 4, 128], F32) for _ in range(2)]
            pu = [P("pu", [40, 512], F32) for _ in range(2)]
            pdA = [P("pdA", [64, 512], F32) for _ in range(2)]
            r_c = Res(); r_Ah = [Res(), Res()]; r_tok = [Res(), Res()]; r_cols = [Res(), Res()]; r_hm1 = Res(); r_hm = [r_hm1, r_hm1]
            r_Um = [Res(), Res()]; r_Vm = [Res(), Res()]; r_Ym = [Res(), Res()]; r_yb1 = Res(); r_yb = [r_yb1, r_yb1]; r_S = Res()
            r_ptr = [Res(), Res()]; r_pu = [Res(), Res()]; r_pdA = [Res(), Res()]; r_py = [Res(), Res()]
            s.op("pool", lambda e: e.memset(bm[:], 1.0), writes=[r_c])
            s.op("pool", lambda e: e.affine_select(out=bm[:], in_=bm[:], pattern=[[-1, 8], [0, 64]], compare_op=ALU.is_equal, fill=0.0,
                                                    base=0, channel_multiplier=1), reads=[r_c], writes=[r_c])
            s.op("pool", lambda e: e.memset(A[:], 0.0), writes=r_Ah)
            s.op("pool", lambda e: e.memset(hm1[:], 0.0), writes=[r_hm1])
            s.op("pool", lambda e: e.memset(bm40[:], 1.0), writes=[r_c])
            s.op("pool", lambda e: e.affine_select(out=bm40[:], in_=bm40[:], pattern=[[-1, 8], [0, 64]], compare_op=ALU.is_equal, fill=0.0,
                                                    base=-32, channel_multiplier=1), reads=[r_c], writes=[r_c])
            for c_ in range(2):
                s.op("pool", lambda e, c_=c_: e.memset(UV[c_][:], 0.0), writes=[r_Um[c_], r_Vm[c_]])
                s.op("pool", lambda e, c_=c_: e.memset(L[c_][:], 0.0), writes=[r_cols[c_]])
            bm40f = bm40[:].rearrange("p h v -> p (h v)")
            Af = A[:].rearrange("p h v -> p (h v)")
            bmf = bm[:].rearrange("p h v -> p (h v)")
            cnt = [0]

            def load_tile(i, k):
                rows = slice(i * 128, (i + 1) * 128)
                for j, src in enumerate((4, 0, 1)):
                    s.dma("sp", lambda e, j=j, src=src: e.dma_start(out=tok[k][:, j, :], in_=self.rw[src, rows, :]), reads=[self.rwres[i]], writes=[r_tok[k]])
                for j, ro, src in ((0, 0, 5), (0, 32, 2), (1, 32, 3)):
                    s.dma("sp", lambda e, j=j, ro=ro, src=src: e.dma_start(out=hm[k][ro:ro + 8, j, :, :], in_=self.rw[src, rows, :].rearrange("t (h c) -> h t c", h=8)),
                          reads=[self.rwres[i]], writes=[r_hm[k]])
                it = 0
                for j in range(3):
                    for h0 in (0, 4):
                        kp = it % 2; it += 1
                        for hh in range(4):
                            h = h0 + hh
                            s.op("pe", lambda e, j=j, h=h, hh=hh, kp=kp: e.transpose(out=ptr[kp][:, hh, :], in_=tok[k][:, j, h * 64:(h + 1) * 64], identity=self.ident_f[:]),
                                 reads=[r_tok[k], self.rconst], writes=[r_ptr[kp]])
                        if j == 0:
                            dst = L[k][:, 0:128, h0:h0 + 4]
                        elif j == 1:
                            dst = L[k][:, 1:129, 32 + h0:32 + h0 + 4]
                        else:
                            dst = Wt[k][:, :, h0:h0 + 4]
                        s.op("act", lambda e, dst=dst, kp=kp: e.copy(out=dst, in_=ptr[kp][:].rearrange("p h t -> p t h")),
                             reads=[r_ptr[kp]], writes=[r_cols[k]])

            def pu_half(k, slot, hf):
                cs = slice(hf * 256, (hf + 1) * 256)
                s.op("pe", lambda e: e.matmul(pu[hf][:, 0:256], lhsT=L[k][:, slot, :], rhs=Af[:, cs], start=True, stop=True),
                     reads=[r_cols[k], r_Ah[hf]], writes=[r_pu[hf]])

            def ym_half(c, hf):
                cs = slice(hf * 256, (hf + 1) * 256)
                s.op("dve", lambda e: e.tensor_tensor(out=Ym[c][32:40, hf * 4:(hf + 1) * 4, :].rearrange("p h v -> p (h v)"), in0=pu[hf][32:40, 0:256],
                                                      in1=bm40f[32:40, cs], op=ALU.mult), reads=[r_pu[hf], r_c], writes=[r_Ym[c]])

            def y_reduce(k, c, t):
                s.op("dve", lambda e: e.tensor_reduce(out=yb[k][32:40, t, :], in_=Ym[c][32:40, :, :].rearrange("p h v -> p v h"), axis=AX.X, op=ALU.add),
                     reads=[r_Ym[c]], writes=[r_yb[k]])

            def step(k, t, emit_prev):
                c = cnt[0] % 2
                cnt[0] += 1
                for hf in range(2):
                    pu_half(k, t, hf)
                for hf in range(2):
                    cs = slice(hf * 256, (hf + 1) * 256)
                    s.op("dve", lambda e, hf=hf, cs=cs: e.tensor_tensor(out=UV[hf][0:8, 0:256], in0=pu[hf][0:8, 0:256], in1=bmf[:, cs], op=ALU.mult),
                         reads=[r_pu[hf], r_c], writes=[r_Um[hf]])
                    s.op("pool", lambda e, hf=hf: e.tensor_tensor(out=UV[hf][32:40, 0:256].rearrange("p (h v) -> p h v", h=4),
                                                                  in0=hm[k][32:40, 1, t, :].unsqueeze(1).to_broadcast([8, 4, 64]),
                                                                  in1=bm40[32:40, hf * 4:(hf + 1) * 4, :], op=ALU.mult),
                         reads=[r_hm[k], r_c], writes=[r_Vm[hf]])
                    s.op("pe", lambda e, hf=hf: e.matmul(pdA[hf][:, 0:256], lhsT=hm[k][0:40, 0, t, :], rhs=UV[hf][0:40, 0:256], start=True, stop=True),
                         reads=[r_hm[k], r_Um[hf], r_Vm[hf]], writes=[r_pdA[hf]])
                for hf in range(2):
                    s.op("dve", lambda e, hf=hf: e.tensor_tensor(out=A[:, hf * 4:(hf + 1) * 4, :], in0=A[:, hf * 4:(hf + 1) * 4, :],
                                                                 in1=Wt[k][:, t, hf * 4:(hf + 1) * 4].unsqueeze(2).to_broadcast([64, 4, 64]), op=ALU.mult),
                         reads=[r_Ah[hf], r_cols[k]], writes=[r_Ah[hf]])
                if emit_prev:
                    for hf in range(2):
                        ym_half(c, hf)
                for hf in range(2):
                    cs = slice(hf * 256, (hf + 1) * 256)
                    s.op("dve", lambda e, hf=hf, cs=cs: e.tensor_tensor(out=Af[:, cs], in0=Af[:, cs], in1=pdA[hf][:, 0:256], op=ALU.add),
                         reads=[r_Ah[hf], r_pdA[hf]], writes=[r_Ah[hf]])
                if emit_prev:
                    y_reduce(k, c, t - 1)

            def flush(k, t):
                c = cnt[0] % 2
                cnt[0] += 1
                for hf in range(2):
                    pu_half(k, t + 1, hf)
                for hf in range(2):
                    ym_half(c, hf)
                y_reduce(k, c, t)

            def store_y(i, k, nt):
                rows = slice(i * 128, i * 128 + nt)
                s.dma("pool", lambda e: e.dma_start(out=self.ybuf[rows, :].rearrange("t (h v) -> h t v", h=8), in_=yb[k][32:40, 0:nt, :]),
                      reads=[r_yb[k]], writes=[self.yres[i]])

            def state_out(dst_ap):
                for h0 in (0, 4):
                    kp = (h0 // 4)
                    for hh in range(4):
                        s.op("pe", lambda e, hh=hh, h0=h0, kp=kp: e.transpose(out=ptr[kp][:, hh, 0:64], in_=A[:, h0 + hh, :], identity=self.ident_f[0:64, 0:64]),
                             reads=[r_Ah[0], r_Ah[1], self.rconst], writes=[r_ptr[kp]])
                    s.op("act", lambda e, h0=h0, kp=kp: e.copy(out=Sio[:, h0:h0 + 4, :], in_=ptr[kp][:, :, 0:64]), reads=[r_ptr[kp]], writes=[r_S])
                s.dma("pool", lambda e: e.dma_start(out=dst_ap.rearrange("(h v) c -> v h c", h=8), in_=Sio[:]), reads=[r_S])

            ntp = self.cfg.get("rw_tiles", 32)
            for i in range(ntp):
                k = i % 2
                load_tile(i, k)
                for t in range(128):
                    step(k, t, t > 0)
                flush(k, 127)
                store_y(i, k, 128)
            state_out(do["dwp"][:, :])
            k = 0
            load_tile(32, k)
            for b in range(NSB):
                s.dma("sp", lambda e, b=b: e.dma_start(out=Sio[:], in_=di["state_d_wkv"][b * 512:(b + 1) * 512, :].rearrange("(h v) c -> v h c", h=8)),
                      reads=[r_S], writes=[r_S])
                for h0 in (0, 4):
                    kp = (h0 // 4)
                    for hh in range(4):
                        s.op("pe", lambda e, hh=hh, h0=h0, kp=kp: e.transpose(out=ptr[kp][:, hh, 0:64], in_=Sio[:, h0 + hh, :], identity=self.ident_f[0:64, 0:64]),
                             reads=[r_S, self.rconst], writes=[r_ptr[kp]])
                    s.op("act", lambda e, h0=h0, kp=kp: e.copy(out=A[:, h0:h0 + 4, :], in_=ptr[kp][:, :, 0:64]), reads=[r_ptr[kp]], writes=r_Ah)
                for j in range(4):
                    step(k, 4 * b + j, j > 0)
                flush(k, 4 * b + 3)
                state_out(do["dws"][b * 512:(b + 1) * 512, :])
            store_y(32, k, 64)
            s.barrier()

    def mixer1_Q(self, cur, nxt):
        nc, s, di, do = self.nc, self.s, self.din, self.dout
        GN_EPS = 64e-5
        with ExitStack() as es:
            T = lambda n, sh, dt: es.enter_context(nc.sbuf_tensor(self.name(n), sh, dt))
            P = lambda n, sh, dt: es.enter_context(nc.psum_tensor(self.name(n), sh, dt))
            m = self.alloc_mod(es)
            owout = T("owout", [128, NKC, D], BF16)
            GN = T("GNc", [128, 2, 512], F32)
            gne = T("gne", [128, 1], F32)
            xt = [T("xt", [128, D], F32) for _ in range(2)]
            yv = [T("yv", [128, 3, 512], F32) for _ in range(2)]
            ft = [T("ft", [128, 2, 512], BF16) for _ in range(2)]
            fT = [T("fT", [128, 8, 128], BF16) for _ in range(2)]
            ta = T("ta", [128, 512], F32); tb = T("tb", [128, 512], F32); sm = T("smq", [128, 32], F32)
            t1 = [T("t1", [128, D], F32) for _ in range(2)]
            stt = [T("st", [128, 16], F32) for _ in range(2)]
            tpb = P("tpb", [128, 8, 128], BF16)
            po = [P("po", [128, 512], F32) for _ in range(2)]
            pmisc = P("pmisc", [128, 512], F32)
            r_c = Res(); r_x = [Res(), Res()]; r_yv = [Res(), Res()]; r_ft = [Res(), Res()]; r_fT = [Res(), Res()]; r_t = Res()
            r_t1 = [Res(), Res()]; r_st = [Res(), Res()]; r_tp = Res(); r_po = [Res(), Res()]; r_pm = Res(); r_ow = Res()
            s.dma("sp", lambda e: e.dma_start(out=owout[:], in_=self.owout_b[:, :].rearrange("(kc p) n -> p kc n", p=128)), writes=[r_ow])
            s.dma("sp", lambda e: e.dma_start(out=GN[:, 0, :], in_=di["rwkv_gn_w"][0:1, :].to_broadcast([128, 512])), writes=[r_c])
            s.dma("sp", lambda e: e.dma_start(out=GN[:, 1, :], in_=di["rwkv_gn_b"][0:1, :].to_broadcast([128, 512])), writes=[r_c])
            s.op("pool", lambda e: e.memset(gne[:], GN_EPS), writes=[r_c])
            self.load_mod(m, 1, 1, None)
            self.bcast_mod(m, False, pmisc, r_pm)
            rt = [r_t]
            v3 = lambda ap: ap.rearrange("p (h k) -> p h k", h=8)
            for i in range(NTILE):
                k = i % 2
                sample = (i == 32)
                if sample:
                    self.bcast_mod(m, True, pmisc, r_pm)
                rows = slice(i * 128, (i + 1) * 128)
                ap, nrows, res = self.x_src(False, cur, i)
                s.dma("sp", lambda e, k=k, ap=ap: e.dma_start(out=xt[k][:], in_=ap), reads=[res], writes=[r_x[k]])
                s.dma("sp", lambda e, k=k, rows=rows: e.dma_start(out=yv[k][:, 0, :], in_=self.ybuf[rows, :]), reads=[self.yres[i]], writes=[r_yv[k]])
                s.dma("sp", lambda e, k=k, rows=rows: e.dma_start(out=yv[k][:, 1, :], in_=self.rw[6, rows, :]), reads=[self.rwres[i]], writes=[r_yv[k]])
                s.dma("sp", lambda e, k=k, rows=rows: e.dma_start(out=yv[k][:, 2, :], in_=self.rw[7, rows, :]), reads=[self.rwres[i]], writes=[r_yv[k]])
                s.dma("sp", lambda e, k=k, rows=rows: e.dma_start(out=ft[k][:, 0, :], in_=self.mixbuf[rows, 0:512]), reads=[self.mixres[i]], writes=[r_ft[k]])
                Y = yv[k]
                ry = [r_yv[k]]
                s.op("dve", lambda e, Y=Y: e.tensor_reduce(out=sm[:, 0:8], in_=v3(Y[:, 0, :]), axis=AX.X, op=ALU.add), reads=ry + rt, writes=rt)
                s.op("dve", lambda e: e.tensor_scalar(out=sm[:, 0:8], in0=sm[:, 0:8], scalar1=1.0 / 64.0, scalar2=None, op0=ALU.mult), reads=rt, writes=rt)
                s.op("dve", lambda e, Y=Y: e.tensor_tensor(out=v3(ta[:]), in0=v3(Y[:, 0, :]), in1=sm[:, 0:8].unsqueeze(2).to_broadcast([128, 8, 64]), op=ALU.subtract),
                     reads=ry + rt, writes=rt)
                s.op("pool", lambda e: e.tensor_tensor(out=tb[:], in0=ta[:], in1=ta[:], op=ALU.mult), reads=rt, writes=rt)
                s.op("dve", lambda e: e.tensor_reduce(out=sm[:, 8:16], in_=v3(tb[:]), axis=AX.X, op=ALU.add), reads=rt, writes=rt)
                s.op("act", lambda e: e.activation(out=sm[:, 16:24], in_=sm[:, 8:16], func=AF.Sqrt, bias=gne[:, 0:1], scale=1.0 / 64.0), reads=rt + [r_c], writes=rt)
                s.op("dve", lambda e: e.reciprocal(out=sm[:, 24:32], in_=sm[:, 16:24]), reads=rt, writes=rt)
                s.op("dve", lambda e: e.tensor_tensor(out=v3(ta[:]), in0=v3(ta[:]), in1=sm[:, 24:32].unsqueeze(2).to_broadcast([128, 8, 64]), op=ALU.mult), reads=rt, writes=rt)
                s.op("pool", lambda e: e.tensor_tensor(out=ta[:], in0=ta[:], in1=GN[:, 0, :], op=ALU.mult), reads=rt + [r_c], writes=rt)
                s.op("pool", lambda e: e.tensor_tensor(out=ta[:], in0=ta[:], in1=GN[:, 1, :], op=ALU.add), reads=rt + [r_c], writes=rt)
                s.op("dve", lambda e, Y=Y: e.tensor_tensor(out=ta[:], in0=ta[:], in1=Y[:, 2, :], op=ALU.add), reads=rt + ry, writes=rt)
                s.op("dve", lambda e, Y=Y, k=k: e.tensor_tensor(out=ft[k][:, 1, :], in0=ta[:], in1=Y[:, 1, :], op=ALU.mult), reads=rt + ry + [r_ft[k]], writes=[r_ft[k]])
                for c in range(8):
                    s.op("pe", lambda e, k=k, c=c: e.transpose(out=tpb[:, c, :], in_=ft[k][:, c // 4, (c % 4) * 128:(c % 4 + 1) * 128],
                                                               identity=self.ident_b[:]), reads=[r_ft[k], self.rconst], writes=[r_tp])
                s.op("act", lambda e, k=k: e.copy(out=fT[k][:], in_=tpb[:]), reads=[r_tp], writes=[r_fT[k]])
                for h in range(2):
                    for kc in range(8):
                        s.op("pe", lambda e, h=h, kc=kc, k=k: e.matmul(po[h][:], lhsT=fT[k][:, kc, :], rhs=owout[:, kc, h * 512:(h + 1) * 512],
                                                                       start=(kc == 0), stop=(kc == 7)), reads=[r_fT[k], r_ow], writes=[r_po[h]])
                self.epilogue(m, [po[0][:], po[1][:]], r_po, xt[k][:], r_x[k], t1[k], r_t1[k], stt[k], r_st[k],
                              self.x_dst(False, nxt, i), None)
            s.barrier()

    def dump_dbg(self, cur):
        nc, s = self.nc, self.s
        with ExitStack() as es:
            xt = [es.enter_context(nc.sbuf_tensor(self.name("dx"), [128, D], F32)) for _ in range(2)]
            r = [Res(), Res()]
            for tile in range(NTILE):
                k = tile % 2
                s.dma("sp", lambda e, k=k, tile=tile: e.dma_start(out=xt[k][:], in_=self.XS[cur][tile * 128:(tile + 1) * 128, :]),
                      reads=[self.xres[cur][tile]], writes=[r[k]])
                s.dma("pool", lambda e, k=k, tile=tile: e.dma_start(out=self.dout["dbg"][tile * 128:(tile + 1) * 128, :], in_=xt[k][:]),
                      reads=[r[k]])
            s.barrier()

    def build(self):
        cfg = self.cfg
        self.setup_consts()
        self.cast_weights()
        cur = 0
        nsub = cfg.get("nsub", 6)
        isub_g = 0
        for l in range(DEPTH):
            self.adaln(l)
            for isub in range(3):
                if isub_g >= nsub:
                    break
                first = (isub_g == 0)
                last = (isub_g == 5)
                nxt = 1 - cur
                if isub == 1 and l == 0 and not cfg.get("stub0"):
                    parts = cfg.get("parts", "ASB")
                    if "A" in parts:
                        self.mixer0_A(cur)
                    if "S" in parts:
                        self.mixer0_S()
                    if "B" in parts:
                        self.mixer0_B(cur, nxt)
                    else:
                        self.stub_mixer(l, cur, nxt)
                elif isub == 1:
                    if not cfg.get("stub1"):
                        self.mixer1_A(cur)
                        if not cfg.get("nodil"):
                            self.mixer1_S()
                    if cfg.get("stub1") or cfg.get("norwkv"):
                        self.stub_mixer(l, cur, nxt)
                    else:
                        self.mixer1_P()
                        self.mixer1_R()
                        self.mixer1_Q(cur, nxt)
                else:
                    self.ffn_sublayer(l, 0 if isub == 0 else 1, isub, first, last, cur, nxt)
                cur = nxt
                isub_g += 1
        if cfg.get("dbg"):
            self.dump_dbg(cur)
        self.s.barrier()
        self.s.emit()
        return self.nc


_OUT_SHAPES = None


def make_in_maps(inputs, cfg):
    maps = []
    f32 = lambda a: np.ascontiguousarray(np.asarray(a, dtype=np.float32))
    shared = {
        "ada_w": f32(inputs["ada_w"]), "ada_b": f32(inputs["ada_b"]),
        "ln_g": f32(inputs["ln_g"]), "ln_b": f32(inputs["ln_b"]),
        "ffn_w_gate": f32(inputs["ffn_w_gate"]).reshape(4 * D, DFF),
        "ffn_w_up": f32(inputs["ffn_w_up"]).reshape(4 * D, DFF),
        "ffn_w_down": f32(inputs["ffn_w_down"]).reshape(4 * DFF, D),
        "even_w_in": f32(inputs["even_w_in"]).reshape(D, 2048),
        "even_w_out": f32(inputs["even_w_out"]).reshape(D, D),
        "odd_w_in": f32(inputs["odd_w_in"]).reshape(D, 3328),
        "odd_w_out": f32(inputs["odd_w_out"]).reshape(D, D),
        "diff_lambda": f32(inputs["diff_lambda"]).reshape(1, 256), "diff_subln": f32(inputs["diff_subln"]).reshape(1, 128),
        "s5_a_re": f32(inputs["s5_a_re"]).reshape(32, 64), "s5_a_im": f32(inputs["s5_a_im"]).reshape(32, 64),
        "s5_log_dt": f32(inputs["s5_log_dt"]).reshape(1, 32),
        "s5_b_re": f32(inputs["s5_b_re"]).reshape(32, 64, 16), "s5_b_im": f32(inputs["s5_b_im"]).reshape(32, 64, 16),
        "s5_c_re": f32(inputs["s5_c_re"]).reshape(32, 16, 64), "s5_c_im": f32(inputs["s5_c_im"]).reshape(32, 16, 64),
        "s5_d": f32(inputs["s5_d"]).reshape(1, 512), "s5_glu_w": f32(inputs["s5_glu_w"]).reshape(512, 512),
        "s5_glu_b": f32(inputs["s5_glu_b"]).reshape(1, 512),
        "cache_a_k": f32(inputs["cache_a_k"]).reshape(2560 * 128, 512), "cache_a_v": f32(inputs["cache_a_v"]).reshape(2560 * 128, 512),
    }
    cck = f32(inputs["cache_c_k"]).reshape(128, 2048, 512); ccv = f32(inputs["cache_c_v"]).reshape(128, 2048, 512)
    for nm in ("rwkv_mu", "rwkv_w0", "rwkv_a0", "rwkv_k_k", "rwkv_k_a", "rwkv_r_k", "rwkv_gn_w", "rwkv_gn_b"):
        shared[nm] = f32(inputs[nm]).reshape(1, -1)
    shared["rwkv_w2"] = f32(inputs["rwkv_w2"]).reshape(64, 512); shared["rwkv_a2"] = f32(inputs["rwkv_a2"]).reshape(64, 512)
    shared["rwkv_g2"] = f32(inputs["rwkv_g2"]).reshape(128, 512)
    dwkv = f32(inputs["state_d_wkv"]).reshape(128, 512, 64); dsh = f32(inputs["state_d_shift"]).reshape(128, 1792)
    s5r = f32(inputs["state_s5_re"]).reshape(128, 2048); s5i = f32(inputs["state_s5_im"]).reshape(128, 2048)
    ptab = np.ascontiguousarray(np.asarray(inputs["page_table"], dtype=np.int32))
    xp = f32(inputs["x_prompt"]); xs = f32(inputs["x_sample"])
    cp = f32(inputs["c_prompt"]); cs = f32(inputs["c_sample"])
    if "S" not in cfg.get("parts", "ASB"):
        shared["cache_a_k"] = shared["cache_a_k"][:128]
        shared["cache_a_v"] = shared["cache_a_v"][:128]
    for c in range(N_CORES):
        b = c % 4
        mp = dict(shared)
        mp["xp"] = xp[b]
        mp["xs"] = xs[c * NSB:(c + 1) * NSB].reshape(NS, D)
        mp["c17"] = np.ascontiguousarray(np.concatenate([cp[b:b + 1], cs[c * NSB:(c + 1) * NSB]], axis=0))
        mp["state_s5_re"] = s5r[c * NSB:(c + 1) * NSB]; mp["state_s5_im"] = s5i[c * NSB:(c + 1) * NSB]
        mp["page_table"] = ptab[c * NSB:(c + 1) * NSB].reshape(1, NSB * 16)
        mp["state_d_wkv"] = dwkv[c * NSB:(c + 1) * NSB].reshape(NSB * 512, 64)
        mp["state_d_shift"] = dsh[c * NSB:(c + 1) * NSB]
        mp["cache_c_k"] = cck[c * NSB:(c + 1) * NSB].reshape(NSB * 2048, 512)
        mp["cache_c_v"] = ccv[c * NSB:(c + 1) * NSB].reshape(NSB * 2048, 512)
        maps.append(mp)
    return maps


CFG = {}


def kernel(**inputs):
    cfg = dict(CFG)
    nc = Builder(cfg).build()
    maps = make_in_maps(inputs, cfg)
    res = run_bass_kernel_spmd(nc, maps, core_ids=list(range(N_CORES)))
    R = res.results
    yp = np.stack([R[b]["yp"] for b in range(4)], 0)
    ys = np.concatenate([R[c]["ys"].reshape(NSB, 4, D) for c in range(N_CORES)], 0)
    E, O, B, DB = 1, 1, 4, 128
    z = lambda *sh: np.zeros(sh, np.float32)
    catp = lambda k, sh: np.stack([R[b][k] for b in range(4)], 0).reshape(sh)
    cats = lambda k, sh: np.concatenate([R[c][k] for c in range(N_CORES)], 0).reshape(sh)
    return (yp, ys,
            catp("akp", (E, B, SEQ, 4, 2, 64)), cats("aks", (E, DB, 4, 4, 2, 64)),
            catp("avp", (E, B, SEQ, 4, 128)), cats("avs", (E, DB, 4, 4, 128)),
            catp("s5rp", (E, B, 32, 64)), cats("s5rs", (E, DB, 32, 64)), catp("s5ip", (E, B, 32, 64)), cats("s5is", (E, DB, 32, 64)),
            catp("ckp", (O, B, 2048, 8, 64)), cats("cks", (O, DB, 4, 8, 64)), catp("cvp", (O, B, 2048, 8, 64)), cats("cvs", (O, DB, 4, 8, 64)),
            catp("dwp", (O, B, 8, 64, 64)), cats("dws", (O, DB, 8, 64, 64)), catp("dsp", (O, B, 1792)), cats("dss", (O, DB, 1792)))
```
